# Optimizing a Trainium2 kernel written in Bass

```python
import math
import jax, jax.numpy as jnp
from jax import lax
import numpy as np

D_MODEL = 1024
BATCH = 8
SEQ = 4096
DEPTH = 1
DEC_BATCH = 128
DEC_SEQ = 1
PAST_LEN = 16384
PAGE_SIZE = 128

HEAD_DIM = 64
A_HEADS = D_MODEL // HEAD_DIM // 2
B_HEADS = D_MODEL // HEAD_DIM // 2
B_KV_HEADS = 2
B_GROUP = B_HEADS // B_KV_HEADS
A_W = A_HEADS * HEAD_DIM
B_W = B_HEADS * HEAD_DIM
B_KV_W = B_KV_HEADS * HEAD_DIM
IN_SPLITS = (A_W, A_W, A_W, B_W, B_KV_W, B_KV_W)
IN_DIM = sum(IN_SPLITS)
DIL_GROUPS = ((128, 1), (512, 4), (2048, 16))
A_WINDOW = 2048
B_WINDOW = 128
BLOCK = 128
ROPE_THETA = 10000.0
MEM_TOKENS = 256
X_HEADS = 4
X_HEAD_DIM = D_MODEL // X_HEADS
D_FF = ((8 * D_MODEL // 3 + 127) // 128) * 128
CONV_W = 3
EPS = 1e-6
NEG = -1e30

kernel_name = "hybrid_dilated_swa_sink_memory_convffn_step"


def rms_norm(x, g):
    xf = x.astype(jnp.float32)
    y = xf * lax.rsqrt(jnp.mean(xf * xf, axis=-1, keepdims=True) + EPS)
    return (y * g.astype(jnp.float32)).astype(x.dtype)


def rope(x, pos):
    half = HEAD_DIM // 2
    inv = jnp.power(ROPE_THETA, -jnp.arange(half, dtype=jnp.float32) / half)
    ang = pos.astype(jnp.float32)[:, None] * inv[None, :]
    cos = jnp.cos(ang)[:, None, :]
    sin = jnp.sin(ang)[:, None, :]
    xf = x.astype(jnp.float32)
    x1, x2 = xf[..., :half], xf[..., half:]
    return jnp.concatenate([x1 * cos - x2 * sin, x2 * cos + x1 * sin], axis=-1).astype(x.dtype)


def banded_attention(q, k, v, window):
    n, L = q.shape[:2]
    nb = -(-L // BLOCK)
    pad = nb * BLOCK - L
    padl = lambda a: jnp.pad(a, ((0, 0), (0, pad)) + ((0, 0),) * (a.ndim - 2))
    q, k, v = padl(q), padl(k), padl(v)
    qb = q.reshape((n, nb, BLOCK) + q.shape[2:])
    kb = k.reshape((n, nb, BLOCK) + k.shape[2:])
    vb = v.reshape((n, nb, BLOCK) + v.shape[2:])

    def with_prev(a):
        prev = jnp.pad(a, ((0, 0), (1, 0)) + ((0, 0),) * (a.ndim - 2))[:, :-1]
        return jnp.concatenate([prev, a], axis=2)

    kk, vv = with_prev(kb), with_prev(vb)
    scale = q.shape[-1] ** -0.5
    s = jnp.einsum('nbqhgd,nbkhd->nbhgqk', qb, kk, preferred_element_type=jnp.float32) * scale
    qpos = jnp.arange(BLOCK)[:, None] + BLOCK
    kpos = jnp.arange(2 * BLOCK)[None, :]
    dist = qpos - kpos
    band = (dist >= 0) & (dist <= window)
    first = (jnp.arange(nb) == 0)[:, None, None]
    valid = band[None] & ~(first & (kpos < BLOCK)[None])
    s = jnp.where(valid[None, :, None, None], s, NEG)
    lse = jax.nn.logsumexp(s, axis=-1)
    p = jnp.exp(s - lse[..., None])
    o = jnp.einsum('nbhgqk,nbkhd->nbqhgd', p.astype(vv.dtype), vv)
    o = o.reshape((n, nb * BLOCK) + o.shape[3:])[:, :L]
    lse = jnp.moveaxis(lse, 4, 2).reshape((n, nb * BLOCK) + lse.shape[2:4])[:, :L]
    return o, lse


def combine_by_denominator(o, lse, axis):
    w = jax.nn.softmax(lse, axis=axis)
    return jnp.sum(o.astype(jnp.float32) * w[..., None], axis=axis).astype(o.dtype)


def dilated_prompt(q, k, v):
    B, S, H, D = q.shape
    outs, lses = [], []
    for w, d in DIL_GROUPS:
        to_res = lambda a: a.reshape(B, S // d, d, H, D).transpose(0, 2, 1, 3, 4).reshape(B * d, S // d, H, D)
        o, lse = banded_attention(to_res(q)[:, :, :, None], to_res(k), to_res(v), w // d)
        o = o[:, :, :, 0].reshape(B, d, S // d, H, D).transpose(0, 2, 1, 3, 4).reshape(B, S, H, D)
        lse = lse[..., 0].reshape(B, d, S // d, H).transpose(0, 2, 1, 3).reshape(B, S, H)
        outs.append(o)
        lses.append(lse)
    return combine_by_denominator(jnp.stack(outs), jnp.stack(lses), axis=0)


def dilated_sample(q, kc, vc):
    T = q.shape[1]
    L = kc.shape[1] - T
    idx = np.stack([L + np.arange(T)[:, None] - (np.arange(w // d + 1) * d)[None, :] for w, d in DIL_GROUPS])
    valid = jnp.asarray(idx >= 0)
    idx_c = np.maximum(idx, 0)
    kg = kc[:, idx_c]
    vg = vc[:, idx_c]
    s = jnp.einsum('nthd,ngtkhd->ngthk', q, kg, preferred_element_type=jnp.float32) * (HEAD_DIM ** -0.5)
    s = jnp.where(valid[None, :, :, None, :], s, NEG)
    lse = jax.nn.logsumexp(s, axis=-1)
    p = jnp.exp(s - lse[..., None])
    o = jnp.einsum('ngthk,ngtkhd->ngthd', p.astype(vg.dtype), vg)
    return combine_by_denominator(o, lse, axis=1)


def apply_sinks(o, lse, sinks):
    f = jax.nn.sigmoid(lse - sinks.reshape(B_KV_HEADS, B_GROUP).astype(jnp.float32))
    return o * f[..., None].astype(o.dtype)


def swa_sample(q, kc, vc, sinks):
    T = q.shape[1]
    L = kc.shape[1] - T
    dist = jnp.arange(T)[:, None] - (jnp.arange(L + T) - L)[None, :]
    valid = (dist >= 0) & (dist <= B_WINDOW)
    s = jnp.einsum('nthgd,nkhd->nhgtk', q, kc, preferred_element_type=jnp.float32) * (HEAD_DIM ** -0.5)
    s = jnp.where(valid[None, None, None], s, NEG)
    lse = jax.nn.logsumexp(s, axis=-1)
    p = jnp.exp(s - lse[..., None])
    o = jnp.einsum('nhgtk,nkhd->nthgd', p.astype(vc.dtype), vc)
    return apply_sinks(o, jnp.moveaxis(lse, 3, 1), sinks)


def project_mixers(h, w_in, pos):
    N, T, _ = h.shape
    z = h @ w_in
    points = [int(c) for c in np.cumsum(IN_SPLITS)[:-1]]
    qa, ka, va, qb, kb, vb = jnp.split(z, points, axis=-1)
    qa = rope(qa.reshape(N, T, A_HEADS, HEAD_DIM), pos)
    ka = rope(ka.reshape(N, T, A_HEADS, HEAD_DIM), pos)
    va = va.reshape(N, T, A_HEADS, HEAD_DIM)
    qb = rope(qb.reshape(N, T, B_HEADS, HEAD_DIM), pos).reshape(N, T, B_KV_HEADS, B_GROUP, HEAD_DIM)
    kb = rope(kb.reshape(N, T, B_KV_HEADS, HEAD_DIM), pos)
    vb = vb.reshape(N, T, B_KV_HEADS, HEAD_DIM)
    return qa, ka, va, qb, kb, vb


def mixers_out(oa, ob, g_out_a, g_out_b, w_out):
    N, T = oa.shape[:2]
    oa = rms_norm(oa.reshape(N, T, A_W), g_out_a)
    ob = rms_norm(ob.reshape(N, T, B_W), g_out_b)
    return jnp.concatenate([oa, ob], axis=-1) @ w_out


def memory_kv(mem, g_mem, w_mem_kv):
    N, M, _ = mem.shape
    kv = rms_norm(mem, g_mem) @ w_mem_kv
    k, v = jnp.split(kv, 2, axis=-1)
    return k.reshape(N, M, X_HEADS, X_HEAD_DIM), v.reshape(N, M, X_HEADS, X_HEAD_DIM)


def cross_attention(h, mk, mv, w_xq, w_xo):
    N, T, _ = h.shape
    q = (h @ w_xq).reshape(N, T, X_HEADS, X_HEAD_DIM)
    s = jnp.einsum('nthd,nmhd->nhtm', q, mk, preferred_element_type=jnp.float32) * (X_HEAD_DIM ** -0.5)
    p = jax.nn.softmax(s, axis=-1)
    o = jnp.einsum('nhtm,nmhd->nthd', p.astype(mv.dtype), mv).reshape(N, T, D_MODEL)
    return o @ w_xo


def conv_ffn(h, conv_state, w_up, conv_w, conv_b, w_down):
    T = h.shape[1]
    gate, val = jnp.split(h @ w_up, 2, axis=-1)
    gx = jnp.concatenate([conv_state, gate], axis=1)
    conv = conv_b + sum(gx[:, i:i + T] * conv_w[i] for i in range(CONV_W))
    y = (jax.nn.silu(conv) * val) @ w_down
    return y, gx[:, -(CONV_W - 1):]


def prompt_layer(x, mem, g_mix, w_in, g_out_a, g_out_b, sinks, w_out, g_cross, g_mem, w_xq, w_mem_kv, w_xo,
                 g_ffn, w_up, conv_w, conv_b, w_down):
    N, S, _ = x.shape
    pos = jnp.arange(S, dtype=jnp.int32)
    qa, ka, va, qb, kb, vb = project_mixers(rms_norm(x, g_mix), w_in, pos)
    oa = dilated_prompt(qa, ka, va)
    ob, lse_b = banded_attention(qb, kb, vb, B_WINDOW)
    ob = apply_sinks(ob, lse_b, sinks)
    x = x + mixers_out(oa, ob, g_out_a, g_out_b, w_out)
    mk, mv = memory_kv(mem, g_mem, w_mem_kv)
    x = x + cross_attention(rms_norm(x, g_cross), mk, mv, w_xq, w_xo)
    y, conv_state = conv_ffn(rms_norm(x, g_ffn), jnp.zeros((N, CONV_W - 1, D_FF), x.dtype), w_up, conv_w, conv_b, w_down)
    x = x + y
    la, lb = min(A_WINDOW, S), min(B_WINDOW, S)
    return x, (ka[:, S - la:], va[:, S - la:], kb[:, S - lb:], vb[:, S - lb:], mk, mv, conv_state)


def sample_layer(x, a_k, a_v, b_k, b_v, mem_k, mem_v, conv_state, g_mix, w_in, g_out_a, g_out_b, sinks, w_out,
                 g_cross, w_xq, w_xo, g_ffn, w_up, conv_w, conv_b, w_down):
    N, T, _ = x.shape
    pos = PAST_LEN + jnp.arange(T, dtype=jnp.int32)
    qa, ka, va, qb, kb, vb = project_mixers(rms_norm(x, g_mix), w_in, pos)
    kca, vca = jnp.concatenate([a_k, ka], axis=1), jnp.concatenate([a_v, va], axis=1)
    kcb, vcb = jnp.concatenate([b_k, kb], axis=1), jnp.concatenate([b_v, vb], axis=1)
    oa = dilated_sample(qa, kca, vca)
    ob = swa_sample(qb, kcb, vcb, sinks)
    x = x + mixers_out(oa, ob, g_out_a, g_out_b, w_out)
    x = x + cross_attention(rms_norm(x, g_cross), mem_k, mem_v, w_xq, w_xo)
    y, new_conv = conv_ffn(rms_norm(x, g_ffn), conv_state, w_up, conv_w, conv_b, w_down)
    x = x + y
    la, lb = a_k.shape[1], b_k.shape[1]
    return x, (kca[:, -la:], vca[:, -la:], kcb[:, -lb:], vcb[:, -lb:], new_conv)


def setup_inputs(seed: int = 0) -> dict:
    key = jax.random.key(seed)
    ks = iter(jax.random.split(key, 40))
    nrm = lambda shape, scale: jax.random.normal(next(ks), shape, jnp.float32) * scale
    gain = lambda shape: 1.0 + nrm(shape, 0.02)
    la, lb = min(A_WINDOW, PAST_LEN), min(B_WINDOW, PAST_LEN)
    return {
        "x_prompt": nrm((BATCH, SEQ, D_MODEL), 1.0),
        "x_sample": nrm((DEC_BATCH, DEC_SEQ, D_MODEL), 1.0),
        "cache_a_k": nrm((DEPTH, DEC_BATCH, la, A_HEADS, HEAD_DIM), 1.0),
        "cache_a_v": nrm((DEPTH, DEC_BATCH, la, A_HEADS, HEAD_DIM), 1.0),
        "cache_b_k": nrm((DEPTH, DEC_BATCH, lb, B_KV_HEADS, HEAD_DIM), 1.0),
        "cache_b_v": nrm((DEPTH, DEC_BATCH, lb, B_KV_HEADS, HEAD_DIM), 1.0),
        "cache_mem_k": nrm((DEPTH, DEC_BATCH, MEM_TOKENS, X_HEADS, X_HEAD_DIM), 1.0),
        "cache_mem_v": nrm((DEPTH, DEC_BATCH, MEM_TOKENS, X_HEADS, X_HEAD_DIM), 1.0),
        "state_conv": nrm((DEPTH, DEC_BATCH, CONV_W - 1, D_FF), 1.0),
        "mem_prompt": nrm((BATCH, MEM_TOKENS, D_MODEL), 1.0),
        "g_mix": gain((DEPTH, D_MODEL)),
        "w_in": nrm((DEPTH, D_MODEL, IN_DIM), D_MODEL ** -0.5),
        "g_out_a": gain((DEPTH, A_W)),
        "g_out_b": gain((DEPTH, B_W)),
        "sinks": nrm((DEPTH, B_HEADS), 0.5),
        "w_out": nrm((DEPTH, A_W + B_W, D_MODEL), (A_W + B_W) ** -0.5),
        "g_cross": gain((DEPTH, D_MODEL)),
        "g_mem": gain((DEPTH, D_MODEL)),
        "w_xq": nrm((DEPTH, D_MODEL, X_HEADS * X_HEAD_DIM), D_MODEL ** -0.5),
        "w_mem_kv": nrm((DEPTH, D_MODEL, 2 * X_HEADS * X_HEAD_DIM), D_MODEL ** -0.5),
        "w_xo": nrm((DEPTH, X_HEADS * X_HEAD_DIM, D_MODEL), (X_HEADS * X_HEAD_DIM) ** -0.5),
        "g_ffn": gain((DEPTH, D_MODEL)),
        "w_up": nrm((DEPTH, D_MODEL, 2 * D_FF), D_MODEL ** -0.5),
        "conv_w": nrm((DEPTH, CONV_W, D_FF), CONV_W ** -0.5),
        "conv_b": nrm((DEPTH, D_FF), 0.01),
        "w_down": nrm((DEPTH, D_FF, D_MODEL), D_FF ** -0.5),
        "g_final": gain((D_MODEL,)),
    }


def reference(x_prompt, x_sample, cache_a_k, cache_a_v, cache_b_k, cache_b_v, cache_mem_k, cache_mem_v, state_conv,
              mem_prompt, g_mix, w_in, g_out_a, g_out_b, sinks, w_out, g_cross, g_mem, w_xq, w_mem_kv, w_xo,
              g_ffn, w_up, conv_w, conv_b, w_down, g_final):
    hp, hs = x_prompt, x_sample
    p_states, s_states = [], []
    for l in range(DEPTH):
        hp, ps = prompt_layer(hp, mem_prompt, g_mix[l], w_in[l], g_out_a[l], g_out_b[l], sinks[l], w_out[l],
                              g_cross[l], g_mem[l], w_xq[l], w_mem_kv[l], w_xo[l], g_ffn[l], w_up[l], conv_w[l],
                              conv_b[l], w_down[l])
        hs, ss = sample_layer(hs, cache_a_k[l], cache_a_v[l], cache_b_k[l], cache_b_v[l], cache_mem_k[l],
                              cache_mem_v[l], state_conv[l], g_mix[l], w_in[l], g_out_a[l], g_out_b[l], sinks[l],
                              w_out[l], g_cross[l], w_xq[l], w_xo[l], g_ffn[l], w_up[l], conv_w[l], conv_b[l],
                              w_down[l])
        p_states.append(ps)
        s_states.append(ss)
    stack = lambda states, i: jnp.stack([st[i] for st in states])
    y_prompt = rms_norm(hp, g_final)
    y_sample = rms_norm(hs, g_final)
    p_a_k, p_a_v, p_b_k, p_b_v = stack(p_states, 0), stack(p_states, 1), stack(p_states, 2), stack(p_states, 3)
    p_mem_k, p_mem_v, p_conv = stack(p_states, 4), stack(p_states, 5), stack(p_states, 6)
    s_a_k, s_a_v, s_b_k, s_b_v = stack(s_states, 0), stack(s_states, 1), stack(s_states, 2), stack(s_states, 3)
    s_conv = stack(s_states, 4)
    return (y_prompt, y_sample, p_a_k, p_a_v, p_b_k, p_b_v, p_mem_k, p_mem_v, p_conv, s_a_k, s_a_v, s_b_k, s_b_v, s_conv)
```

```python
import numpy as np
from contextlib import ExitStack
import ml_dtypes
import concourse.bass as bass
import concourse.mybir as mybir
from concourse.bass_utils import run_bass_kernel_spmd

F32 = mybir.dt.float32
BF16 = mybir.dt.bfloat16
ALU = mybir.AluOpType
AF = mybir.ActivationFunctionType
AX = mybir.AxisListType

NCORES = 8
SEQ = 4096
DM = 1024
NT = SEQ // 128
NS = 16
LA = 2048
LB = 128
MEM = 256
DFF = 2816
NFC = DFF // 128
IN_DIM = 2304
EPS = 1e-6
PAST = 16384


class V:
    __slots__ = ("ap", "res")

    def __init__(self, ap, res):
        self.ap = ap
        self.res = res

    def __getitem__(self, k):
        return V(self.ap[k], self.res)

    def re(self, s, **kw):
        return V(self.ap.rearrange(s, **kw), self.res)

    def bc(self, shape):
        return V(self.ap.to_broadcast(shape), self.res)

    def cast(self, dt):
        return V(self.ap.bitcast(dt), self.res)


def _keys(v):
    if v is None:
        return []
    r = v.res
    if isinstance(r, list):
        return r
    return [r]


class Prog:
    ENGS = ("pe", "act", "dve", "pool", "sp")

    def __init__(self):
        self.nc = bass.Bass("TRN2", target_bir_lowering=False)
        self.es = ExitStack()
        self.ins = []
        self.res = {}
        self.out_tokens = []

    def dram(self, name, shape, dt, kind):
        t = self.nc.dram_tensor(name, list(shape), dt, kind=kind)
        return V(t.ap(), ("dram", name))

    def sbuf(self, name, shape, dt):
        t = self.es.enter_context(self.nc.sbuf_tensor(name, list(shape), dt))
        return V(t[:], ("sb", name))

    def psum(self, name, shape, dt):
        t = self.es.enter_context(self.nc.psum_tensor(name, list(shape), dt))
        return V(t[:], ("ps", name))

    def _rec(self, eng, emit, reads, writes, dma_sem=None, track_dram=False):
        iid = len(self.ins)
        is_dma = dma_sem is not None
        deps = {}

        def add_dep(tok, kind):
            if tok is None:
                return
            pr = self.ins[tok]
            if (not is_dma) and (pr["dma_sem"] is None) and pr["eng"] == eng:
                if eng == "pe":
                    return
                if eng in ("act", "dve") and kind == 2:
                    return
            deps[tok] = True

        rk = [k for v in reads for k in _keys(v)]
        wk = [k for v in writes for k in _keys(v)]
        if not track_dram:
            rk = [k for k in rk if k[0] != "dram"]
            wk = [k for k in wk if k[0] != "dram"]
        for r in rk:
            st = self.res.setdefault(r, {"w": None, "r": {}})
            add_dep(st["w"], 0)
        for w in wk:
            st = self.res.setdefault(w, {"w": None, "r": {}})
            add_dep(st["w"], 1)
            for t in st["r"].values():
                add_dep(t, 2)
        semkey = ("dma", dma_sem) if is_dma else ("eng", eng)
        for r in rk:
            self.res[r]["r"][semkey] = iid
        for w in wk:
            self.res[w]["w"] = iid
            self.res[w]["r"] = {}
        self.ins.append({"eng": eng, "emit": emit, "deps": list(deps), "dma_sem": dma_sem,
                         "flag": is_dma, "semkey": semkey, "val": None})
        for d in deps:
            self.ins[d]["flag"] = True
        return iid

    def mm(self, out, lhsT, rhs, start=True, stop=True):
        return self._rec("pe", lambda e: e.matmul(out.ap, lhsT.ap, rhs.ap, start=start, stop=stop),
                         [lhsT, rhs], [out])

    def tr(self, out, in_, ident):
        return self._rec("pe", lambda e: e.transpose(out.ap, in_.ap, ident.ap), [in_, ident], [out])

    def act(self, out, in_, func, bias=None, scale=1.0, accum=None):
        kw = {}
        rd = [in_]
        if bias is not None:
            if isinstance(bias, V):
                kw["bias"] = bias.ap
                rd.append(bias)
            else:
                kw["bias"] = bias
        if isinstance(scale, V):
            kw["scale"] = scale.ap
            rd.append(scale)
        else:
            kw["scale"] = scale
        wr = [out]
        if accum is not None:
            kw["accum_out"] = accum.ap
            wr.append(accum)
        return self._rec("act", lambda e: e.activation(out.ap, in_.ap, func, **kw), rd, wr)

    def tt(self, eng, out, a, b, op):
        return self._rec(eng, lambda e: e.tensor_tensor(out.ap, a.ap, b.ap, op), [a, b], [out])

    def ts(self, eng, out, a, s1, s2, op0, op1=None, accum=None):
        rd = [a]
        s1a = s1.ap if isinstance(s1, V) else s1
        s2a = s2.ap if isinstance(s2, V) else s2
        if isinstance(s1, V):
            rd.append(s1)
        if isinstance(s2, V):
            rd.append(s2)
        wr = [out]
        kw = {}
        if op1 is not None:
            kw["op1"] = op1
        if accum is not None:
            kw["accum_out"] = accum.ap
            wr.append(accum)
        return self._rec(eng, lambda e: e.tensor_scalar(out.ap, a.ap, s1a, s2a, op0, **kw), rd, wr)

    def stt(self, eng, out, a, s, b, op0, op1):
        rd = [a, b]
        sa = s.ap if isinstance(s, V) else s
        if isinstance(s, V):
            rd.append(s)
        return self._rec(eng, lambda e: e.scalar_tensor_tensor(out.ap, a.ap, sa, b.ap, op0, op1), rd, [out])

    def copy(self, eng, out, in_):
        if eng == "act":
            return self._rec("act", lambda e: e.copy(out.ap, in_.ap), [in_], [out])
        return self._rec(eng, lambda e: e.tensor_copy(out.ap, in_.ap), [in_], [out])

    def memset(self, eng, out, val):
        return self._rec(eng, lambda e: e.memset(out.ap, val), [], [out])

    def reduce(self, eng, out, in_, op, axis=AX.X):
        return self._rec(eng, lambda e: e.tensor_reduce(out.ap, in_.ap, axis, op), [in_], [out])

    def recip(self, out, in_):
        return self._rec("dve", lambda e: e.reciprocal(out.ap, in_.ap), [in_], [out])

    def dma(self, q, out, in_, sem, is_output=False, **kw):
        iid = self._rec(q, lambda e: e.dma_start(out=out.ap, in_=in_.ap, **kw), [in_], [out], dma_sem=sem)
        if is_output:
            self.out_tokens.append(iid)
        return iid

    def wait_for(self, eng, tokens):
        iid = self._rec(eng, None, [], [])
        self.ins[iid]["deps"] = list(tokens)
        for d in tokens:
            self.ins[d]["flag"] = True
        return iid

    def build(self):
        nc = self.nc
        self.wait_for("sp", list(self.out_tokens))
        counts = {}
        for r in self.ins:
            if r["flag"]:
                k = r["semkey"]
                inc = 16 if r["dma_sem"] is not None else 1
                counts[k] = counts.get(k, 0) + inc
                r["val"] = counts[k]
        semh = {}
        for k in counts:
            nm = "s_" + "_".join(str(x) for x in k)
            semh[k] = self.es.enter_context(nc.semaphore(nm))
        per_eng = {e: [] for e in self.ENGS}
        for r in self.ins:
            per_eng[r["eng"]].append(r)
        ins = self.ins
        stats = {e: [0, 0] for e in self.ENGS}

        def run(engname, eobj):
            waited = {}
            for r in per_eng[engname]:
                need = {}
                for d in r["deps"]:
                    pr = ins[d]
                    k = pr["semkey"]
                    v = pr["val"]
                    if waited.get(k, 0) >= v:
                        continue
                    if need.get(k, 0) < v:
                        need[k] = v
                for k, v in need.items():
                    eobj.wait_ge(semh[k], v)
                    waited[k] = v
                    stats[engname][1] += 1
                if r["emit"] is not None:
                    bi = r["emit"](eobj)
                    stats[engname][0] += 1
                    if r["flag"]:
                        bi.then_inc(semh[r["semkey"]], 16 if r["dma_sem"] is not None else 1)

        with nc.Block() as block:
            @block.tensor
            def _(e):
                run("pe", e)

            @block.scalar
            def _(e):
                run("act", e)

            @block.vector
            def _(e):
                run("dve", e)

            @block.gpsimd
            def _(e):
                run("pool", e)

            @block.sync
            def _(e):
                run("sp", e)
        self.stats = stats
        self.sem_counts = counts
        self.es.close()
        return nc


class Arena:
    def __init__(self, p, nbytes):
        self.p = p
        self.nbytes = nbytes
        self.base = p.sbuf("arena", [128, nbytes // 2], BF16)
        self.regs = []
        self.n = 0

    def carve(self, off, shape, dt, name):
        esz = 4 if dt == F32 else 2
        n = int(np.prod(shape[1:]))
        nb = n * esz
        assert off % 4 == 0 and off + nb <= self.nbytes, (name, off, nb, self.nbytes)
        a = self.base.ap[0:shape[0], off // 2:(off + nb) // 2]
        if dt == F32:
            a = a.bitcast(F32)
        if len(shape) == 3:
            a = a.rearrange("p (a b) -> p a b", a=shape[1])
        elif len(shape) == 4:
            a = a.rearrange("p (a b c) -> p a b c", a=shape[1], b=shape[2])
        key = ("ar", name, self.n)
        self.n += 1
        inherit = {}
        for (s, e, k) in self.regs:
            if s < off + nb and off < e:
                st = self.p.res.get(k)
                if st:
                    toks = list(st["r"].values())
                    if st["w"] is not None:
                        toks.append(st["w"])
                    for t in toks:
                        sk = self.p.ins[t]["semkey"]
                        if sk not in inherit or inherit[sk] < t:
                            inherit[sk] = t
        self.regs.append((off, off + nb, key))
        self.p.res[key] = {"w": None, "r": inherit}
        return V(a, key)


def subview(arena, v, tag, ap=None):
    key = (v.res, tag)
    p = arena.p
    par = p.res.get(v.res, {"w": None, "r": {}})
    r = dict(par["r"])
    if par["w"] is not None:
        sk = p.ins[par["w"]]["semkey"]
        if sk not in r or r[sk] < par["w"]:
            r[sk] = par["w"]
    p.res[key] = {"w": None, "r": r}
    for (s_, e_, k_) in list(arena.regs):
        if k_ == v.res:
            arena.regs.append((s_, e_, key))
            break
    return V(v.ap if ap is None else ap, key)


class Bump:
    def __init__(self, arena, start, end=None):
        self.a = arena
        self.off = start
        self.end = end if end is not None else arena.nbytes

    def alloc(self, shape, dt, name):
        esz = 4 if dt == F32 else 2
        nb = int(np.prod(shape[1:])) * esz
        nb4 = (nb + 3) // 4 * 4
        assert self.off + nb4 <= self.end, ("bump overflow", name, self.off, nb4, self.end)
        v = self.a.carve(self.off, shape, dt, name)
        self.off += nb4
        return v


def bc_mid(v, h):
    shp = list(v.ap.shape)
    return V(v.ap.unsqueeze(1).to_broadcast([shp[0], h, shp[1]]), v.res)


def bc_last(v, n):
    shp = list(v.ap.shape)
    return V(v.ap.unsqueeze(2).to_broadcast([shp[0], shp[1], n]), v.res)


class KB:
    ARENA = 204800
    PERS = 34816

    def __init__(self, stage=99):
        self.stage = stage
        p = self.p = Prog()
        self.D = D = {}

        def din(name, shape, dt=F32):
            D[name] = p.dram(name, shape, dt, "ExternalInput")

        def dout(name, shape):
            D[name] = p.dram(name, shape, F32, "ExternalOutput")

        def dscr(name, shape, dt):
            D[name] = p.dram(name, shape, dt, "Internal")

        din("xp", [SEQ, DM]); din("xs", [NS, DM])
        din("cak", [NS, LA, 512]); din("cav", [NS, LA, 512])
        din("cbk", [NS, LB, 128]); din("cbv", [NS, LB, 128])
        din("cmk", [NS, MEM, DM]); din("cmv", [NS, MEM, DM])
        din("sconv", [NS, 2, DFF]); din("memp", [MEM, DM])
        din("g_mix", [DM]); din("w_in", [DM, IN_DIM]); din("g_out_a", [512]); din("g_out_b", [512])
        din("sinks", [8]); din("w_out", [DM, DM]); din("g_cross", [DM]); din("g_mem", [DM])
        din("w_xq", [DM, DM]); din("w_mem_kv", [DM, 2 * DM]); din("w_xo", [DM, DM]); din("g_ffn", [DM])
        din("w_up", [DM, 2 * DFF]); din("conv_w", [3, DFF]); din("conv_b", [DFF]); din("w_down", [DFF, DM])
        din("g_final", [DM])
        din("c_rope", [SEQ, 128]); din("c_ropes", [NS, 128]); din("c_mask", [128, 256])
        din("c_ident", [128, 128]); din("c_sel", [NS, NS * 128])
        dout("yp", [SEQ, DM]); dout("ys", [NS, DM])
        dout("pak", [LA, 512]); dout("pav", [LA, 512]); dout("pbk", [LB, 128]); dout("pbv", [LB, 128])
        dout("pmk", [MEM, DM]); dout("pmv", [MEM, DM]); dout("pconv", [2, DFF])
        dout("sak", [NS, LA, 512]); dout("sav", [NS, LA, 512]); dout("sbk", [NS, LB, 128]); dout("sbv", [NS, LB, 128])
        dout("sconvo", [NS, 2, DFF])
        dscr("scrA", [SEQ, 1536], BF16); dscr("scrB", [SEQ, 768], BF16)
        dscr("scrO", [4, SEQ, 520], F32); dscr("scrX2", [SEQ, DM], F32)

        self.arena = Arena(p, self.ARENA)
        self.ps = p.psum("psall", [128, 4096], F32)
        self.bank_ctr = 0
        self.nb_mod = 6
        self.tb_ctr = 0
        self.pers = Bump(self.arena, 0, self.PERS)
        self.setup_consts()

    def bank(self, i):
        return V(self.ps.ap[:, 512 * i:512 * (i + 1)], ("ps", i))

    def nb(self):
        i = self.bank_ctr % self.nb_mod
        self.bank_ctr += 1
        return self.bank(i)

    def tbank(self):
        j = 6 + self.tb_ctr % 2
        self.tb_ctr += 1
        a = self.ps.ap[:, 512 * j:512 * (j + 1)].bitcast(BF16)
        return V(a.rearrange("p (a b) -> p a b", a=8), ("ps", j))

    def bcast_dram(self, name, n, parts=128):
        t = self.D[name].ap.tensor
        return V(bass.AP(t, 0, [[0, parts], [1, n]]), ("dram", name))

    def setup_consts(self):
        p, D, b = self.p, self.D, self.pers
        self.identf = b.alloc([128, 128], F32, "identf")
        self.ident = b.alloc([128, 128], BF16, "ident")
        self.ones_bf = b.alloc([128, 128], BF16, "ones_bf")
        self.epsb = b.alloc([128, 1], F32, "epsb")
        self.ss = [b.alloc([128, 1], F32, f"ss{i}") for i in range(4)]
        self.ss_ctr = 0
        self.junk = b.alloc([128, DM], BF16, "junk")
        self.g_ffn = b.alloc([128, DM], F32, "g_ffn")
        self.g_final = b.alloc([128, DM], F32, "g_final")
        self.PERS_A = b.off
        self.mask2 = b.alloc([128, 2, 128], BF16, "mask2")
        self.g_mix = b.alloc([128, DM], F32, "g_mix")
        self.g_cross = b.alloc([128, DM], F32, "g_cross")
        self.g_out = b.alloc([128, DM], F32, "g_out")
        self.esink = b.alloc([128, 8], F32, "esink")
        self.ropes = b.alloc([NS, 128], F32, "ropes")
        self.mkT = b.alloc([128, 8, MEM], BF16, "mkT")
        self.mv = b.alloc([128, 2, DM], BF16, "mv")
        p.memset("pool", self.epsb, EPS)
        p.dma("sp", self.identf, D["c_ident"], "ld_c_identf")
        p.dma("pool", self.ident, D["c_ident"], "ld_c_ident")
        p.dma("pool", self.mask2.re("p a b -> p (a b)"), D["c_mask"], "ld_c_mask")
        p.memset("pool", self.ones_bf, 1.0)
        p.dma("sp", self.g_mix, self.bcast_dram("g_mix", DM), "ld_c_gmix")
        p.dma("sp", self.g_cross, self.bcast_dram("g_cross", DM), "ld_c_gcross")
        p.dma("sp", self.g_ffn, self.bcast_dram("g_ffn", DM), "ld_c_gffn")
        p.dma("sp", self.g_final, self.bcast_dram("g_final", DM), "ld_c_gfinal")
        p.dma("sp", self.g_out[:, 0:512], self.bcast_dram("g_out_a", 512), "ld_c_goa")
        p.dma("sp", self.g_out[:, 512:1024], self.bcast_dram("g_out_b", 512), "ld_c_gob")
        p.dma("sp", self.esink, self.bcast_dram("sinks", 8), "ld_c_sinks")
        p.dma("sp", self.ropes, D["c_ropes"], "ld_c_ropes")
        p.act(self.esink, self.esink, AF.Exp)

    def load_w(self, dst, name, kchunks, sem):
        src = self.D[name]
        keys = []
        for kc in range(kchunks):
            sub = subview(self.arena, dst, ("kc", kc), dst.ap[:, kc, :])
            self.p.dma("pool", sub, src[kc * 128:(kc + 1) * 128, :], sem)
            keys.append(sub.res)
        return V(dst.ap, keys)

    def next_ss(self):
        s = self.ss[self.ss_ctr % 4]
        self.ss_ctr += 1
        return s

    def rmsnorm(self, x, g, out, P, Dn):
        p = self.p
        ss = self.next_ss()[0:P, :]
        p.memset("pool", ss, 0.0)
        p.act(self.junk[0:P, 0:Dn], x, AF.Square, accum=ss)
        p.act(ss, ss, AF.Ln, scale=1.0 / Dn, bias=self.epsb[0:P, :])
        p.act(ss, ss, AF.Exp, scale=-0.5)
        p.stt("dve", out, x, ss, g, ALU.mult, ALU.mult)

    def transposes(self, dst, src, P, n, evac=("act", "dve"), ident=None):
        p = self.p
        ident = self.ident if ident is None else ident
        gi = 0
        for g in range(0, n, 8):
            m = min(8, n - g)
            pst = self.tbank()
            for j in range(m):
                p.tr(pst[:, j, 0:P], src[0:P, (g + j) * 128:(g + j + 1) * 128], ident[0:P, 0:P])
            p.copy(evac[gi % len(evac)], dst[:, g:g + m, 0:P], pst[:, 0:m, 0:P])
            gi += 1

    def proj_tok(self, hT, w, P, n_out, kchunks, cb):
        p = self.p
        c0 = 0
        while c0 < n_out:
            n = min(512, n_out - c0)
            bk = self.nb()
            for kc in range(kchunks):
                p.mm(bk[0:P, 0:n], hT[:, kc, 0:P], w[:, kc, c0:c0 + n], start=(kc == 0), stop=(kc == kchunks - 1))
            cb(bk[0:P, 0:n], c0, n)
            c0 += n

    def bulk_copies(self):
        p, D = self.p, self.D
        for n in range(NS):
            p.dma("act", D["sak"][n, 0:LA - 1, :], D["cak"][n, 1:LA, :], "bulk", is_output=True)
            p.dma("act", D["sav"][n, 0:LA - 1, :], D["cav"][n, 1:LA, :], "bulk", is_output=True)
        p.dma("act", D["sbk"][:, 0:LB - 1, :], D["cbk"][:, 1:LB, :], "bulk", is_output=True)
        p.dma("act", D["sbv"][:, 0:LB - 1, :], D["cbv"][:, 1:LB, :], "bulk", is_output=True)
        p.dma("act", D["sconvo"][:, 0, :], D["sconv"][:, 1, :], "bulk", is_output=True)

    def phase0(self):
        p, D = self.p, self.D
        b = Bump(self.arena, self.PERS)
        wkv = b.alloc([128, 8, 2 * DM], BF16, "wkv")
        gm = b.alloc([128, DM], F32, "g_mem")
        wkv = self.load_w(wkv, "w_mem_kv", 8, "ld_w0")
        p.dma("sp", gm, self.bcast_dram("g_mem", DM), "ld_c0_12")
        xm = [b.alloc([128, DM], F32, f"xm{i}") for i in range(2)]
        hb = [b.alloc([128, DM], BF16, f"hbm{i}") for i in range(2)]
        hT = [b.alloc([128, 8, 128], BF16, f"hTm{i}") for i in range(2)]
        kvf = [b.alloc([128, 2 * DM], F32, f"kvf{i}") for i in range(2)]
        kb = [b.alloc([128, DM], BF16, f"kbm{i}") for i in range(2)]
        for mt in range(2):
            rows = slice(mt * 128, (mt + 1) * 128)
            p.dma("sp", xm[mt], D["memp"][rows, :], f"ld_xm{mt}")
            self.rmsnorm(xm[mt], gm, hb[mt], 128, DM)
            self.transposes(hT[mt], hb[mt], 128, 8)

            def cb(bk, c0, n, mt=mt):
                p.copy("act", kvf[mt][:, c0:c0 + n], bk)
            self.proj_tok(hT[mt], wkv, 128, 2 * DM, 8, cb)
            p.dma("sp", D["pmk"][rows, :], kvf[mt][:, 0:DM], f"st_kvfk{mt}", is_output=True)
            p.dma("sp", D["pmv"][rows, :], kvf[mt][:, DM:2 * DM], f"st_kvfv{mt}", is_output=True)
            p.copy("pool", kb[mt], kvf[mt][:, 0:DM])
            self.transposes(self.mkT[:, :, rows], kb[mt], 128, 8)
            p.copy("pool", self.mv[:, mt, :], kvf[mt][:, DM:2 * DM])

    def inproj_tile(self, P, x, hb, hT, w_in, cos2, sinS, zqk, zv, zqkB, zvB, tmp, zbA, zbB, want_vf32, want_vBf32):
        p = self.p
        self.rmsnorm(x, self.g_mix[0:P, :], hb, P, DM)
        self.transposes(hT, hb, P, 8)
        cosb = lambda h: bc_mid(cos2, h)
        sin_lo = lambda h: bc_mid(sinS[:, 0:32], h)
        sin_hi = lambda h: bc_mid(sinS[:, 32:64], h)

        def rope(bk, dst, ncols):
            h = ncols // 64
            src = bk[:, 0:ncols].re("p (h d) -> p h d", d=64)
            d3 = dst.re("p (h d) -> p h d", d=64)
            t3 = tmp[:, 0:ncols].re("p (h d) -> p h d", d=64)
            p.tt("dve", d3, src, cosb(h), ALU.mult)
            p.tt("dve", t3[:, :, 0:32], src[:, :, 32:64], sin_lo(h), ALU.mult)
            p.tt("dve", t3[:, :, 32:64], src[:, :, 0:32], sin_hi(h), ALU.mult)
            p.tt("pool", d3, d3, t3, ALU.add)

        def cb(bk, c0, n):
            if c0 == 0:
                rope(bk, zqk[:, 0:512], 512)
            elif c0 == 512:
                rope(bk, zqk[:, 512:1024], 512)
                if zbA is not None:
                    p.copy("pool", zbA[0][:, 0:1024], zqk)
            elif c0 == 1024:
                if zbA is not None:
                    p.copy("act", zbA[1][:, 1024:1536], bk)
                if want_vf32:
                    p.copy("act", zv, bk)
            elif c0 == 1536:
                rope(bk, zqkB[:, 0:512], 512)
            else:
                rope(bk, zqkB[:, 512:640], 128)
                if zbB is not None:
                    p.copy("pool", zbB[0][:, 0:512].re("p (f t d) -> p f t d", f=4, t=2),
                           zqkB[:, 0:512].re("p (t f d) -> p f t d", t=2, f=4))
                    p.copy("pool", zbB[0][:, 512:640], zqkB[:, 512:640])
                    p.copy("act", zbB[1][:, 640:768], bk[:, 128:256])
                if want_vBf32:
                    p.copy("act", zvB, bk[:, 128:256])
        self.proj_tok(hT, w_in, P, IN_DIM, 8, cb)

    def phase1(self):
        p, D = self.p, self.D
        b = Bump(self.arena, self.PERS)
        w_in = b.alloc([128, 8, IN_DIM], BF16, "w_in")
        w_in = self.load_w(w_in, "w_in", 8, "ld_w1")
        rt = b.alloc([128, NT, 128], F32, "rope_tab")
        p.dma("sp", rt, D["c_rope"].re("(t p) c -> p t c", p=128), "ld_c0_13")
        xt = [b.alloc([128, DM], F32, f"xt{i}") for i in range(3)]
        hb = [b.alloc([128, DM], BF16, f"hb{i}") for i in range(2)]
        hT = [b.alloc([128, 8, 128], BF16, f"hT{i}") for i in range(2)]
        zqk = [b.alloc([128, 1024], F32, f"zqk{i}") for i in range(2)]
        zv = [b.alloc([128, 512], F32, f"zv{i}") for i in range(2)]
        zqkB = [b.alloc([128, 640], F32, f"zqkB{i}") for i in range(2)]
        zvB = b.alloc([128, 128], F32, "zvB")
        tmp = [b.alloc([128, 512], F32, f"tmp{i}") for i in range(2)]
        zbA, zbB = [], []
        for i in range(2):
            a = b.alloc([128, 1536], BF16, f"zbA{i}")
            v1, v2 = subview(self.arena, a, "qk"), subview(self.arena, a, "v")
            zbA.append((v1, v2, V(a.ap, [v1.res, v2.res])))
            a = b.alloc([128, 768], BF16, f"zbB{i}")
            v1, v2 = subview(self.arena, a, "qk"), subview(self.arena, a, "v")
            zbB.append((v1, v2, V(a.ap, [v1.res, v2.res])))
        self.p1_end = b.off
        for t in range(NT):
            rows = slice(t * 128, (t + 1) * 128)
            x = xt[t % 3]
            p.dma("sp", x, D["xp"][rows, :], f"ld_xt{t % 3}")
            i = t % 2
            self.inproj_tile(128, x, hb[i], hT[i], w_in, rt[:, t, 0:64], rt[:, t, 64:128],
                             zqk[i], zv[i], zqkB[i], zvB, tmp[i], zbA[i], zbB[i], t >= NT // 2, t == NT - 1)
            if t >= NT // 2:
                orow = slice((t - NT // 2) * 128, (t - NT // 2 + 1) * 128)
                p.dma("sp", D["pak"][orow, :], zqk[i][:, 512:1024], f"st_zqk{i}", is_output=True)
                p.dma("sp", D["pav"][orow, :], zv[i], f"st_zv{i}", is_output=True)
            if t == NT - 1:
                p.dma("sp", D["pbk"], zqkB[i][:, 512:640], f"st_zqkB{i}", is_output=True)
                p.dma("sp", D["pbv"], zvB, "st_zvB", is_output=True)
            self.scr_tokens.append(p.dma("sp", D["scrA"][rows, :], zbA[i][2], f"st_zbA{i}"))
            self.scr_tokens.append(p.dma("sp", D["scrB"][rows, :], zbB[i][2], f"st_zbB{i}"))
        sb = self.samp
        xs = sb["xs"]
        p.dma("sp", xs, D["xs"], "ld_xs")
        self.inproj_tile(NS, xs, hb[0][0:NS, :], hT[0], w_in, self.ropes[:, 0:64], self.ropes[:, 64:128],
                         sb["zqk"], sb["zv"], sb["zqkB"], sb["zvB"], tmp[0][0:NS, :], None, None, True, True)
        p.dma("sp", D["sak"][:, LA - 1, :], sb["zqk"][:, 512:1024], "st_s1a", is_output=True)
        p.dma("sp", D["sav"][:, LA - 1, :], sb["zv"], "st_s1b", is_output=True)
        p.dma("sp", D["sbk"][:, LB - 1, :], sb["zqkB"][:, 512:640], "st_s1c", is_output=True)
        p.dma("sp", D["sbv"][:, LB - 1, :], sb["zvB"], "st_s1d", is_output=True)

    def phase2(self):
        p, D = self.p, self.D
        b = Bump(self.arena, self.PERS, self.PH_END)
        blk = [b.alloc([128, 1536], BF16, f"blk{i}") for i in range(3)]
        QT = [b.alloc([128, 4, 2, 128], BF16, f"QZ{i}") for i in range(2)]
        for v in QT:
            p.memset("pool", v, 0.0)
        KT = [b.alloc([128, 4, 128], BF16, f"KT{i}") for i in range(3)]
        VX = [b.alloc([128, 8, 65], BF16, f"VX{i}") for i in range(3)]
        PT = [b.alloc([128, 2, 2, 128], BF16, f"PT{i}") for i in range(4)]
        OS = [b.alloc([128, 520], F32, f"OS{i}") for i in range(3)]
        for v in VX:
            p.memset("pool", v, 1.0)
        p.wait_for("sp", list(self.scr_tokens))
        mask4 = V(self.mask2.ap.unsqueeze(2).to_broadcast([128, 2, 2, 128]), self.mask2.res)
        mask_own = bc_mid(self.mask2[:, 0, :], 2)
        st = {"s": 0, "pc": 0}
        self.o_tokens = []

        def block(kind, br, d, r, bb, first):
            s = st["s"]
            st["s"] += 1
            cur = blk[s % 3]
            qt, kt, vx = QT[s % 2], KT[s % 3], VX[s % 3]
            ktp, vxp = KT[(s - 1) % 3], VX[(s - 1) % 3]
            if kind == "A":
                src = D["scrA"].re("(j r) c -> r j c", r=d)[r, 128 * bb:128 * (bb + 1), :]
                p.dma("sp", cur, src, f"ld_blk{s % 3}")
                pq = self.tbank()
                for j in range(4):
                    p.tr(pq[:, j, :], cur[:, j * 128:(j + 1) * 128], self.ident)
                p.copy("dve", qt[0:64, :, 0, :], pq[0:64, 0:4, :])
                p.copy("dve", qt[64:128, :, 1, :], pq[64:128, 0:4, :])
                self.transposes(kt, cur[:, 512:1024], 128, 4, evac=("act",))
                p.copy("pool", vx[:, :, 0:64], cur[:, 1024:1536].re("p (h d) -> p h d", d=64))
            else:
                src = D["scrB"][128 * bb:128 * (bb + 1), :]
                p.dma("sp", cur[:, 0:768], src, f"ld_blk{s % 3}")
                pq = self.tbank()
                for j in range(4):
                    p.tr(pq[:, j, :], cur[:, j * 128:(j + 1) * 128], self.ident)
                p.copy("dve", qt[0:64, :, 0, :], pq[0:64, 0:4, :])
                p.copy("dve", qt[64:128, :, 1, :], pq[64:128, 0:4, :])
                self.transposes(kt[:, 0:1, :], cur[:, 512:640], 128, 1, evac=("act",))
                p.copy("pool", vx[:, 0:2, 0:64], cur[:, 640:768].re("p (h d) -> p h d", d=64))
            psO = [self.nb(), self.nb()]
            for j in range(4):
                psS = self.nb()
                kj = j if kind == "A" else 0
                q2 = qt[:, j].re("p a q -> p (a q)")
                p.mm(psS[:, 0:256], kt[:, kj, :], q2)
                if not first:
                    p.mm(psS[:, 256:512], ktp[:, kj, :], q2)
                pt = PT[st["pc"] % 4]
                eng = "dve" if st["pc"] % 2 == 0 else "pool"
                st["pc"] += 1
                if not first:
                    p.act(pt.re("p b a q -> p (b a q)"), psS, AF.Exp, scale=0.125)
                    p.tt(eng, pt, pt, mask4, ALU.mult)
                else:
                    p.act(pt[:, 0].re("p a q -> p (a q)"), psS[:, 0:256], AF.Exp, scale=0.125)
                    p.tt(eng, pt[:, 0], pt[:, 0], mask_own, ALU.mult)
                for hh in range(2):
                    if kind == "A":
                        h = 2 * j + hh
                        vi = h
                    else:
                        h = j + 4 * hh
                        vi = hh
                    o = psO[h // 4][:, (h % 4) * 65:(h % 4) * 65 + 65]
                    p.mm(o, pt[:, 0, hh, :], vx[:, vi, :], start=True, stop=first)
                    if not first:
                        p.mm(o, pt[:, 1, hh, :], vxp[:, vi, :], start=False, stop=True)
            osb = OS[s % 3]
            p.copy("act", osb[:, 0:260], psO[0][:, 0:260])
            p.copy("act", osb[:, 260:520], psO[1][:, 0:260])
            if kind == "A":
                dst = D["scrO"][br].re("(j r) c -> r j c", r=d)[r, 128 * bb:128 * (bb + 1), :]
            else:
                dst = D["scrO"][3][128 * bb:128 * (bb + 1), :]
            self.o_tokens.append(p.dma("sp", dst, osb, f"st_os{s % 3}"))

        for br, d in ((2, 16), (1, 4), (0, 1)):
            nblk = SEQ // d // 128
            for r in range(d):
                for bb in range(nblk):
                    block("A", br, d, r, bb, bb == 0)
        for bb in range(NT):
            block("B", 3, 1, 0, bb, bb == 0)

    def phase2b(self):
        p, D = self.p, self.D
        b = Bump(self.arena, self.PERS, self.PH_END)
        w_out = b.alloc([128, 8, DM], BF16, "w_out")
        w_xq = b.alloc([128, 8, DM], BF16, "w_xq")
        w_xo = b.alloc([128, 8, DM], BF16, "w_xo")
        w_out = self.load_w(w_out, "w_out", 8, "ld_w2a")
        w_xq = self.load_w(w_xq, "w_xq", 8, "ld_w2b")
        w_xo = self.load_w(w_xo, "w_xo", 8, "ld_w2c")
        self.w2 = (w_out, w_xq, w_xo)
        OL = [[b.alloc([128, 520], F32, f"OL{i}_{k}") for k in range(4)] for i in range(2)]
        cat = [b.alloc([128, DM], F32, f"cat{i}") for i in range(2)]
        hm = [b.alloc([128, DM], BF16, f"hm{i}") for i in range(2)]
        rd = [b.alloc([128, 16], F32, f"rd{i}") for i in range(2)]
        x1 = [b.alloc([128, DM], F32, f"x1_{i}") for i in range(4)]
        hmT = b.alloc([128, 8, 512], BF16, "hmT")
        h2T = b.alloc([128, 8, 512], BF16, "h2T")
        qxT = b.alloc([128, 8, 512], BF16, "qxT")
        oxT = b.alloc([128, 8, 512], BF16, "oxT")
        h2 = [b.alloc([128, DM], BF16, f"h2_{i}") for i in range(2)]
        PTx = [b.alloc([128, 2, 512], BF16, f"PTx{i}") for i in range(2)]
        rden = [b.alloc([128, 512], F32, f"rden{i}") for i in range(2)]
        self.p2b_end = b.off
        p.wait_for("sp", list(self.o_tokens))
        self.x2_tokens = []
        for sti in range(NT // 4):
            for tl in range(4):
                t = sti * 4 + tl
                rows = slice(t * 128, (t + 1) * 128)
                ol = OL[t % 2]
                for k in range(4):
                    p.dma("sp", ol[k], D["scrO"][k][rows, :], f"ld_ol{t % 2}_{k}")
                p.dma("sp", x1[tl], D["xp"][rows, :], f"ld_x1_{tl}")
                self.combine(128, ol, rd[t % 2], cat[t % 2], hm[t % 2], self.esink)
                self.transposes(hmT[:, :, tl * 128:(tl + 1) * 128], hm[t % 2], 128, 8)
            for tl in range(4):
                def cb(bk, c0, n, tl=tl):
                    p.tt("dve", x1[tl][:, c0:c0 + n], x1[tl][:, c0:c0 + n], bk, ALU.add)
                self.proj_tok(hmT[:, :, tl * 128:(tl + 1) * 128], w_out, 128, DM, 8, cb)
                self.rmsnorm(x1[tl], self.g_cross, h2[tl % 2], 128, DM)
                self.transposes(h2T[:, :, tl * 128:(tl + 1) * 128], h2[tl % 2], 128, 8)
            for fc in range(8):
                bk = self.nb()
                for kc in range(8):
                    p.mm(bk, w_xq[:, kc, fc * 128:(fc + 1) * 128], h2T[:, kc, :], start=(kc == 0), stop=(kc == 7))
                p.copy("act" if fc % 2 == 0 else "dve", qxT[:, fc, :], bk)
            for h in range(4):
                pt = PTx[h % 2]
                for mc in range(2):
                    bk = self.nb()
                    for j in range(2):
                        p.mm(bk, self.mkT[:, 2 * h + j, mc * 128:(mc + 1) * 128], qxT[:, 2 * h + j, :],
                             start=(j == 0), stop=(j == 1))
                    p.act(pt[:, mc, :], bk, AF.Exp, scale=1.0 / 16.0)
                bd = self.nb()
                for mc in range(2):
                    p.mm(bd, self.ones_bf, pt[:, mc, :], start=(mc == 0), stop=(mc == 1))
                rdn = rden[h % 2]
                p.recip(rdn, bd)
                for dj in range(2):
                    bk = self.nb()
                    for mc in range(2):
                        p.mm(bk, self.mv[:, mc, h * 256 + dj * 128:h * 256 + (dj + 1) * 128], pt[:, mc, :],
                             start=(mc == 0), stop=(mc == 1))
                    p.tt("dve", oxT[:, 2 * h + dj, :], bk, rdn, ALU.mult)
            for tl in range(4):
                t = sti * 4 + tl
                rows = slice(t * 128, (t + 1) * 128)

                def cb(bk, c0, n, tl=tl):
                    p.tt("dve", x1[tl][:, c0:c0 + n], x1[tl][:, c0:c0 + n], bk, ALU.add)
                self.proj_tok(oxT[:, :, tl * 128:(tl + 1) * 128], w_xo, 128, DM, 8, cb)
                self.x2_tokens.append(p.dma("sp", D["scrX2"][rows, :], x1[tl], f"st_x1_{tl}"))

    def combine(self, P, ol, r, c, hmv, esink):
        p = self.p
        if ol[1] is not None:
            p.tt("pool", ol[0], ol[0], ol[1], ALU.add)
            p.tt("pool", ol[0], ol[0], ol[2], ALU.add)
        a3 = ol[0].re("p (h c) -> p h c", c=65)
        b3 = ol[3].re("p (h c) -> p h c", c=65)
        p.recip(r[:, 0:8], a3[:, :, 64])
        p.tt("dve", c[:, 0:512].re("p (h d) -> p h d", d=64), a3[:, :, 0:64], bc_last(r[:, 0:8], 64), ALU.mult)
        p.tt("dve", r[:, 8:16], b3[:, :, 64], esink[0:P, :], ALU.add)
        p.recip(r[:, 8:16], r[:, 8:16])
        p.tt("dve", c[:, 512:1024].re("p (h d) -> p h d", d=64), b3[:, :, 0:64], bc_last(r[:, 8:16], 64), ALU.mult)
        self.rmsnorm(c[:, 0:512], self.g_out[0:P, 0:512], hmv[:, 0:512], P, 512)
        self.rmsnorm(c[:, 512:1024], self.g_out[0:P, 512:1024], hmv[:, 512:1024], P, 512)

    def phase3(self):
        p, D = self.p, self.D
        b = Bump(self.arena, self.PERS_A, self.ARENA - 4096)
        w_up = b.alloc([128, 8, 2 * DFF], BF16, "w_up")
        w_dn = b.alloc([128, NFC, DM], BF16, "w_dn")
        w_up = self.load_w(w_up, "w_up", 8, "ld_w3a")
        w_dn = self.load_w(w_dn, "w_down", NFC, "ld_w3b")
        self.w3 = (w_up, w_dn)
        cw = b.alloc([128, 3, NFC], F32, "cw")
        cbias = b.alloc([128, NFC], F32, "cbias")
        for i3 in range(3):
            p.dma("sp", cw[:, i3, :], D["conv_w"][i3].re("(fc f) -> f fc", f=128), "ld_cw",
                  allow_slow_non_contiguous=True)
        p.dma("sp", cbias, D["conv_b"].re("(fc f) -> f fc", f=128), "ld_cb", allow_slow_non_contiguous=True)
        gcar = b.alloc([128, NFC, 2], F32, "gcar")
        p.memset("pool", gcar, 0.0)
        ST = 256
        self.p3_tmp_start = b.off
        xl = [b.alloc([128, DM], F32, f"xl{i}") for i in range(4)]
        h3 = [b.alloc([128, DM], BF16, f"h3_{i}") for i in range(2)]
        h3T = b.alloc([128, 8, ST], BF16, "h3T")
        aT = b.alloc([128, NFC, ST], BF16, "aT")
        gsb = [b.alloc([128, ST + 2], F32, f"gsb{i}") for i in range(2)]
        tq = [b.alloc([128, ST], F32, f"tq{i}") for i in range(2)]
        sq = [b.alloc([128, ST], F32, f"sq{i}") for i in range(2)]
        yt = [b.alloc([128, DM], F32, f"yt{i}") for i in range(1)]
        p.wait_for("sp", list(self.x2_tokens))
        for s_ in range(SEQ // ST):
            for tl in range(2):
                t = 2 * s_ + tl
                rows = slice(t * 128, (t + 1) * 128)
                xi = (s_ % 2) * 2 + tl
                p.dma("sp", xl[xi], D["scrX2"][rows, :], f"ld_xl{xi}")
                self.rmsnorm(xl[xi], self.g_ffn, h3[tl], 128, DM)
                self.transposes(h3T[:, :, tl * 128:(tl + 1) * 128], h3[tl], 128, 8)
            for fc in range(NFC):
                bg = self.nb()
                bv = self.nb()
                for kc in range(8):
                    p.mm(bg[:, 0:ST], w_up[:, kc, fc * 128:(fc + 1) * 128], h3T[:, kc, :], start=(kc == 0), stop=(kc == 7))
                for kc in range(8):
                    p.mm(bv[:, 0:ST], w_up[:, kc, DFF + fc * 128:DFF + (fc + 1) * 128], h3T[:, kc, :],
                         start=(kc == 0), stop=(kc == 7))
                g = gsb[fc % 2]
                tt_ = tq[fc % 2]
                ss_ = sq[fc % 2]
                p.copy("pool", g[:, 0:2], gcar[:, fc, :])
                p.copy("act", g[:, 2:ST + 2], bg[:, 0:ST])
                p.copy("pool", gcar[:, fc, :], g[:, ST:ST + 2])
                p.ts("dve", tt_, g[:, 0:ST], cw[:, 0, fc:fc + 1], cbias[:, fc:fc + 1], ALU.mult, ALU.add)
                p.stt("dve", tt_, g[:, 1:ST + 1], cw[:, 1, fc:fc + 1], tt_, ALU.mult, ALU.add)
                p.stt("dve", tt_, g[:, 2:ST + 2], cw[:, 2, fc:fc + 1], tt_, ALU.mult, ALU.add)
                p.act(ss_, tt_, AF.Silu)
                p.tt("dve", aT[:, fc, :], ss_, bv[:, 0:ST], ALU.mult)
            for tl in range(2):
                t = 2 * s_ + tl
                rows = slice(t * 128, (t + 1) * 128)
                x = xl[(s_ % 2) * 2 + tl]

                def cb(bk, c0, n, x=x):
                    p.tt("dve", x[:, c0:c0 + n], x[:, c0:c0 + n], bk, ALU.add)
                self.proj_tok(aT[:, :, tl * 128:(tl + 1) * 128], w_dn, 128, DM, NFC, cb)
                y = yt[0]
                self.rmsnorm(x, self.g_final, y, 128, DM)
                p.dma("sp", D["yp"][rows, :], y, "st_y0", is_output=True)
        for ti in range(2):
            p.dma("sp", D["pconv"][ti].re("(fc f) -> f fc", f=128), gcar[:, :, ti], "st_pconv", is_output=True,
                  allow_slow_non_contiguous=True)

    def samp_attn(self):
        p, D, sb = self.p, self.D, self.samp
        b = Bump(self.arena, self.PERS, self.PH_END)
        sel = b.alloc([NS, NS, 128], F32, "sel")
        p.dma("sp", sel.re("p a b -> p (a b)"), D["c_sel"], "ld_sel")
        qb = [b.alloc([128, 512], F32, f"s_qb{i}") for i in range(2)]
        Kt = [b.alloc([128, 512], F32, f"s_Kt{i}") for i in range(3)]
        Vt = [b.alloc([128, 8, 65], F32, f"s_Vt{i}") for i in range(3)]
        prod = [b.alloc([128, 512], F32, f"s_prod{i}") for i in range(2)]
        sc = [b.alloc([128, 8], F32, f"s_sc{i}") for i in range(2)]
        Pz = [b.alloc([128, 8, NS], F32, f"s_Pz{i}") for i in range(4)]
        oA = b.alloc([NS, 8, 65], F32, "s_oA")
        oB = b.alloc([NS, 8, 65], F32, "s_oB")
        prn = b.alloc([NS, 512], F32, "s_prn")
        sn = b.alloc([NS, 8], F32, "s_sn")
        en = b.alloc([NS, 8], F32, "s_en")
        tv = b.alloc([NS, 8, 64], F32, "s_tv")
        r16 = b.alloc([NS, 16], F32, "s_r16")
        for v in Vt:
            p.memset("pool", v, 1.0)
        for v in Pz:
            p.memset("pool", v, 0.0)
        p.memset("pool", oA, 0.0)
        p.memset("pool", oB, 0.0)
        cnt = 0
        for n in range(NS):
            bk = self.nb()
            p.mm(bk, sel[:, n, :], sb["zqk"][:, 0:512])
            q = qb[n % 2]
            p.copy("act", q, bk)
            for g in range(3):
                if g == 0:
                    ksrc = D["cak"][n, LA - 128:LA, :]
                    vsrc = D["cav"][n, LA - 128:LA, :]
                elif g == 1:
                    ksrc = D["cak"][n].re("(j r) c -> r j c", r=4)[0, 384:512, :]
                    vsrc = D["cav"][n].re("(j r) c -> r j c", r=4)[0, 384:512, :]
                else:
                    ksrc = D["cak"][n].re("(j r) c -> r j c", r=16)[0, 0:128, :]
                    vsrc = D["cav"][n].re("(j r) c -> r j c", r=16)[0, 0:128, :]
                kt, vt = Kt[cnt % 3], Vt[cnt % 3]
                p.dma("sp", kt, ksrc, f"ld_sK{cnt % 3}")
                p.dma("sp", vt[:, :, 0:64], vsrc.re("j (h d) -> j h d", d=64), f"ld_sV{cnt % 3}")
                pr = prod[cnt % 2]
                p.tt("dve", pr, kt, q, ALU.mult)
                s8 = sc[cnt % 2]
                p.reduce("dve", s8, pr.re("p (h d) -> p h d", d=64), ALU.add)
                pz = Pz[cnt % 4]
                p.act(pz[:, :, n], s8, AF.Exp, scale=0.125)
                psA = [self.nb(), self.nb()]
                for h in range(8):
                    o = psA[h // 4][0:NS, (h % 4) * 65:(h % 4) * 65 + 65]
                    p.mm(o, pz[:, h, :], vt[:, h, :])
                for hb in range(2):
                    av = oA[:, 4 * hb:4 * hb + 4, :]
                    p.tt("dve", av, av, psA[hb][0:NS, 0:260].re("p (h c) -> p h c", c=65), ALU.add)
                p.memset("pool", pz[:, :, n], 0.0)
                cnt += 1
        zqk, zv = sb["zqk"], sb["zv"]
        p.tt("dve", prn, zqk[:, 0:512], zqk[:, 512:1024], ALU.mult)
        p.reduce("dve", sn, prn.re("p (h d) -> p h d", d=64), ALU.add)
        p.act(en, sn, AF.Exp, scale=0.125)
        p.ts("dve", en, en, 3.0, None, ALU.mult)
        p.tt("dve", tv, zv.re("p (h d) -> p h d", d=64), bc_last(en, 64), ALU.mult)
        p.tt("dve", oA[:, :, 0:64], oA[:, :, 0:64], tv, ALU.add)
        p.tt("dve", oA[:, :, 64], oA[:, :, 64], en, ALU.add)
        KtB = [V(k.ap[:, 0:128], k.res) for k in Kt]
        VtB = [V(v.ap[:, 0:2, :], v.res) for v in Vt]
        zqkB, zvB = sb["zqkB"], sb["zvB"]
        for n in range(NS):
            bk = self.nb()
            p.mm(bk, sel[:, n, :], zqkB[:, 0:512])
            q = qb[n % 2]
            p.copy("act", q, bk)
            kt, vt = KtB[cnt % 3], VtB[cnt % 3]
            p.dma("sp", kt, D["cbk"][n], f"ld_sK{cnt % 3}")
            p.dma("sp", vt[:, :, 0:64], D["cbv"][n].re("j (h d) -> j h d", d=64), f"ld_sV{cnt % 3}")
            pr = prod[cnt % 2]
            k4 = V(kt.ap.rearrange("p (k d) -> p k d", d=64).unsqueeze(2).to_broadcast([128, 2, 4, 64]), kt.res)
            p.tt("dve", pr.re("p (k g d) -> p k g d", k=2, g=4), q.re("p (k g d) -> p k g d", k=2, g=4), k4, ALU.mult)
            s8 = sc[cnt % 2]
            p.reduce("dve", s8, pr.re("p (h d) -> p h d", d=64), ALU.add)
            pz = Pz[cnt % 4]
            p.act(pz[:, :, n], s8, AF.Exp, scale=0.125)
            psA = [self.nb(), self.nb()]
            for h in range(8):
                o = psA[h // 4][0:NS, (h % 4) * 65:(h % 4) * 65 + 65]
                p.mm(o, pz[:, h, :], vt[:, h // 4, :])
            for hb in range(2):
                av = oB[:, 4 * hb:4 * hb + 4, :]
                p.tt("dve", av, av, psA[hb][0:NS, 0:260].re("p (h c) -> p h c", c=65), ALU.add)
            p.memset("pool", pz[:, :, n], 0.0)
            cnt += 1
        kn4 = V(zqkB.ap[:, 512:640].rearrange("p (k d) -> p k d", d=64).unsqueeze(2).to_broadcast([NS, 2, 4, 64]), zqkB.res)
        p.tt("dve", prn.re("p (k g d) -> p k g d", k=2, g=4), zqkB[:, 0:512].re("p (k g d) -> p k g d", k=2, g=4), kn4, ALU.mult)
        p.reduce("dve", sn, prn.re("p (h d) -> p h d", d=64), ALU.add)
        p.act(en, sn, AF.Exp, scale=0.125)
        vn4 = V(zvB.ap.rearrange("p (k d) -> p k d", d=64).unsqueeze(2).to_broadcast([NS, 2, 4, 64]), zvB.res)
        e4 = V(en.ap.rearrange("p (k g) -> p k g", k=2).unsqueeze(3).to_broadcast([NS, 2, 4, 64]), en.res)
        p.tt("dve", tv.re("p (k g) d -> p k g d", k=2), vn4, e4, ALU.mult)
        p.tt("dve", oB[:, :, 0:64], oB[:, :, 0:64], tv, ALU.add)
        p.tt("dve", oB[:, :, 64], oB[:, :, 64], en, ALU.add)
        self.combine(NS, [oA.re("p h c -> p (h c)"), None, None, oB.re("p h c -> p (h c)")], r16, sb["cat"], sb["hm"],
                     self.esink)

    def samp_mix(self):
        p, D, sb = self.p, self.D, self.samp
        w_out, w_xq, w_xo = self.w2
        b = Bump(self.arena, self.PERS + 3 * 16384, self.PH_END)
        sel = b.alloc([NS, NS, 128], F32, "sel2")
        p.dma("sp", sel.re("p a b -> p (a b)"), D["c_sel"], "ld_sel2")
        hT = b.alloc([128, 8, NS], BF16, "s_hT")
        h2s = b.alloc([NS, DM], BF16, "s_h2")
        qx = b.alloc([NS, DM], F32, "s_qx")
        qbx = [b.alloc([128, DM], F32, f"s_qbx{i}") for i in range(2)]
        Kx = [b.alloc([128, DM], F32, f"s_Kx{i}") for i in range(2)]
        Vx = [b.alloc([128, 4, 257], F32, f"s_Vx{i}") for i in range(2)]
        prodx = b.alloc([128, DM], F32, "s_prodx")
        s4 = [b.alloc([128, 4], F32, f"s_s4{i}") for i in range(2)]
        Pzx = [b.alloc([128, 4, NS], F32, f"s_Pzx{i}") for i in range(4)]
        oX = b.alloc([NS, 4, 257], F32, "s_oX")
        r4 = b.alloc([NS, 4], F32, "s_r4")
        oxn = b.alloc([NS, DM], BF16, "s_oxn")
        xs = sb["xs"]
        self.transposes(hT, sb["hm"], NS, 8)

        def cb(bk, c0, n):
            p.tt("dve", xs[:, c0:c0 + n], xs[:, c0:c0 + n], bk, ALU.add)
        self.proj_tok(hT, w_out, NS, DM, 8, cb)
        self.rmsnorm(xs, self.g_cross[0:NS, :], h2s, NS, DM)
        self.transposes(hT, h2s, NS, 8)

        def cb2(bk, c0, n):
            p.copy("act", qx[:, c0:c0 + n], bk)
        self.proj_tok(hT, w_xq, NS, DM, 8, cb2)
        for v in Vx:
            p.memset("pool", v, 1.0)
        for v in Pzx:
            p.memset("pool", v, 0.0)
        p.memset("pool", oX, 0.0)
        cnt = 0
        for n in range(NS):
            q = qbx[n % 2]
            for half in range(2):
                bk = self.nb()
                p.mm(bk, sel[:, n, :], qx[:, half * 512:(half + 1) * 512])
                p.copy("act", q[:, half * 512:(half + 1) * 512], bk)
            for mc in range(2):
                kx, vx = Kx[cnt % 2], Vx[cnt % 2]
                rows = slice(mc * 128, (mc + 1) * 128)
                p.dma("sp", kx, D["cmk"][n, rows, :], f"ld_sKx{cnt % 2}")
                p.dma("sp", vx[:, :, 0:256], D["cmv"][n, rows, :].re("j (h d) -> j h d", d=256), f"ld_sVx{cnt % 2}")
                p.tt("dve", prodx, kx, q, ALU.mult)
                s_ = s4[cnt % 2]
                p.reduce("dve", s_, prodx.re("p (h d) -> p h d", d=256), ALU.add)
                pz = Pzx[cnt % 4]
                p.act(pz[:, :, n], s_, AF.Exp, scale=1.0 / 16.0)
                for h in range(4):
                    bo = self.nb()
                    p.mm(bo[0:NS, 0:257], pz[:, h, :], vx[:, h, :])
                    p.tt("dve", oX[:, h, :], oX[:, h, :], bo[0:NS, 0:257], ALU.add)
                p.memset("pool", pz[:, :, n], 0.0)
                cnt += 1
        p.recip(r4, oX[:, :, 256])
        p.tt("dve", oxn.re("p (h d) -> p h d", d=256), oX[:, :, 0:256], bc_last(r4, 256), ALU.mult)
        self.transposes(hT, oxn, NS, 8)
        self.proj_tok(hT, w_xo, NS, DM, 8, cb)

    def samp_ffn(self):
        p, D, sb = self.p, self.D, self.samp
        w_up, w_dn = self.w3
        b = Bump(self.arena, self.p3_tmp_start, self.ARENA - 4096)
        h3s = b.alloc([NS, DM], BF16, "s_h3")
        hT = b.alloc([128, 8, NS], BF16, "s_h3T")
        gs = b.alloc([NS, DFF], F32, "s_gs")
        vs = b.alloc([NS, DFF], F32, "s_vs")
        abf = b.alloc([NS, DFF], BF16, "s_abf")
        aT = b.alloc([128, NFC, NS], BF16, "s_aT")
        ysb = b.alloc([NS, DM], F32, "s_ysb")
        xs = sb["xs"]
        self.rmsnorm(xs, self.g_ffn[0:NS, :], h3s, NS, DM)
        self.transposes(hT, h3s, NS, 8)

        def cbu(bk, c0, n):
            lo, hi = c0, c0 + n
            if lo < DFF:
                m = min(hi, DFF) - lo
                p.copy("act", gs[:, lo:lo + m], bk[:, 0:m])
            if hi > DFF:
                s0 = max(lo, DFF)
                p.copy("act", vs[:, s0 - DFF:hi - DFF], bk[:, s0 - lo:n])
        self.proj_tok(hT, w_up, NS, 2 * DFF, 8, cbu)
        p.dma("sp", D["sconvo"][:, 1, :], gs, "st_sgs", is_output=True)
        b2 = Bump(self.arena, self.PERS_A, self.PERS_A + 90112)
        s0t = b2.alloc([NS, DFF], F32, "s_s0")
        s1t = b2.alloc([NS, DFF], F32, "s_s1")
        cwb = b2.alloc([NS, 3, DFF], F32, "s_cwb")
        cbb = b2.alloc([NS, DFF], F32, "s_cbb")
        t1 = b2.alloc([NS, DFF], F32, "s_t1")
        t2 = b2.alloc([NS, DFF], F32, "s_t2")
        p.dma("sp", s0t, D["sconv"][:, 0, :], "ld_ss0")
        p.dma("sp", s1t, D["sconv"][:, 1, :], "ld_ss1")
        tcw = D["conv_w"].ap.tensor
        p.dma("sp", cwb.re("p a b -> p (a b)"), V(bass.AP(tcw, 0, [[0, NS], [1, 3 * DFF]]), ("dram", "conv_w")), "ld_scw")
        p.dma("sp", cbb, self.bcast_dram("conv_b", DFF, NS), "ld_scb")
        p.tt("dve", t1, s0t, cwb[:, 0, :], ALU.mult)
        p.tt("pool", t2, s1t, cwb[:, 1, :], ALU.mult)
        p.tt("dve", t1, t1, t2, ALU.add)
        p.tt("pool", t2, gs, cwb[:, 2, :], ALU.mult)
        p.tt("dve", t1, t1, t2, ALU.add)
        p.tt("dve", t1, t1, cbb, ALU.add)
        p.act(t2, t1, AF.Silu)
        p.tt("dve", abf, t2, vs, ALU.mult)
        self.transposes(aT, abf, NS, NFC)

        def cb(bk, c0, n):
            p.tt("dve", xs[:, c0:c0 + n], xs[:, c0:c0 + n], bk, ALU.add)
        self.proj_tok(aT, w_dn, NS, DM, NFC, cb)
        self.rmsnorm(xs, self.g_final[0:NS, :], ysb, NS, DM)
        p.dma("sp", D["ys"], ysb, "st_ys", is_output=True)

    def alloc_sample(self):
        top = Bump(self.arena, self.ARENA - 4096)
        b = Bump(self.arena, self.ARENA - 20480, self.ARENA - 4096)
        self.samp = {
            "xs": top.alloc([NS, DM], F32, "s_xs"),
            "zqk": b.alloc([NS, 1024], F32, "s_zqk"),
            "zv": b.alloc([NS, 512], F32, "s_zv"),
            "zqkB": b.alloc([NS, 640], F32, "s_zqkB"),
            "zvB": b.alloc([NS, 128], F32, "s_zvB"),
            "cat": b.alloc([NS, DM], F32, "s_cat"),
            "hm": b.alloc([NS, DM], BF16, "s_hm"),
        }
        self.samp_bump = b
        self.PH_END = self.ARENA - 20480

    def build(self):
        self.scr_tokens = []
        self.alloc_sample()
        self.bulk_copies()
        self.phase0()
        self.phase1()
        if self.stage >= 2:
            self.samp_attn()
            self.phase2()
        if self.stage >= 3:
            self.phase2b()
            self.samp_mix()
        if self.stage >= 4:
            self.phase3()
            self.samp_ffn()
        return self.p.build()


def make_consts():
    half = 32
    inv = np.power(np.float32(10000.0), -np.arange(half, dtype=np.float32) / np.float32(half)).astype(np.float32)

    def tab(pos):
        ang = pos.astype(np.float32)[:, None] * inv[None, :]
        c = np.cos(ang).astype(np.float32)
        s = np.sin(ang).astype(np.float32)
        return np.concatenate([c, c, -s, s], axis=1).astype(np.float32)
    c_rope = tab(np.arange(SEQ))
    c_ropes = tab(np.full((NS,), PAST))
    k = np.arange(128)[:, None]
    q = np.arange(128)[None, :]
    own = (k <= q).astype(np.float32)
    prev = (k >= q).astype(np.float32)
    c_mask = np.concatenate([own, prev], axis=1).astype(np.float32)
    c_ident = np.eye(128, dtype=np.float32)
    c_sel = np.zeros((NS, NS, 128), np.float32)
    for n in range(NS):
        c_sel[n, n, :] = 1.0
    return {"c_rope": c_rope, "c_ropes": c_ropes, "c_mask": c_mask, "c_ident": c_ident,
            "c_sel": c_sel.reshape(NS, NS * 128)}


_STAGE = 4


def kernel(x_prompt, x_sample, cache_a_k, cache_a_v, cache_b_k, cache_b_v, cache_mem_k, cache_mem_v, state_conv,
           mem_prompt, g_mix, w_in, g_out_a, g_out_b, sinks, w_out, g_cross, g_mem, w_xq, w_mem_kv, w_xo,
           g_ffn, w_up, conv_w, conv_b, w_down, g_final):
    f = lambda a: np.ascontiguousarray(np.asarray(a, dtype=np.float32))
    kb = KB(stage=_STAGE)
    nc = kb.build()
    consts = make_consts()
    shared = {
        "g_mix": f(g_mix[0]), "w_in": f(w_in[0]), "g_out_a": f(g_out_a[0]), "g_out_b": f(g_out_b[0]),
        "sinks": f(sinks[0]), "w_out": f(w_out[0]), "g_cross": f(g_cross[0]), "g_mem": f(g_mem[0]),
        "w_xq": f(w_xq[0]), "w_mem_kv": f(w_mem_kv[0]), "w_xo": f(w_xo[0]), "g_ffn": f(g_ffn[0]),
        "w_up": f(w_up[0]), "conv_w": f(conv_w[0]), "conv_b": f(conv_b[0]), "w_down": f(w_down[0]),
        "g_final": f(g_final),
    }
    shared.update(consts)
    in_maps = []
    for c in range(NCORES):
        s = slice(c * NS, (c + 1) * NS)
        m = dict(shared)
        m["xp"] = f(x_prompt[c])
        m["xs"] = f(x_sample[s, 0])
        m["cak"] = f(cache_a_k[0, s]).reshape(NS, LA, 512)
        m["cav"] = f(cache_a_v[0, s]).reshape(NS, LA, 512)
        m["cbk"] = f(cache_b_k[0, s]).reshape(NS, LB, 128)
        m["cbv"] = f(cache_b_v[0, s]).reshape(NS, LB, 128)
        m["cmk"] = f(cache_mem_k[0, s]).reshape(NS, MEM, DM)
        m["cmv"] = f(cache_mem_v[0, s]).reshape(NS, MEM, DM)
        m["sconv"] = f(state_conv[0, s])
        m["memp"] = f(mem_prompt[c])
        in_maps.append(m)
    res = run_bass_kernel_spmd(nc, in_maps, core_ids=list(range(NCORES)))
    R = res.results
    cat = lambda k: np.stack([np.asarray(R[c][k], dtype=np.float32) for c in range(NCORES)])
    catn = lambda k: np.concatenate([np.asarray(R[c][k], dtype=np.float32) for c in range(NCORES)], axis=0)
    y_prompt = cat("yp")
    y_sample = catn("ys").reshape(NCORES * NS, 1, DM)
    p_a_k = cat("pak").reshape(1, NCORES, LA, 8, 64)
    p_a_v = cat("pav").reshape(1, NCORES, LA, 8, 64)
    p_b_k = cat("pbk").reshape(1, NCORES, LB, 2, 64)
    p_b_v = cat("pbv").reshape(1, NCORES, LB, 2, 64)
    p_mem_k = cat("pmk").reshape(1, NCORES, MEM, 4, 256)
    p_mem_v = cat("pmv").reshape(1, NCORES, MEM, 4, 256)
    p_conv = cat("pconv").reshape(1, NCORES, 2, DFF)
    s_a_k = catn("sak").reshape(1, NCORES * NS, LA, 8, 64)
    s_a_v = catn("sav").reshape(1, NCORES * NS, LA, 8, 64)
    s_b_k = catn("sbk").reshape(1, NCORES * NS, LB, 2, 64)
    s_b_v = catn("sbv").reshape(1, NCORES * NS, LB, 2, 64)
    s_conv = catn("sconvo").reshape(1, NCORES * NS, 2, DFF)
    return (y_prompt, y_sample, p_a_k, p_a_v, p_b_k, p_b_v, p_mem_k, p_mem_v, p_conv,
            s_a_k, s_a_v, s_b_k, s_b_v, s_conv)
```

```python
import numpy as np
from contextlib import ExitStack
import ml_dtypes
import concourse.bass as bass
import concourse.mybir as mybir
from concourse.bass_utils import run_bass_kernel_spmd

F32 = mybir.dt.float32
BF16 = mybir.dt.bfloat16
ALU = mybir.AluOpType
AF = mybir.ActivationFunctionType
AX = mybir.AxisListType

NCORES = 8
SEQ = 4096
DM = 1024
NT = SEQ // 128
NS = 16
LA = 2048
LB = 128
MEM = 256
DFF = 2816
NFC = DFF // 128
IN_DIM = 2304
EPS = 1e-6
PAST = 16384


class V:
    __slots__ = ("ap", "res")

    def __init__(self, ap, res):
        self.ap = ap
        self.res = res

    def __getitem__(self, k):
        return V(self.ap[k], self.res)

    def re(self, s, **kw):
        return V(self.ap.rearrange(s, **kw), self.res)

    def bc(self, shape):
        return V(self.ap.to_broadcast(shape), self.res)

    def cast(self, dt):
        return V(self.ap.bitcast(dt), self.res)


def _keys(v):
    if v is None:
        return []
    r = v.res
    if isinstance(r, list):
        return r
    return [r]


class Prog:
    ENGS = ("pe", "act", "dve", "pool", "sp")

    def __init__(self):
        self.nc = bass.Bass("TRN2", target_bir_lowering=False)
        self.es = ExitStack()
        self.ins = []
        self.res = {}
        self.out_tokens = []

    def dram(self, name, shape, dt, kind):
        t = self.nc.dram_tensor(name, list(shape), dt, kind=kind)
        return V(t.ap(), ("dram", name))

    def sbuf(self, name, shape, dt):
        t = self.es.enter_context(self.nc.sbuf_tensor(name, list(shape), dt))
        return V(t[:], ("sb", name))

    def psum(self, name, shape, dt):
        t = self.es.enter_context(self.nc.psum_tensor(name, list(shape), dt))
        return V(t[:], ("ps", name))

    def _rec(self, eng, emit, reads, writes, dma_sem=None, track_dram=False):
        iid = len(self.ins)
        is_dma = dma_sem is not None
        deps = {}

        def add_dep(tok, kind):
            if tok is None:
                return
            pr = self.ins[tok]
            if (not is_dma) and (pr["dma_sem"] is None) and pr["eng"] == eng:
                if eng == "pe":
                    return
            deps[tok] = True

        rk = [k for v in reads for k in _keys(v)]
        wk = [k for v in writes for k in _keys(v)]
        if not track_dram:
            rk = [k for k in rk if k[0] != "dram"]
            wk = [k for k in wk if k[0] != "dram"]
        for r in rk:
            st = self.res.setdefault(r, {"w": None, "r": {}})
            add_dep(st["w"], 0)
        for w in wk:
            st = self.res.setdefault(w, {"w": None, "r": {}})
            add_dep(st["w"], 1)
            for t in st["r"].values():
                add_dep(t, 2)
        semkey = ("dma", dma_sem) if is_dma else ("eng", eng)
        for r in rk:
            self.res[r]["r"][semkey] = iid
        for w in wk:
            self.res[w]["w"] = iid
            self.res[w]["r"] = {}
        self.ins.append({"eng": eng, "emit": emit, "deps": list(deps), "dma_sem": dma_sem,
                         "flag": is_dma, "semkey": semkey, "val": None})
        for d in deps:
            self.ins[d]["flag"] = True
        return iid

    def mm(self, out, lhsT, rhs, start=True, stop=True):
        return self._rec("pe", lambda e: e.matmul(out.ap, lhsT.ap, rhs.ap, start=start, stop=stop),
                         [lhsT, rhs], [out])

    def tr(self, out, in_, ident):
        return self._rec("pe", lambda e: e.transpose(out.ap, in_.ap, ident.ap), [in_, ident], [out])

    def act(self, out, in_, func, bias=None, scale=1.0, accum=None):
        kw = {}
        rd = [in_]
        if bias is not None:
            if isinstance(bias, V):
                kw["bias"] = bias.ap
                rd.append(bias)
            else:
                kw["bias"] = bias
        if isinstance(scale, V):
            kw["scale"] = scale.ap
            rd.append(scale)
        else:
            kw["scale"] = scale
        wr = [out]
        if accum is not None:
            kw["accum_out"] = accum.ap
            wr.append(accum)
        return self._rec("act", lambda e: e.activation(out.ap, in_.ap, func, **kw), rd, wr)

    def tt(self, eng, out, a, b, op):
        return self._rec(eng, lambda e: e.tensor_tensor(out.ap, a.ap, b.ap, op), [a, b], [out])

    def ts(self, eng, out, a, s1, s2, op0, op1=None, accum=None):
        rd = [a]
        s1a = s1.ap if isinstance(s1, V) else s1
        s2a = s2.ap if isinstance(s2, V) else s2
        if isinstance(s1, V):
            rd.append(s1)
        if isinstance(s2, V):
            rd.append(s2)
        wr = [out]
        kw = {}
        if op1 is not None:
            kw["op1"] = op1
        if accum is not None:
            kw["accum_out"] = accum.ap
            wr.append(accum)
        return self._rec(eng, lambda e: e.tensor_scalar(out.ap, a.ap, s1a, s2a, op0, **kw), rd, wr)

    def stt(self, eng, out, a, s, b, op0, op1):
        rd = [a, b]
        sa = s.ap if isinstance(s, V) else s
        if isinstance(s, V):
            rd.append(s)
        return self._rec(eng, lambda e: e.scalar_tensor_tensor(out.ap, a.ap, sa, b.ap, op0, op1), rd, [out])

    def copy(self, eng, out, in_):
        if eng == "act":
            return self._rec("act", lambda e: e.copy(out.ap, in_.ap), [in_], [out])
        return self._rec(eng, lambda e: e.tensor_copy(out.ap, in_.ap), [in_], [out])

    def memset(self, eng, out, val):
        return self._rec(eng, lambda e: e.memset(out.ap, val), [], [out])

    def reduce(self, eng, out, in_, op, axis=AX.X):
        return self._rec(eng, lambda e: e.tensor_reduce(out.ap, in_.ap, axis, op), [in_], [out])

    def recip(self, out, in_):
        return self._rec("dve", lambda e: e.reciprocal(out.ap, in_.ap), [in_], [out])

    def dma(self, q, out, in_, sem, is_output=False, **kw):
        iid = self._rec(q, lambda e: e.dma_start(out=out.ap, in_=in_.ap, **kw), [in_], [out], dma_sem=sem)
        if is_output:
            self.out_tokens.append(iid)
        return iid

    def wait_for(self, eng, tokens):
        iid = self._rec(eng, None, [], [])
        self.ins[iid]["deps"] = list(tokens)
        for d in tokens:
            self.ins[d]["flag"] = True
        return iid

    def build(self):
        nc = self.nc
        self.wait_for("sp", list(self.out_tokens))
        counts = {}
        for r in self.ins:
            if r["flag"]:
                k = r["semkey"]
                inc = 16 if r["dma_sem"] is not None else 1
                counts[k] = counts.get(k, 0) + inc
                r["val"] = counts[k]
        semh = {}
        for k in counts:
            nm = "s_" + "_".join(str(x) for x in k)
            semh[k] = self.es.enter_context(nc.semaphore(nm))
        per_eng = {e: [] for e in self.ENGS}
        for r in self.ins:
            per_eng[r["eng"]].append(r)
        ins = self.ins
        stats = {e: [0, 0] for e in self.ENGS}

        def run(engname, eobj):
            waited = {}
            for r in per_eng[engname]:
                need = {}
                for d in r["deps"]:
                    pr = ins[d]
                    k = pr["semkey"]
                    v = pr["val"]
                    if waited.get(k, 0) >= v:
                        continue
                    if need.get(k, 0) < v:
                        need[k] = v
                for k, v in need.items():
                    eobj.wait_ge(semh[k], v)
                    waited[k] = v
                    stats[engname][1] += 1
                if r["emit"] is not None:
                    bi = r["emit"](eobj)
                    stats[engname][0] += 1
                    if r["flag"]:
                        bi.then_inc(semh[r["semkey"]], 16 if r["dma_sem"] is not None else 1)

        with nc.Block() as block:
            @block.tensor
            def _(e):
                run("pe", e)

            @block.scalar
            def _(e):
                run("act", e)

            @block.vector
            def _(e):
                run("dve", e)

            @block.gpsimd
            def _(e):
                run("pool", e)

            @block.sync
            def _(e):
                run("sp", e)
        self.stats = stats
        self.sem_counts = counts
        self.es.close()
        return nc


class Arena:
    def __init__(self, p, nbytes):
        self.p = p
        self.nbytes = nbytes
        self.base = p.sbuf("arena", [128, nbytes // 2], BF16)
        self.regs = []
        self.n = 0

    def carve(self, off, shape, dt, name):
        esz = 4 if dt == F32 else 2
        n = int(np.prod(shape[1:]))
        nb = n * esz
        assert off % 4 == 0 and off + nb <= self.nbytes, (name, off, nb, self.nbytes)
        a = self.base.ap[0:shape[0], off // 2:(off + nb) // 2]
        if dt == F32:
            a = a.bitcast(F32)
        if len(shape) == 3:
            a = a.rearrange("p (a b) -> p a b", a=shape[1])
        elif len(shape) == 4:
            a = a.rearrange("p (a b c) -> p a b c", a=shape[1], b=shape[2])
        key = ("ar", name, self.n)
        self.n += 1
        inherit = {}
        for (s, e, k) in self.regs:
            if s < off + nb and off < e:
                st = self.p.res.get(k)
                if st:
                    toks = list(st["r"].values())
                    if st["w"] is not None:
                        toks.append(st["w"])
                    for t in toks:
                        sk = self.p.ins[t]["semkey"]
                        if sk not in inherit or inherit[sk] < t:
                            inherit[sk] = t
        self.regs.append((off, off + nb, key))
        self.p.res[key] = {"w": None, "r": inherit}
        return V(a, key)


def subview(arena, v, tag, ap=None):
    key = (v.res, tag)
    p = arena.p
    par = p.res.get(v.res, {"w": None, "r": {}})
    r = dict(par["r"])
    if par["w"] is not None:
        sk = p.ins[par["w"]]["semkey"]
        if sk not in r or r[sk] < par["w"]:
            r[sk] = par["w"]
    p.res[key] = {"w": None, "r": r}
    for (s_, e_, k_) in list(arena.regs):
        if k_ == v.res:
            arena.regs.append((s_, e_, key))
            break
    return V(v.ap if ap is None else ap, key)


class Bump:
    def __init__(self, arena, start, end=None):
        self.a = arena
        self.off = start
        self.end = end if end is not None else arena.nbytes

    def alloc(self, shape, dt, name):
        esz = 4 if dt == F32 else 2
        nb = int(np.prod(shape[1:])) * esz
        nb4 = (nb + 3) // 4 * 4
        assert self.off + nb4 <= self.end, ("bump overflow", name, self.off, nb4, self.end)
        v = self.a.carve(self.off, shape, dt, name)
        self.off += nb4
        return v


def bc_mid(v, h):
    shp = list(v.ap.shape)
    return V(v.ap.unsqueeze(1).to_broadcast([shp[0], h, shp[1]]), v.res)


def bc_last(v, n):
    shp = list(v.ap.shape)
    return V(v.ap.unsqueeze(2).to_broadcast([shp[0], shp[1], n]), v.res)


class KB:
    ARENA = 204800
    PERS = 34816

    def __init__(self, stage=99):
        self.stage = stage
        p = self.p = Prog()
        self.D = D = {}

        def din(name, shape, dt=F32):
            D[name] = p.dram(name, shape, dt, "ExternalInput")

        def dout(name, shape):
            D[name] = p.dram(name, shape, F32, "ExternalOutput")

        def dscr(name, shape, dt):
            D[name] = p.dram(name, shape, dt, "Internal")

        din("xp", [SEQ, DM]); din("xs", [NS, DM])
        din("cak", [NS, LA, 512]); din("cav", [NS, LA, 512])
        din("cbk", [NS, LB, 128]); din("cbv", [NS, LB, 128])
        din("cmk", [NS, MEM, DM]); din("cmv", [NS, MEM, DM])
        din("sconv", [NS, 2, DFF]); din("memp", [MEM, DM])
        din("g_mix", [DM]); din("w_in", [DM, IN_DIM]); din("g_out_a", [512]); din("g_out_b", [512])
        din("sinks", [8]); din("w_out", [DM, DM]); din("g_cross", [DM]); din("g_mem", [DM])
        din("w_xq", [DM, DM]); din("w_mem_kv", [DM, 2 * DM]); din("w_xo", [DM, DM]); din("g_ffn", [DM])
        din("w_up", [DM, 2 * DFF]); din("conv_w", [3, DFF]); din("conv_b", [DFF]); din("w_down", [DFF, DM])
        din("g_final", [DM])
        din("c_rope", [SEQ, 128]); din("c_ropes", [NS, 128]); din("c_mask", [128, 256])
        din("c_ident", [128, 128]); din("c_sel", [NS, NS * 128])
        dout("yp", [SEQ, DM]); dout("ys", [NS, DM])
        dout("pak", [LA, 512]); dout("pav", [LA, 512]); dout("pbk", [LB, 128]); dout("pbv", [LB, 128])
        dout("pmk", [MEM, DM]); dout("pmv", [MEM, DM]); dout("pconv", [2, DFF])
        dout("sak", [NS, LA, 512]); dout("sav", [NS, LA, 512]); dout("sbk", [NS, LB, 128]); dout("sbv", [NS, LB, 128])
        dout("sconvo", [NS, 2, DFF])
        dscr("scrA", [SEQ, 1536], BF16); dscr("scrB", [SEQ, 768], BF16)
        dscr("scrO", [4, SEQ, 520], F32); dscr("scrX2", [SEQ, DM], F32)

        self.arena = Arena(p, self.ARENA)
        self.ps = p.psum("psall", [128, 4096], F32)
        self.bank_ctr = 0
        self.nb_mod = 6
        self.tb_ctr = 0
        self.pers = Bump(self.arena, 0, self.PERS)
        self.setup_consts()

    def bank(self, i):
        return V(self.ps.ap[:, 512 * i:512 * (i + 1)], ("ps", i))

    def nb(self):
        i = self.bank_ctr % self.nb_mod
        self.bank_ctr += 1
        return self.bank(i)

    def tbank(self):
        j = 6 + self.tb_ctr % 2
        self.tb_ctr += 1
        a = self.ps.ap[:, 512 * j:512 * (j + 1)].bitcast(BF16)
        return V(a.rearrange("p (a b) -> p a b", a=8), ("ps", j))

    def bcast_dram(self, name, n, parts=128):
        t = self.D[name].ap.tensor
        return V(bass.AP(t, 0, [[0, parts], [1, n]]), ("dram", name))

    def setup_consts(self):
        p, D, b = self.p, self.D, self.pers
        self.identf = b.alloc([128, 128], F32, "identf")
        self.ident = b.alloc([128, 128], BF16, "ident")
        self.ones_bf = b.alloc([128, 128], BF16, "ones_bf")
        self.epsb = b.alloc([128, 1], F32, "epsb")
        self.ss = [b.alloc([128, 1], F32, f"ss{i}") for i in range(4)]
        self.ss_ctr = 0
        self.junk = b.alloc([128, DM], BF16, "junk")
        self.g_ffn = b.alloc([128, DM], F32, "g_ffn")
        self.g_final = b.alloc([128, DM], F32, "g_final")
        self.PERS_A = b.off
        self.mask2 = b.alloc([128, 2, 128], BF16, "mask2")
        self.g_mix = b.alloc([128, DM], F32, "g_mix")
        self.g_cross = b.alloc([128, DM], F32, "g_cross")
        self.g_out = b.alloc([128, DM], F32, "g_out")
        self.esink = b.alloc([128, 8], F32, "esink")
        self.ropes = b.alloc([NS, 128], F32, "ropes")
        self.mkT = b.alloc([128, 8, MEM], BF16, "mkT")
        self.mv = b.alloc([128, 2, DM], BF16, "mv")
        p.memset("pool", self.epsb, EPS)
        p.dma("sp", self.identf, D["c_ident"], "ld_c_identf")
        p.dma("pool", self.ident, D["c_ident"], "ld_c_ident")
        p.dma("pool", self.mask2.re("p a b -> p (a b)"), D["c_mask"], "ld_c_mask")
        p.memset("pool", self.ones_bf, 1.0)
        p.dma("sp", self.g_mix, self.bcast_dram("g_mix", DM), "ld_c_gmix")
        p.dma("sp", self.g_cross, self.bcast_dram("g_cross", DM), "ld_c_gcross")
        p.dma("sp", self.g_ffn, self.bcast_dram("g_ffn", DM), "ld_c_gffn")
        p.dma("sp", self.g_final, self.bcast_dram("g_final", DM), "ld_c_gfinal")
        p.dma("sp", self.g_out[:, 0:512], self.bcast_dram("g_out_a", 512), "ld_c_goa")
        p.dma("sp", self.g_out[:, 512:1024], self.bcast_dram("g_out_b", 512), "ld_c_gob")
        p.dma("sp", self.esink, self.bcast_dram("sinks", 8), "ld_c_sinks")
        p.dma("sp", self.ropes, D["c_ropes"], "ld_c_ropes")
        p.act(self.esink, self.esink, AF.Exp)

    def load_w(self, dst, name, kchunks, sem):
        src = self.D[name]
        keys = []
        for kc in range(kchunks):
            sub = subview(self.arena, dst, ("kc", kc), dst.ap[:, kc, :])
            self.p.dma("pool", sub, src[kc * 128:(kc + 1) * 128, :], sem)
            keys.append(sub.res)
        return V(dst.ap, keys)

    def next_ss(self):
        s = self.ss[self.ss_ctr % 4]
        self.ss_ctr += 1
        return s

    def rmsnorm(self, x, g, out, P, Dn):
        p = self.p
        ss = self.next_ss()[0:P, :]
        p.memset("dve", ss, 0.0)
        p.act(self.junk[0:P, 0:Dn], x, AF.Square, accum=ss)
        p.act(ss, ss, AF.Ln, scale=1.0 / Dn, bias=self.epsb[0:P, :])
        p.act(ss, ss, AF.Exp, scale=-0.5)
        p.stt("dve", out, x, ss, g, ALU.mult, ALU.mult)

    def transposes(self, dst, src, P, n, evac=("act", "dve"), ident=None):
        p = self.p
        ident = self.ident if ident is None else ident
        gi = 0
        for g in range(0, n, 8):
            m = min(8, n - g)
            pst = self.tbank()
            for j in range(m):
                p.tr(pst[:, j, 0:P], src[0:P, (g + j) * 128:(g + j + 1) * 128], ident[0:P, 0:P])
            p.copy(evac[gi % len(evac)], dst[:, g:g + m, 0:P], pst[:, 0:m, 0:P])
            gi += 1

    def proj_tok(self, hT, w, P, n_out, kchunks, cb):
        p = self.p
        c0 = 0
        while c0 < n_out:
            n = min(512, n_out - c0)
            bk = self.nb()
            for kc in range(kchunks):
                p.mm(bk[0:P, 0:n], hT[:, kc, 0:P], w[:, kc, c0:c0 + n], start=(kc == 0), stop=(kc == kchunks - 1))
            cb(bk[0:P, 0:n], c0, n)
            c0 += n

    def bulk_copies(self):
        p, D = self.p, self.D
        for n in range(NS):
            p.dma("act", D["sak"][n, 0:LA - 1, :], D["cak"][n, 1:LA, :], "bulk", is_output=True)
            p.dma("act", D["sav"][n, 0:LA - 1, :], D["cav"][n, 1:LA, :], "bulk", is_output=True)
        p.dma("act", D["sbk"][:, 0:LB - 1, :], D["cbk"][:, 1:LB, :], "bulk", is_output=True)
        p.dma("act", D["sbv"][:, 0:LB - 1, :], D["cbv"][:, 1:LB, :], "bulk", is_output=True)
        p.dma("act", D["sconvo"][:, 0, :], D["sconv"][:, 1, :], "bulk", is_output=True)

    def phase0(self):
        p, D = self.p, self.D
        b = Bump(self.arena, self.PERS)
        wkv = b.alloc([128, 8, 2 * DM], BF16, "wkv")
        gm = b.alloc([128, DM], F32, "g_mem")
        wkv = self.load_w(wkv, "w_mem_kv", 8, "ld_w0")
        p.dma("sp", gm, self.bcast_dram("g_mem", DM), "ld_c0_12")
        xm = [b.alloc([128, DM], F32, f"xm{i}") for i in range(2)]
        hb = [b.alloc([128, DM], BF16, f"hbm{i}") for i in range(2)]
        hT = [b.alloc([128, 8, 128], BF16, f"hTm{i}") for i in range(2)]
        kvf = [b.alloc([128, 2 * DM], F32, f"kvf{i}") for i in range(2)]
        kb = [b.alloc([128, DM], BF16, f"kbm{i}") for i in range(2)]
        for mt in range(2):
            rows = slice(mt * 128, (mt + 1) * 128)
            p.dma("sp", xm[mt], D["memp"][rows, :], f"ld_xm{mt}")
            self.rmsnorm(xm[mt], gm, hb[mt], 128, DM)
            self.transposes(hT[mt], hb[mt], 128, 8)

            def cb(bk, c0, n, mt=mt):
                p.copy("act", kvf[mt][:, c0:c0 + n], bk)
            self.proj_tok(hT[mt], wkv, 128, 2 * DM, 8, cb)
            p.dma("sp", D["pmk"][rows, :], kvf[mt][:, 0:DM], f"st_kvfk{mt}", is_output=True)
            p.dma("sp", D["pmv"][rows, :], kvf[mt][:, DM:2 * DM], f"st_kvfv{mt}", is_output=True)
            p.copy("pool", kb[mt], kvf[mt][:, 0:DM])
            self.transposes(self.mkT[:, :, rows], kb[mt], 128, 8)
            p.copy("pool", self.mv[:, mt, :], kvf[mt][:, DM:2 * DM])

    def inproj_tile(self, P, x, hb, hT, w_in, cos2, sinS, zqk, zv, zqkB, zvB, tmp, zbA, zbB, want_vf32, want_vBf32):
        p = self.p
        self.rmsnorm(x, self.g_mix[0:P, :], hb, P, DM)
        self.transposes(hT, hb, P, 8)
        cosb = lambda h: bc_mid(cos2, h)
        sin_lo = lambda h: bc_mid(sinS[:, 0:32], h)
        sin_hi = lambda h: bc_mid(sinS[:, 32:64], h)

        def rope(bk, dst, ncols):
            h = ncols // 64
            src = bk[:, 0:ncols].re("p (h d) -> p h d", d=64)
            d3 = dst.re("p (h d) -> p h d", d=64)
            t3 = tmp[:, 0:ncols].re("p (h d) -> p h d", d=64)
            p.tt("dve", d3, src, cosb(h), ALU.mult)
            p.tt("dve", t3[:, :, 0:32], src[:, :, 32:64], sin_lo(h), ALU.mult)
            p.tt("dve", t3[:, :, 32:64], src[:, :, 0:32], sin_hi(h), ALU.mult)
            p.tt("dve", d3, d3, t3, ALU.add)

        def cb(bk, c0, n):
            if c0 == 0:
                rope(bk, zqk[:, 0:512], 512)
            elif c0 == 512:
                rope(bk, zqk[:, 512:1024], 512)
                if zbA is not None:
                    p.copy("act", zbA[0][:, 0:1024], zqk)
            elif c0 == 1024:
                if zbA is not None:
                    p.copy("act", zbA[1][:, 1024:1536], bk)
                if want_vf32:
                    p.copy("act", zv, bk)
            elif c0 == 1536:
                rope(bk, zqkB[:, 0:512], 512)
            else:
                rope(bk, zqkB[:, 512:640], 128)
                if zbB is not None:
                    p.copy("act", zbB[0][:, 0:512].re("p (f t d) -> p f t d", f=4, t=2),
                           zqkB[:, 0:512].re("p (t f d) -> p f t d", t=2, f=4))
                    p.copy("act", zbB[0][:, 512:640], zqkB[:, 512:640])
                    p.copy("act", zbB[1][:, 640:768], bk[:, 128:256])
                if want_vBf32:
                    p.copy("act", zvB, bk[:, 128:256])
        self.proj_tok(hT, w_in, P, IN_DIM, 8, cb)

    def phase1(self):
        p, D = self.p, self.D
        b = Bump(self.arena, self.PERS)
        w_in = b.alloc([128, 8, IN_DIM], BF16, "w_in")
        w_in = self.load_w(w_in, "w_in", 8, "ld_w1")
        rt = b.alloc([128, NT, 128], F32, "rope_tab")
        p.dma("sp", rt, D["c_rope"].re("(t p) c -> p t c", p=128), "ld_c0_13")
        xt = [b.alloc([128, DM], F32, f"xt{i}") for i in range(3)]
        hb = [b.alloc([128, DM], BF16, f"hb{i}") for i in range(2)]
        hT = [b.alloc([128, 8, 128], BF16, f"hT{i}") for i in range(2)]
        zqk = [b.alloc([128, 1024], F32, f"zqk{i}") for i in range(2)]
        zv = [b.alloc([128, 512], F32, f"zv{i}") for i in range(2)]
        zqkB = [b.alloc([128, 640], F32, f"zqkB{i}") for i in range(2)]
        zvB = b.alloc([128, 128], F32, "zvB")
        tmp = [b.alloc([128, 512], F32, f"tmp{i}") for i in range(2)]
        zbA, zbB = [], []
        for i in range(2):
            a = b.alloc([128, 1536], BF16, f"zbA{i}")
            v1, v2 = subview(self.arena, a, "qk"), subview(self.arena, a, "v")
            zbA.append((v1, v2, V(a.ap, [v1.res, v2.res])))
            a = b.alloc([128, 768], BF16, f"zbB{i}")
            v1, v2 = subview(self.arena, a, "qk"), subview(self.arena, a, "v")
            zbB.append((v1, v2, V(a.ap, [v1.res, v2.res])))
        self.p1_end = b.off
        for t in range(NT):
            rows = slice(t * 128, (t + 1) * 128)
            x = xt[t % 3]
            p.dma("sp", x, D["xp"][rows, :], f"ld_xt{t % 3}")
            i = t % 2
            self.inproj_tile(128, x, hb[i], hT[i], w_in, rt[:, t, 0:64], rt[:, t, 64:128],
                             zqk[i], zv[i], zqkB[i], zvB, tmp[i], zbA[i], zbB[i], t >= NT // 2, t == NT - 1)
            if t >= NT // 2:
                orow = slice((t - NT // 2) * 128, (t - NT // 2 + 1) * 128)
                p.dma("sp", D["pak"][orow, :], zqk[i][:, 512:1024], f"st_zqk{i}", is_output=True)
                p.dma("sp", D["pav"][orow, :], zv[i], f"st_zv{i}", is_output=True)
            if t == NT - 1:
                p.dma("sp", D["pbk"], zqkB[i][:, 512:640], f"st_zqkB{i}", is_output=True)
                p.dma("sp", D["pbv"], zvB, "st_zvB", is_output=True)
            self.scr_tokens.append(p.dma("sp", D["scrA"][rows, :], zbA[i][2], f"st_zbA{i}"))
            self.scr_tokens.append(p.dma("sp", D["scrB"][rows, :], zbB[i][2], f"st_zbB{i}"))
        sb = self.samp
        xs = sb["xs"]
        p.dma("sp", xs, D["xs"], "ld_xs")
        self.inproj_tile(NS, xs, hb[0][0:NS, :], hT[0], w_in, self.ropes[:, 0:64], self.ropes[:, 64:128],
                         sb["zqk"], sb["zv"], sb["zqkB"], sb["zvB"], tmp[0][0:NS, :], None, None, True, True)
        p.dma("sp", D["sak"][:, LA - 1, :], sb["zqk"][:, 512:1024], "st_s1a", is_output=True)
        p.dma("sp", D["sav"][:, LA - 1, :], sb["zv"], "st_s1b", is_output=True)
        p.dma("sp", D["sbk"][:, LB - 1, :], sb["zqkB"][:, 512:640], "st_s1c", is_output=True)
        p.dma("sp", D["sbv"][:, LB - 1, :], sb["zvB"], "st_s1d", is_output=True)

    def phase2(self):
        p, D = self.p, self.D
        b = Bump(self.arena, self.PERS, self.PH_END)
        blk = [b.alloc([128, 1536], BF16, f"blk{i}") for i in range(3)]
        QT = [b.alloc([128, 4, 2, 128], BF16, f"QZ{i}") for i in range(2)]
        for v in QT:
            p.memset("pool", v, 0.0)
        KT = [b.alloc([128, 4, 128], BF16, f"KT{i}") for i in range(3)]
        VX = [b.alloc([128, 8, 65], BF16, f"VX{i}") for i in range(3)]
        PT = [b.alloc([128, 2, 2, 128], BF16, f"PT{i}") for i in range(8)]
        OS = [b.alloc([128, 520], F32, f"OS{i}") for i in range(3)]
        for v in VX:
            p.memset("pool", v, 1.0)
        p.wait_for("sp", list(self.scr_tokens))
        mask4 = V(self.mask2.ap.unsqueeze(2).to_broadcast([128, 2, 2, 128]), self.mask2.res)
        mask_own = bc_mid(self.mask2[:, 0, :], 2)
        st = {"s": 0, "pc": 0}
        self.o_tokens = []

        def st_load(c):
            kind, br, d, r, bb, first, s = c
            cur = blk[s % 3]
            qt, kt, vx = QT[s % 2], KT[s % 3], VX[s % 3]
            if kind == "A":
                src = D["scrA"].re("(j r) c -> r j c", r=d)[r, 128 * bb:128 * (bb + 1), :]
                p.dma("sp", cur, src, f"ld_blk{s % 3}")
            else:
                src = D["scrB"][128 * bb:128 * (bb + 1), :]
                p.dma("sp", cur[:, 0:768], src, f"ld_blk{s % 3}")
            pq = self.tbank()
            for j in range(4):
                p.tr(pq[:, j, :], cur[:, j * 128:(j + 1) * 128], self.ident)
            p.copy("dve", qt[0:64, :, 0, :], pq[0:64, 0:4, :])
            p.copy("dve", qt[64:128, :, 1, :], pq[64:128, 0:4, :])
            if kind == "A":
                self.transposes(kt, cur[:, 512:1024], 128, 4, evac=("act",))
                p.copy("pool", vx[:, :, 0:64], cur[:, 1024:1536].re("p (h d) -> p h d", d=64))
            else:
                self.transposes(kt[:, 0:1, :], cur[:, 512:640], 128, 1, evac=("act",))
                p.copy("pool", vx[:, 0:2, 0:64], cur[:, 640:768].re("p (h d) -> p h d", d=64))

        def st_scores(c):
            kind, br, d, r, bb, first, s = c
            qt, kt, ktp = QT[s % 2], KT[s % 3], KT[(s - 1) % 3]
            for j in range(4):
                psS = self.bank(j)
                kj = j if kind == "A" else 0
                q2 = qt[:, j].re("p a q -> p (a q)")
                p.mm(psS[:, 0:256], kt[:, kj, :], q2)
                if not first:
                    p.mm(psS[:, 256:512], ktp[:, kj, :], q2)
                pt = PT[(4 * s + j) % 8]
                if not first:
                    p.act(pt.re("p b a q -> p (b a q)"), psS, AF.Exp, scale=0.125)
                    p.tt("dve", pt, pt, mask4, ALU.mult)
                else:
                    p.act(pt[:, 0].re("p a q -> p (a q)"), psS[:, 0:256], AF.Exp, scale=0.125)
                    p.tt("dve", pt[:, 0], pt[:, 0], mask_own, ALU.mult)

        def st_pv(c):
            kind, br, d, r, bb, first, s = c
            vx, vxp = VX[s % 3], VX[(s - 1) % 3]
            psO = [self.bank(4), self.bank(5)]
            for j in range(4):
                pt = PT[(4 * s + j) % 8]
                for hh in range(2):
                    if kind == "A":
                        h = 2 * j + hh
                        vi = h
                    else:
                        h = j + 4 * hh
                        vi = hh
                    o = psO[h // 4][:, (h % 4) * 65:(h % 4) * 65 + 65]
                    p.mm(o, pt[:, 0, hh, :], vx[:, vi, :], start=True, stop=first)
                    if not first:
                        p.mm(o, pt[:, 1, hh, :], vxp[:, vi, :], start=False, stop=True)
            osb = OS[s % 3]
            p.copy("act", osb[:, 0:260], psO[0][:, 0:260])
            p.copy("act", osb[:, 260:520], psO[1][:, 0:260])
            if kind == "A":
                dst = D["scrO"][br].re("(j r) c -> r j c", r=d)[r, 128 * bb:128 * (bb + 1), :]
            else:
                dst = D["scrO"][3][128 * bb:128 * (bb + 1), :]
            self.o_tokens.append(p.dma("sp", dst, osb, f"st_os{s % 3}"))

        cfgs = []
        for br, d in ((2, 16), (1, 4), (0, 1)):
            nblk = SEQ // d // 128
            for r in range(d):
                for bb in range(nblk):
                    cfgs.append(("A", br, d, r, bb, bb == 0, len(cfgs)))
        for bb in range(NT):
            cfgs.append(("B", 3, 1, 0, bb, bb == 0, len(cfgs)))
        st_load(cfgs[0])
        for i, c in enumerate(cfgs):
            st_scores(c)
            if i + 1 < len(cfgs):
                st_load(cfgs[i + 1])
            st_pv(c)

    def phase2b(self):
        p, D = self.p, self.D
        b = Bump(self.arena, self.PERS, self.PH_END)
        w_out = b.alloc([128, 8, DM], BF16, "w_out")
        w_xq = b.alloc([128, 8, DM], BF16, "w_xq")
        w_xo = b.alloc([128, 8, DM], BF16, "w_xo")
        w_out = self.load_w(w_out, "w_out", 8, "ld_w2a")
        w_xq = self.load_w(w_xq, "w_xq", 8, "ld_w2b")
        w_xo = self.load_w(w_xo, "w_xo", 8, "ld_w2c")
        self.w2 = (w_out, w_xq, w_xo)
        OL = [[b.alloc([128, 520], F32, f"OL{i}_{k}") for k in range(4)] for i in range(2)]
        cat = [b.alloc([128, DM], F32, f"cat{i}") for i in range(2)]
        hm = [b.alloc([128, DM], BF16, f"hm{i}") for i in range(2)]
        rd = [b.alloc([128, 16], F32, f"rd{i}") for i in range(2)]
        x1 = [b.alloc([128, DM], F32, f"x1_{i}") for i in range(4)]
        hmT = b.alloc([128, 8, 512], BF16, "hmT")
        h2T = b.alloc([128, 8, 512], BF16, "h2T")
        qxT = b.alloc([128, 8, 512], BF16, "qxT")
        oxT = b.alloc([128, 8, 512], BF16, "oxT")
        h2 = [b.alloc([128, DM], BF16, f"h2_{i}") for i in range(2)]
        PTx = [b.alloc([128, 2, 512], BF16, f"PTx{i}") for i in range(2)]
        rden = [b.alloc([128, 512], F32, f"rden{i}") for i in range(2)]
        self.p2b_end = b.off
        p.wait_for("sp", list(self.o_tokens))
        self.x2_tokens = []
        for sti in range(NT // 4):
            for tl in range(4):
                t = sti * 4 + tl
                rows = slice(t * 128, (t + 1) * 128)
                ol = OL[t % 2]
                for k in range(4):
                    p.dma("sp", ol[k], D["scrO"][k][rows, :], f"ld_ol{t % 2}_{k}")
                p.dma("sp", x1[tl], D["xp"][rows, :], f"ld_x1_{tl}")
                self.combine(128, ol, rd[t % 2], cat[t % 2], hm[t % 2], self.esink)
                self.transposes(hmT[:, :, tl * 128:(tl + 1) * 128], hm[t % 2], 128, 8)
            for tl in range(4):
                def cb(bk, c0, n, tl=tl):
                    p.tt("dve", x1[tl][:, c0:c0 + n], x1[tl][:, c0:c0 + n], bk, ALU.add)
                self.proj_tok(hmT[:, :, tl * 128:(tl + 1) * 128], w_out, 128, DM, 8, cb)
                self.rmsnorm(x1[tl], self.g_cross, h2[tl % 2], 128, DM)
                self.transposes(h2T[:, :, tl * 128:(tl + 1) * 128], h2[tl % 2], 128, 8)
            for fc in range(8):
                bk = self.nb()
                for kc in range(8):
                    p.mm(bk, w_xq[:, kc, fc * 128:(fc + 1) * 128], h2T[:, kc, :], start=(kc == 0), stop=(kc == 7))
                p.copy("act" if fc % 2 == 0 else "dve", qxT[:, fc, :], bk)
            for h in range(4):
                pt = PTx[h % 2]
                for mc in range(2):
                    bk = self.nb()
                    for j in range(2):
                        p.mm(bk, self.mkT[:, 2 * h + j, mc * 128:(mc + 1) * 128], qxT[:, 2 * h + j, :],
                             start=(j == 0), stop=(j == 1))
                    p.act(pt[:, mc, :], bk, AF.Exp, scale=1.0 / 16.0)
                bd = self.nb()
                for mc in range(2):
                    p.mm(bd, self.ones_bf, pt[:, mc, :], start=(mc == 0), stop=(mc == 1))
                rdn = rden[h % 2]
                p.recip(rdn, bd)
                for dj in range(2):
                    bk = self.nb()
                    for mc in range(2):
                        p.mm(bk, self.mv[:, mc, h * 256 + dj * 128:h * 256 + (dj + 1) * 128], pt[:, mc, :],
                             start=(mc == 0), stop=(mc == 1))
                    p.tt("dve", oxT[:, 2 * h + dj, :], bk, rdn, ALU.mult)
            for tl in range(4):
                t = sti * 4 + tl
                rows = slice(t * 128, (t + 1) * 128)

                def cb(bk, c0, n, tl=tl):
                    p.tt("dve", x1[tl][:, c0:c0 + n], x1[tl][:, c0:c0 + n], bk, ALU.add)
                self.proj_tok(oxT[:, :, tl * 128:(tl + 1) * 128], w_xo, 128, DM, 8, cb)
                self.x2_tokens.append(p.dma("sp", D["scrX2"][rows, :], x1[tl], f"st_x1_{tl}"))

    def combine(self, P, ol, r, c, hmv, esink):
        p = self.p
        if ol[1] is not None:
            p.tt("pool", ol[0], ol[0], ol[1], ALU.add)
            p.tt("pool", ol[0], ol[0], ol[2], ALU.add)
        a3 = ol[0].re("p (h c) -> p h c", c=65)
        b3 = ol[3].re("p (h c) -> p h c", c=65)
        p.recip(r[:, 0:8], a3[:, :, 64])
        p.tt("dve", c[:, 0:512].re("p (h d) -> p h d", d=64), a3[:, :, 0:64], bc_last(r[:, 0:8], 64), ALU.mult)
        p.tt("dve", r[:, 8:16], b3[:, :, 64], esink[0:P, :], ALU.add)
        p.recip(r[:, 8:16], r[:, 8:16])
        p.tt("dve", c[:, 512:1024].re("p (h d) -> p h d", d=64), b3[:, :, 0:64], bc_last(r[:, 8:16], 64), ALU.mult)
        self.rmsnorm(c[:, 0:512], self.g_out[0:P, 0:512], hmv[:, 0:512], P, 512)
        self.rmsnorm(c[:, 512:1024], self.g_out[0:P, 512:1024], hmv[:, 512:1024], P, 512)

    def phase3(self):
        p, D = self.p, self.D
        b = Bump(self.arena, self.PERS_A, self.ARENA - 4096)
        w_up = b.alloc([128, 8, 2 * DFF], BF16, "w_up")
        w_dn = b.alloc([128, NFC, DM], BF16, "w_dn")
        w_up = self.load_w(w_up, "w_up", 8, "ld_w3a")
        w_dn = self.load_w(w_dn, "w_down", NFC, "ld_w3b")
        self.w3 = (w_up, w_dn)
        cw = b.alloc([128, 3, NFC], F32, "cw")
        cbias = b.alloc([128, NFC], F32, "cbias")
        for i3 in range(3):
            p.dma("sp", cw[:, i3, :], D["conv_w"][i3].re("(fc f) -> f fc", f=128), "ld_cw",
                  allow_slow_non_contiguous=True)
        p.dma("sp", cbias, D["conv_b"].re("(fc f) -> f fc", f=128), "ld_cb", allow_slow_non_contiguous=True)
        gcar = b.alloc([128, NFC, 2], F32, "gcar")
        p.memset("pool", gcar, 0.0)
        ST = 256
        self.p3_tmp_start = b.off
        xl = [b.alloc([128, DM], F32, f"xl{i}") for i in range(4)]
        h3 = [b.alloc([128, DM], BF16, f"h3_{i}") for i in range(2)]
        h3T = b.alloc([128, 8, ST], BF16, "h3T")
        aT = b.alloc([128, NFC, ST], BF16, "aT")
        gsb = [b.alloc([128, ST + 2], F32, f"gsb{i}") for i in range(2)]
        tq = [b.alloc([128, ST], F32, f"tq{i}") for i in range(2)]
        sq = [b.alloc([128, ST], F32, f"sq{i}") for i in range(2)]
        yt = [b.alloc([128, DM], F32, f"yt{i}") for i in range(1)]
        p.wait_for("sp", list(self.x2_tokens))
        for s_ in range(SEQ // ST):
            for tl in range(2):
                t = 2 * s_ + tl
                rows = slice(t * 128, (t + 1) * 128)
                xi = (s_ % 2) * 2 + tl
                p.dma("sp", xl[xi], D["scrX2"][rows, :], f"ld_xl{xi}")
                self.rmsnorm(xl[xi], self.g_ffn, h3[tl], 128, DM)
                self.transposes(h3T[:, :, tl * 128:(tl + 1) * 128], h3[tl], 128, 8)
            for fc in range(NFC):
                bg = self.nb()
                bv = self.nb()
                for kc in range(8):
                    p.mm(bg[:, 0:ST], w_up[:, kc, fc * 128:(fc + 1) * 128], h3T[:, kc, :], start=(kc == 0), stop=(kc == 7))
                for kc in range(8):
                    p.mm(bv[:, 0:ST], w_up[:, kc, DFF + fc * 128:DFF + (fc + 1) * 128], h3T[:, kc, :],
                         start=(kc == 0), stop=(kc == 7))
                g = gsb[fc % 2]
                tt_ = tq[fc % 2]
                ss_ = sq[fc % 2]
                p.copy("pool", g[:, 0:2], gcar[:, fc, :])
                p.copy("act", g[:, 2:ST + 2], bg[:, 0:ST])
                p.copy("pool", gcar[:, fc, :], g[:, ST:ST + 2])
                p.ts("dve", tt_, g[:, 0:ST], cw[:, 0, fc:fc + 1], cbias[:, fc:fc + 1], ALU.mult, ALU.add)
                p.stt("dve", tt_, g[:, 1:ST + 1], cw[:, 1, fc:fc + 1], tt_, ALU.mult, ALU.add)
                p.stt("dve", tt_, g[:, 2:ST + 2], cw[:, 2, fc:fc + 1], tt_, ALU.mult, ALU.add)
                p.act(ss_, tt_, AF.Silu)
                p.tt("dve", aT[:, fc, :], ss_, bv[:, 0:ST], ALU.mult)
            for tl in range(2):
                t = 2 * s_ + tl
                rows = slice(t * 128, (t + 1) * 128)
                x = xl[(s_ % 2) * 2 + tl]

                def cb(bk, c0, n, x=x):
                    p.tt("dve", x[:, c0:c0 + n], x[:, c0:c0 + n], bk, ALU.add)
                self.proj_tok(aT[:, :, tl * 128:(tl + 1) * 128], w_dn, 128, DM, NFC, cb)
                y = yt[0]
                self.rmsnorm(x, self.g_final, y, 128, DM)
                p.dma("sp", D["yp"][rows, :], y, "st_y0", is_output=True)
        for ti in range(2):
            p.dma("sp", D["pconv"][ti].re("(fc f) -> f fc", f=128), gcar[:, :, ti], "st_pconv", is_output=True,
                  allow_slow_non_contiguous=True)

    def samp_attn(self):
        p, D, sb = self.p, self.D, self.samp
        b = Bump(self.arena, self.PERS, self.PH_END)
        sel = b.alloc([NS, NS, 128], F32, "sel")
        p.dma("sp", sel.re("p a b -> p (a b)"), D["c_sel"], "ld_sel")
        qb = [b.alloc([128, 512], F32, f"s_qb{i}") for i in range(2)]
        Kt = [b.alloc([128, 512], F32, f"s_Kt{i}") for i in range(3)]
        Vt = [b.alloc([128, 8, 65], F32, f"s_Vt{i}") for i in range(3)]
        prod = [b.alloc([128, 512], F32, f"s_prod{i}") for i in range(2)]
        sc = [b.alloc([128, 8], F32, f"s_sc{i}") for i in range(2)]
        Pz = [b.alloc([128, 8, NS], F32, f"s_Pz{i}") for i in range(4)]
        oA = b.alloc([NS, 8, 65], F32, "s_oA")
        oB = b.alloc([NS, 8, 65], F32, "s_oB")
        prn = b.alloc([NS, 512], F32, "s_prn")
        sn = b.alloc([NS, 8], F32, "s_sn")
        en = b.alloc([NS, 8], F32, "s_en")
        tv = b.alloc([NS, 8, 64], F32, "s_tv")
        r16 = b.alloc([NS, 16], F32, "s_r16")
        for v in Vt:
            p.memset("pool", v, 1.0)
        for v in Pz:
            p.memset("pool", v, 0.0)
        p.memset("pool", oA, 0.0)
        p.memset("pool", oB, 0.0)
        cnt = 0
        for n in range(NS):
            bk = self.nb()
            p.mm(bk, sel[:, n, :], sb["zqk"][:, 0:512])
            q = qb[n % 2]
            p.copy("act", q, bk)
            for g in range(3):
                if g == 0:
                    ksrc = D["cak"][n, LA - 128:LA, :]
                    vsrc = D["cav"][n, LA - 128:LA, :]
                elif g == 1:
                    ksrc = D["cak"][n].re("(j r) c -> r j c", r=4)[0, 384:512, :]
                    vsrc = D["cav"][n].re("(j r) c -> r j c", r=4)[0, 384:512, :]
                else:
                    ksrc = D["cak"][n].re("(j r) c -> r j c", r=16)[0, 0:128, :]
                    vsrc = D["cav"][n].re("(j r) c -> r j c", r=16)[0, 0:128, :]
                kt, vt = Kt[cnt % 3], Vt[cnt % 3]
                p.dma("sp", kt, ksrc, f"ld_sK{cnt % 3}")
                p.dma("sp", vt[:, :, 0:64], vsrc.re("j (h d) -> j h d", d=64), f"ld_sV{cnt % 3}")
                pr = prod[cnt % 2]
                p.tt("dve", pr, kt, q, ALU.mult)
                s8 = sc[cnt % 2]
                p.reduce("dve", s8, pr.re("p (h d) -> p h d", d=64), ALU.add)
                pz = Pz[cnt % 4]
                p.act(pz[:, :, n], s8, AF.Exp, scale=0.125)
                psA = [self.nb(), self.nb()]
                for h in range(8):
                    o = psA[h // 4][0:NS, (h % 4) * 65:(h % 4) * 65 + 65]
                    p.mm(o, pz[:, h, :], vt[:, h, :])
                for hb in range(2):
                    av = oA[:, 4 * hb:4 * hb + 4, :]
                    p.tt("dve", av, av, psA[hb][0:NS, 0:260].re("p (h c) -> p h c", c=65), ALU.add)
                p.memset("pool", pz[:, :, n], 0.0)
                cnt += 1
        zqk, zv = sb["zqk"], sb["zv"]
        p.tt("dve", prn, zqk[:, 0:512], zqk[:, 512:1024], ALU.mult)
        p.reduce("dve", sn, prn.re("p (h d) -> p h d", d=64), ALU.add)
        p.act(en, sn, AF.Exp, scale=0.125)
        p.ts("dve", en, en, 3.0, None, ALU.mult)
        p.tt("dve", tv, zv.re("p (h d) -> p h d", d=64), bc_last(en, 64), ALU.mult)
        p.tt("dve", oA[:, :, 0:64], oA[:, :, 0:64], tv, ALU.add)
        p.tt("dve", oA[:, :, 64], oA[:, :, 64], en, ALU.add)
        KtB = [V(k.ap[:, 0:128], k.res) for k in Kt]
        VtB = [V(v.ap[:, 0:2, :], v.res) for v in Vt]
        zqkB, zvB = sb["zqkB"], sb["zvB"]
        for n in range(NS):
            bk = self.nb()
            p.mm(bk, sel[:, n, :], zqkB[:, 0:512])
            q = qb[n % 2]
            p.copy("act", q, bk)
            kt, vt = KtB[cnt % 3], VtB[cnt % 3]
            p.dma("sp", kt, D["cbk"][n], f"ld_sK{cnt % 3}")
            p.dma("sp", vt[:, :, 0:64], D["cbv"][n].re("j (h d) -> j h d", d=64), f"ld_sV{cnt % 3}")
            pr = prod[cnt % 2]
            k4 = V(kt.ap.rearrange("p (k d) -> p k d", d=64).unsqueeze(2).to_broadcast([128, 2, 4, 64]), kt.res)
            p.tt("dve", pr.re("p (k g d) -> p k g d", k=2, g=4), q.re("p (k g d) -> p k g d", k=2, g=4), k4, ALU.mult)
            s8 = sc[cnt % 2]
            p.reduce("dve", s8, pr.re("p (h d) -> p h d", d=64), ALU.add)
            pz = Pz[cnt % 4]
            p.act(pz[:, :, n], s8, AF.Exp, scale=0.125)
            psA = [self.nb(), self.nb()]
            for h in range(8):
                o = psA[h // 4][0:NS, (h % 4) * 65:(h % 4) * 65 + 65]
                p.mm(o, pz[:, h, :], vt[:, h // 4, :])
            for hb in range(2):
                av = oB[:, 4 * hb:4 * hb + 4, :]
                p.tt("dve", av, av, psA[hb][0:NS, 0:260].re("p (h c) -> p h c", c=65), ALU.add)
            p.memset("pool", pz[:, :, n], 0.0)
            cnt += 1
        kn4 = V(zqkB.ap[:, 512:640].rearrange("p (k d) -> p k d", d=64).unsqueeze(2).to_broadcast([NS, 2, 4, 64]), zqkB.res)
        p.tt("dve", prn.re("p (k g d) -> p k g d", k=2, g=4), zqkB[:, 0:512].re("p (k g d) -> p k g d", k=2, g=4), kn4, ALU.mult)
        p.reduce("dve", sn, prn.re("p (h d) -> p h d", d=64), ALU.add)
        p.act(en, sn, AF.Exp, scale=0.125)
        vn4 = V(zvB.ap.rearrange("p (k d) -> p k d", d=64).unsqueeze(2).to_broadcast([NS, 2, 4, 64]), zvB.res)
        e4 = V(en.ap.rearrange("p (k g) -> p k g", k=2).unsqueeze(3).to_broadcast([NS, 2, 4, 64]), en.res)
        p.tt("dve", tv.re("p (k g) d -> p k g d", k=2), vn4, e4, ALU.mult)
        p.tt("dve", oB[:, :, 0:64], oB[:, :, 0:64], tv, ALU.add)
        p.tt("dve", oB[:, :, 64], oB[:, :, 64], en, ALU.add)
        self.combine(NS, [oA.re("p h c -> p (h c)"), None, None, oB.re("p h c -> p (h c)")], r16, sb["cat"], sb["hm"],
                     self.esink)

    def samp_mix(self):
        p, D, sb = self.p, self.D, self.samp
        w_out, w_xq, w_xo = self.w2
        b = Bump(self.arena, self.PERS + 3 * 16384, self.PH_END)
        sel = b.alloc([NS, NS, 128], F32, "sel2")
        p.dma("sp", sel.re("p a b -> p (a b)"), D["c_sel"], "ld_sel2")
        hT = b.alloc([128, 8, NS], BF16, "s_hT")
        h2s = b.alloc([NS, DM], BF16, "s_h2")
        qx = b.alloc([NS, DM], F32, "s_qx")
        qbx = [b.alloc([128, DM], F32, f"s_qbx{i}") for i in range(2)]
        Kx = [b.alloc([128, DM], F32, f"s_Kx{i}") for i in range(2)]
        Vx = [b.alloc([128, 4, 257], F32, f"s_Vx{i}") for i in range(2)]
        prodx = b.alloc([128, DM], F32, "s_prodx")
        s4 = [b.alloc([128, 4], F32, f"s_s4{i}") for i in range(2)]
        Pzx = [b.alloc([128, 4, NS], F32, f"s_Pzx{i}") for i in range(4)]
        oX = b.alloc([NS, 4, 257], F32, "s_oX")
        r4 = b.alloc([NS, 4], F32, "s_r4")
        oxn = b.alloc([NS, DM], BF16, "s_oxn")
        xs = sb["xs"]
        self.transposes(hT, sb["hm"], NS, 8)

        def cb(bk, c0, n):
            p.tt("dve", xs[:, c0:c0 + n], xs[:, c0:c0 + n], bk, ALU.add)
        self.proj_tok(hT, w_out, NS, DM, 8, cb)
        self.rmsnorm(xs, self.g_cross[0:NS, :], h2s, NS, DM)
        self.transposes(hT, h2s, NS, 8)

        def cb2(bk, c0, n):
            p.copy("act", qx[:, c0:c0 + n], bk)
        self.proj_tok(hT, w_xq, NS, DM, 8, cb2)
        for v in Vx:
            p.memset("pool", v, 1.0)
        for v in Pzx:
            p.memset("pool", v, 0.0)
        p.memset("pool", oX, 0.0)
        cnt = 0
        for n in range(NS):
            q = qbx[n % 2]
            for half in range(2):
                bk = self.nb()
                p.mm(bk, sel[:, n, :], qx[:, half * 512:(half + 1) * 512])
                p.copy("act", q[:, half * 512:(half + 1) * 512], bk)
            for mc in range(2):
                kx, vx = Kx[cnt % 2], Vx[cnt % 2]
                rows = slice(mc * 128, (mc + 1) * 128)
                p.dma("sp", kx, D["cmk"][n, rows, :], f"ld_sKx{cnt % 2}")
                p.dma("sp", vx[:, :, 0:256], D["cmv"][n, rows, :].re("j (h d) -> j h d", d=256), f"ld_sVx{cnt % 2}")
                p.tt("dve", prodx, kx, q, ALU.mult)
                s_ = s4[cnt % 2]
                p.reduce("dve", s_, prodx.re("p (h d) -> p h d", d=256), ALU.add)
                pz = Pzx[cnt % 4]
                p.act(pz[:, :, n], s_, AF.Exp, scale=1.0 / 16.0)
                for h in range(4):
                    bo = self.nb()
                    p.mm(bo[0:NS, 0:257], pz[:, h, :], vx[:, h, :])
                    p.tt("dve", oX[:, h, :], oX[:, h, :], bo[0:NS, 0:257], ALU.add)
                p.memset("pool", pz[:, :, n], 0.0)
                cnt += 1
        p.recip(r4, oX[:, :, 256])
        p.tt("dve", oxn.re("p (h d) -> p h d", d=256), oX[:, :, 0:256], bc_last(r4, 256), ALU.mult)
        self.transposes(hT, oxn, NS, 8)
        self.proj_tok(hT, w_xo, NS, DM, 8, cb)

    def samp_ffn(self):
        p, D, sb = self.p, self.D, self.samp
        w_up, w_dn = self.w3
        b = Bump(self.arena, self.p3_tmp_start, self.ARENA - 4096)
        h3s = b.alloc([NS, DM], BF16, "s_h3")
        hT = b.alloc([128, 8, NS], BF16, "s_h3T")
        gs = b.alloc([NS, DFF], F32, "s_gs")
        vs = b.alloc([NS, DFF], F32, "s_vs")
        abf = b.alloc([NS, DFF], BF16, "s_abf")
        aT = b.alloc([128, NFC, NS], BF16, "s_aT")
        ysb = b.alloc([NS, DM], F32, "s_ysb")
        xs = sb["xs"]
        self.rmsnorm(xs, self.g_ffn[0:NS, :], h3s, NS, DM)
        self.transposes(hT, h3s, NS, 8)

        def cbu(bk, c0, n):
            lo, hi = c0, c0 + n
            if lo < DFF:
                m = min(hi, DFF) - lo
                p.copy("act", gs[:, lo:lo + m], bk[:, 0:m])
            if hi > DFF:
                s0 = max(lo, DFF)
                p.copy("act", vs[:, s0 - DFF:hi - DFF], bk[:, s0 - lo:n])
        self.proj_tok(hT, w_up, NS, 2 * DFF, 8, cbu)
        p.dma("sp", D["sconvo"][:, 1, :], gs, "st_sgs", is_output=True)
        b2 = Bump(self.arena, self.PERS_A, self.PERS_A + 90112)
        s0t = b2.alloc([NS, DFF], F32, "s_s0")
        s1t = b2.alloc([NS, DFF], F32, "s_s1")
        cwb = b2.alloc([NS, 3, DFF], F32, "s_cwb")
        cbb = b2.alloc([NS, DFF], F32, "s_cbb")
        t1 = b2.alloc([NS, DFF], F32, "s_t1")
        t2 = b2.alloc([NS, DFF], F32, "s_t2")
        p.dma("sp", s0t, D["sconv"][:, 0, :], "ld_ss0")
        p.dma("sp", s1t, D["sconv"][:, 1, :], "ld_ss1")
        tcw = D["conv_w"].ap.tensor
        p.dma("sp", cwb.re("p a b -> p (a b)"), V(bass.AP(tcw, 0, [[0, NS], [1, 3 * DFF]]), ("dram", "conv_w")), "ld_scw")
        p.dma("sp", cbb, self.bcast_dram("conv_b", DFF, NS), "ld_scb")
        p.tt("dve", t1, s0t, cwb[:, 0, :], ALU.mult)
        p.tt("pool", t2, s1t, cwb[:, 1, :], ALU.mult)
        p.tt("dve", t1, t1, t2, ALU.add)
        p.tt("pool", t2, gs, cwb[:, 2, :], ALU.mult)
        p.tt("dve", t1, t1, t2, ALU.add)
        p.tt("dve", t1, t1, cbb, ALU.add)
        p.act(t2, t1, AF.Silu)
        p.tt("dve", abf, t2, vs, ALU.mult)
        self.transposes(aT, abf, NS, NFC)

        def cb(bk, c0, n):
            p.tt("dve", xs[:, c0:c0 + n], xs[:, c0:c0 + n], bk, ALU.add)
        self.proj_tok(aT, w_dn, NS, DM, NFC, cb)
        self.rmsnorm(xs, self.g_final[0:NS, :], ysb, NS, DM)
        p.dma("sp", D["ys"], ysb, "st_ys", is_output=True)

    def alloc_sample(self):
        top = Bump(self.arena, self.ARENA - 4096)
        b = Bump(self.arena, self.ARENA - 20480, self.ARENA - 4096)
        self.samp = {
            "xs": top.alloc([NS, DM], F32, "s_xs"),
            "zqk": b.alloc([NS, 1024], F32, "s_zqk"),
            "zv": b.alloc([NS, 512], F32, "s_zv"),
            "zqkB": b.alloc([NS, 640], F32, "s_zqkB"),
            "zvB": b.alloc([NS, 128], F32, "s_zvB"),
            "cat": b.alloc([NS, DM], F32, "s_cat"),
            "hm": b.alloc([NS, DM], BF16, "s_hm"),
        }
        self.samp_bump = b
        self.PH_END = self.ARENA - 20480

    def build(self):
        self.scr_tokens = []
        self.alloc_sample()
        self.bulk_copies()
        self.phase0()
        self.phase1()
        if self.stage >= 2:
            self.samp_attn()
            self.phase2()
        if self.stage >= 3:
            self.phase2b()
            self.samp_mix()
        if self.stage >= 4:
            self.phase3()
            self.samp_ffn()
        return self.p.build()


def make_consts():
    half = 32
    inv = np.power(np.float32(10000.0), -np.arange(half, dtype=np.float32) / np.float32(half)).astype(np.float32)

    def tab(pos):
        ang = pos.astype(np.float32)[:, None] * inv[None, :]
        c = np.cos(ang).astype(np.float32)
        s = np.sin(ang).astype(np.float32)
        return np.concatenate([c, c, -s, s], axis=1).astype(np.float32)
    c_rope = tab(np.arange(SEQ))
    c_ropes = tab(np.full((NS,), PAST))
    k = np.arange(128)[:, None]
    q = np.arange(128)[None, :]
    own = (k <= q).astype(np.float32)
    prev = (k >= q).astype(np.float32)
    c_mask = np.concatenate([own, prev], axis=1).astype(np.float32)
    c_ident = np.eye(128, dtype=np.float32)
    c_sel = np.zeros((NS, NS, 128), np.float32)
    for n in range(NS):
        c_sel[n, n, :] = 1.0
    return {"c_rope": c_rope, "c_ropes": c_ropes, "c_mask": c_mask, "c_ident": c_ident,
            "c_sel": c_sel.reshape(NS, NS * 128)}


_STAGE = 4


def kernel(x_prompt, x_sample, cache_a_k, cache_a_v, cache_b_k, cache_b_v, cache_mem_k, cache_mem_v, state_conv,
           mem_prompt, g_mix, w_in, g_out_a, g_out_b, sinks, w_out, g_cross, g_mem, w_xq, w_mem_kv, w_xo,
           g_ffn, w_up, conv_w, conv_b, w_down, g_final):
    f = lambda a: np.ascontiguousarray(np.asarray(a, dtype=np.float32))
    kb = KB(stage=_STAGE)
    nc = kb.build()
    consts = make_consts()
    shared = {
        "g_mix": f(g_mix[0]), "w_in": f(w_in[0]), "g_out_a": f(g_out_a[0]), "g_out_b": f(g_out_b[0]),
        "sinks": f(sinks[0]), "w_out": f(w_out[0]), "g_cross": f(g_cross[0]), "g_mem": f(g_mem[0]),
        "w_xq": f(w_xq[0]), "w_mem_kv": f(w_mem_kv[0]), "w_xo": f(w_xo[0]), "g_ffn": f(g_ffn[0]),
        "w_up": f(w_up[0]), "conv_w": f(conv_w[0]), "conv_b": f(conv_b[0]), "w_down": f(w_down[0]),
        "g_final": f(g_final),
    }
    shared.update(consts)
    in_maps = []
    for c in range(NCORES):
        s = slice(c * NS, (c + 1) * NS)
        m = dict(shared)
        m["xp"] = f(x_prompt[c])
        m["xs"] = f(x_sample[s, 0])
        m["cak"] = f(cache_a_k[0, s]).reshape(NS, LA, 512)
        m["cav"] = f(cache_a_v[0, s]).reshape(NS, LA, 512)
        m["cbk"] = f(cache_b_k[0, s]).reshape(NS, LB, 128)
        m["cbv"] = f(cache_b_v[0, s]).reshape(NS, LB, 128)
        m["cmk"] = f(cache_mem_k[0, s]).reshape(NS, MEM, DM)
        m["cmv"] = f(cache_mem_v[0, s]).reshape(NS, MEM, DM)
        m["sconv"] = f(state_conv[0, s])
        m["memp"] = f(mem_prompt[c])
        in_maps.append(m)
    res = run_bass_kernel_spmd(nc, in_maps, core_ids=list(range(NCORES)))
    R = res.results
    cat = lambda k: np.stack([np.asarray(R[c][k], dtype=np.float32) for c in range(NCORES)])
    catn = lambda k: np.concatenate([np.asarray(R[c][k], dtype=np.float32) for c in range(NCORES)], axis=0)
    y_prompt = cat("yp")
    y_sample = catn("ys").reshape(NCORES * NS, 1, DM)
    p_a_k = cat("pak").reshape(1, NCORES, LA, 8, 64)
    p_a_v = cat("pav").reshape(1, NCORES, LA, 8, 64)
    p_b_k = cat("pbk").reshape(1, NCORES, LB, 2, 64)
    p_b_v = cat("pbv").reshape(1, NCORES, LB, 2, 64)
    p_mem_k = cat("pmk").reshape(1, NCORES, MEM, 4, 256)
    p_mem_v = cat("pmv").reshape(1, NCORES, MEM, 4, 256)
    p_conv = cat("pconv").reshape(1, NCORES, 2, DFF)
    s_a_k = catn("sak").reshape(1, NCORES * NS, LA, 8, 64)
    s_a_v = catn("sav").reshape(1, NCORES * NS, LA, 8, 64)
    s_b_k = catn("sbk").reshape(1, NCORES * NS, LB, 2, 64)
    s_b_v = catn("sbv").reshape(1, NCORES * NS, LB, 2, 64)
    s_conv = catn("sconvo").reshape(1, NCORES * NS, 2, DFF)
    return (y_prompt, y_sample, p_a_k, p_a_v, p_b_k, p_b_v, p_mem_k, p_mem_v, p_conv,
            s_a_k, s_a_v, s_b_k, s_b_v, s_conv)
```

```python
import numpy as np
from contextlib import ExitStack
import ml_dtypes
import concourse.bass as bass
import concourse.mybir as mybir
from concourse.bass_utils import run_bass_kernel_spmd

F32 = mybir.dt.float32
BF16 = mybir.dt.bfloat16
ALU = mybir.AluOpType
AF = mybir.ActivationFunctionType
AX = mybir.AxisListType

NCORES = 8
SEQ = 4096
DM = 1024
NT = SEQ // 128
NS = 16
LA = 2048
LB = 128
MEM = 256
DFF = 2816
NFC = DFF // 128
IN_DIM = 2304
EPS = 1e-6
PAST = 16384


class V:
    __slots__ = ("ap", "res")

    def __init__(self, ap, res):
        self.ap = ap
        self.res = res

    def __getitem__(self, k):
        return V(self.ap[k], self.res)

    def re(self, s, **kw):
        return V(self.ap.rearrange(s, **kw), self.res)

    def bc(self, shape):
        return V(self.ap.to_broadcast(shape), self.res)

    def cast(self, dt):
        return V(self.ap.bitcast(dt), self.res)


def _keys(v):
    if v is None:
        return []
    r = v.res
    if isinstance(r, list):
        return r
    return [r]


class Prog:
    ENGS = ("pe", "act", "dve", "pool", "sp")

    def __init__(self):
        self.nc = bass.Bass("TRN2", target_bir_lowering=False)
        self.es = ExitStack()
        self.ins = []
        self.res = {}
        self.out_tokens = []

    def dram(self, name, shape, dt, kind):
        t = self.nc.dram_tensor(name, list(shape), dt, kind=kind)
        return V(t.ap(), ("dram", name))

    def sbuf(self, name, shape, dt):
        t = self.es.enter_context(self.nc.sbuf_tensor(name, list(shape), dt))
        return V(t[:], ("sb", name))

    def psum(self, name, shape, dt):
        t = self.es.enter_context(self.nc.psum_tensor(name, list(shape), dt))
        return V(t[:], ("ps", name))

    def _rec(self, eng, emit, reads, writes, dma_sem=None, track_dram=False):
        iid = len(self.ins)
        is_dma = dma_sem is not None
        deps = {}

        def add_dep(tok, kind):
            if tok is None:
                return
            pr = self.ins[tok]
            if (not is_dma) and (pr["dma_sem"] is None) and pr["eng"] == eng:
                if eng == "pe":
                    return
            deps[tok] = True

        rk = [k for v in reads for k in _keys(v)]
        wk = [k for v in writes for k in _keys(v)]
        if not track_dram:
            rk = [k for k in rk if k[0] != "dram"]
            wk = [k for k in wk if k[0] != "dram"]
        for r in rk:
            st = self.res.setdefault(r, {"w": None, "r": {}})
            add_dep(st["w"], 0)
        for w in wk:
            st = self.res.setdefault(w, {"w": None, "r": {}})
            add_dep(st["w"], 1)
            for t in st["r"].values():
                add_dep(t, 2)
        semkey = ("dma", dma_sem) if is_dma else ("eng", eng)
        for r in rk:
            self.res[r]["r"][semkey] = iid
        for w in wk:
            self.res[w]["w"] = iid
            self.res[w]["r"] = {}
        self.ins.append({"eng": eng, "emit": emit, "deps": list(deps), "dma_sem": dma_sem,
                         "flag": is_dma, "semkey": semkey, "val": None})
        for d in deps:
            self.ins[d]["flag"] = True
        return iid

    def mm(self, out, lhsT, rhs, start=True, stop=True):
        return self._rec("pe", lambda e: e.matmul(out.ap, lhsT.ap, rhs.ap, start=start, stop=stop),
                         [lhsT, rhs], [out])

    def tr(self, out, in_, ident):
        return self._rec("pe", lambda e: e.transpose(out.ap, in_.ap, ident.ap), [in_, ident], [out])

    def act(self, out, in_, func, bias=None, scale=1.0, accum=None):
        kw = {}
        rd = [in_]
        if bias is not None:
            if isinstance(bias, V):
                kw["bias"] = bias.ap
                rd.append(bias)
            else:
                kw["bias"] = bias
        if isinstance(scale, V):
            kw["scale"] = scale.ap
            rd.append(scale)
        else:
            kw["scale"] = scale
        wr = [out]
        if accum is not None:
            kw["accum_out"] = accum.ap
            wr.append(accum)
        return self._rec("act", lambda e: e.activation(out.ap, in_.ap, func, **kw), rd, wr)

    def tt(self, eng, out, a, b, op):
        return self._rec(eng, lambda e: e.tensor_tensor(out.ap, a.ap, b.ap, op), [a, b], [out])

    def ts(self, eng, out, a, s1, s2, op0, op1=None, accum=None):
        rd = [a]
        s1a = s1.ap if isinstance(s1, V) else s1
        s2a = s2.ap if isinstance(s2, V) else s2
        if isinstance(s1, V):
            rd.append(s1)
        if isinstance(s2, V):
            rd.append(s2)
        wr = [out]
        kw = {}
        if op1 is not None:
            kw["op1"] = op1
        if accum is not None:
            kw["accum_out"] = accum.ap
            wr.append(accum)
        return self._rec(eng, lambda e: e.tensor_scalar(out.ap, a.ap, s1a, s2a, op0, **kw), rd, wr)

    def stt(self, eng, out, a, s, b, op0, op1):
        rd = [a, b]
        sa = s.ap if isinstance(s, V) else s
        if isinstance(s, V):
            rd.append(s)
        return self._rec(eng, lambda e: e.scalar_tensor_tensor(out.ap, a.ap, sa, b.ap, op0, op1), rd, [out])

    def copy(self, eng, out, in_):
        if eng == "act":
            return self._rec("act", lambda e: e.copy(out.ap, in_.ap), [in_], [out])
        return self._rec(eng, lambda e: e.tensor_copy(out.ap, in_.ap), [in_], [out])

    def memset(self, eng, out, val):
        return self._rec(eng, lambda e: e.memset(out.ap, val), [], [out])

    def reduce(self, eng, out, in_, op, axis=AX.X):
        return self._rec(eng, lambda e: e.tensor_reduce(out.ap, in_.ap, axis, op), [in_], [out])

    def recip(self, out, in_):
        return self._rec("dve", lambda e: e.reciprocal(out.ap, in_.ap), [in_], [out])

    def dma(self, q, out, in_, sem, is_output=False, **kw):
        iid = self._rec(q, lambda e: e.dma_start(out=out.ap, in_=in_.ap, **kw), [in_], [out], dma_sem=sem)
        if is_output:
            self.out_tokens.append(iid)
        return iid

    def wait_for(self, eng, tokens):
        iid = self._rec(eng, None, [], [])
        self.ins[iid]["deps"] = list(tokens)
        for d in tokens:
            self.ins[d]["flag"] = True
        return iid

    def build(self):
        nc = self.nc
        self.wait_for("sp", list(self.out_tokens))
        counts = {}
        for r in self.ins:
            if r["flag"]:
                k = r["semkey"]
                inc = 16 if r["dma_sem"] is not None else 1
                counts[k] = counts.get(k, 0) + inc
                r["val"] = counts[k]
        semh = {}
        for k in counts:
            nm = "s_" + "_".join(str(x) for x in k)
            semh[k] = self.es.enter_context(nc.semaphore(nm))
        per_eng = {e: [] for e in self.ENGS}
        for r in self.ins:
            per_eng[r["eng"]].append(r)
        ins = self.ins
        stats = {e: [0, 0] for e in self.ENGS}

        def run(engname, eobj):
            waited = {}
            for r in per_eng[engname]:
                need = {}
                for d in r["deps"]:
                    pr = ins[d]
                    k = pr["semkey"]
                    v = pr["val"]
                    if waited.get(k, 0) >= v:
                        continue
                    if need.get(k, 0) < v:
                        need[k] = v
                for k, v in need.items():
                    eobj.wait_ge(semh[k], v)
                    waited[k] = v
                    stats[engname][1] += 1
                if r["emit"] is not None:
                    bi = r["emit"](eobj)
                    stats[engname][0] += 1
                    if r["flag"]:
                        bi.then_inc(semh[r["semkey"]], 16 if r["dma_sem"] is not None else 1)

        with nc.Block() as block:
            @block.tensor
            def _(e):
                run("pe", e)

            @block.scalar
            def _(e):
                run("act", e)

            @block.vector
            def _(e):
                run("dve", e)

            @block.gpsimd
            def _(e):
                run("pool", e)

            @block.sync
            def _(e):
                run("sp", e)
        self.stats = stats
        self.sem_counts = counts
        self.es.close()
        return nc


class Arena:
    def __init__(self, p, nbytes):
        self.p = p
        self.nbytes = nbytes
        self.base = p.sbuf("arena", [128, nbytes // 2], BF16)
        self.regs = []
        self.n = 0

    def carve(self, off, shape, dt, name):
        esz = 4 if dt == F32 else 2
        n = int(np.prod(shape[1:]))
        nb = n * esz
        assert off % 4 == 0 and off + nb <= self.nbytes, (name, off, nb, self.nbytes)
        a = self.base.ap[0:shape[0], off // 2:(off + nb) // 2]
        if dt == F32:
            a = a.bitcast(F32)
        if len(shape) == 3:
            a = a.rearrange("p (a b) -> p a b", a=shape[1])
        elif len(shape) == 4:
            a = a.rearrange("p (a b c) -> p a b c", a=shape[1], b=shape[2])
        key = ("ar", name, self.n)
        self.n += 1
        inherit = {}
        for (s, e, k) in self.regs:
            if s < off + nb and off < e:
                st = self.p.res.get(k)
                if st:
                    toks = list(st["r"].values())
                    if st["w"] is not None:
                        toks.append(st["w"])
                    for t in toks:
                        sk = self.p.ins[t]["semkey"]
                        if sk not in inherit or inherit[sk] < t:
                            inherit[sk] = t
        self.regs.append((off, off + nb, key))
        self.p.res[key] = {"w": None, "r": inherit}
        return V(a, key)


def subview(arena, v, tag, ap=None):
    key = (v.res, tag)
    p = arena.p
    par = p.res.get(v.res, {"w": None, "r": {}})
    r = dict(par["r"])
    if par["w"] is not None:
        sk = p.ins[par["w"]]["semkey"]
        if sk not in r or r[sk] < par["w"]:
            r[sk] = par["w"]
    p.res[key] = {"w": None, "r": r}
    for (s_, e_, k_) in list(arena.regs):
        if k_ == v.res:
            arena.regs.append((s_, e_, key))
            break
    return V(v.ap if ap is None else ap, key)


class Bump:
    def __init__(self, arena, start, end=None):
        self.a = arena
        self.off = start
        self.end = end if end is not None else arena.nbytes

    def alloc(self, shape, dt, name):
        esz = 4 if dt == F32 else 2
        nb = int(np.prod(shape[1:])) * esz
        nb4 = (nb + 3) // 4 * 4
        assert self.off + nb4 <= self.end, ("bump overflow", name, self.off, nb4, self.end)
        v = self.a.carve(self.off, shape, dt, name)
        self.off += nb4
        return v


def bc_mid(v, h):
    shp = list(v.ap.shape)
    return V(v.ap.unsqueeze(1).to_broadcast([shp[0], h, shp[1]]), v.res)


def bc_last(v, n):
    shp = list(v.ap.shape)
    return V(v.ap.unsqueeze(2).to_broadcast([shp[0], shp[1], n]), v.res)


class KB:
    ARENA = 204800
    PERS = 34816

    def __init__(self, stage=99):
        self.stage = stage
        p = self.p = Prog()
        self.D = D = {}

        def din(name, shape, dt=F32):
            D[name] = p.dram(name, shape, dt, "ExternalInput")

        def dout(name, shape):
            D[name] = p.dram(name, shape, F32, "ExternalOutput")

        def dscr(name, shape, dt):
            D[name] = p.dram(name, shape, dt, "Internal")

        din("xp", [SEQ, DM]); din("xs", [NS, DM])
        din("cak", [NS, LA, 512]); din("cav", [NS, LA, 512])
        din("cbk", [NS, LB, 128]); din("cbv", [NS, LB, 128])
        din("cmk", [NS, MEM, DM]); din("cmv", [NS, MEM, DM])
        din("sconv", [NS, 2, DFF]); din("memp", [MEM, DM])
        din("g_mix", [DM]); din("w_in", [DM, IN_DIM]); din("g_out_a", [512]); din("g_out_b", [512])
        din("sinks", [8]); din("w_out", [DM, DM]); din("g_cross", [DM]); din("g_mem", [DM])
        din("w_xq", [DM, DM]); din("w_mem_kv", [DM, 2 * DM]); din("w_xo", [DM, DM]); din("g_ffn", [DM])
        din("w_up", [DM, 2 * DFF]); din("conv_w", [3, DFF]); din("conv_b", [DFF]); din("w_down", [DFF, DM])
        din("g_final", [DM])
        din("c_rope", [SEQ, 128]); din("c_ropes", [NS, 128]); din("c_mask", [128, 256])
        din("c_ident", [128, 128]); din("c_sel", [NS, NS * 128])
        dout("yp", [SEQ, DM]); dout("ys", [NS, DM])
        dout("pak", [LA, 512]); dout("pav", [LA, 512]); dout("pbk", [LB, 128]); dout("pbv", [LB, 128])
        dout("pmk", [MEM, DM]); dout("pmv", [MEM, DM]); dout("pconv", [2, DFF])
        dout("sak", [NS, LA, 512]); dout("sav", [NS, LA, 512]); dout("sbk", [NS, LB, 128]); dout("sbv", [NS, LB, 128])
        dout("sconvo", [NS, 2, DFF])
        dscr("scrA", [SEQ, 1536], BF16); dscr("scrB", [SEQ, 768], BF16)
        dscr("scrO", [4, SEQ, 520], F32); dscr("scrX2", [SEQ, DM], F32)

        self.arena = Arena(p, self.ARENA)
        self.ps = p.psum("psall", [128, 4096], F32)
        self.bank_ctr = 0
        self.nb_mod = 6
        self.tb_ctr = 0
        self.pers = Bump(self.arena, 0, self.PERS)
        self.setup_consts()

    def bank(self, i):
        return V(self.ps.ap[:, 512 * i:512 * (i + 1)], ("ps", i))

    def nb(self):
        i = self.bank_ctr % self.nb_mod
        self.bank_ctr += 1
        return self.bank(i)

    def tbank(self):
        j = 6 + self.tb_ctr % 2
        self.tb_ctr += 1
        a = self.ps.ap[:, 512 * j:512 * (j + 1)].bitcast(BF16)
        return V(a.rearrange("p (a b) -> p a b", a=8), ("ps", j))

    def bcast_dram(self, name, n, parts=128):
        t = self.D[name].ap.tensor
        return V(bass.AP(t, 0, [[0, parts], [1, n]]), ("dram", name))

    def setup_consts(self):
        p, D, b = self.p, self.D, self.pers
        self.identf = b.alloc([128, 128], F32, "identf")
        self.ident = b.alloc([128, 128], BF16, "ident")
        self.ones_bf = b.alloc([128, 128], BF16, "ones_bf")
        self.epsb = b.alloc([128, 1], F32, "epsb")
        self.ss = [b.alloc([128, 1], F32, f"ss{i}") for i in range(4)]
        self.ss_ctr = 0
        self.junk = b.alloc([128, DM], BF16, "junk")
        self.g_ffn = b.alloc([128, DM], F32, "g_ffn")
        self.g_final = b.alloc([128, DM], F32, "g_final")
        self.PERS_A = b.off
        self.mask2 = b.alloc([128, 2, 128], BF16, "mask2")
        self.g_mix = b.alloc([128, DM], F32, "g_mix")
        self.g_cross = b.alloc([128, DM], F32, "g_cross")
        self.g_out = b.alloc([128, DM], F32, "g_out")
        self.esink = b.alloc([128, 8], F32, "esink")
        self.ropes = b.alloc([NS, 128], F32, "ropes")
        self.mkT = b.alloc([128, 8, MEM], BF16, "mkT")
        self.mv = b.alloc([128, 2, DM], BF16, "mv")
        p.memset("pool", self.epsb, EPS)
        p.dma("sp", self.identf, D["c_ident"], "ld_c_identf")
        p.dma("pool", self.ident, D["c_ident"], "ld_c_ident")
        p.dma("pool", self.mask2.re("p a b -> p (a b)"), D["c_mask"], "ld_c_mask")
        p.memset("pool", self.ones_bf, 1.0)
        p.dma("sp", self.g_mix, self.bcast_dram("g_mix", DM), "ld_c_gmix")
        p.dma("sp", self.g_cross, self.bcast_dram("g_cross", DM), "ld_c_gcross")
        p.dma("sp", self.g_ffn, self.bcast_dram("g_ffn", DM), "ld_c_gffn")
        p.dma("sp", self.g_final, self.bcast_dram("g_final", DM), "ld_c_gfinal")
        p.dma("sp", self.g_out[:, 0:512], self.bcast_dram("g_out_a", 512), "ld_c_goa")
        p.dma("sp", self.g_out[:, 512:1024], self.bcast_dram("g_out_b", 512), "ld_c_gob")
        p.dma("sp", self.esink, self.bcast_dram("sinks", 8), "ld_c_sinks")
        p.dma("sp", self.ropes, D["c_ropes"], "ld_c_ropes")
        p.act(self.esink, self.esink, AF.Exp)

    def load_w(self, dst, name, kchunks, sem):
        src = self.D[name]
        keys = []
        for kc in range(kchunks):
            sub = subview(self.arena, dst, ("kc", kc), dst.ap[:, kc, :])
            self.p.dma("pool", sub, src[kc * 128:(kc + 1) * 128, :], sem)
            keys.append(sub.res)
        return V(dst.ap, keys)

    def next_ss(self):
        s = self.ss[self.ss_ctr % 4]
        self.ss_ctr += 1
        return s

    def rmsnorm(self, x, g, out, P, Dn):
        p = self.p
        ss = self.next_ss()[0:P, :]
        p.memset("dve", ss, 0.0)
        p.act(self.junk[0:P, 0:Dn], x, AF.Square, accum=ss)
        p.act(ss, ss, AF.Ln, scale=1.0 / Dn, bias=self.epsb[0:P, :])
        p.act(ss, ss, AF.Exp, scale=-0.5)
        p.stt("dve", out, x, ss, g, ALU.mult, ALU.mult)

    def transposes(self, dst, src, P, n, evac=("act", "dve"), ident=None):
        p = self.p
        ident = self.ident if ident is None else ident
        gi = 0
        for g in range(0, n, 8):
            m = min(8, n - g)
            pst = self.tbank()
            for j in range(m):
                p.tr(pst[:, j, 0:P], src[0:P, (g + j) * 128:(g + j + 1) * 128], ident[0:P, 0:P])
            p.copy(evac[gi % len(evac)], dst[:, g:g + m, 0:P], pst[:, 0:m, 0:P])
            gi += 1

    def proj_tok(self, hT, w, P, n_out, kchunks, cb):
        p = self.p
        c0 = 0
        while c0 < n_out:
            n = min(512, n_out - c0)
            bk = self.nb()
            for kc in range(kchunks):
                p.mm(bk[0:P, 0:n], hT[:, kc, 0:P], w[:, kc, c0:c0 + n], start=(kc == 0), stop=(kc == kchunks - 1))
            cb(bk[0:P, 0:n], c0, n)
            c0 += n

    def bulk_copies(self):
        p, D = self.p, self.D
        for n in range(NS):
            p.dma("act", D["sak"][n, 0:LA - 1, :], D["cak"][n, 1:LA, :], "bulk", is_output=True)
            p.dma("act", D["sav"][n, 0:LA - 1, :], D["cav"][n, 1:LA, :], "bulk", is_output=True)
        p.dma("act", D["sbk"][:, 0:LB - 1, :], D["cbk"][:, 1:LB, :], "bulk", is_output=True)
        p.dma("act", D["sbv"][:, 0:LB - 1, :], D["cbv"][:, 1:LB, :], "bulk", is_output=True)
        p.dma("act", D["sconvo"][:, 0, :], D["sconv"][:, 1, :], "bulk", is_output=True)

    def phase0(self):
        p, D = self.p, self.D
        b = Bump(self.arena, self.PERS)
        wkv = b.alloc([128, 8, 2 * DM], BF16, "wkv")
        gm = b.alloc([128, DM], F32, "g_mem")
        wkv = self.load_w(wkv, "w_mem_kv", 8, "ld_w0")
        p.dma("sp", gm, self.bcast_dram("g_mem", DM), "ld_c0_12")
        xm = [b.alloc([128, DM], F32, f"xm{i}") for i in range(2)]
        hb = [b.alloc([128, DM], BF16, f"hbm{i}") for i in range(2)]
        hT = [b.alloc([128, 8, 128], BF16, f"hTm{i}") for i in range(2)]
        kvf = [b.alloc([128, 2 * DM], F32, f"kvf{i}") for i in range(2)]
        kb = [b.alloc([128, DM], BF16, f"kbm{i}") for i in range(2)]
        for mt in range(2):
            rows = slice(mt * 128, (mt + 1) * 128)
            p.dma("sp", xm[mt], D["memp"][rows, :], f"ld_xm{mt}")
            self.rmsnorm(xm[mt], gm, hb[mt], 128, DM)
            self.transposes(hT[mt], hb[mt], 128, 8)

            def cb(bk, c0, n, mt=mt):
                p.copy("act", kvf[mt][:, c0:c0 + n], bk)
            self.proj_tok(hT[mt], wkv, 128, 2 * DM, 8, cb)
            p.dma("sp", D["pmk"][rows, :], kvf[mt][:, 0:DM], f"st_kvfk{mt}", is_output=True)
            p.dma("sp", D["pmv"][rows, :], kvf[mt][:, DM:2 * DM], f"st_kvfv{mt}", is_output=True)
            p.copy("pool", kb[mt], kvf[mt][:, 0:DM])
            self.transposes(self.mkT[:, :, rows], kb[mt], 128, 8)
            p.copy("pool", self.mv[:, mt, :], kvf[mt][:, DM:2 * DM])

    def inproj_front(self, P, x, hb, hT):
        self.rmsnorm(x, self.g_mix[0:P, :], hb, P, DM)
        self.transposes(hT, hb, P, 8)

    def inproj_tile(self, P, x, hb, hT, w_in, cos2, sinS, zqk, zv, zqkB, zvB, tmp, zbA, zbB, want_vf32, want_vBf32,
                    skip_front=False):
        p = self.p
        if not skip_front:
            self.inproj_front(P, x, hb, hT)
        cosb = lambda h: bc_mid(cos2, h)
        sin_lo = lambda h: bc_mid(sinS[:, 0:32], h)
        sin_hi = lambda h: bc_mid(sinS[:, 32:64], h)

        def rope(bk, dst, ncols):
            h = ncols // 64
            src = bk[:, 0:ncols].re("p (h d) -> p h d", d=64)
            d3 = dst.re("p (h d) -> p h d", d=64)
            t3 = tmp[:, 0:ncols].re("p (h d) -> p h d", d=64)
            p.tt("dve", d3, src, cosb(h), ALU.mult)
            p.tt("dve", t3[:, :, 0:32], src[:, :, 32:64], sin_lo(h), ALU.mult)
            p.tt("dve", t3[:, :, 32:64], src[:, :, 0:32], sin_hi(h), ALU.mult)
            p.tt("dve", d3, d3, t3, ALU.add)

        def cb(bk, c0, n):
            if c0 == 0:
                rope(bk, zqk[:, 0:512], 512)
            elif c0 == 512:
                rope(bk, zqk[:, 512:1024], 512)
                if zbA is not None:
                    p.copy("act", zbA[0][:, 0:1024], zqk)
            elif c0 == 1024:
                if zbA is not None:
                    p.copy("act", zbA[1][:, 1024:1536], bk)
                if want_vf32:
                    p.copy("act", zv, bk)
            elif c0 == 1536:
                rope(bk, zqkB[:, 0:512], 512)
            else:
                rope(bk, zqkB[:, 512:640], 128)
                if zbB is not None:
                    p.copy("act", zbB[0][:, 0:512].re("p (f t d) -> p f t d", f=4, t=2),
                           zqkB[:, 0:512].re("p (t f d) -> p f t d", t=2, f=4))
                    p.copy("act", zbB[0][:, 512:640], zqkB[:, 512:640])
                    p.copy("act", zbB[1][:, 640:768], bk[:, 128:256])
                if want_vBf32:
                    p.copy("act", zvB, bk[:, 128:256])
        self.proj_tok(hT, w_in, P, IN_DIM, 8, cb)

    def phase1(self):
        p, D = self.p, self.D
        b = Bump(self.arena, self.PERS)
        w_in = b.alloc([128, 8, IN_DIM], BF16, "w_in")
        w_in = self.load_w(w_in, "w_in", 8, "ld_w1")
        rt = b.alloc([128, NT, 128], F32, "rope_tab")
        p.dma("sp", rt, D["c_rope"].re("(t p) c -> p t c", p=128), "ld_c0_13")
        xt = [b.alloc([128, DM], F32, f"xt{i}") for i in range(3)]
        hb = [b.alloc([128, DM], BF16, f"hb{i}") for i in range(2)]
        hT = [b.alloc([128, 8, 128], BF16, f"hT{i}") for i in range(2)]
        zqk = [b.alloc([128, 1024], F32, f"zqk{i}") for i in range(2)]
        zv = [b.alloc([128, 512], F32, f"zv{i}") for i in range(2)]
        zqkB = [b.alloc([128, 640], F32, f"zqkB{i}") for i in range(2)]
        zvB = b.alloc([128, 128], F32, "zvB")
        tmp = [b.alloc([128, 512], F32, f"tmp{i}") for i in range(2)]
        zbA, zbB = [], []
        for i in range(2):
            a = b.alloc([128, 1536], BF16, f"zbA{i}")
            v1, v2 = subview(self.arena, a, "qk"), subview(self.arena, a, "v")
            zbA.append((v1, v2, V(a.ap, [v1.res, v2.res])))
            a = b.alloc([128, 768], BF16, f"zbB{i}")
            v1, v2 = subview(self.arena, a, "qk"), subview(self.arena, a, "v")
            zbB.append((v1, v2, V(a.ap, [v1.res, v2.res])))
        self.p1_end = b.off
        def front(t):
            x = xt[t % 3]
            p.dma("sp", x, D["xp"][t * 128:(t + 1) * 128, :], f"ld_xt{t % 3}")
            self.inproj_front(128, x, hb[t % 2], hT[t % 2])
        front(0)
        for t in range(NT):
            rows = slice(t * 128, (t + 1) * 128)
            x = xt[t % 3]
            i = t % 2
            if t + 1 < NT:
                front(t + 1)
            self.inproj_tile(128, x, hb[i], hT[i], w_in, rt[:, t, 0:64], rt[:, t, 64:128],
                             zqk[i], zv[i], zqkB[i], zvB, tmp[i], zbA[i], zbB[i], t >= NT // 2, t == NT - 1,
                             skip_front=True)
            if t >= NT // 2:
                orow = slice((t - NT // 2) * 128, (t - NT // 2 + 1) * 128)
                p.dma("sp", D["pak"][orow, :], zqk[i][:, 512:1024], f"st_zqk{i}", is_output=True)
                p.dma("sp", D["pav"][orow, :], zv[i], f"st_zv{i}", is_output=True)
            if t == NT - 1:
                p.dma("sp", D["pbk"], zqkB[i][:, 512:640], f"st_zqkB{i}", is_output=True)
                p.dma("sp", D["pbv"], zvB, "st_zvB", is_output=True)
            self.scr_tokens.append(p.dma("sp", D["scrA"][rows, :], zbA[i][2], f"st_zbA{i}"))
            self.scr_tokens.append(p.dma("sp", D["scrB"][rows, :], zbB[i][2], f"st_zbB{i}"))
        sb = self.samp
        xs = sb["xs"]
        p.dma("sp", xs, D["xs"], "ld_xs")
        self.inproj_tile(NS, xs, hb[0][0:NS, :], hT[0], w_in, self.ropes[:, 0:64], self.ropes[:, 64:128],
                         sb["zqk"], sb["zv"], sb["zqkB"], sb["zvB"], tmp[0][0:NS, :], None, None, True, True)
        p.dma("sp", D["sak"][:, LA - 1, :], sb["zqk"][:, 512:1024], "st_s1a", is_output=True)
        p.dma("sp", D["sav"][:, LA - 1, :], sb["zv"], "st_s1b", is_output=True)
        p.dma("sp", D["sbk"][:, LB - 1, :], sb["zqkB"][:, 512:640], "st_s1c", is_output=True)
        p.dma("sp", D["sbv"][:, LB - 1, :], sb["zvB"], "st_s1d", is_output=True)

    def phase2(self):
        p, D = self.p, self.D
        b = Bump(self.arena, self.PERS, self.PH_END)
        blk = [b.alloc([128, 1536], BF16, f"blk{i}") for i in range(3)]
        QT = [b.alloc([128, 4, 2, 128], BF16, f"QZ{i}") for i in range(2)]
        for v in QT:
            p.memset("pool", v, 0.0)
        KT = [b.alloc([128, 4, 128], BF16, f"KT{i}") for i in range(3)]
        VX = [b.alloc([128, 8, 65], BF16, f"VX{i}") for i in range(3)]
        PT = [b.alloc([128, 2, 2, 128], BF16, f"PT{i}") for i in range(8)]
        OS = [b.alloc([128, 520], F32, f"OS{i}") for i in range(3)]
        for v in VX:
            p.memset("pool", v, 1.0)
        p.wait_for("sp", list(self.scr_tokens))
        mask4 = V(self.mask2.ap.unsqueeze(2).to_broadcast([128, 2, 2, 128]), self.mask2.res)
        mask_own = bc_mid(self.mask2[:, 0, :], 2)
        st = {"s": 0, "pc": 0}
        self.o_tokens = []

        def st_load(c):
            kind, br, d, r, bb, first, s = c
            cur = blk[s % 3]
            qt, kt, vx = QT[s % 2], KT[s % 3], VX[s % 3]
            if kind == "A":
                src = D["scrA"].re("(j r) c -> r j c", r=d)[r, 128 * bb:128 * (bb + 1), :]
                p.dma("sp", cur, src, f"ld_blk{s % 3}")
            else:
                src = D["scrB"][128 * bb:128 * (bb + 1), :]
                p.dma("sp", cur[:, 0:768], src, f"ld_blk{s % 3}")
            pq = self.tbank()
            for j in range(4):
                p.tr(pq[:, j, :], cur[:, j * 128:(j + 1) * 128], self.ident)
            p.copy("dve", qt[0:64, :, 0, :], pq[0:64, 0:4, :])
            p.copy("dve", qt[64:128, :, 1, :], pq[64:128, 0:4, :])
            if kind == "A":
                self.transposes(kt, cur[:, 512:1024], 128, 4, evac=("act",))
                p.copy("pool", vx[:, :, 0:64], cur[:, 1024:1536].re("p (h d) -> p h d", d=64))
            else:
                self.transposes(kt[:, 0:1, :], cur[:, 512:640], 128, 1, evac=("act",))
                p.copy("pool", vx[:, 0:2, 0:64], cur[:, 640:768].re("p (h d) -> p h d", d=64))

        def st_scores(c):
            kind, br, d, r, bb, first, s = c
            qt, kt, ktp = QT[s % 2], KT[s % 3], KT[(s - 1) % 3]
            for j in range(4):
                psS = self.bank(j)
                kj = j if kind == "A" else 0
                q2 = qt[:, j].re("p a q -> p (a q)")
                p.mm(psS[:, 0:256], kt[:, kj, :], q2)
                if not first:
                    p.mm(psS[:, 256:512], ktp[:, kj, :], q2)
                pt = PT[(4 * s + j) % 8]
                if not first:
                    p.act(pt.re("p b a q -> p (b a q)"), psS, AF.Exp, scale=0.125)
                    p.tt("dve", pt, pt, mask4, ALU.mult)
                else:
                    p.act(pt[:, 0].re("p a q -> p (a q)"), psS[:, 0:256], AF.Exp, scale=0.125)
                    p.tt("dve", pt[:, 0], pt[:, 0], mask_own, ALU.mult)

        def st_pv(c):
            kind, br, d, r, bb, first, s = c
            vx, vxp = VX[s % 3], VX[(s - 1) % 3]
            psO = [self.bank(4), self.bank(5)]
            for j in range(4):
                pt = PT[(4 * s + j) % 8]
                for hh in range(2):
                    if kind == "A":
                        h = 2 * j + hh
                        vi = h
                    else:
                        h = j + 4 * hh
                        vi = hh
                    o = psO[h // 4][:, (h % 4) * 65:(h % 4) * 65 + 65]
                    p.mm(o, pt[:, 0, hh, :], vx[:, vi, :], start=True, stop=first)
                    if not first:
                        p.mm(o, pt[:, 1, hh, :], vxp[:, vi, :], start=False, stop=True)
            osb = OS[s % 3]
            p.copy("act", osb[:, 0:260], psO[0][:, 0:260])
            p.copy("act", osb[:, 260:520], psO[1][:, 0:260])
            if kind == "A":
                dst = D["scrO"][br].re("(j r) c -> r j c", r=d)[r, 128 * bb:128 * (bb + 1), :]
            else:
                dst = D["scrO"][3][128 * bb:128 * (bb + 1), :]
            self.o_tokens.append(p.dma("sp", dst, osb, f"st_os{s % 3}"))

        cfgs = []
        for br, d in ((2, 16), (1, 4), (0, 1)):
            nblk = SEQ // d // 128
            for r in range(d):
                for bb in range(nblk):
                    cfgs.append(("A", br, d, r, bb, bb == 0, len(cfgs)))
        for bb in range(NT):
            cfgs.append(("B", 3, 1, 0, bb, bb == 0, len(cfgs)))
        st_load(cfgs[0])
        for i, c in enumerate(cfgs):
            st_scores(c)
            if i + 1 < len(cfgs):
                st_load(cfgs[i + 1])
            st_pv(c)

    def phase2b(self):
        p, D = self.p, self.D
        b = Bump(self.arena, self.PERS, self.PH_END)
        w_out = b.alloc([128, 8, DM], BF16, "w_out")
        w_xq = b.alloc([128, 8, DM], BF16, "w_xq")
        w_xo = b.alloc([128, 8, DM], BF16, "w_xo")
        w_out = self.load_w(w_out, "w_out", 8, "ld_w2a")
        w_xq = self.load_w(w_xq, "w_xq", 8, "ld_w2b")
        w_xo = self.load_w(w_xo, "w_xo", 8, "ld_w2c")
        self.w2 = (w_out, w_xq, w_xo)
        OL = [[b.alloc([128, 520], F32, f"OL{i}_{k}") for k in range(4)] for i in range(2)]
        cat = [b.alloc([128, DM], F32, f"cat{i}") for i in range(2)]
        hm = [b.alloc([128, DM], BF16, f"hm{i}") for i in range(2)]
        rd = [b.alloc([128, 16], F32, f"rd{i}") for i in range(2)]
        x1 = [b.alloc([128, DM], F32, f"x1_{i}") for i in range(4)]
        hmT = b.alloc([128, 8, 512], BF16, "hmT")
        h2T = b.alloc([128, 8, 512], BF16, "h2T")
        qxT = b.alloc([128, 8, 512], BF16, "qxT")
        oxT = b.alloc([128, 8, 512], BF16, "oxT")
        h2 = [b.alloc([128, DM], BF16, f"h2_{i}") for i in range(2)]
        PTx = [b.alloc([128, 2, 512], BF16, f"PTx{i}") for i in range(2)]
        rden = [b.alloc([128, 512], F32, f"rden{i}") for i in range(2)]
        self.p2b_end = b.off
        hmTb = [hmT, b.alloc([128, 8, 512], BF16, "hmT1")]
        p.wait_for("sp", list(self.o_tokens))
        self.x2_tokens = []

        def front(sti):
            for tl in range(4):
                t = sti * 4 + tl
                rows = slice(t * 128, (t + 1) * 128)
                ol = OL[t % 2]
                for k in range(4):
                    p.dma("sp", ol[k], D["scrO"][k][rows, :], f"ld_ol{t % 2}_{k}")
                self.combine(128, ol, rd[t % 2], cat[t % 2], hm[t % 2], self.esink)
                self.transposes(hmTb[sti % 2][:, :, tl * 128:(tl + 1) * 128], hm[t % 2], 128, 8)

        def mid(sti):
            hmT_ = hmTb[sti % 2]
            for tl in range(4):
                t = sti * 4 + tl
                rows = slice(t * 128, (t + 1) * 128)
                p.dma("sp", x1[tl], D["xp"][rows, :], f"ld_x1_{tl}")

                def cb(bk, c0, n, tl=tl):
                    p.tt("dve", x1[tl][:, c0:c0 + n], x1[tl][:, c0:c0 + n], bk, ALU.add)
                self.proj_tok(hmT_[:, :, tl * 128:(tl + 1) * 128], w_out, 128, DM, 8, cb)
                self.rmsnorm(x1[tl], self.g_cross, h2[tl % 2], 128, DM)
                self.transposes(h2T[:, :, tl * 128:(tl + 1) * 128], h2[tl % 2], 128, 8)
            for fc in range(8):
                bk = self.nb()
                for kc in range(8):
                    p.mm(bk, w_xq[:, kc, fc * 128:(fc + 1) * 128], h2T[:, kc, :], start=(kc == 0), stop=(kc == 7))
                p.copy("act" if fc % 2 == 0 else "dve", qxT[:, fc, :], bk)
            for h in range(4):
                pt = PTx[h % 2]
                for mc in range(2):
                    bk = self.nb()
                    for j in range(2):
                        p.mm(bk, self.mkT[:, 2 * h + j, mc * 128:(mc + 1) * 128], qxT[:, 2 * h + j, :],
                             start=(j == 0), stop=(j == 1))
                    p.act(pt[:, mc, :], bk, AF.Exp, scale=1.0 / 16.0)
                bd = self.nb()
                for mc in range(2):
                    p.mm(bd, self.ones_bf, pt[:, mc, :], start=(mc == 0), stop=(mc == 1))
                rdn = rden[h % 2]
                p.recip(rdn, bd)
                for dj in range(2):
                    bk = self.nb()
                    for mc in range(2):
                        p.mm(bk, self.mv[:, mc, h * 256 + dj * 128:h * 256 + (dj + 1) * 128], pt[:, mc, :],
                             start=(mc == 0), stop=(mc == 1))
                    p.tt("dve", oxT[:, 2 * h + dj, :], bk, rdn, ALU.mult)

        def back(sti):
            for tl in range(4):
                t = sti * 4 + tl
                rows = slice(t * 128, (t + 1) * 128)

                def cb(bk, c0, n, tl=tl):
                    p.tt("dve", x1[tl][:, c0:c0 + n], x1[tl][:, c0:c0 + n], bk, ALU.add)
                self.proj_tok(oxT[:, :, tl * 128:(tl + 1) * 128], w_xo, 128, DM, 8, cb)
                self.x2_tokens.append(p.dma("sp", D["scrX2"][rows, :], x1[tl], f"st_x1_{tl}"))

        nsup = NT // 4
        front(0)
        for sti in range(nsup):
            mid(sti)
            if sti + 1 < nsup:
                front(sti + 1)
            back(sti)

    def combine(self, P, ol, r, c, hmv, esink):
        p = self.p
        if ol[1] is not None:
            p.tt("pool", ol[0], ol[0], ol[1], ALU.add)
            p.tt("pool", ol[0], ol[0], ol[2], ALU.add)
        a3 = ol[0].re("p (h c) -> p h c", c=65)
        b3 = ol[3].re("p (h c) -> p h c", c=65)
        p.recip(r[:, 0:8], a3[:, :, 64])
        p.tt("dve", c[:, 0:512].re("p (h d) -> p h d", d=64), a3[:, :, 0:64], bc_last(r[:, 0:8], 64), ALU.mult)
        p.tt("dve", r[:, 8:16], b3[:, :, 64], esink[0:P, :], ALU.add)
        p.recip(r[:, 8:16], r[:, 8:16])
        p.tt("dve", c[:, 512:1024].re("p (h d) -> p h d", d=64), b3[:, :, 0:64], bc_last(r[:, 8:16], 64), ALU.mult)
        self.rmsnorm(c[:, 0:512], self.g_out[0:P, 0:512], hmv[:, 0:512], P, 512)
        self.rmsnorm(c[:, 512:1024], self.g_out[0:P, 512:1024], hmv[:, 512:1024], P, 512)

    def phase3(self):
        p, D = self.p, self.D
        b = Bump(self.arena, self.PERS_A, self.ARENA - 4096)
        w_up = b.alloc([128, 8, 2 * DFF], BF16, "w_up")
        w_dn = b.alloc([128, NFC, DM], BF16, "w_dn")
        w_up = self.load_w(w_up, "w_up", 8, "ld_w3a")
        w_dn = self.load_w(w_dn, "w_down", NFC, "ld_w3b")
        self.w3 = (w_up, w_dn)
        cw = b.alloc([128, 3, NFC], F32, "cw")
        cbias = b.alloc([128, NFC], F32, "cbias")
        for i3 in range(3):
            p.dma("sp", cw[:, i3, :], D["conv_w"][i3].re("(fc f) -> f fc", f=128), "ld_cw",
                  allow_slow_non_contiguous=True)
        p.dma("sp", cbias, D["conv_b"].re("(fc f) -> f fc", f=128), "ld_cb", allow_slow_non_contiguous=True)
        gcar = b.alloc([128, NFC, 2], F32, "gcar")
        p.memset("pool", gcar, 0.0)
        ST = 256
        self.p3_tmp_start = b.off
        xl = [b.alloc([128, DM], F32, f"xl{i}") for i in range(4)]
        h3 = [b.alloc([128, DM], BF16, f"h3_{i}") for i in range(2)]
        h3T = b.alloc([128, 8, ST], BF16, "h3T")
        aT = b.alloc([128, NFC, ST], BF16, "aT")
        gsb = [b.alloc([128, ST + 2], F32, f"gsb{i}") for i in range(2)]
        tq = [b.alloc([128, ST], F32, f"tq{i}") for i in range(2)]
        sq = [b.alloc([128, ST], F32, f"sq{i}") for i in range(2)]
        yt = [b.alloc([128, DM], F32, f"yt{i}") for i in range(1)]
        h3Tb = [h3T, b.alloc([128, 8, ST], BF16, "h3T1")]
        p.wait_for("sp", list(self.x2_tokens))

        def front(s_):
            for tl in range(2):
                t = 2 * s_ + tl
                rows = slice(t * 128, (t + 1) * 128)
                xi = (s_ % 2) * 2 + tl
                p.dma("sp", xl[xi], D["scrX2"][rows, :], f"ld_xl{xi}")
                self.rmsnorm(xl[xi], self.g_ffn, h3[tl], 128, DM)
                self.transposes(h3Tb[s_ % 2][:, :, tl * 128:(tl + 1) * 128], h3[tl], 128, 8)

        def mid(s_):
            hT_ = h3Tb[s_ % 2]
            for fc in range(NFC):
                bg = self.nb()
                bv = self.nb()
                for kc in range(8):
                    p.mm(bg[:, 0:ST], w_up[:, kc, fc * 128:(fc + 1) * 128], hT_[:, kc, :], start=(kc == 0), stop=(kc == 7))
                for kc in range(8):
                    p.mm(bv[:, 0:ST], w_up[:, kc, DFF + fc * 128:DFF + (fc + 1) * 128], hT_[:, kc, :],
                         start=(kc == 0), stop=(kc == 7))
                g = gsb[fc % 2]
                tt_ = tq[fc % 2]
                ss_ = sq[fc % 2]
                p.copy("pool", g[:, 0:2], gcar[:, fc, :])
                p.copy("act", g[:, 2:ST + 2], bg[:, 0:ST])
                p.copy("pool", gcar[:, fc, :], g[:, ST:ST + 2])
                p.ts("dve", tt_, g[:, 0:ST], cw[:, 0, fc:fc + 1], cbias[:, fc:fc + 1], ALU.mult, ALU.add)
                p.stt("dve", tt_, g[:, 1:ST + 1], cw[:, 1, fc:fc + 1], tt_, ALU.mult, ALU.add)
                p.stt("dve", tt_, g[:, 2:ST + 2], cw[:, 2, fc:fc + 1], tt_, ALU.mult, ALU.add)
                p.act(ss_, tt_, AF.Silu)
                p.tt("dve", aT[:, fc, :], ss_, bv[:, 0:ST], ALU.mult)

        def back(s_):
            for tl in range(2):
                t = 2 * s_ + tl
                rows = slice(t * 128, (t + 1) * 128)
                x = xl[(s_ % 2) * 2 + tl]

                def cb(bk, c0, n, x=x):
                    p.tt("dve", x[:, c0:c0 + n], x[:, c0:c0 + n], bk, ALU.add)
                self.proj_tok(aT[:, :, tl * 128:(tl + 1) * 128], w_dn, 128, DM, NFC, cb)
                y = yt[0]
                self.rmsnorm(x, self.g_final, y, 128, DM)
                p.dma("sp", D["yp"][rows, :], y, "st_y0", is_output=True)

        nsup = SEQ // ST
        front(0)
        for s_ in range(nsup):
            mid(s_)
            if s_ + 1 < nsup:
                front(s_ + 1)
            back(s_)
        for ti in range(2):
            p.dma("sp", D["pconv"][ti].re("(fc f) -> f fc", f=128), gcar[:, :, ti], "st_pconv", is_output=True,
                  allow_slow_non_contiguous=True)

    def samp_attn(self):
        p, D, sb = self.p, self.D, self.samp
        b = Bump(self.arena, self.PERS, self.PH_END)
        sel = b.alloc([NS, NS, 128], F32, "sel")
        p.dma("sp", sel.re("p a b -> p (a b)"), D["c_sel"], "ld_sel")
        qb = [b.alloc([128, 512], F32, f"s_qb{i}") for i in range(2)]
        Kt = [b.alloc([128, 512], F32, f"s_Kt{i}") for i in range(3)]
        Vt = [b.alloc([128, 8, 65], F32, f"s_Vt{i}") for i in range(3)]
        prod = [b.alloc([128, 512], F32, f"s_prod{i}") for i in range(2)]
        sc = [b.alloc([128, 8], F32, f"s_sc{i}") for i in range(2)]
        Pz = [b.alloc([128, 8, NS], F32, f"s_Pz{i}") for i in range(4)]
        oA = b.alloc([NS, 8, 65], F32, "s_oA")
        oB = b.alloc([NS, 8, 65], F32, "s_oB")
        prn = b.alloc([NS, 512], F32, "s_prn")
        sn = b.alloc([NS, 8], F32, "s_sn")
        en = b.alloc([NS, 8], F32, "s_en")
        tv = b.alloc([NS, 8, 64], F32, "s_tv")
        r16 = b.alloc([NS, 16], F32, "s_r16")
        for v in Vt:
            p.memset("pool", v, 1.0)
        for v in Pz:
            p.memset("pool", v, 0.0)
        p.memset("pool", oA, 0.0)
        p.memset("pool", oB, 0.0)
        cnt = 0
        for n in range(NS):
            bk = self.nb()
            p.mm(bk, sel[:, n, :], sb["zqk"][:, 0:512])
            q = qb[n % 2]
            p.copy("act", q, bk)
            for g in range(3):
                if g == 0:
                    ksrc = D["cak"][n, LA - 128:LA, :]
                    vsrc = D["cav"][n, LA - 128:LA, :]
                elif g == 1:
                    ksrc = D["cak"][n].re("(j r) c -> r j c", r=4)[0, 384:512, :]
                    vsrc = D["cav"][n].re("(j r) c -> r j c", r=4)[0, 384:512, :]
                else:
                    ksrc = D["cak"][n].re("(j r) c -> r j c", r=16)[0, 0:128, :]
                    vsrc = D["cav"][n].re("(j r) c -> r j c", r=16)[0, 0:128, :]
                kt, vt = Kt[cnt % 3], Vt[cnt % 3]
                p.dma("sp", kt, ksrc, f"ld_sK{cnt % 3}")
                p.dma("sp", vt[:, :, 0:64], vsrc.re("j (h d) -> j h d", d=64), f"ld_sV{cnt % 3}")
                pr = prod[cnt % 2]
                p.tt("dve", pr, kt, q, ALU.mult)
                s8 = sc[cnt % 2]
                p.reduce("dve", s8, pr.re("p (h d) -> p h d", d=64), ALU.add)
                pz = Pz[cnt % 4]
                p.act(pz[:, :, n], s8, AF.Exp, scale=0.125)
                psA = [self.nb(), self.nb()]
                for h in range(8):
                    o = psA[h // 4][0:NS, (h % 4) * 65:(h % 4) * 65 + 65]
                    p.mm(o, pz[:, h, :], vt[:, h, :])
                for hb in range(2):
                    av = oA[:, 4 * hb:4 * hb + 4, :]
                    p.tt("dve", av, av, psA[hb][0:NS, 0:260].re("p (h c) -> p h c", c=65), ALU.add)
                p.memset("pool", pz[:, :, n], 0.0)
                cnt += 1
        zqk, zv = sb["zqk"], sb["zv"]
        p.tt("dve", prn, zqk[:, 0:512], zqk[:, 512:1024], ALU.mult)
        p.reduce("dve", sn, prn.re("p (h d) -> p h d", d=64), ALU.add)
        p.act(en, sn, AF.Exp, scale=0.125)
        p.ts("dve", en, en, 3.0, None, ALU.mult)
        p.tt("dve", tv, zv.re("p (h d) -> p h d", d=64), bc_last(en, 64), ALU.mult)
        p.tt("dve", oA[:, :, 0:64], oA[:, :, 0:64], tv, ALU.add)
        p.tt("dve", oA[:, :, 64], oA[:, :, 64], en, ALU.add)
        KtB = [V(k.ap[:, 0:128], k.res) for k in Kt]
        VtB = [V(v.ap[:, 0:2, :], v.res) for v in Vt]
        zqkB, zvB = sb["zqkB"], sb["zvB"]
        for n in range(NS):
            bk = self.nb()
            p.mm(bk, sel[:, n, :], zqkB[:, 0:512])
            q = qb[n % 2]
            p.copy("act", q, bk)
            kt, vt = KtB[cnt % 3], VtB[cnt % 3]
            p.dma("sp", kt, D["cbk"][n], f"ld_sK{cnt % 3}")
            p.dma("sp", vt[:, :, 0:64], D["cbv"][n].re("j (h d) -> j h d", d=64), f"ld_sV{cnt % 3}")
            pr = prod[cnt % 2]
            k4 = V(kt.ap.rearrange("p (k d) -> p k d", d=64).unsqueeze(2).to_broadcast([128, 2, 4, 64]), kt.res)
            p.tt("dve", pr.re("p (k g d) -> p k g d", k=2, g=4), q.re("p (k g d) -> p k g d", k=2, g=4), k4, ALU.mult)
            s8 = sc[cnt % 2]
            p.reduce("dve", s8, pr.re("p (h d) -> p h d", d=64), ALU.add)
            pz = Pz[cnt % 4]
            p.act(pz[:, :, n], s8, AF.Exp, scale=0.125)
            psA = [self.nb(), self.nb()]
            for h in range(8):
                o = psA[h // 4][0:NS, (h % 4) * 65:(h % 4) * 65 + 65]
                p.mm(o, pz[:, h, :], vt[:, h // 4, :])
            for hb in range(2):
                av = oB[:, 4 * hb:4 * hb + 4, :]
                p.tt("dve", av, av, psA[hb][0:NS, 0:260].re("p (h c) -> p h c", c=65), ALU.add)
            p.memset("pool", pz[:, :, n], 0.0)
            cnt += 1
        kn4 = V(zqkB.ap[:, 512:640].rearrange("p (k d) -> p k d", d=64).unsqueeze(2).to_broadcast([NS, 2, 4, 64]), zqkB.res)
        p.tt("dve", prn.re("p (k g d) -> p k g d", k=2, g=4), zqkB[:, 0:512].re("p (k g d) -> p k g d", k=2, g=4), kn4, ALU.mult)
        p.reduce("dve", sn, prn.re("p (h d) -> p h d", d=64), ALU.add)
        p.act(en, sn, AF.Exp, scale=0.125)
        vn4 = V(zvB.ap.rearrange("p (k d) -> p k d", d=64).unsqueeze(2).to_broadcast([NS, 2, 4, 64]), zvB.res)
        e4 = V(en.ap.rearrange("p (k g) -> p k g", k=2).unsqueeze(3).to_broadcast([NS, 2, 4, 64]), en.res)
        p.tt("dve", tv.re("p (k g) d -> p k g d", k=2), vn4, e4, ALU.mult)
        p.tt("dve", oB[:, :, 0:64], oB[:, :, 0:64], tv, ALU.add)
        p.tt("dve", oB[:, :, 64], oB[:, :, 64], en, ALU.add)
        self.combine(NS, [oA.re("p h c -> p (h c)"), None, None, oB.re("p h c -> p (h c)")], r16, sb["cat"], sb["hm"],
                     self.esink)

    def samp_mix(self):
        p, D, sb = self.p, self.D, self.samp
        w_out, w_xq, w_xo = self.w2
        b = Bump(self.arena, self.PERS + 3 * 16384, self.PH_END)
        sel = b.alloc([NS, NS, 128], F32, "sel2")
        p.dma("sp", sel.re("p a b -> p (a b)"), D["c_sel"], "ld_sel2")
        hT = b.alloc([128, 8, NS], BF16, "s_hT")
        h2s = b.alloc([NS, DM], BF16, "s_h2")
        qx = b.alloc([NS, DM], F32, "s_qx")
        qbx = [b.alloc([128, DM], F32, f"s_qbx{i}") for i in range(2)]
        Kx = [b.alloc([128, DM], F32, f"s_Kx{i}") for i in range(2)]
        Vx = [b.alloc([128, 4, 257], F32, f"s_Vx{i}") for i in range(2)]
        prodx = b.alloc([128, DM], F32, "s_prodx")
        s4 = [b.alloc([128, 4], F32, f"s_s4{i}") for i in range(2)]
        Pzx = [b.alloc([128, 4, NS], F32, f"s_Pzx{i}") for i in range(4)]
        oX = b.alloc([NS, 4, 257], F32, "s_oX")
        r4 = b.alloc([NS, 4], F32, "s_r4")
        oxn = b.alloc([NS, DM], BF16, "s_oxn")
        xs = sb["xs"]
        self.transposes(hT, sb["hm"], NS, 8)

        def cb(bk, c0, n):
            p.tt("dve", xs[:, c0:c0 + n], xs[:, c0:c0 + n], bk, ALU.add)
        self.proj_tok(hT, w_out, NS, DM, 8, cb)
        self.rmsnorm(xs, self.g_cross[0:NS, :], h2s, NS, DM)
        self.transposes(hT, h2s, NS, 8)

        def cb2(bk, c0, n):
            p.copy("act", qx[:, c0:c0 + n], bk)
        self.proj_tok(hT, w_xq, NS, DM, 8, cb2)
        for v in Vx:
            p.memset("pool", v, 1.0)
        for v in Pzx:
            p.memset("pool", v, 0.0)
        p.memset("pool", oX, 0.0)
        cnt = 0
        for n in range(NS):
            q = qbx[n % 2]
            for half in range(2):
                bk = self.nb()
                p.mm(bk, sel[:, n, :], qx[:, half * 512:(half + 1) * 512])
                p.copy("act", q[:, half * 512:(half + 1) * 512], bk)
            for mc in range(2):
                kx, vx = Kx[cnt % 2], Vx[cnt % 2]
                rows = slice(mc * 128, (mc + 1) * 128)
                p.dma("sp", kx, D["cmk"][n, rows, :], f"ld_sKx{cnt % 2}")
                p.dma("sp", vx[:, :, 0:256], D["cmv"][n, rows, :].re("j (h d) -> j h d", d=256), f"ld_sVx{cnt % 2}")
                p.tt("dve", prodx, kx, q, ALU.mult)
                s_ = s4[cnt % 2]
                p.reduce("dve", s_, prodx.re("p (h d) -> p h d", d=256), ALU.add)
                pz = Pzx[cnt % 4]
                p.act(pz[:, :, n], s_, AF.Exp, scale=1.0 / 16.0)
                for h in range(4):
                    bo = self.nb()
                    p.mm(bo[0:NS, 0:257], pz[:, h, :], vx[:, h, :])
                    p.tt("dve", oX[:, h, :], oX[:, h, :], bo[0:NS, 0:257], ALU.add)
                p.memset("pool", pz[:, :, n], 0.0)
                cnt += 1
        p.recip(r4, oX[:, :, 256])
        p.tt("dve", oxn.re("p (h d) -> p h d", d=256), oX[:, :, 0:256], bc_last(r4, 256), ALU.mult)
        self.transposes(hT, oxn, NS, 8)
        self.proj_tok(hT, w_xo, NS, DM, 8, cb)

    def samp_ffn(self):
        p, D, sb = self.p, self.D, self.samp
        w_up, w_dn = self.w3
        b = Bump(self.arena, self.p3_tmp_start, self.ARENA - 4096)
        h3s = b.alloc([NS, DM], BF16, "s_h3")
        hT = b.alloc([128, 8, NS], BF16, "s_h3T")
        gs = b.alloc([NS, DFF], F32, "s_gs")
        vs = b.alloc([NS, DFF], F32, "s_vs")
        abf = b.alloc([NS, DFF], BF16, "s_abf")
        aT = b.alloc([128, NFC, NS], BF16, "s_aT")
        ysb = b.alloc([NS, DM], F32, "s_ysb")
        xs = sb["xs"]
        self.rmsnorm(xs, self.g_ffn[0:NS, :], h3s, NS, DM)
        self.transposes(hT, h3s, NS, 8)

        def cbu(bk, c0, n):
            lo, hi = c0, c0 + n
            if lo < DFF:
                m = min(hi, DFF) - lo
                p.copy("act", gs[:, lo:lo + m], bk[:, 0:m])
            if hi > DFF:
                s0 = max(lo, DFF)
                p.copy("act", vs[:, s0 - DFF:hi - DFF], bk[:, s0 - lo:n])
        self.proj_tok(hT, w_up, NS, 2 * DFF, 8, cbu)
        p.dma("sp", D["sconvo"][:, 1, :], gs, "st_sgs", is_output=True)
        b2 = Bump(self.arena, self.PERS_A, self.PERS_A + 90112)
        s0t = b2.alloc([NS, DFF], F32, "s_s0")
        s1t = b2.alloc([NS, DFF], F32, "s_s1")
        cwb = b2.alloc([NS, 3, DFF], F32, "s_cwb")
        cbb = b2.alloc([NS, DFF], F32, "s_cbb")
        t1 = b2.alloc([NS, DFF], F32, "s_t1")
        t2 = b2.alloc([NS, DFF], F32, "s_t2")
        p.dma("sp", s0t, D["sconv"][:, 0, :], "ld_ss0")
        p.dma("sp", s1t, D["sconv"][:, 1, :], "ld_ss1")
        tcw = D["conv_w"].ap.tensor
        p.dma("sp", cwb.re("p a b -> p (a b)"), V(bass.AP(tcw, 0, [[0, NS], [1, 3 * DFF]]), ("dram", "conv_w")), "ld_scw")
        p.dma("sp", cbb, self.bcast_dram("conv_b", DFF, NS), "ld_scb")
        p.tt("dve", t1, s0t, cwb[:, 0, :], ALU.mult)
        p.tt("pool", t2, s1t, cwb[:, 1, :], ALU.mult)
        p.tt("dve", t1, t1, t2, ALU.add)
        p.tt("pool", t2, gs, cwb[:, 2, :], ALU.mult)
        p.tt("dve", t1, t1, t2, ALU.add)
        p.tt("dve", t1, t1, cbb, ALU.add)
        p.act(t2, t1, AF.Silu)
        p.tt("dve", abf, t2, vs, ALU.mult)
        self.transposes(aT, abf, NS, NFC)

        def cb(bk, c0, n):
            p.tt("dve", xs[:, c0:c0 + n], xs[:, c0:c0 + n], bk, ALU.add)
        self.proj_tok(aT, w_dn, NS, DM, NFC, cb)
        self.rmsnorm(xs, self.g_final[0:NS, :], ysb, NS, DM)
        p.dma("sp", D["ys"], ysb, "st_ys", is_output=True)

    def alloc_sample(self):
        top = Bump(self.arena, self.ARENA - 4096)
        b = Bump(self.arena, self.ARENA - 20480, self.ARENA - 4096)
        self.samp = {
            "xs": top.alloc([NS, DM], F32, "s_xs"),
            "zqk": b.alloc([NS, 1024], F32, "s_zqk"),
            "zv": b.alloc([NS, 512], F32, "s_zv"),
            "zqkB": b.alloc([NS, 640], F32, "s_zqkB"),
            "zvB": b.alloc([NS, 128], F32, "s_zvB"),
            "cat": b.alloc([NS, DM], F32, "s_cat"),
            "hm": b.alloc([NS, DM], BF16, "s_hm"),
        }
        self.samp_bump = b
        self.PH_END = self.ARENA - 20480

    def build(self):
        self.scr_tokens = []
        self.alloc_sample()
        self.bulk_copies()
        self.phase0()
        self.phase1()
        if self.stage >= 2:
            self.samp_attn()
            self.phase2()
        if self.stage >= 3:
            self.phase2b()
            self.samp_mix()
        if self.stage >= 4:
            self.phase3()
            self.samp_ffn()
        return self.p.build()


def make_consts():
    half = 32
    inv = np.power(np.float32(10000.0), -np.arange(half, dtype=np.float32) / np.float32(half)).astype(np.float32)

    def tab(pos):
        ang = pos.astype(np.float32)[:, None] * inv[None, :]
        c = np.cos(ang).astype(np.float32)
        s = np.sin(ang).astype(np.float32)
        return np.concatenate([c, c, -s, s], axis=1).astype(np.float32)
    c_rope = tab(np.arange(SEQ))
    c_ropes = tab(np.full((NS,), PAST))
    k = np.arange(128)[:, None]
    q = np.arange(128)[None, :]
    own = (k <= q).astype(np.float32)
    prev = (k >= q).astype(np.float32)
    c_mask = np.concatenate([own, prev], axis=1).astype(np.float32)
    c_ident = np.eye(128, dtype=np.float32)
    c_sel = np.zeros((NS, NS, 128), np.float32)
    for n in range(NS):
        c_sel[n, n, :] = 1.0
    return {"c_rope": c_rope, "c_ropes": c_ropes, "c_mask": c_mask, "c_ident": c_ident,
            "c_sel": c_sel.reshape(NS, NS * 128)}


_STAGE = 4


def kernel(x_prompt, x_sample, cache_a_k, cache_a_v, cache_b_k, cache_b_v, cache_mem_k, cache_mem_v, state_conv,
           mem_prompt, g_mix, w_in, g_out_a, g_out_b, sinks, w_out, g_cross, g_mem, w_xq, w_mem_kv, w_xo,
           g_ffn, w_up, conv_w, conv_b, w_down, g_final):
    f = lambda a: np.ascontiguousarray(np.asarray(a, dtype=np.float32))
    kb = KB(stage=_STAGE)
    nc = kb.build()
    consts = make_consts()
    shared = {
        "g_mix": f(g_mix[0]), "w_in": f(w_in[0]), "g_out_a": f(g_out_a[0]), "g_out_b": f(g_out_b[0]),
        "sinks": f(sinks[0]), "w_out": f(w_out[0]), "g_cross": f(g_cross[0]), "g_mem": f(g_mem[0]),
        "w_xq": f(w_xq[0]), "w_mem_kv": f(w_mem_kv[0]), "w_xo": f(w_xo[0]), "g_ffn": f(g_ffn[0]),
        "w_up": f(w_up[0]), "conv_w": f(conv_w[0]), "conv_b": f(conv_b[0]), "w_down": f(w_down[0]),
        "g_final": f(g_final),
    }
    shared.update(consts)
    in_maps = []
    for c in range(NCORES):
        s = slice(c * NS, (c + 1) * NS)
        m = dict(shared)
        m["xp"] = f(x_prompt[c])
        m["xs"] = f(x_sample[s, 0])
        m["cak"] = f(cache_a_k[0, s]).reshape(NS, LA, 512)
        m["cav"] = f(cache_a_v[0, s]).reshape(NS, LA, 512)
        m["cbk"] = f(cache_b_k[0, s]).reshape(NS, LB, 128)
        m["cbv"] = f(cache_b_v[0, s]).reshape(NS, LB, 128)
        m["cmk"] = f(cache_mem_k[0, s]).reshape(NS, MEM, DM)
        m["cmv"] = f(cache_mem_v[0, s]).reshape(NS, MEM, DM)
        m["sconv"] = f(state_conv[0, s])
        m["memp"] = f(mem_prompt[c])
        in_maps.append(m)
    res = run_bass_kernel_spmd(nc, in_maps, core_ids=list(range(NCORES)))
    R = res.results
    cat = lambda k: np.stack([np.asarray(R[c][k], dtype=np.float32) for c in range(NCORES)])
    catn = lambda k: np.concatenate([np.asarray(R[c][k], dtype=np.float32) for c in range(NCORES)], axis=0)
    y_prompt = cat("yp")
    y_sample = catn("ys").reshape(NCORES * NS, 1, DM)
    p_a_k = cat("pak").reshape(1, NCORES, LA, 8, 64)
    p_a_v = cat("pav").reshape(1, NCORES, LA, 8, 64)
    p_b_k = cat("pbk").reshape(1, NCORES, LB, 2, 64)
    p_b_v = cat("pbv").reshape(1, NCORES, LB, 2, 64)
    p_mem_k = cat("pmk").reshape(1, NCORES, MEM, 4, 256)
    p_mem_v = cat("pmv").reshape(1, NCORES, MEM, 4, 256)
    p_conv = cat("pconv").reshape(1, NCORES, 2, DFF)
    s_a_k = catn("sak").reshape(1, NCORES * NS, LA, 8, 64)
    s_a_v = catn("sav").reshape(1, NCORES * NS, LA, 8, 64)
    s_b_k = catn("sbk").reshape(1, NCORES * NS, LB, 2, 64)
    s_b_v = catn("sbv").reshape(1, NCORES * NS, LB, 2, 64)
    s_conv = catn("sconvo").reshape(1, NCORES * NS, 2, DFF)
    return (y_prompt, y_sample, p_a_k, p_a_v, p_b_k, p_b_v, p_mem_k, p_mem_v, p_conv,
            s_a_k, s_a_v, s_b_k, s_b_v, s_conv)
```

```python
import numpy as np
from contextlib import ExitStack
import ml_dtypes
import concourse.bass as bass
import concourse.mybir as mybir
from concourse.bass_utils import run_bass_kernel_spmd

F32 = mybir.dt.float32
BF16 = mybir.dt.bfloat16
ALU = mybir.AluOpType
AF = mybir.ActivationFunctionType
AX = mybir.AxisListType

NCORES = 8
SEQ = 4096
DM = 1024
NT = SEQ // 128
NS = 16
LA = 2048
LB = 128
MEM = 256
DFF = 2816
NFC = DFF // 128
IN_DIM = 2304
EPS = 1e-6
PAST = 16384


class V:
    __slots__ = ("ap", "res")

    def __init__(self, ap, res):
        self.ap = ap
        self.res = res

    def __getitem__(self, k):
        return V(self.ap[k], self.res)

    def re(self, s, **kw):
        return V(self.ap.rearrange(s, **kw), self.res)

    def bc(self, shape):
        return V(self.ap.to_broadcast(shape), self.res)

    def cast(self, dt):
        return V(self.ap.bitcast(dt), self.res)


def _keys(v):
    if v is None:
        return []
    r = v.res
    if isinstance(r, list):
        return r
    return [r]


class Prog:
    ENGS = ("pe", "act", "dve", "pool", "sp")

    def __init__(self):
        self.nc = bass.Bass("TRN2", target_bir_lowering=False)
        self.es = ExitStack()
        self.ins = []
        self.res = {}
        self.out_tokens = []

    def dram(self, name, shape, dt, kind):
        t = self.nc.dram_tensor(name, list(shape), dt, kind=kind)
        return V(t.ap(), ("dram", name))

    def sbuf(self, name, shape, dt):
        t = self.es.enter_context(self.nc.sbuf_tensor(name, list(shape), dt))
        return V(t[:], ("sb", name))

    def psum(self, name, shape, dt):
        t = self.es.enter_context(self.nc.psum_tensor(name, list(shape), dt))
        return V(t[:], ("ps", name))

    def _rec(self, eng, emit, reads, writes, dma_sem=None, track_dram=False):
        iid = len(self.ins)
        is_dma = dma_sem is not None
        deps = {}

        def add_dep(tok, kind):
            if tok is None:
                return
            pr = self.ins[tok]
            if (not is_dma) and (pr["dma_sem"] is None) and pr["eng"] == eng:
                if eng == "pe":
                    return
            deps[tok] = True

        rk = [k for v in reads for k in _keys(v)]
        wk = [k for v in writes for k in _keys(v)]
        if not track_dram:
            rk = [k for k in rk if k[0] != "dram"]
            wk = [k for k in wk if k[0] != "dram"]
        for r in rk:
            st = self.res.setdefault(r, {"w": None, "r": {}})
            add_dep(st["w"], 0)
        for w in wk:
            st = self.res.setdefault(w, {"w": None, "r": {}})
            add_dep(st["w"], 1)
            for t in st["r"].values():
                add_dep(t, 2)
        semkey = ("dma", dma_sem) if is_dma else ("eng", eng)
        for r in rk:
            self.res[r]["r"][semkey] = iid
        for w in wk:
            self.res[w]["w"] = iid
            self.res[w]["r"] = {}
        self.ins.append({"eng": eng, "emit": emit, "deps": list(deps), "dma_sem": dma_sem,
                         "flag": is_dma, "semkey": semkey, "val": None})
        for d in deps:
            self.ins[d]["flag"] = True
        return iid

    def mm(self, out, lhsT, rhs, start=True, stop=True):
        return self._rec("pe", lambda e: e.matmul(out.ap, lhsT.ap, rhs.ap, start=start, stop=stop),
                         [lhsT, rhs], [out])

    def tr(self, out, in_, ident):
        return self._rec("pe", lambda e: e.transpose(out.ap, in_.ap, ident.ap), [in_, ident], [out])

    def act(self, out, in_, func, bias=None, scale=1.0, accum=None):
        kw = {}
        rd = [in_]
        if bias is not None:
            if isinstance(bias, V):
                kw["bias"] = bias.ap
                rd.append(bias)
            else:
                kw["bias"] = bias
        if isinstance(scale, V):
            kw["scale"] = scale.ap
            rd.append(scale)
        else:
            kw["scale"] = scale
        wr = [out]
        if accum is not None:
            kw["accum_out"] = accum.ap
            wr.append(accum)
        return self._rec("act", lambda e: e.activation(out.ap, in_.ap, func, **kw), rd, wr)

    def tt(self, eng, out, a, b, op):
        return self._rec(eng, lambda e: e.tensor_tensor(out.ap, a.ap, b.ap, op), [a, b], [out])

    def ts(self, eng, out, a, s1, s2, op0, op1=None, accum=None):
        rd = [a]
        s1a = s1.ap if isinstance(s1, V) else s1
        s2a = s2.ap if isinstance(s2, V) else s2
        if isinstance(s1, V):
            rd.append(s1)
        if isinstance(s2, V):
            rd.append(s2)
        wr = [out]
        kw = {}
        if op1 is not None:
            kw["op1"] = op1
        if accum is not None:
            kw["accum_out"] = accum.ap
            wr.append(accum)
        return self._rec(eng, lambda e: e.tensor_scalar(out.ap, a.ap, s1a, s2a, op0, **kw), rd, wr)

    def stt(self, eng, out, a, s, b, op0, op1):
        rd = [a, b]
        sa = s.ap if isinstance(s, V) else s
        if isinstance(s, V):
            rd.append(s)
        return self._rec(eng, lambda e: e.scalar_tensor_tensor(out.ap, a.ap, sa, b.ap, op0, op1), rd, [out])

    def copy(self, eng, out, in_):
        if eng == "act":
            return self._rec("act", lambda e: e.copy(out.ap, in_.ap), [in_], [out])
        return self._rec(eng, lambda e: e.tensor_copy(out.ap, in_.ap), [in_], [out])

    def memset(self, eng, out, val):
        return self._rec(eng, lambda e: e.memset(out.ap, val), [], [out])

    def reduce(self, eng, out, in_, op, axis=AX.X):
        return self._rec(eng, lambda e: e.tensor_reduce(out.ap, in_.ap, axis, op), [in_], [out])

    def recip(self, out, in_):
        return self._rec("dve", lambda e: e.reciprocal(out.ap, in_.ap), [in_], [out])

    def dma(self, q, out, in_, sem, is_output=False, **kw):
        iid = self._rec(q, lambda e: e.dma_start(out=out.ap, in_=in_.ap, **kw), [in_], [out], dma_sem=sem)
        if is_output:
            self.out_tokens.append(iid)
        return iid

    def wait_for(self, eng, tokens):
        iid = self._rec(eng, None, [], [])
        self.ins[iid]["deps"] = list(tokens)
        for d in tokens:
            self.ins[d]["flag"] = True
        return iid

    def build(self):
        nc = self.nc
        self.wait_for("sp", list(self.out_tokens))
        counts = {}
        for r in self.ins:
            if r["flag"]:
                k = r["semkey"]
                inc = 16 if r["dma_sem"] is not None else 1
                counts[k] = counts.get(k, 0) + inc
                r["val"] = counts[k]
        semh = {}
        for k in counts:
            nm = "s_" + "_".join(str(x) for x in k)
            semh[k] = self.es.enter_context(nc.semaphore(nm))
        per_eng = {e: [] for e in self.ENGS}
        for r in self.ins:
            per_eng[r["eng"]].append(r)
        ins = self.ins
        stats = {e: [0, 0] for e in self.ENGS}

        def run(engname, eobj):
            waited = {}
            for r in per_eng[engname]:
                need = {}
                for d in r["deps"]:
                    pr = ins[d]
                    k = pr["semkey"]
                    v = pr["val"]
                    if waited.get(k, 0) >= v:
                        continue
                    if need.get(k, 0) < v:
                        need[k] = v
                for k, v in need.items():
                    eobj.wait_ge(semh[k], v)
                    waited[k] = v
                    stats[engname][1] += 1
                if r["emit"] is not None:
                    bi = r["emit"](eobj)
                    stats[engname][0] += 1
                    if r["flag"]:
                        bi.then_inc(semh[r["semkey"]], 16 if r["dma_sem"] is not None else 1)

        with nc.Block() as block:
            @block.tensor
            def _(e):
                run("pe", e)

            @block.scalar
            def _(e):
                run("act", e)

            @block.vector
            def _(e):
                run("dve", e)

            @block.gpsimd
            def _(e):
                run("pool", e)

            @block.sync
            def _(e):
                run("sp", e)
        self.stats = stats
        self.sem_counts = counts
        self.es.close()
        return nc


class Arena:
    def __init__(self, p, nbytes):
        self.p = p
        self.nbytes = nbytes
        self.base = p.sbuf("arena", [128, nbytes // 2], BF16)
        self.regs = []
        self.n = 0

    def carve(self, off, shape, dt, name):
        esz = 4 if dt == F32 else 2
        n = int(np.prod(shape[1:]))
        nb = n * esz
        assert off % 4 == 0 and off + nb <= self.nbytes, (name, off, nb, self.nbytes)
        a = self.base.ap[0:shape[0], off // 2:(off + nb) // 2]
        if dt == F32:
            a = a.bitcast(F32)
        if len(shape) == 3:
            a = a.rearrange("p (a b) -> p a b", a=shape[1])
        elif len(shape) == 4:
            a = a.rearrange("p (a b c) -> p a b c", a=shape[1], b=shape[2])
        key = ("ar", name, self.n)
        self.n += 1
        inherit = {}
        for (s, e, k) in self.regs:
            if s < off + nb and off < e:
                st = self.p.res.get(k)
                if st:
                    toks = list(st["r"].values())
                    if st["w"] is not None:
                        toks.append(st["w"])
                    for t in toks:
                        sk = self.p.ins[t]["semkey"]
                        if sk not in inherit or inherit[sk] < t:
                            inherit[sk] = t
        self.regs.append((off, off + nb, key))
        self.p.res[key] = {"w": None, "r": inherit}
        return V(a, key)


def subview(arena, v, tag, ap=None):
    key = (v.res, tag)
    p = arena.p
    par = p.res.get(v.res, {"w": None, "r": {}})
    r = dict(par["r"])
    if par["w"] is not None:
        sk = p.ins[par["w"]]["semkey"]
        if sk not in r or r[sk] < par["w"]:
            r[sk] = par["w"]
    p.res[key] = {"w": None, "r": r}
    for (s_, e_, k_) in list(arena.regs):
        if k_ == v.res:
            arena.regs.append((s_, e_, key))
            break
    return V(v.ap if ap is None else ap, key)


class Bump:
    def __init__(self, arena, start, end=None):
        self.a = arena
        self.off = start
        self.end = end if end is not None else arena.nbytes

    def alloc(self, shape, dt, name):
        esz = 4 if dt == F32 else 2
        nb = int(np.prod(shape[1:])) * esz
        nb4 = (nb + 3) // 4 * 4
        assert self.off + nb4 <= self.end, ("bump overflow", name, self.off, nb4, self.end)
        v = self.a.carve(self.off, shape, dt, name)
        self.off += nb4
        return v


def bc_mid(v, h):
    shp = list(v.ap.shape)
    return V(v.ap.unsqueeze(1).to_broadcast([shp[0], h, shp[1]]), v.res)


def bc_last(v, n):
    shp = list(v.ap.shape)
    return V(v.ap.unsqueeze(2).to_broadcast([shp[0], shp[1], n]), v.res)


class KB:
    ARENA = 204800
    PERS = 34816

    def __init__(self, stage=99):
        self.stage = stage
        p = self.p = Prog()
        self.D = D = {}

        def din(name, shape, dt=F32):
            D[name] = p.dram(name, shape, dt, "ExternalInput")

        def dout(name, shape):
            D[name] = p.dram(name, shape, F32, "ExternalOutput")

        def dscr(name, shape, dt):
            D[name] = p.dram(name, shape, dt, "Internal")

        din("xp", [SEQ, DM]); din("xs", [NS, DM])
        din("cak", [NS, LA, 512]); din("cav", [NS, LA, 512])
        din("cbk", [NS, LB, 128]); din("cbv", [NS, LB, 128])
        din("cmk", [NS, MEM, DM]); din("cmv", [NS, MEM, DM])
        din("sconv", [NS, 2, DFF]); din("memp", [MEM, DM])
        din("g_mix", [DM]); din("w_in", [DM, IN_DIM]); din("g_out_a", [512]); din("g_out_b", [512])
        din("sinks", [8]); din("w_out", [DM, DM]); din("g_cross", [DM]); din("g_mem", [DM])
        din("w_xq", [DM, DM]); din("w_mem_kv", [DM, 2 * DM]); din("w_xo", [DM, DM]); din("g_ffn", [DM])
        din("w_up", [DM, 2 * DFF]); din("conv_w", [3, DFF]); din("conv_b", [DFF]); din("w_down", [DFF, DM])
        din("g_final", [DM])
        din("c_rope", [SEQ, 128]); din("c_ropes", [NS, 128]); din("c_mask", [128, 256])
        din("c_ident", [128, 128]); din("c_sel", [NS, NS * 128])
        dout("yp", [SEQ, DM]); dout("ys", [NS, DM])
        dout("pak", [LA, 512]); dout("pav", [LA, 512]); dout("pbk", [LB, 128]); dout("pbv", [LB, 128])
        dout("pmk", [MEM, DM]); dout("pmv", [MEM, DM]); dout("pconv", [2, DFF])
        dout("sak", [NS, LA, 512]); dout("sav", [NS, LA, 512]); dout("sbk", [NS, LB, 128]); dout("sbv", [NS, LB, 128])
        dout("sconvo", [NS, 2, DFF])
        dscr("scrA", [SEQ, 1536], BF16); dscr("scrB", [SEQ, 768], BF16)
        dscr("scrO", [4, SEQ, 520], F32); dscr("scrX2", [SEQ, DM], F32)

        self.arena = Arena(p, self.ARENA)
        self.ps = p.psum("psall", [128, 4096], F32)
        self.bank_ctr = 0
        self.nb_mod = 6
        self.tb_ctr = 0
        self.pers = Bump(self.arena, 0, self.PERS)
        self.setup_consts()

    def bank(self, i):
        return V(self.ps.ap[:, 512 * i:512 * (i + 1)], ("ps", i))

    def nb(self):
        i = self.bank_ctr % self.nb_mod
        self.bank_ctr += 1
        return self.bank(i)

    def tbank(self):
        j = 6 + self.tb_ctr % 2
        self.tb_ctr += 1
        a = self.ps.ap[:, 512 * j:512 * (j + 1)].bitcast(BF16)
        return V(a.rearrange("p (a b) -> p a b", a=8), ("ps", j))

    def bcast_dram(self, name, n, parts=128):
        t = self.D[name].ap.tensor
        return V(bass.AP(t, 0, [[0, parts], [1, n]]), ("dram", name))

    def setup_consts(self):
        p, D, b = self.p, self.D, self.pers
        self.identf = b.alloc([128, 128], F32, "identf")
        self.ident = b.alloc([128, 128], BF16, "ident")
        self.ones_bf = b.alloc([128, 128], BF16, "ones_bf")
        self.epsb = b.alloc([128, 1], F32, "epsb")
        self.ss = [b.alloc([128, 1], F32, f"ss{i}") for i in range(4)]
        self.ss_ctr = 0
        self.junk = b.alloc([128, DM], BF16, "junk")
        self.g_ffn = b.alloc([128, DM], F32, "g_ffn")
        self.g_final = b.alloc([128, DM], F32, "g_final")
        self.PERS_A = b.off
        self.mask2 = b.alloc([128, 2, 128], BF16, "mask2")
        self.g_mix = b.alloc([128, DM], F32, "g_mix")
        self.g_cross = b.alloc([128, DM], F32, "g_cross")
        self.g_out = b.alloc([128, DM], F32, "g_out")
        self.esink = b.alloc([128, 8], F32, "esink")
        self.ropes = b.alloc([NS, 128], F32, "ropes")
        self.mkT = b.alloc([128, 8, MEM], BF16, "mkT")
        self.mv = b.alloc([128, 2, DM], BF16, "mv")
        p.memset("pool", self.epsb, EPS)
        p.dma("sp", self.identf, D["c_ident"], "ld_c_identf")
        p.dma("pool", self.ident, D["c_ident"], "ld_c_ident")
        p.dma("pool", self.mask2.re("p a b -> p (a b)"), D["c_mask"], "ld_c_mask")
        p.memset("pool", self.ones_bf, 1.0)
        p.dma("sp", self.g_mix, self.bcast_dram("g_mix", DM), "ld_c_gmix")
        p.dma("sp", self.g_cross, self.bcast_dram("g_cross", DM), "ld_c_gcross")
        p.dma("sp", self.g_ffn, self.bcast_dram("g_ffn", DM), "ld_c_gffn")
        p.dma("sp", self.g_final, self.bcast_dram("g_final", DM), "ld_c_gfinal")
        p.dma("sp", self.g_out[:, 0:512], self.bcast_dram("g_out_a", 512), "ld_c_goa")
        p.dma("sp", self.g_out[:, 512:1024], self.bcast_dram("g_out_b", 512), "ld_c_gob")
        p.dma("sp", self.esink, self.bcast_dram("sinks", 8), "ld_c_sinks")
        p.dma("sp", self.ropes, D["c_ropes"], "ld_c_ropes")
        p.act(self.esink, self.esink, AF.Exp)

    def load_w(self, dst, name, kchunks, sem):
        src = self.D[name]
        keys = []
        for kc in range(kchunks):
            sub = subview(self.arena, dst, ("kc", kc), dst.ap[:, kc, :])
            self.p.dma("pool", sub, src[kc * 128:(kc + 1) * 128, :], sem)
            keys.append(sub.res)
        return V(dst.ap, keys)

    def next_ss(self):
        s = self.ss[self.ss_ctr % 4]
        self.ss_ctr += 1
        return s

    def rmsnorm(self, x, g, out, P, Dn):
        p = self.p
        ss = self.next_ss()[0:P, :]
        p.memset("dve", ss, 0.0)
        p.act(self.junk[0:P, 0:Dn], x, AF.Square, accum=ss)
        p.act(ss, ss, AF.Ln, scale=1.0 / Dn, bias=self.epsb[0:P, :])
        p.act(ss, ss, AF.Exp, scale=-0.5)
        p.stt("dve", out, x, ss, g, ALU.mult, ALU.mult)

    def transposes(self, dst, src, P, n, evac=("act", "dve"), ident=None):
        p = self.p
        ident = self.ident if ident is None else ident
        gi = 0
        for g in range(0, n, 8):
            m = min(8, n - g)
            pst = self.tbank()
            for j in range(m):
                p.tr(pst[:, j, 0:P], src[0:P, (g + j) * 128:(g + j + 1) * 128], ident[0:P, 0:P])
            p.copy(evac[gi % len(evac)], dst[:, g:g + m, 0:P], pst[:, 0:m, 0:P])
            gi += 1

    def proj_tok(self, hT, w, P, n_out, kchunks, cb):
        p = self.p
        c0 = 0
        while c0 < n_out:
            n = min(512, n_out - c0)
            bk = self.nb()
            for kc in range(kchunks):
                p.mm(bk[0:P, 0:n], hT[:, kc, 0:P], w[:, kc, c0:c0 + n], start=(kc == 0), stop=(kc == kchunks - 1))
            cb(bk[0:P, 0:n], c0, n)
            c0 += n

    def bulk_copies(self):
        p, D = self.p, self.D
        for n in range(NS):
            p.dma("act", D["sak"][n, 0:LA - 1, :], D["cak"][n, 1:LA, :], "bulk", is_output=True)
            p.dma("act", D["sav"][n, 0:LA - 1, :], D["cav"][n, 1:LA, :], "bulk", is_output=True)
        p.dma("act", D["sbk"][:, 0:LB - 1, :], D["cbk"][:, 1:LB, :], "bulk", is_output=True)
        p.dma("act", D["sbv"][:, 0:LB - 1, :], D["cbv"][:, 1:LB, :], "bulk", is_output=True)
        p.dma("act", D["sconvo"][:, 0, :], D["sconv"][:, 1, :], "bulk", is_output=True)

    def phase0(self):
        p, D = self.p, self.D
        b = Bump(self.arena, self.PERS)
        wkv = b.alloc([128, 8, 2 * DM], BF16, "wkv")
        gm = b.alloc([128, DM], F32, "g_mem")
        wkv = self.load_w(wkv, "w_mem_kv", 8, "ld_w0")
        p.dma("sp", gm, self.bcast_dram("g_mem", DM), "ld_c0_12")
        xm = [b.alloc([128, DM], F32, f"xm{i}") for i in range(2)]
        hb = [b.alloc([128, DM], BF16, f"hbm{i}") for i in range(2)]
        hT = [b.alloc([128, 8, 128], BF16, f"hTm{i}") for i in range(2)]
        kvf = [b.alloc([128, 2 * DM], F32, f"kvf{i}") for i in range(2)]
        kb = [b.alloc([128, DM], BF16, f"kbm{i}") for i in range(2)]
        for mt in range(2):
            rows = slice(mt * 128, (mt + 1) * 128)
            p.dma("sp", xm[mt], D["memp"][rows, :], f"ld_xm{mt}")
            self.rmsnorm(xm[mt], gm, hb[mt], 128, DM)
            self.transposes(hT[mt], hb[mt], 128, 8)

            def cb(bk, c0, n, mt=mt):
                p.copy("act", kvf[mt][:, c0:c0 + n], bk)
            self.proj_tok(hT[mt], wkv, 128, 2 * DM, 8, cb)
            p.dma("sp", D["pmk"][rows, :], kvf[mt][:, 0:DM], f"st_kvfk{mt}", is_output=True)
            p.dma("sp", D["pmv"][rows, :], kvf[mt][:, DM:2 * DM], f"st_kvfv{mt}", is_output=True)
            p.copy("pool", kb[mt], kvf[mt][:, 0:DM])
            self.transposes(self.mkT[:, :, rows], kb[mt], 128, 8)
            p.copy("pool", self.mv[:, mt, :], kvf[mt][:, DM:2 * DM])

    def inproj_front(self, P, x, hb, hT):
        self.rmsnorm(x, self.g_mix[0:P, :], hb, P, DM)
        self.transposes(hT, hb, P, 8)

    def inproj_tile(self, P, x, hb, hT, w_in, cos2, sinS, zqk, zv, zqkB, zvB, tmp, zbA, zbB, want_vf32, want_vBf32,
                    skip_front=False):
        p = self.p
        if not skip_front:
            self.inproj_front(P, x, hb, hT)
        cosb = lambda h: bc_mid(cos2, h)
        sin_lo = lambda h: bc_mid(sinS[:, 0:32], h)
        sin_hi = lambda h: bc_mid(sinS[:, 32:64], h)

        def rope(bk, dst, ncols):
            h = ncols // 64
            src = bk[:, 0:ncols].re("p (h d) -> p h d", d=64)
            d3 = dst.re("p (h d) -> p h d", d=64)
            t3 = tmp[:, 0:ncols].re("p (h d) -> p h d", d=64)
            p.tt("dve", d3, src, cosb(h), ALU.mult)
            p.tt("dve", t3[:, :, 0:32], src[:, :, 32:64], sin_lo(h), ALU.mult)
            p.tt("dve", t3[:, :, 32:64], src[:, :, 0:32], sin_hi(h), ALU.mult)
            p.tt("dve", d3, d3, t3, ALU.add)

        def cb(bk, c0, n):
            if c0 == 0:
                rope(bk, zqk[:, 0:512], 512)
            elif c0 == 512:
                rope(bk, zqk[:, 512:1024], 512)
                if zbA is not None:
                    p.copy("act", zbA[0][:, 0:1024], zqk)
            elif c0 == 1024:
                if zbA is not None:
                    p.copy("act", zbA[1][:, 1024:1536], bk)
                if want_vf32:
                    p.copy("act", zv, bk)
            elif c0 == 1536:
                rope(bk, zqkB[:, 0:512], 512)
            else:
                rope(bk, zqkB[:, 512:640], 128)
                if zbB is not None:
                    p.copy("act", zbB[0][:, 0:512].re("p (f t d) -> p f t d", f=4, t=2),
                           zqkB[:, 0:512].re("p (t f d) -> p f t d", t=2, f=4))
                    p.copy("act", zbB[0][:, 512:640], zqkB[:, 512:640])
                    p.copy("act", zbB[1][:, 640:768], bk[:, 128:256])
                if want_vBf32:
                    p.copy("act", zvB, bk[:, 128:256])
        self.proj_tok(hT, w_in, P, IN_DIM, 8, cb)

    def phase1(self):
        p, D = self.p, self.D
        b = Bump(self.arena, self.PERS)
        w_in = b.alloc([128, 8, IN_DIM], BF16, "w_in")
        w_in = self.load_w(w_in, "w_in", 8, "ld_w1")
        rt = b.alloc([128, NT, 128], F32, "rope_tab")
        p.dma("sp", rt, D["c_rope"].re("(t p) c -> p t c", p=128), "ld_c0_13")
        xt = [b.alloc([128, DM], F32, f"xt{i}") for i in range(3)]
        hb = [b.alloc([128, DM], BF16, f"hb{i}") for i in range(2)]
        hT = [b.alloc([128, 8, 128], BF16, f"hT{i}") for i in range(2)]
        zqk = [b.alloc([128, 1024], F32, f"zqk{i}") for i in range(2)]
        zv = [b.alloc([128, 512], F32, f"zv{i}") for i in range(2)]
        zqkB = [b.alloc([128, 640], F32, f"zqkB{i}") for i in range(2)]
        zvB = b.alloc([128, 128], F32, "zvB")
        tmp = [b.alloc([128, 512], F32, f"tmp{i}") for i in range(2)]
        zbA, zbB = [], []
        for i in range(2):
            a = b.alloc([128, 1536], BF16, f"zbA{i}")
            v1, v2 = subview(self.arena, a, "qk"), subview(self.arena, a, "v")
            zbA.append((v1, v2, V(a.ap, [v1.res, v2.res])))
            a = b.alloc([128, 768], BF16, f"zbB{i}")
            v1, v2 = subview(self.arena, a, "qk"), subview(self.arena, a, "v")
            zbB.append((v1, v2, V(a.ap, [v1.res, v2.res])))
        self.p1_end = b.off
        def xload(t):
            p.dma("sp", xt[t % 3], D["xp"][t * 128:(t + 1) * 128, :], f"ld_xt{t % 3}")

        def front(t):
            self.inproj_front(128, xt[t % 3], hb[t % 2], hT[t % 2])
        xload(0)
        xload(1)
        front(0)
        for t in range(NT):
            rows = slice(t * 128, (t + 1) * 128)
            x = xt[t % 3]
            i = t % 2
            if t + 2 < NT:
                xload(t + 2)
            if t + 1 < NT:
                front(t + 1)
            self.inproj_tile(128, x, hb[i], hT[i], w_in, rt[:, t, 0:64], rt[:, t, 64:128],
                             zqk[i], zv[i], zqkB[i], zvB, tmp[i], zbA[i], zbB[i], t >= NT // 2, t == NT - 1,
                             skip_front=True)
            if t >= NT // 2:
                orow = slice((t - NT // 2) * 128, (t - NT // 2 + 1) * 128)
                p.dma("sp", D["pak"][orow, :], zqk[i][:, 512:1024], f"st_zqk{i}", is_output=True)
                p.dma("sp", D["pav"][orow, :], zv[i], f"st_zv{i}", is_output=True)
            if t == NT - 1:
                p.dma("sp", D["pbk"], zqkB[i][:, 512:640], f"st_zqkB{i}", is_output=True)
                p.dma("sp", D["pbv"], zvB, "st_zvB", is_output=True)
            self.scr_tokens.append(p.dma("sp", D["scrA"][rows, :], zbA[i][2], f"st_zbA{i}"))
            self.scr_tokens.append(p.dma("sp", D["scrB"][rows, :], zbB[i][2], f"st_zbB{i}"))
        sb = self.samp
        xs = sb["xs"]
        p.dma("sp", xs, D["xs"], "ld_xs")
        self.inproj_tile(NS, xs, hb[0][0:NS, :], hT[0], w_in, self.ropes[:, 0:64], self.ropes[:, 64:128],
                         sb["zqk"], sb["zv"], sb["zqkB"], sb["zvB"], tmp[0][0:NS, :], None, None, True, True)
        p.dma("sp", D["sak"][:, LA - 1, :], sb["zqk"][:, 512:1024], "st_s1a", is_output=True)
        p.dma("sp", D["sav"][:, LA - 1, :], sb["zv"], "st_s1b", is_output=True)
        p.dma("sp", D["sbk"][:, LB - 1, :], sb["zqkB"][:, 512:640], "st_s1c", is_output=True)
        p.dma("sp", D["sbv"][:, LB - 1, :], sb["zvB"], "st_s1d", is_output=True)

    def phase2(self):
        p, D = self.p, self.D
        b = Bump(self.arena, self.PERS + 3 * 16384, self.PH_END)
        blk = [b.alloc([128, 1536], BF16, f"blk{i}") for i in range(3)]
        QT = [b.alloc([128, 4, 2, 128], BF16, f"QZ{i}") for i in range(2)]
        for v in QT:
            p.memset("pool", v, 0.0)
        KT = [b.alloc([128, 4, 128], BF16, f"KT{i}") for i in range(3)]
        VX = [b.alloc([128, 8, 65], BF16, f"VX{i}") for i in range(3)]
        PT = [b.alloc([128, 2, 2, 128], BF16, f"PT{i}") for i in range(8)]
        OS = [b.alloc([128, 520], F32, f"OS{i}") for i in range(3)]
        for v in VX:
            p.memset("pool", v, 1.0)
        p.wait_for("sp", list(self.scr_tokens))
        mask4 = V(self.mask2.ap.unsqueeze(2).to_broadcast([128, 2, 2, 128]), self.mask2.res)
        mask_own = bc_mid(self.mask2[:, 0, :], 2)
        st = {"s": 0, "pc": 0}
        self.o_tokens = []

        def st_dma(c):
            kind, br, d, r, bb, first, s = c
            cur = blk[s % 3]
            if kind == "A":
                src = D["scrA"].re("(j r) c -> r j c", r=d)[r, 128 * bb:128 * (bb + 1), :]
                p.dma("sp", cur, src, f"ld_blk{s % 3}")
            else:
                src = D["scrB"][128 * bb:128 * (bb + 1), :]
                p.dma("sp", cur[:, 0:768], src, f"ld_blk{s % 3}")

        def st_load(c):
            kind, br, d, r, bb, first, s = c
            cur = blk[s % 3]
            qt, kt, vx = QT[s % 2], KT[s % 3], VX[s % 3]
            pq = self.tbank()
            for j in range(4):
                p.tr(pq[:, j, :], cur[:, j * 128:(j + 1) * 128], self.ident)
            p.copy("dve", qt[0:64, :, 0, :], pq[0:64, 0:4, :])
            p.copy("dve", qt[64:128, :, 1, :], pq[64:128, 0:4, :])
            if kind == "A":
                self.transposes(kt, cur[:, 512:1024], 128, 4, evac=("act",))
                p.copy("pool", vx[:, :, 0:64], cur[:, 1024:1536].re("p (h d) -> p h d", d=64))
            else:
                self.transposes(kt[:, 0:1, :], cur[:, 512:640], 128, 1, evac=("act",))
                p.copy("pool", vx[:, 0:2, 0:64], cur[:, 640:768].re("p (h d) -> p h d", d=64))

        def st_scores(c):
            kind, br, d, r, bb, first, s = c
            qt, kt, ktp = QT[s % 2], KT[s % 3], KT[(s - 1) % 3]
            for j in range(4):
                psS = self.bank(j)
                kj = j if kind == "A" else 0
                q2 = qt[:, j].re("p a q -> p (a q)")
                p.mm(psS[:, 0:256], kt[:, kj, :], q2)
                if not first:
                    p.mm(psS[:, 256:512], ktp[:, kj, :], q2)
                pt = PT[(4 * s + j) % 8]
                if not first:
                    p.act(pt.re("p b a q -> p (b a q)"), psS, AF.Exp, scale=0.125)
                    p.tt("dve", pt, pt, mask4, ALU.mult)
                else:
                    p.act(pt[:, 0].re("p a q -> p (a q)"), psS[:, 0:256], AF.Exp, scale=0.125)
                    p.tt("dve", pt[:, 0], pt[:, 0], mask_own, ALU.mult)

        def st_pv(c):
            kind, br, d, r, bb, first, s = c
            vx, vxp = VX[s % 3], VX[(s - 1) % 3]
            psO = [self.bank(4), self.bank(5)]
            for j in range(4):
                pt = PT[(4 * s + j) % 8]
                for hh in range(2):
                    if kind == "A":
                        h = 2 * j + hh
                        vi = h
                    else:
                        h = j + 4 * hh
                        vi = hh
                    o = psO[h // 4][:, (h % 4) * 65:(h % 4) * 65 + 65]
                    p.mm(o, pt[:, 0, hh, :], vx[:, vi, :], start=True, stop=first)
                    if not first:
                        p.mm(o, pt[:, 1, hh, :], vxp[:, vi, :], start=False, stop=True)
            osb = OS[s % 3]
            p.copy("act", osb[:, 0:260], psO[0][:, 0:260])
            p.copy("act", osb[:, 260:520], psO[1][:, 0:260])
            if kind == "A":
                dst = D["scrO"][br].re("(j r) c -> r j c", r=d)[r, 128 * bb:128 * (bb + 1), :]
            else:
                dst = D["scrO"][3][128 * bb:128 * (bb + 1), :]
            self.o_tokens.append(p.dma("sp", dst, osb, f"st_os{s % 3}"))

        cfgs = []
        for br, d in ((2, 16), (1, 4), (0, 1)):
            nblk = SEQ // d // 128
            for r in range(d):
                for bb in range(nblk):
                    cfgs.append(("A", br, d, r, bb, bb == 0, len(cfgs)))
        for bb in range(NT):
            cfgs.append(("B", 3, 1, 0, bb, bb == 0, len(cfgs)))
        st_dma(cfgs[0])
        st_dma(cfgs[1])
        st_load(cfgs[0])
        for i, c in enumerate(cfgs):
            if i + 2 < len(cfgs):
                st_dma(cfgs[i + 2])
            st_scores(c)
            if i + 1 < len(cfgs):
                st_load(cfgs[i + 1])
            st_pv(c)

    def prefetch_w2(self):
        b = Bump(self.arena, self.PERS, self.PERS + 3 * 16384)
        w_out = b.alloc([128, 8, DM], BF16, "w_out")
        w_xq = b.alloc([128, 8, DM], BF16, "w_xq")
        w_xo = b.alloc([128, 8, DM], BF16, "w_xo")
        w_out = self.load_w(w_out, "w_out", 8, "ld_w2a")
        w_xq = self.load_w(w_xq, "w_xq", 8, "ld_w2b")
        w_xo = self.load_w(w_xo, "w_xo", 8, "ld_w2c")
        self.w2 = (w_out, w_xq, w_xo)

    def phase2b(self):
        p, D = self.p, self.D
        b = Bump(self.arena, self.PERS + 3 * 16384, self.PH_END)
        w_out, w_xq, w_xo = self.w2
        OL = [[b.alloc([128, 520], F32, f"OL{i}_{k}") for k in range(4)] for i in range(2)]
        cat = [b.alloc([128, DM], F32, f"cat{i}") for i in range(2)]
        hm = [b.alloc([128, DM], BF16, f"hm{i}") for i in range(2)]
        rd = [b.alloc([128, 16], F32, f"rd{i}") for i in range(2)]
        x1 = [b.alloc([128, DM], F32, f"x1_{i}") for i in range(4)]
        hmT = b.alloc([128, 8, 512], BF16, "hmT")
        h2T = b.alloc([128, 8, 512], BF16, "h2T")
        qxT = b.alloc([128, 8, 512], BF16, "qxT")
        oxT = b.alloc([128, 8, 512], BF16, "oxT")
        h2 = [b.alloc([128, DM], BF16, f"h2_{i}") for i in range(2)]
        PTx = [b.alloc([128, 2, 512], BF16, f"PTx{i}") for i in range(2)]
        rden = [b.alloc([128, 512], F32, f"rden{i}") for i in range(2)]
        self.p2b_end = b.off
        hmTb = [hmT, b.alloc([128, 8, 512], BF16, "hmT1")]
        p.wait_for("sp", list(self.o_tokens))
        self.x2_tokens = []

        def front(sti):
            for tl in range(4):
                t = sti * 4 + tl
                rows = slice(t * 128, (t + 1) * 128)
                ol = OL[t % 2]
                for k in range(4):
                    p.dma("sp", ol[k], D["scrO"][k][rows, :], f"ld_ol{t % 2}_{k}")
                self.combine(128, ol, rd[t % 2], cat[t % 2], hm[t % 2], self.esink)
                self.transposes(hmTb[sti % 2][:, :, tl * 128:(tl + 1) * 128], hm[t % 2], 128, 8)

        def mid(sti):
            hmT_ = hmTb[sti % 2]
            for tl in range(4):
                t = sti * 4 + tl
                rows = slice(t * 128, (t + 1) * 128)
                p.dma("sp", x1[tl], D["xp"][rows, :], f"ld_x1_{tl}")

                def cb(bk, c0, n, tl=tl):
                    p.tt("dve", x1[tl][:, c0:c0 + n], x1[tl][:, c0:c0 + n], bk, ALU.add)
                self.proj_tok(hmT_[:, :, tl * 128:(tl + 1) * 128], w_out, 128, DM, 8, cb)
                self.rmsnorm(x1[tl], self.g_cross, h2[tl % 2], 128, DM)
                self.transposes(h2T[:, :, tl * 128:(tl + 1) * 128], h2[tl % 2], 128, 8)
            for fc in range(8):
                bk = self.nb()
                for kc in range(8):
                    p.mm(bk, w_xq[:, kc, fc * 128:(fc + 1) * 128], h2T[:, kc, :], start=(kc == 0), stop=(kc == 7))
                p.copy("act" if fc % 2 == 0 else "dve", qxT[:, fc, :], bk)
            for h in range(4):
                pt = PTx[h % 2]
                for mc in range(2):
                    bk = self.nb()
                    for j in range(2):
                        p.mm(bk, self.mkT[:, 2 * h + j, mc * 128:(mc + 1) * 128], qxT[:, 2 * h + j, :],
                             start=(j == 0), stop=(j == 1))
                    p.act(pt[:, mc, :], bk, AF.Exp, scale=1.0 / 16.0)
                bd = self.nb()
                for mc in range(2):
                    p.mm(bd, self.ones_bf, pt[:, mc, :], start=(mc == 0), stop=(mc == 1))
                rdn = rden[h % 2]
                p.recip(rdn, bd)
                for dj in range(2):
                    bk = self.nb()
                    for mc in range(2):
                        p.mm(bk, self.mv[:, mc, h * 256 + dj * 128:h * 256 + (dj + 1) * 128], pt[:, mc, :],
                             start=(mc == 0), stop=(mc == 1))
                    p.tt("dve", oxT[:, 2 * h + dj, :], bk, rdn, ALU.mult)

        def back(sti):
            for tl in range(4):
                t = sti * 4 + tl
                rows = slice(t * 128, (t + 1) * 128)

                def cb(bk, c0, n, tl=tl):
                    p.tt("dve", x1[tl][:, c0:c0 + n], x1[tl][:, c0:c0 + n], bk, ALU.add)
                self.proj_tok(oxT[:, :, tl * 128:(tl + 1) * 128], w_xo, 128, DM, 8, cb)
                self.x2_tokens.append(p.dma("sp", D["scrX2"][rows, :], x1[tl], f"st_x1_{tl}"))

        nsup = NT // 4
        front(0)
        for sti in range(nsup):
            mid(sti)
            if sti + 1 < nsup:
                front(sti + 1)
            back(sti)

    def combine(self, P, ol, r, c, hmv, esink):
        p = self.p
        if ol[1] is not None:
            p.tt("pool", ol[0], ol[0], ol[1], ALU.add)
            p.tt("pool", ol[0], ol[0], ol[2], ALU.add)
        a3 = ol[0].re("p (h c) -> p h c", c=65)
        b3 = ol[3].re("p (h c) -> p h c", c=65)
        p.recip(r[:, 0:8], a3[:, :, 64])
        p.tt("dve", c[:, 0:512].re("p (h d) -> p h d", d=64), a3[:, :, 0:64], bc_last(r[:, 0:8], 64), ALU.mult)
        p.tt("dve", r[:, 8:16], b3[:, :, 64], esink[0:P, :], ALU.add)
        p.recip(r[:, 8:16], r[:, 8:16])
        p.tt("dve", c[:, 512:1024].re("p (h d) -> p h d", d=64), b3[:, :, 0:64], bc_last(r[:, 8:16], 64), ALU.mult)
        self.rmsnorm(c[:, 0:512], self.g_out[0:P, 0:512], hmv[:, 0:512], P, 512)
        self.rmsnorm(c[:, 512:1024], self.g_out[0:P, 512:1024], hmv[:, 512:1024], P, 512)

    def phase3(self):
        p, D = self.p, self.D
        b = Bump(self.arena, self.PERS_A, self.ARENA - 4096)
        w_up = b.alloc([128, 8, 2 * DFF], BF16, "w_up")
        w_dn = b.alloc([128, NFC, DM], BF16, "w_dn")
        w_up = self.load_w(w_up, "w_up", 8, "ld_w3a")
        w_dn = self.load_w(w_dn, "w_down", NFC, "ld_w3b")
        self.w3 = (w_up, w_dn)
        cw = b.alloc([128, 3, NFC], F32, "cw")
        cbias = b.alloc([128, NFC], F32, "cbias")
        for i3 in range(3):
            p.dma("sp", cw[:, i3, :], D["conv_w"][i3].re("(fc f) -> f fc", f=128), "ld_cw",
                  allow_slow_non_contiguous=True)
        p.dma("sp", cbias, D["conv_b"].re("(fc f) -> f fc", f=128), "ld_cb", allow_slow_non_contiguous=True)
        gcar = b.alloc([128, NFC, 2], F32, "gcar")
        p.memset("pool", gcar, 0.0)
        ST = 256
        self.p3_tmp_start = b.off
        xl = [b.alloc([128, DM], F32, f"xl{i}") for i in range(4)]
        h3 = [b.alloc([128, DM], BF16, f"h3_{i}") for i in range(2)]
        h3T = b.alloc([128, 8, ST], BF16, "h3T")
        aT = b.alloc([128, NFC, ST], BF16, "aT")
        gsb = [b.alloc([128, ST + 2], F32, f"gsb{i}") for i in range(2)]
        tq = [b.alloc([128, ST], F32, f"tq{i}") for i in range(2)]
        sq = [b.alloc([128, ST], F32, f"sq{i}") for i in range(2)]
        yt = [b.alloc([128, DM], F32, f"yt{i}") for i in range(1)]
        h3Tb = [h3T, b.alloc([128, 8, ST], BF16, "h3T1")]
        p.wait_for("sp", list(self.x2_tokens))

        def xload(s_):
            for tl in range(2):
                t = 2 * s_ + tl
                xi = (s_ % 2) * 2 + tl
                p.dma("sp", xl[xi], D["scrX2"][t * 128:(t + 1) * 128, :], f"ld_xl{xi}")

        def front_norm(s_):
            for tl in range(2):
                xi = (s_ % 2) * 2 + tl
                self.rmsnorm(xl[xi], self.g_ffn, h3[tl], 128, DM)

        def front_T(s_):
            for tl in range(2):
                self.transposes(h3Tb[s_ % 2][:, :, tl * 128:(tl + 1) * 128], h3[tl], 128, 8)

        def mid(s_):
            hT_ = h3Tb[s_ % 2]
            for fc in range(NFC):
                bg = self.nb()
                bv = self.nb()
                for kc in range(8):
                    p.mm(bg[:, 0:ST], w_up[:, kc, fc * 128:(fc + 1) * 128], hT_[:, kc, :], start=(kc == 0), stop=(kc == 7))
                for kc in range(8):
                    p.mm(bv[:, 0:ST], w_up[:, kc, DFF + fc * 128:DFF + (fc + 1) * 128], hT_[:, kc, :],
                         start=(kc == 0), stop=(kc == 7))
                g = gsb[fc % 2]
                tt_ = tq[fc % 2]
                ss_ = sq[fc % 2]
                p.copy("pool", g[:, 0:2], gcar[:, fc, :])
                p.copy("act", g[:, 2:ST + 2], bg[:, 0:ST])
                p.copy("pool", gcar[:, fc, :], g[:, ST:ST + 2])
                p.ts("dve", tt_, g[:, 0:ST], cw[:, 0, fc:fc + 1], cbias[:, fc:fc + 1], ALU.mult, ALU.add)
                p.stt("dve", tt_, g[:, 1:ST + 1], cw[:, 1, fc:fc + 1], tt_, ALU.mult, ALU.add)
                p.stt("dve", tt_, g[:, 2:ST + 2], cw[:, 2, fc:fc + 1], tt_, ALU.mult, ALU.add)
                p.act(ss_, tt_, AF.Silu)
                p.tt("dve", aT[:, fc, :], ss_, bv[:, 0:ST], ALU.mult)

        def back(s_):
            for tl in range(2):
                t = 2 * s_ + tl
                rows = slice(t * 128, (t + 1) * 128)
                x = xl[(s_ % 2) * 2 + tl]

                def cb(bk, c0, n, x=x):
                    p.tt("dve", x[:, c0:c0 + n], x[:, c0:c0 + n], bk, ALU.add)
                self.proj_tok(aT[:, :, tl * 128:(tl + 1) * 128], w_dn, 128, DM, NFC, cb)
                y = yt[0]
                self.rmsnorm(x, self.g_final, y, 128, DM)
                p.dma("sp", D["yp"][rows, :], y, "st_y0", is_output=True)

        nsup = SEQ // ST
        xload(0)
        front_norm(0)
        front_T(0)
        for s_ in range(nsup):
            if s_ + 1 < nsup:
                xload(s_ + 1)
            mid(s_)
            if s_ + 1 < nsup:
                front_norm(s_ + 1)
            back(s_)
            if s_ + 1 < nsup:
                front_T(s_ + 1)
        for ti in range(2):
            p.dma("sp", D["pconv"][ti].re("(fc f) -> f fc", f=128), gcar[:, :, ti], "st_pconv", is_output=True,
                  allow_slow_non_contiguous=True)

    def samp_attn(self):
        p, D, sb = self.p, self.D, self.samp
        b = Bump(self.arena, self.PERS + 3 * 16384, self.PH_END)
        sel = b.alloc([NS, NS, 128], F32, "sel")
        p.dma("sp", sel.re("p a b -> p (a b)"), D["c_sel"], "ld_sel")
        qb = [b.alloc([128, 512], F32, f"s_qb{i}") for i in range(2)]
        Kt = [b.alloc([128, 512], F32, f"s_Kt{i}") for i in range(3)]
        Vt = [b.alloc([128, 8, 65], F32, f"s_Vt{i}") for i in range(3)]
        prod = [b.alloc([128, 512], F32, f"s_prod{i}") for i in range(2)]
        sc = [b.alloc([128, 8], F32, f"s_sc{i}") for i in range(2)]
        Pz = [b.alloc([128, 8, NS], F32, f"s_Pz{i}") for i in range(4)]
        oA = b.alloc([NS, 8, 65], F32, "s_oA")
        oB = b.alloc([NS, 8, 65], F32, "s_oB")
        prn = b.alloc([NS, 512], F32, "s_prn")
        sn = b.alloc([NS, 8], F32, "s_sn")
        en = b.alloc([NS, 8], F32, "s_en")
        tv = b.alloc([NS, 8, 64], F32, "s_tv")
        r16 = b.alloc([NS, 16], F32, "s_r16")
        for v in Vt:
            p.memset("pool", v, 1.0)
        for v in Pz:
            p.memset("pool", v, 0.0)
        p.memset("pool", oA, 0.0)
        p.memset("pool", oB, 0.0)
        cnt = 0
        for n in range(NS):
            bk = self.nb()
            p.mm(bk, sel[:, n, :], sb["zqk"][:, 0:512])
            q = qb[n % 2]
            p.copy("act", q, bk)
            for g in range(3):
                if g == 0:
                    ksrc = D["cak"][n, LA - 128:LA, :]
                    vsrc = D["cav"][n, LA - 128:LA, :]
                elif g == 1:
                    ksrc = D["cak"][n].re("(j r) c -> r j c", r=4)[0, 384:512, :]
                    vsrc = D["cav"][n].re("(j r) c -> r j c", r=4)[0, 384:512, :]
                else:
                    ksrc = D["cak"][n].re("(j r) c -> r j c", r=16)[0, 0:128, :]
                    vsrc = D["cav"][n].re("(j r) c -> r j c", r=16)[0, 0:128, :]
                kt, vt = Kt[cnt % 3], Vt[cnt % 3]
                p.dma("sp", kt, ksrc, f"ld_sK{cnt % 3}")
                p.dma("sp", vt[:, :, 0:64], vsrc.re("j (h d) -> j h d", d=64), f"ld_sV{cnt % 3}")
                pr = prod[cnt % 2]
                p.tt("dve", pr, kt, q, ALU.mult)
                s8 = sc[cnt % 2]
                p.reduce("dve", s8, pr.re("p (h d) -> p h d", d=64), ALU.add)
                pz = Pz[cnt % 4]
                p.act(pz[:, :, n], s8, AF.Exp, scale=0.125)
                psA = [self.nb(), self.nb()]
                for h in range(8):
                    o = psA[h // 4][0:NS, (h % 4) * 65:(h % 4) * 65 + 65]
                    p.mm(o, pz[:, h, :], vt[:, h, :])
                for hb in range(2):
                    av = oA[:, 4 * hb:4 * hb + 4, :]
                    p.tt("dve", av, av, psA[hb][0:NS, 0:260].re("p (h c) -> p h c", c=65), ALU.add)
                p.memset("pool", pz[:, :, n], 0.0)
                cnt += 1
        zqk, zv = sb["zqk"], sb["zv"]
        p.tt("dve", prn, zqk[:, 0:512], zqk[:, 512:1024], ALU.mult)
        p.reduce("dve", sn, prn.re("p (h d) -> p h d", d=64), ALU.add)
        p.act(en, sn, AF.Exp, scale=0.125)
        p.ts("dve", en, en, 3.0, None, ALU.mult)
        p.tt("dve", tv, zv.re("p (h d) -> p h d", d=64), bc_last(en, 64), ALU.mult)
        p.tt("dve", oA[:, :, 0:64], oA[:, :, 0:64], tv, ALU.add)
        p.tt("dve", oA[:, :, 64], oA[:, :, 64], en, ALU.add)
        KtB = [V(k.ap[:, 0:128], k.res) for k in Kt]
        VtB = [V(v.ap[:, 0:2, :], v.res) for v in Vt]
        zqkB, zvB = sb["zqkB"], sb["zvB"]
        for n in range(NS):
            bk = self.nb()
            p.mm(bk, sel[:, n, :], zqkB[:, 0:512])
            q = qb[n % 2]
            p.copy("act", q, bk)
            kt, vt = KtB[cnt % 3], VtB[cnt % 3]
            p.dma("sp", kt, D["cbk"][n], f"ld_sK{cnt % 3}")
            p.dma("sp", vt[:, :, 0:64], D["cbv"][n].re("j (h d) -> j h d", d=64), f"ld_sV{cnt % 3}")
            pr = prod[cnt % 2]
            k4 = V(kt.ap.rearrange("p (k d) -> p k d", d=64).unsqueeze(2).to_broadcast([128, 2, 4, 64]), kt.res)
            p.tt("dve", pr.re("p (k g d) -> p k g d", k=2, g=4), q.re("p (k g d) -> p k g d", k=2, g=4), k4, ALU.mult)
            s8 = sc[cnt % 2]
            p.reduce("dve", s8, pr.re("p (h d) -> p h d", d=64), ALU.add)
            pz = Pz[cnt % 4]
            p.act(pz[:, :, n], s8, AF.Exp, scale=0.125)
            psA = [self.nb(), self.nb()]
            for h in range(8):
                o = psA[h // 4][0:NS, (h % 4) * 65:(h % 4) * 65 + 65]
                p.mm(o, pz[:, h, :], vt[:, h // 4, :])
            for hb in range(2):
                av = oB[:, 4 * hb:4 * hb + 4, :]
                p.tt("dve", av, av, psA[hb][0:NS, 0:260].re("p (h c) -> p h c", c=65), ALU.add)
            p.memset("pool", pz[:, :, n], 0.0)
            cnt += 1
        kn4 = V(zqkB.ap[:, 512:640].rearrange("p (k d) -> p k d", d=64).unsqueeze(2).to_broadcast([NS, 2, 4, 64]), zqkB.res)
        p.tt("dve", prn.re("p (k g d) -> p k g d", k=2, g=4), zqkB[:, 0:512].re("p (k g d) -> p k g d", k=2, g=4), kn4, ALU.mult)
        p.reduce("dve", sn, prn.re("p (h d) -> p h d", d=64), ALU.add)
        p.act(en, sn, AF.Exp, scale=0.125)
        vn4 = V(zvB.ap.rearrange("p (k d) -> p k d", d=64).unsqueeze(2).to_broadcast([NS, 2, 4, 64]), zvB.res)
        e4 = V(en.ap.rearrange("p (k g) -> p k g", k=2).unsqueeze(3).to_broadcast([NS, 2, 4, 64]), en.res)
        p.tt("dve", tv.re("p (k g) d -> p k g d", k=2), vn4, e4, ALU.mult)
        p.tt("dve", oB[:, :, 0:64], oB[:, :, 0:64], tv, ALU.add)
        p.tt("dve", oB[:, :, 64], oB[:, :, 64], en, ALU.add)
        self.combine(NS, [oA.re("p h c -> p (h c)"), None, None, oB.re("p h c -> p (h c)")], r16, sb["cat"], sb["hm"],
                     self.esink)

    def samp_mix(self):
        p, D, sb = self.p, self.D, self.samp
        w_out, w_xq, w_xo = self.w2
        b = Bump(self.arena, self.PERS + 3 * 16384, self.PH_END)
        sel = b.alloc([NS, NS, 128], F32, "sel2")
        p.dma("sp", sel.re("p a b -> p (a b)"), D["c_sel"], "ld_sel2")
        hT = b.alloc([128, 8, NS], BF16, "s_hT")
        h2s = b.alloc([NS, DM], BF16, "s_h2")
        qx = b.alloc([NS, DM], F32, "s_qx")
        qbx = [b.alloc([128, DM], F32, f"s_qbx{i}") for i in range(2)]
        Kx = [b.alloc([128, DM], F32, f"s_Kx{i}") for i in range(2)]
        Vx = [b.alloc([128, 4, 257], F32, f"s_Vx{i}") for i in range(2)]
        prodx = b.alloc([128, DM], F32, "s_prodx")
        s4 = [b.alloc([128, 4], F32, f"s_s4{i}") for i in range(2)]
        Pzx = [b.alloc([128, 4, NS], F32, f"s_Pzx{i}") for i in range(4)]
        oX = b.alloc([NS, 4, 257], F32, "s_oX")
        r4 = b.alloc([NS, 4], F32, "s_r4")
        oxn = b.alloc([NS, DM], BF16, "s_oxn")
        xs = sb["xs"]
        self.transposes(hT, sb["hm"], NS, 8)

        def cb(bk, c0, n):
            p.tt("dve", xs[:, c0:c0 + n], xs[:, c0:c0 + n], bk, ALU.add)
        self.proj_tok(hT, w_out, NS, DM, 8, cb)
        self.rmsnorm(xs, self.g_cross[0:NS, :], h2s, NS, DM)
        self.transposes(hT, h2s, NS, 8)

        def cb2(bk, c0, n):
            p.copy("act", qx[:, c0:c0 + n], bk)
        self.proj_tok(hT, w_xq, NS, DM, 8, cb2)
        for v in Vx:
            p.memset("pool", v, 1.0)
        for v in Pzx:
            p.memset("pool", v, 0.0)
        p.memset("pool", oX, 0.0)
        cnt = 0
        for n in range(NS):
            q = qbx[n % 2]
            for half in range(2):
                bk = self.nb()
                p.mm(bk, sel[:, n, :], qx[:, half * 512:(half + 1) * 512])
                p.copy("act", q[:, half * 512:(half + 1) * 512], bk)
            for mc in range(2):
                kx, vx = Kx[cnt % 2], Vx[cnt % 2]
                rows = slice(mc * 128, (mc + 1) * 128)
                p.dma("sp", kx, D["cmk"][n, rows, :], f"ld_sKx{cnt % 2}")
                p.dma("sp", vx[:, :, 0:256], D["cmv"][n, rows, :].re("j (h d) -> j h d", d=256), f"ld_sVx{cnt % 2}")
                p.tt("dve", prodx, kx, q, ALU.mult)
                s_ = s4[cnt % 2]
                p.reduce("dve", s_, prodx.re("p (h d) -> p h d", d=256), ALU.add)
                pz = Pzx[cnt % 4]
                p.act(pz[:, :, n], s_, AF.Exp, scale=1.0 / 16.0)
                for h in range(4):
                    bo = self.nb()
                    p.mm(bo[0:NS, 0:257], pz[:, h, :], vx[:, h, :])
                    p.tt("dve", oX[:, h, :], oX[:, h, :], bo[0:NS, 0:257], ALU.add)
                p.memset("pool", pz[:, :, n], 0.0)
                cnt += 1
        p.recip(r4, oX[:, :, 256])
        p.tt("dve", oxn.re("p (h d) -> p h d", d=256), oX[:, :, 0:256], bc_last(r4, 256), ALU.mult)
        self.transposes(hT, oxn, NS, 8)
        self.proj_tok(hT, w_xo, NS, DM, 8, cb)

    def samp_ffn(self):
        p, D, sb = self.p, self.D, self.samp
        w_up, w_dn = self.w3
        b = Bump(self.arena, self.p3_tmp_start, self.ARENA - 4096)
        h3s = b.alloc([NS, DM], BF16, "s_h3")
        hT = b.alloc([128, 8, NS], BF16, "s_h3T")
        gs = b.alloc([NS, DFF], F32, "s_gs")
        vs = b.alloc([NS, DFF], F32, "s_vs")
        abf = b.alloc([NS, DFF], BF16, "s_abf")
        aT = b.alloc([128, NFC, NS], BF16, "s_aT")
        ysb = b.alloc([NS, DM], F32, "s_ysb")
        xs = sb["xs"]
        self.rmsnorm(xs, self.g_ffn[0:NS, :], h3s, NS, DM)
        self.transposes(hT, h3s, NS, 8)

        def cbu(bk, c0, n):
            lo, hi = c0, c0 + n
            if lo < DFF:
                m = min(hi, DFF) - lo
                p.copy("act", gs[:, lo:lo + m], bk[:, 0:m])
            if hi > DFF:
                s0 = max(lo, DFF)
                p.copy("act", vs[:, s0 - DFF:hi - DFF], bk[:, s0 - lo:n])
        self.proj_tok(hT, w_up, NS, 2 * DFF, 8, cbu)
        p.dma("sp", D["sconvo"][:, 1, :], gs, "st_sgs", is_output=True)
        b2 = Bump(self.arena, self.PERS_A, self.PERS_A + 90112)
        s0t = b2.alloc([NS, DFF], F32, "s_s0")
        s1t = b2.alloc([NS, DFF], F32, "s_s1")
        cwb = b2.alloc([NS, 3, DFF], F32, "s_cwb")
        cbb = b2.alloc([NS, DFF], F32, "s_cbb")
        t1 = b2.alloc([NS, DFF], F32, "s_t1")
        t2 = b2.alloc([NS, DFF], F32, "s_t2")
        p.dma("sp", s0t, D["sconv"][:, 0, :], "ld_ss0")
        p.dma("sp", s1t, D["sconv"][:, 1, :], "ld_ss1")
        tcw = D["conv_w"].ap.tensor
        p.dma("sp", cwb.re("p a b -> p (a b)"), V(bass.AP(tcw, 0, [[0, NS], [1, 3 * DFF]]), ("dram", "conv_w")), "ld_scw")
        p.dma("sp", cbb, self.bcast_dram("conv_b", DFF, NS), "ld_scb")
        p.tt("dve", t1, s0t, cwb[:, 0, :], ALU.mult)
        p.tt("pool", t2, s1t, cwb[:, 1, :], ALU.mult)
        p.tt("dve", t1, t1, t2, ALU.add)
        p.tt("pool", t2, gs, cwb[:, 2, :], ALU.mult)
        p.tt("dve", t1, t1, t2, ALU.add)
        p.tt("dve", t1, t1, cbb, ALU.add)
        p.act(t2, t1, AF.Silu)
        p.tt("dve", abf, t2, vs, ALU.mult)
        self.transposes(aT, abf, NS, NFC)

        def cb(bk, c0, n):
            p.tt("dve", xs[:, c0:c0 + n], xs[:, c0:c0 + n], bk, ALU.add)
        self.proj_tok(aT, w_dn, NS, DM, NFC, cb)
        self.rmsnorm(xs, self.g_final[0:NS, :], ysb, NS, DM)
        p.dma("sp", D["ys"], ysb, "st_ys", is_output=True)

    def alloc_sample(self):
        top = Bump(self.arena, self.ARENA - 4096)
        b = Bump(self.arena, self.ARENA - 20480, self.ARENA - 4096)
        self.samp = {
            "xs": top.alloc([NS, DM], F32, "s_xs"),
            "zqk": b.alloc([NS, 1024], F32, "s_zqk"),
            "zv": b.alloc([NS, 512], F32, "s_zv"),
            "zqkB": b.alloc([NS, 640], F32, "s_zqkB"),
            "zvB": b.alloc([NS, 128], F32, "s_zvB"),
            "cat": b.alloc([NS, DM], F32, "s_cat"),
            "hm": b.alloc([NS, DM], BF16, "s_hm"),
        }
        self.samp_bump = b
        self.PH_END = self.ARENA - 20480

    def build(self):
        self.scr_tokens = []
        self.alloc_sample()
        self.bulk_copies()
        self.phase0()
        self.phase1()
        if self.stage >= 2:
            self.prefetch_w2()
            self.samp_attn()
            self.phase2()
        if self.stage >= 3:
            self.phase2b()
            self.samp_mix()
        if self.stage >= 4:
            self.phase3()
            self.samp_ffn()
        return self.p.build()


def make_consts():
    half = 32
    inv = np.power(np.float32(10000.0), -np.arange(half, dtype=np.float32) / np.float32(half)).astype(np.float32)

    def tab(pos):
        ang = pos.astype(np.float32)[:, None] * inv[None, :]
        c = np.cos(ang).astype(np.float32)
        s = np.sin(ang).astype(np.float32)
        return np.concatenate([c, c, -s, s], axis=1).astype(np.float32)
    c_rope = tab(np.arange(SEQ))
    c_ropes = tab(np.full((NS,), PAST))
    k = np.arange(128)[:, None]
    q = np.arange(128)[None, :]
    own = (k <= q).astype(np.float32)
    prev = (k >= q).astype(np.float32)
    c_mask = np.concatenate([own, prev], axis=1).astype(np.float32)
    c_ident = np.eye(128, dtype=np.float32)
    c_sel = np.zeros((NS, NS, 128), np.float32)
    for n in range(NS):
        c_sel[n, n, :] = 1.0
    return {"c_rope": c_rope, "c_ropes": c_ropes, "c_mask": c_mask, "c_ident": c_ident,
            "c_sel": c_sel.reshape(NS, NS * 128)}


_STAGE = 4


def kernel(x_prompt, x_sample, cache_a_k, cache_a_v, cache_b_k, cache_b_v, cache_mem_k, cache_mem_v, state_conv,
           mem_prompt, g_mix, w_in, g_out_a, g_out_b, sinks, w_out, g_cross, g_mem, w_xq, w_mem_kv, w_xo,
           g_ffn, w_up, conv_w, conv_b, w_down, g_final):
    f = lambda a: np.ascontiguousarray(np.asarray(a, dtype=np.float32))
    kb = KB(stage=_STAGE)
    nc = kb.build()
    consts = make_consts()
    shared = {
        "g_mix": f(g_mix[0]), "w_in": f(w_in[0]), "g_out_a": f(g_out_a[0]), "g_out_b": f(g_out_b[0]),
        "sinks": f(sinks[0]), "w_out": f(w_out[0]), "g_cross": f(g_cross[0]), "g_mem": f(g_mem[0]),
        "w_xq": f(w_xq[0]), "w_mem_kv": f(w_mem_kv[0]), "w_xo": f(w_xo[0]), "g_ffn": f(g_ffn[0]),
        "w_up": f(w_up[0]), "conv_w": f(conv_w[0]), "conv_b": f(conv_b[0]), "w_down": f(w_down[0]),
        "g_final": f(g_final),
    }
    shared.update(consts)
    in_maps = []
    for c in range(NCORES):
        s = slice(c * NS, (c + 1) * NS)
        m = dict(shared)
        m["xp"] = f(x_prompt[c])
        m["xs"] = f(x_sample[s, 0])
        m["cak"] = f(cache_a_k[0, s]).reshape(NS, LA, 512)
        m["cav"] = f(cache_a_v[0, s]).reshape(NS, LA, 512)
        m["cbk"] = f(cache_b_k[0, s]).reshape(NS, LB, 128)
        m["cbv"] = f(cache_b_v[0, s]).reshape(NS, LB, 128)
        m["cmk"] = f(cache_mem_k[0, s]).reshape(NS, MEM, DM)
        m["cmv"] = f(cache_mem_v[0, s]).reshape(NS, MEM, DM)
        m["sconv"] = f(state_conv[0, s])
        m["memp"] = f(mem_prompt[c])
        in_maps.append(m)
    res = run_bass_kernel_spmd(nc, in_maps, core_ids=list(range(NCORES)))
    R = res.results
    cat = lambda k: np.stack([np.asarray(R[c][k], dtype=np.float32) for c in range(NCORES)])
    catn = lambda k: np.concatenate([np.asarray(R[c][k], dtype=np.float32) for c in range(NCORES)], axis=0)
    y_prompt = cat("yp")
    y_sample = catn("ys").reshape(NCORES * NS, 1, DM)
    p_a_k = cat("pak").reshape(1, NCORES, LA, 8, 64)
    p_a_v = cat("pav").reshape(1, NCORES, LA, 8, 64)
    p_b_k = cat("pbk").reshape(1, NCORES, LB, 2, 64)
    p_b_v = cat("pbv").reshape(1, NCORES, LB, 2, 64)
    p_mem_k = cat("pmk").reshape(1, NCORES, MEM, 4, 256)
    p_mem_v = cat("pmv").reshape(1, NCORES, MEM, 4, 256)
    p_conv = cat("pconv").reshape(1, NCORES, 2, DFF)
    s_a_k = catn("sak").reshape(1, NCORES * NS, LA, 8, 64)
    s_a_v = catn("sav").reshape(1, NCORES * NS, LA, 8, 64)
    s_b_k = catn("sbk").reshape(1, NCORES * NS, LB, 2, 64)
    s_b_v = catn("sbv").reshape(1, NCORES * NS, LB, 2, 64)
    s_conv = catn("sconvo").reshape(1, NCORES * NS, 2, DFF)
    return (y_prompt, y_sample, p_a_k, p_a_v, p_b_k, p_b_v, p_mem_k, p_mem_v, p_conv,
            s_a_k, s_a_v, s_b_k, s_b_v, s_conv)
```

```python
import numpy as np
from contextlib import ExitStack
import ml_dtypes
import concourse.bass as bass
import concourse.mybir as mybir
from concourse.bass_utils import run_bass_kernel_spmd

F32 = mybir.dt.float32
BF16 = mybir.dt.bfloat16
ALU = mybir.AluOpType
AF = mybir.ActivationFunctionType
AX = mybir.AxisListType

NCORES = 8
SEQ = 4096
DM = 1024
NT = SEQ // 128
NS = 16
LA = 2048
LB = 128
MEM = 256
DFF = 2816
NFC = DFF // 128
IN_DIM = 2304
EPS = 1e-6
PAST = 16384


class V:
    __slots__ = ("ap", "res")

    def __init__(self, ap, res):
        self.ap = ap
        self.res = res

    def __getitem__(self, k):
        return V(self.ap[k], self.res)

    def re(self, s, **kw):
        return V(self.ap.rearrange(s, **kw), self.res)

    def bc(self, shape):
        return V(self.ap.to_broadcast(shape), self.res)

    def cast(self, dt):
        return V(self.ap.bitcast(dt), self.res)


def _keys(v):
    if v is None:
        return []
    r = v.res
    if isinstance(r, list):
        return r
    return [r]


class Prog:
    ENGS = ("pe", "act", "dve", "pool", "sp")

    def __init__(self):
        self.nc = bass.Bass("TRN2", target_bir_lowering=False)
        self.es = ExitStack()
        self.ins = []
        self.res = {}
        self.out_tokens = []

    def dram(self, name, shape, dt, kind):
        t = self.nc.dram_tensor(name, list(shape), dt, kind=kind)
        return V(t.ap(), ("dram", name))

    def sbuf(self, name, shape, dt):
        t = self.es.enter_context(self.nc.sbuf_tensor(name, list(shape), dt))
        return V(t[:], ("sb", name))

    def psum(self, name, shape, dt):
        t = self.es.enter_context(self.nc.psum_tensor(name, list(shape), dt))
        return V(t[:], ("ps", name))

    def _rec(self, eng, emit, reads, writes, dma_sem=None, track_dram=False):
        iid = len(self.ins)
        is_dma = dma_sem is not None
        deps = {}

        def add_dep(tok, kind):
            if tok is None:
                return
            pr = self.ins[tok]
            if (not is_dma) and (pr["dma_sem"] is None) and pr["eng"] == eng:
                if eng == "pe":
                    return
            deps[tok] = True

        rk = [k for v in reads for k in _keys(v)]
        wk = [k for v in writes for k in _keys(v)]
        if not track_dram:
            rk = [k for k in rk if k[0] != "dram"]
            wk = [k for k in wk if k[0] != "dram"]
        for r in rk:
            st = self.res.setdefault(r, {"w": None, "r": {}})
            add_dep(st["w"], 0)
        for w in wk:
            st = self.res.setdefault(w, {"w": None, "r": {}})
            add_dep(st["w"], 1)
            for t in st["r"].values():
                add_dep(t, 2)
        semkey = ("dma", dma_sem) if is_dma else ("eng", eng)
        for r in rk:
            self.res[r]["r"][semkey] = iid
        for w in wk:
            self.res[w]["w"] = iid
            self.res[w]["r"] = {}
        self.ins.append({"eng": eng, "emit": emit, "deps": list(deps), "dma_sem": dma_sem,
                         "flag": is_dma, "semkey": semkey, "val": None})
        for d in deps:
            self.ins[d]["flag"] = True
        return iid

    def mm(self, out, lhsT, rhs, start=True, stop=True):
        return self._rec("pe", lambda e: e.matmul(out.ap, lhsT.ap, rhs.ap, start=start, stop=stop),
                         [lhsT, rhs], [out])

    def tr(self, out, in_, ident):
        return self._rec("pe", lambda e: e.transpose(out.ap, in_.ap, ident.ap), [in_, ident], [out])

    def act(self, out, in_, func, bias=None, scale=1.0, accum=None):
        kw = {}
        rd = [in_]
        if bias is not None:
            if isinstance(bias, V):
                kw["bias"] = bias.ap
                rd.append(bias)
            else:
                kw["bias"] = bias
        if isinstance(scale, V):
            kw["scale"] = scale.ap
            rd.append(scale)
        else:
            kw["scale"] = scale
        wr = [out]
        if accum is not None:
            kw["accum_out"] = accum.ap
            wr.append(accum)
        return self._rec("act", lambda e: e.activation(out.ap, in_.ap, func, **kw), rd, wr)

    def tt(self, eng, out, a, b, op):
        return self._rec(eng, lambda e: e.tensor_tensor(out.ap, a.ap, b.ap, op), [a, b], [out])

    def ts(self, eng, out, a, s1, s2, op0, op1=None, accum=None):
        rd = [a]
        s1a = s1.ap if isinstance(s1, V) else s1
        s2a = s2.ap if isinstance(s2, V) else s2
        if isinstance(s1, V):
            rd.append(s1)
        if isinstance(s2, V):
            rd.append(s2)
        wr = [out]
        kw = {}
        if op1 is not None:
            kw["op1"] = op1
        if accum is not None:
            kw["accum_out"] = accum.ap
            wr.append(accum)
        return self._rec(eng, lambda e: e.tensor_scalar(out.ap, a.ap, s1a, s2a, op0, **kw), rd, wr)

    def stt(self, eng, out, a, s, b, op0, op1):
        rd = [a, b]
        sa = s.ap if isinstance(s, V) else s
        if isinstance(s, V):
            rd.append(s)
        return self._rec(eng, lambda e: e.scalar_tensor_tensor(out.ap, a.ap, sa, b.ap, op0, op1), rd, [out])

    def copy(self, eng, out, in_):
        if eng == "act":
            return self._rec("act", lambda e: e.copy(out.ap, in_.ap), [in_], [out])
        return self._rec(eng, lambda e: e.tensor_copy(out.ap, in_.ap), [in_], [out])

    def memset(self, eng, out, val):
        return self._rec(eng, lambda e: e.memset(out.ap, val), [], [out])

    def reduce(self, eng, out, in_, op, axis=AX.X):
        return self._rec(eng, lambda e: e.tensor_reduce(out.ap, in_.ap, axis, op), [in_], [out])

    def recip(self, out, in_):
        return self._rec("dve", lambda e: e.reciprocal(out.ap, in_.ap), [in_], [out])

    def dma(self, q, out, in_, sem, is_output=False, **kw):
        iid = self._rec(q, lambda e: e.dma_start(out=out.ap, in_=in_.ap, **kw), [in_], [out], dma_sem=sem)
        if is_output:
            self.out_tokens.append(iid)
        return iid

    def wait_for(self, eng, tokens):
        iid = self._rec(eng, None, [], [])
        self.ins[iid]["deps"] = list(tokens)
        for d in tokens:
            self.ins[d]["flag"] = True
        return iid

    def build(self):
        nc = self.nc
        self.wait_for("sp", list(self.out_tokens))
        counts = {}
        for r in self.ins:
            if r["flag"]:
                k = r["semkey"]
                inc = 16 if r["dma_sem"] is not None else 1
                counts[k] = counts.get(k, 0) + inc
                r["val"] = counts[k]
        semh = {}
        for k in counts:
            nm = "s_" + "_".join(str(x) for x in k)
            semh[k] = self.es.enter_context(nc.semaphore(nm))
        per_eng = {e: [] for e in self.ENGS}
        for r in self.ins:
            per_eng[r["eng"]].append(r)
        ins = self.ins
        stats = {e: [0, 0] for e in self.ENGS}

        def run(engname, eobj):
            waited = {}
            for r in per_eng[engname]:
                need = {}
                for d in r["deps"]:
                    pr = ins[d]
                    k = pr["semkey"]
                    v = pr["val"]
                    if waited.get(k, 0) >= v:
                        continue
                    if need.get(k, 0) < v:
                        need[k] = v
                for k, v in need.items():
                    eobj.wait_ge(semh[k], v)
                    waited[k] = v
                    stats[engname][1] += 1
                if r["emit"] is not None:
                    bi = r["emit"](eobj)
                    stats[engname][0] += 1
                    if r["flag"]:
                        bi.then_inc(semh[r["semkey"]], 16 if r["dma_sem"] is not None else 1)

        with nc.Block() as block:
            @block.tensor
            def _(e):
                run("pe", e)

            @block.scalar
            def _(e):
                run("act", e)

            @block.vector
            def _(e):
                run("dve", e)

            @block.gpsimd
            def _(e):
                run("pool", e)

            @block.sync
            def _(e):
                run("sp", e)
        self.stats = stats
        self.sem_counts = counts
        self.es.close()
        return nc


class Arena:
    def __init__(self, p, nbytes):
        self.p = p
        self.nbytes = nbytes
        self.base = p.sbuf("arena", [128, nbytes // 2], BF16)
        self.regs = []
        self.n = 0

    def carve(self, off, shape, dt, name):
        esz = 4 if dt == F32 else 2
        n = int(np.prod(shape[1:]))
        nb = n * esz
        assert off % 4 == 0 and off + nb <= self.nbytes, (name, off, nb, self.nbytes)
        a = self.base.ap[0:shape[0], off // 2:(off + nb) // 2]
        if dt == F32:
            a = a.bitcast(F32)
        if len(shape) == 3:
            a = a.rearrange("p (a b) -> p a b", a=shape[1])
        elif len(shape) == 4:
            a = a.rearrange("p (a b c) -> p a b c", a=shape[1], b=shape[2])
        key = ("ar", name, self.n)
        self.n += 1
        inherit = {}
        for (s, e, k) in self.regs:
            if s < off + nb and off < e:
                st = self.p.res.get(k)
                if st:
                    toks = list(st["r"].values())
                    if st["w"] is not None:
                        toks.append(st["w"])
                    for t in toks:
                        sk = self.p.ins[t]["semkey"]
                        if sk not in inherit or inherit[sk] < t:
                            inherit[sk] = t
        self.regs.append((off, off + nb, key))
        self.p.res[key] = {"w": None, "r": inherit}
        return V(a, key)


def subview(arena, v, tag, ap=None):
    key = (v.res, tag)
    p = arena.p
    par = p.res.get(v.res, {"w": None, "r": {}})
    r = dict(par["r"])
    if par["w"] is not None:
        sk = p.ins[par["w"]]["semkey"]
        if sk not in r or r[sk] < par["w"]:
            r[sk] = par["w"]
    p.res[key] = {"w": None, "r": r}
    for (s_, e_, k_) in list(arena.regs):
        if k_ == v.res:
            arena.regs.append((s_, e_, key))
            break
    return V(v.ap if ap is None else ap, key)


class Bump:
    def __init__(self, arena, start, end=None):
        self.a = arena
        self.off = start
        self.end = end if end is not None else arena.nbytes

    def alloc(self, shape, dt, name):
        esz = 4 if dt == F32 else 2
        nb = int(np.prod(shape[1:])) * esz
        nb4 = (nb + 3) // 4 * 4
        assert self.off + nb4 <= self.end, ("bump overflow", name, self.off, nb4, self.end)
        v = self.a.carve(self.off, shape, dt, name)
        self.off += nb4
        return v


def bc_mid(v, h):
    shp = list(v.ap.shape)
    return V(v.ap.unsqueeze(1).to_broadcast([shp[0], h, shp[1]]), v.res)


def bc_last(v, n):
    shp = list(v.ap.shape)
    return V(v.ap.unsqueeze(2).to_broadcast([shp[0], shp[1], n]), v.res)


class KB:
    ARENA = 204800
    PERS = 34816

    def __init__(self, stage=99):
        self.stage = stage
        p = self.p = Prog()
        self.D = D = {}

        def din(name, shape, dt=F32):
            D[name] = p.dram(name, shape, dt, "ExternalInput")

        def dout(name, shape):
            D[name] = p.dram(name, shape, F32, "ExternalOutput")

        def dscr(name, shape, dt):
            D[name] = p.dram(name, shape, dt, "Internal")

        din("xp", [SEQ, DM]); din("xs", [NS, DM])
        din("cak", [NS, LA, 512]); din("cav", [NS, LA, 512])
        din("cbk", [NS, LB, 128]); din("cbv", [NS, LB, 128])
        din("cmk", [NS, MEM, DM]); din("cmv", [NS, MEM, DM])
        din("sconv", [NS, 2, DFF]); din("memp", [MEM, DM])
        din("g_mix", [DM]); din("w_in", [DM, IN_DIM]); din("g_out_a", [512]); din("g_out_b", [512])
        din("sinks", [8]); din("w_out", [DM, DM]); din("g_cross", [DM]); din("g_mem", [DM])
        din("w_xq", [DM, DM]); din("w_mem_kv", [DM, 2 * DM]); din("w_xo", [DM, DM]); din("g_ffn", [DM])
        din("w_up", [DM, 2 * DFF]); din("conv_w", [3, DFF]); din("conv_b", [DFF]); din("w_down", [DFF, DM])
        din("g_final", [DM])
        din("c_rope", [SEQ, 128]); din("c_ropes", [NS, 128]); din("c_mask", [128, 256])
        din("c_ident", [128, 128]); din("c_sel", [NS, NS * 128])
        dout("yp", [SEQ, DM]); dout("ys", [NS, DM])
        dout("pak", [LA, 512]); dout("pav", [LA, 512]); dout("pbk", [LB, 128]); dout("pbv", [LB, 128])
        dout("pmk", [MEM, DM]); dout("pmv", [MEM, DM]); dout("pconv", [2, DFF])
        dout("sak", [NS, LA, 512]); dout("sav", [NS, LA, 512]); dout("sbk", [NS, LB, 128]); dout("sbv", [NS, LB, 128])
        dout("sconvo", [NS, 2, DFF])
        dscr("scrA", [SEQ, 1536], BF16); dscr("scrB", [SEQ, 768], BF16)
        dscr("scrO", [4, SEQ, 520], F32); dscr("scrX2", [SEQ, DM], F32)

        self.arena = Arena(p, self.ARENA)
        self.ps = p.psum("psall", [128, 4096], F32)
        self.bank_ctr = 0
        self.nb_mod = 6
        self.tb_ctr = 0
        self.pers = Bump(self.arena, 0, self.PERS)
        self.setup_consts()

    def bank(self, i):
        return V(self.ps.ap[:, 512 * i:512 * (i + 1)], ("ps", i))

    def nb(self):
        i = self.bank_ctr % self.nb_mod
        self.bank_ctr += 1
        return self.bank(i)

    def tbank(self):
        j = 6 + self.tb_ctr % 2
        self.tb_ctr += 1
        a = self.ps.ap[:, 512 * j:512 * (j + 1)].bitcast(BF16)
        return V(a.rearrange("p (a b) -> p a b", a=8), ("ps", j))

    def bcast_dram(self, name, n, parts=128):
        t = self.D[name].ap.tensor
        return V(bass.AP(t, 0, [[0, parts], [1, n]]), ("dram", name))

    def setup_consts(self):
        p, D, b = self.p, self.D, self.pers
        self.identf = b.alloc([128, 128], F32, "identf")
        self.ident = b.alloc([128, 128], BF16, "ident")
        self.ones_bf = b.alloc([128, 128], BF16, "ones_bf")
        self.epsb = b.alloc([128, 1], F32, "epsb")
        self.ss = [b.alloc([128, 1], F32, f"ss{i}") for i in range(4)]
        self.ss_ctr = 0
        self.junk = b.alloc([128, DM], BF16, "junk")
        self.g_ffn = b.alloc([128, DM], F32, "g_ffn")
        self.g_final = b.alloc([128, DM], F32, "g_final")
        self.PERS_A = b.off
        self.mask2 = b.alloc([128, 2, 128], BF16, "mask2")
        self.g_mix = b.alloc([128, DM], F32, "g_mix")
        self.g_cross = b.alloc([128, DM], F32, "g_cross")
        self.g_out = b.alloc([128, DM], F32, "g_out")
        self.esink = b.alloc([128, 8], F32, "esink")
        self.ropes = b.alloc([NS, 128], F32, "ropes")
        self.mkT = b.alloc([128, 8, MEM], BF16, "mkT")
        self.mv = b.alloc([128, 2, DM], BF16, "mv")
        p.memset("pool", self.epsb, EPS)
        p.dma("sp", self.identf, D["c_ident"], "ld_c_identf")
        p.dma("pool", self.ident, D["c_ident"], "ld_c_ident")
        p.dma("pool", self.mask2.re("p a b -> p (a b)"), D["c_mask"], "ld_c_mask")
        p.memset("pool", self.ones_bf, 1.0)
        p.dma("sp", self.g_mix, self.bcast_dram("g_mix", DM), "ld_c_gmix")
        p.dma("sp", self.g_cross, self.bcast_dram("g_cross", DM), "ld_c_gcross")
        p.dma("sp", self.g_ffn, self.bcast_dram("g_ffn", DM), "ld_c_gffn")
        p.dma("sp", self.g_final, self.bcast_dram("g_final", DM), "ld_c_gfinal")
        p.dma("sp", self.g_out[:, 0:512], self.bcast_dram("g_out_a", 512), "ld_c_goa")
        p.dma("sp", self.g_out[:, 512:1024], self.bcast_dram("g_out_b", 512), "ld_c_gob")
        p.dma("sp", self.esink, self.bcast_dram("sinks", 8), "ld_c_sinks")
        p.dma("sp", self.ropes, D["c_ropes"], "ld_c_ropes")
        p.act(self.esink, self.esink, AF.Exp)

    def load_w(self, dst, name, kchunks, sem):
        src = self.D[name]
        keys = []
        for kc in range(kchunks):
            sub = subview(self.arena, dst, ("kc", kc), dst.ap[:, kc, :])
            self.p.dma("pool", sub, src[kc * 128:(kc + 1) * 128, :], sem)
            keys.append(sub.res)
        return V(dst.ap, keys)

    def next_ss(self):
        s = self.ss[self.ss_ctr % 4]
        self.ss_ctr += 1
        return s

    def rmsnorm(self, x, g, out, P, Dn):
        p = self.p
        ss = self.next_ss()[0:P, :]
        p.memset("dve", ss, 0.0)
        p.act(self.junk[0:P, 0:Dn], x, AF.Square, accum=ss)
        p.act(ss, ss, AF.Ln, scale=1.0 / Dn, bias=self.epsb[0:P, :])
        p.act(ss, ss, AF.Exp, scale=-0.5)
        p.stt("dve", out, x, ss, g, ALU.mult, ALU.mult)

    def transposes(self, dst, src, P, n, evac=("act", "dve"), ident=None):
        p = self.p
        ident = self.ident if ident is None else ident
        gi = 0
        for g in range(0, n, 8):
            m = min(8, n - g)
            pst = self.tbank()
            for j in range(m):
                p.tr(pst[:, j, 0:P], src[0:P, (g + j) * 128:(g + j + 1) * 128], ident[0:P, 0:P])
            p.copy(evac[gi % len(evac)], dst[:, g:g + m, 0:P], pst[:, 0:m, 0:P])
            gi += 1

    def proj_tok(self, hT, w, P, n_out, kchunks, cb):
        p = self.p
        c0 = 0
        while c0 < n_out:
            n = min(512, n_out - c0)
            bk = self.nb()
            for kc in range(kchunks):
                p.mm(bk[0:P, 0:n], hT[:, kc, 0:P], w[:, kc, c0:c0 + n], start=(kc == 0), stop=(kc == kchunks - 1))
            cb(bk[0:P, 0:n], c0, n)
            c0 += n

    def bulk_copies(self):
        p, D = self.p, self.D
        for n in range(NS):
            p.dma("act", D["sak"][n, 0:LA - 1, :], D["cak"][n, 1:LA, :], "bulk", is_output=True)
            p.dma("act", D["sav"][n, 0:LA - 1, :], D["cav"][n, 1:LA, :], "bulk", is_output=True)
        p.dma("act", D["sbk"][:, 0:LB - 1, :], D["cbk"][:, 1:LB, :], "bulk", is_output=True)
        p.dma("act", D["sbv"][:, 0:LB - 1, :], D["cbv"][:, 1:LB, :], "bulk", is_output=True)
        p.dma("act", D["sconvo"][:, 0, :], D["sconv"][:, 1, :], "bulk", is_output=True)

    def phase0(self):
        p, D = self.p, self.D
        b = Bump(self.arena, self.PERS)
        wkv = b.alloc([128, 8, 2 * DM], BF16, "wkv")
        gm = b.alloc([128, DM], F32, "g_mem")
        wkv = self.load_w(wkv, "w_mem_kv", 8, "ld_w0")
        p.dma("sp", gm, self.bcast_dram("g_mem", DM), "ld_c0_12")
        xm = [b.alloc([128, DM], F32, f"xm{i}") for i in range(2)]
        hb = [b.alloc([128, DM], BF16, f"hbm{i}") for i in range(2)]
        hT = [b.alloc([128, 8, 128], BF16, f"hTm{i}") for i in range(2)]
        kvf = [b.alloc([128, 2 * DM], F32, f"kvf{i}") for i in range(2)]
        kb = [b.alloc([128, DM], BF16, f"kbm{i}") for i in range(2)]
        for mt in range(2):
            rows = slice(mt * 128, (mt + 1) * 128)
            p.dma("sp", xm[mt], D["memp"][rows, :], f"ld_xm{mt}")
            self.rmsnorm(xm[mt], gm, hb[mt], 128, DM)
            self.transposes(hT[mt], hb[mt], 128, 8)

            def cb(bk, c0, n, mt=mt):
                p.copy("act", kvf[mt][:, c0:c0 + n], bk)
            self.proj_tok(hT[mt], wkv, 128, 2 * DM, 8, cb)
            p.dma("sp", D["pmk"][rows, :], kvf[mt][:, 0:DM], f"st_kvfk{mt}", is_output=True)
            p.dma("sp", D["pmv"][rows, :], kvf[mt][:, DM:2 * DM], f"st_kvfv{mt}", is_output=True)
            p.copy("pool", kb[mt], kvf[mt][:, 0:DM])
            self.transposes(self.mkT[:, :, rows], kb[mt], 128, 8)
            p.copy("pool", self.mv[:, mt, :], kvf[mt][:, DM:2 * DM])

    def inproj_front(self, P, x, hb, hT):
        self.rmsnorm(x, self.g_mix[0:P, :], hb, P, DM)
        self.transposes(hT, hb, P, 8)

    def inproj_tile(self, P, x, hb, hT, w_in, cos2, sinS, zqk, zv, zqkB, zvB, tmp, zbA, zbB, want_vf32, want_vBf32,
                    skip_front=False):
        p = self.p
        if not skip_front:
            self.inproj_front(P, x, hb, hT)
        cosb = lambda h: bc_mid(cos2, h)
        sin_lo = lambda h: bc_mid(sinS[:, 0:32], h)
        sin_hi = lambda h: bc_mid(sinS[:, 32:64], h)

        def rope(bk, dst, ncols):
            h = ncols // 64
            src = bk[:, 0:ncols].re("p (h d) -> p h d", d=64)
            d3 = dst.re("p (h d) -> p h d", d=64)
            t3 = tmp[:, 0:ncols].re("p (h d) -> p h d", d=64)
            p.tt("dve", d3, src, cosb(h), ALU.mult)
            p.tt("dve", t3[:, :, 0:32], src[:, :, 32:64], sin_lo(h), ALU.mult)
            p.tt("dve", t3[:, :, 32:64], src[:, :, 0:32], sin_hi(h), ALU.mult)
            p.tt("dve", d3, d3, t3, ALU.add)

        def cb(bk, c0, n):
            if c0 == 0:
                rope(bk, zqk[:, 0:512], 512)
            elif c0 == 512:
                rope(bk, zqk[:, 512:1024], 512)
                if zbA is not None:
                    p.copy("act", zbA[0][:, 0:1024], zqk)
            elif c0 == 1024:
                if zbA is not None:
                    p.copy("act", zbA[1][:, 1024:1536], bk)
                if want_vf32:
                    p.copy("act", zv, bk)
            elif c0 == 1536:
                rope(bk, zqkB[:, 0:512], 512)
            else:
                rope(bk, zqkB[:, 512:640], 128)
                if zbB is not None:
                    p.copy("act", zbB[0][:, 0:512].re("p (f t d) -> p f t d", f=4, t=2),
                           zqkB[:, 0:512].re("p (t f d) -> p f t d", t=2, f=4))
                    p.copy("act", zbB[0][:, 512:640], zqkB[:, 512:640])
                    p.copy("act", zbB[1][:, 640:768], bk[:, 128:256])
                if want_vBf32:
                    p.copy("act", zvB, bk[:, 128:256])
        self.proj_tok(hT, w_in, P, IN_DIM, 8, cb)

    def phase1(self):
        p, D = self.p, self.D
        b = Bump(self.arena, self.PERS)
        w_in = b.alloc([128, 8, IN_DIM], BF16, "w_in")
        w_in = self.load_w(w_in, "w_in", 8, "ld_w1")
        rt = b.alloc([128, NT, 128], F32, "rope_tab")
        p.dma("sp", rt, D["c_rope"].re("(t p) c -> p t c", p=128), "ld_c0_13")
        xt = [b.alloc([128, DM], F32, f"xt{i}") for i in range(3)]
        hb = [b.alloc([128, DM], BF16, f"hb{i}") for i in range(2)]
        hT = [b.alloc([128, 8, 128], BF16, f"hT{i}") for i in range(2)]
        zqk = [b.alloc([128, 1024], F32, f"zqk{i}") for i in range(2)]
        zv = [b.alloc([128, 512], F32, f"zv{i}") for i in range(2)]
        zqkB = [b.alloc([128, 640], F32, f"zqkB{i}") for i in range(2)]
        zvB = b.alloc([128, 128], F32, "zvB")
        tmp = [b.alloc([128, 512], F32, f"tmp{i}") for i in range(2)]
        zbA, zbB = [], []
        for i in range(2):
            a = b.alloc([128, 1536], BF16, f"zbA{i}")
            v1, v2 = subview(self.arena, a, "qk"), subview(self.arena, a, "v")
            zbA.append((v1, v2, V(a.ap, [v1.res, v2.res])))
            a = b.alloc([128, 768], BF16, f"zbB{i}")
            v1, v2 = subview(self.arena, a, "qk"), subview(self.arena, a, "v")
            zbB.append((v1, v2, V(a.ap, [v1.res, v2.res])))
        self.p1_end = b.off
        def xload(t):
            p.dma("sp", xt[t % 3], D["xp"][t * 128:(t + 1) * 128, :], f"ld_xt{t % 3}")

        def front(t):
            self.inproj_front(128, xt[t % 3], hb[t % 2], hT[t % 2])
        xload(0)
        xload(1)
        front(0)
        for t in range(NT):
            rows = slice(t * 128, (t + 1) * 128)
            x = xt[t % 3]
            i = t % 2
            if t + 2 < NT:
                xload(t + 2)
            if t + 1 < NT:
                front(t + 1)
            self.inproj_tile(128, x, hb[i], hT[i], w_in, rt[:, t, 0:64], rt[:, t, 64:128],
                             zqk[i], zv[i], zqkB[i], zvB, tmp[i], zbA[i], zbB[i], t >= NT // 2, t == NT - 1,
                             skip_front=True)
            if t >= NT // 2:
                orow = slice((t - NT // 2) * 128, (t - NT // 2 + 1) * 128)
                p.dma("sp", D["pak"][orow, :], zqk[i][:, 512:1024], f"st_zqk{i}", is_output=True)
                p.dma("sp", D["pav"][orow, :], zv[i], f"st_zv{i}", is_output=True)
            if t == NT - 1:
                p.dma("sp", D["pbk"], zqkB[i][:, 512:640], f"st_zqkB{i}", is_output=True)
                p.dma("sp", D["pbv"], zvB, "st_zvB", is_output=True)
            self.scr_tokens.append(p.dma("sp", D["scrA"][rows, :], zbA[i][2], f"st_zbA{i}"))
            self.scr_tokens.append(p.dma("sp", D["scrB"][rows, :], zbB[i][2], f"st_zbB{i}"))
        sb = self.samp
        xs = sb["xs"]
        p.dma("sp", xs, D["xs"], "ld_xs")
        self.inproj_tile(NS, xs, hb[0][0:NS, :], hT[0], w_in, self.ropes[:, 0:64], self.ropes[:, 64:128],
                         sb["zqk"], sb["zv"], sb["zqkB"], sb["zvB"], tmp[0][0:NS, :], None, None, True, True)
        p.dma("sp", D["sak"][:, LA - 1, :], sb["zqk"][:, 512:1024], "st_s1a", is_output=True)
        p.dma("sp", D["sav"][:, LA - 1, :], sb["zv"], "st_s1b", is_output=True)
        p.dma("sp", D["sbk"][:, LB - 1, :], sb["zqkB"][:, 512:640], "st_s1c", is_output=True)
        p.dma("sp", D["sbv"][:, LB - 1, :], sb["zvB"], "st_s1d", is_output=True)

    def phase2(self):
        p, D = self.p, self.D
        b = Bump(self.arena, self.PERS + 3 * 16384, self.PH_END)
        blk = [b.alloc([128, 1536], BF16, f"blk{i}") for i in range(3)]
        QT = [b.alloc([128, 4, 2, 128], BF16, f"QZ{i}") for i in range(2)]
        for v in QT:
            p.memset("pool", v, 0.0)
        KT = [b.alloc([128, 4, 128], BF16, f"KT{i}") for i in range(3)]
        VX = [b.alloc([128, 8, 65], BF16, f"VX{i}") for i in range(3)]
        PT = [b.alloc([128, 2, 2, 128], BF16, f"PT{i}") for i in range(8)]
        OS = [b.alloc([128, 520], F32, f"OS{i}") for i in range(3)]
        for v in VX:
            p.memset("pool", v, 1.0)
        p.wait_for("sp", list(self.scr_tokens))
        mask4 = V(self.mask2.ap.unsqueeze(2).to_broadcast([128, 2, 2, 128]), self.mask2.res)
        mask_own = bc_mid(self.mask2[:, 0, :], 2)
        st = {"s": 0, "pc": 0}
        self.o_tokens = []

        def st_dma(c):
            kind, br, d, r, bb, first, s = c
            cur = blk[s % 3]
            if kind == "A":
                src = D["scrA"].re("(j r) c -> r j c", r=d)[r, 128 * bb:128 * (bb + 1), :]
                p.dma("sp", cur, src, f"ld_blk{s % 3}")
            else:
                src = D["scrB"][128 * bb:128 * (bb + 1), :]
                p.dma("sp", cur[:, 0:768], src, f"ld_blk{s % 3}")

        def st_load(c):
            kind, br, d, r, bb, first, s = c
            cur = blk[s % 3]
            qt, kt, vx = QT[s % 2], KT[s % 3], VX[s % 3]
            pq = self.tbank()
            for j in range(4):
                p.tr(pq[:, j, :], cur[:, j * 128:(j + 1) * 128], self.ident)
            p.copy("dve", qt[0:64, :, 0, :], pq[0:64, 0:4, :])
            p.copy("dve", qt[64:128, :, 1, :], pq[64:128, 0:4, :])
            if kind == "A":
                self.transposes(kt, cur[:, 512:1024], 128, 4, evac=("act",))
                p.copy("pool", vx[:, :, 0:64], cur[:, 1024:1536].re("p (h d) -> p h d", d=64))
            else:
                self.transposes(kt[:, 0:1, :], cur[:, 512:640], 128, 1, evac=("act",))
                p.copy("pool", vx[:, 0:2, 0:64], cur[:, 640:768].re("p (h d) -> p h d", d=64))

        def st_scores(c):
            kind, br, d, r, bb, first, s = c
            qt, kt, ktp = QT[s % 2], KT[s % 3], KT[(s - 1) % 3]
            for j in range(4):
                psS = self.bank(j)
                kj = j if kind == "A" else 0
                q2 = qt[:, j].re("p a q -> p (a q)")
                p.mm(psS[:, 0:256], kt[:, kj, :], q2)
                if not first:
                    p.mm(psS[:, 256:512], ktp[:, kj, :], q2)
                pt = PT[(4 * s + j) % 8]
                if not first:
                    p.act(pt.re("p b a q -> p (b a q)"), psS, AF.Exp, scale=0.125)
                    p.tt("dve", pt, pt, mask4, ALU.mult)
                else:
                    p.act(pt[:, 0].re("p a q -> p (a q)"), psS[:, 0:256], AF.Exp, scale=0.125)
                    p.tt("dve", pt[:, 0], pt[:, 0], mask_own, ALU.mult)

        def st_pv(c):
            kind, br, d, r, bb, first, s = c
            vx, vxp = VX[s % 3], VX[(s - 1) % 3]
            psO = [self.bank(4), self.bank(5)]
            for j in range(4):
                pt = PT[(4 * s + j) % 8]
                for hh in range(2):
                    if kind == "A":
                        h = 2 * j + hh
                        vi = h
                    else:
                        h = j + 4 * hh
                        vi = hh
                    o = psO[h // 4][:, (h % 4) * 65:(h % 4) * 65 + 65]
                    p.mm(o, pt[:, 0, hh, :], vx[:, vi, :], start=True, stop=first)
                    if not first:
                        p.mm(o, pt[:, 1, hh, :], vxp[:, vi, :], start=False, stop=True)
            osb = OS[s % 3]
            p.copy("act", osb[:, 0:260], psO[0][:, 0:260])
            p.copy("act", osb[:, 260:520], psO[1][:, 0:260])
            if kind == "A":
                dst = D["scrO"][br].re("(j r) c -> r j c", r=d)[r, 128 * bb:128 * (bb + 1), :]
            else:
                dst = D["scrO"][3][128 * bb:128 * (bb + 1), :]
            self.o_tokens.append(p.dma("sp", dst, osb, f"st_os{s % 3}"))

        cfgs = []
        for br, d in ((2, 16), (1, 4), (0, 1)):
            nblk = SEQ // d // 128
            for r in range(d):
                for bb in range(nblk):
                    cfgs.append(("A", br, d, r, bb, bb == 0, len(cfgs)))
        for bb in range(NT):
            cfgs.append(("B", 3, 1, 0, bb, bb == 0, len(cfgs)))
        st_dma(cfgs[0])
        st_dma(cfgs[1])
        st_load(cfgs[0])
        for i, c in enumerate(cfgs):
            if i + 2 < len(cfgs):
                st_dma(cfgs[i + 2])
            st_scores(c)
            if i + 1 < len(cfgs):
                st_load(cfgs[i + 1])
            st_pv(c)

    def prefetch_w2(self):
        b = Bump(self.arena, self.PERS, self.PERS + 3 * 16384)
        w_out = b.alloc([128, 8, DM], BF16, "w_out")
        w_xq = b.alloc([128, 8, DM], BF16, "w_xq")
        w_xo = b.alloc([128, 8, DM], BF16, "w_xo")
        w_out = self.load_w(w_out, "w_out", 8, "ld_w2a")
        w_xq = self.load_w(w_xq, "w_xq", 8, "ld_w2b")
        w_xo = self.load_w(w_xo, "w_xo", 8, "ld_w2c")
        self.w2 = (w_out, w_xq, w_xo)

    def phase2b(self):
        p, D = self.p, self.D
        b = Bump(self.arena, self.PERS + 3 * 16384, self.PH_END)
        w_out, w_xq, w_xo = self.w2
        OL = [[b.alloc([128, 520], F32, f"OL{i}_{k}") for k in range(4)] for i in range(2)]
        cat = [b.alloc([128, DM], F32, "cat0")] * 2
        hm = [b.alloc([128, DM], BF16, f"hm{i}") for i in range(2)]
        rd = [b.alloc([128, 16], F32, f"rd{i}") for i in range(2)]
        x1 = [b.alloc([128, DM], F32, f"x1_{i}") for i in range(4)]
        hmT = b.alloc([128, 8, 512], BF16, "hmT")
        h2T = b.alloc([128, 8, 512], BF16, "h2T")
        qxT = b.alloc([128, 8, 512], BF16, "qxT")
        oxT = b.alloc([128, 8, 512], BF16, "oxT")
        h2 = [b.alloc([128, DM], BF16, f"h2_{i}") for i in range(2)]
        PTx = [b.alloc([128, 2, 512], BF16, f"PTx{i}") for i in range(2)]
        rden = [b.alloc([128, 512], F32, f"rden{i}") for i in range(2)]
        self.p2b_end = b.off
        hmTb = [hmT, b.alloc([128, 8, 512], BF16, "hmT1")]
        p.wait_for("sp", list(self.o_tokens))
        self.x2_tokens = []

        hm4 = hm + [b.alloc([128, DM], BF16, f"hm{i}") for i in range(2, 4)]

        def front_norm_tile(sti, tl):
            t = sti * 4 + tl
            rows = slice(t * 128, (t + 1) * 128)
            ol = OL[t % 2]
            for k in range(4):
                p.dma("sp", ol[k], D["scrO"][k][rows, :], f"ld_ol{t % 2}_{k}")
            self.combine(128, ol, rd[t % 2], cat[t % 2], hm4[tl], self.esink)

        def front_T(sti):
            for tl in range(4):
                self.transposes(hmTb[sti % 2][:, :, tl * 128:(tl + 1) * 128], hm4[tl], 128, 8)

        def mid(sti):
            hmT_ = hmTb[sti % 2]
            for tl in range(4):
                t = sti * 4 + tl
                rows = slice(t * 128, (t + 1) * 128)
                p.dma("sp", x1[tl], D["xp"][rows, :], f"ld_x1_{tl}")

                def cb(bk, c0, n, tl=tl):
                    p.tt("dve", x1[tl][:, c0:c0 + n], x1[tl][:, c0:c0 + n], bk, ALU.add)
                self.proj_tok(hmT_[:, :, tl * 128:(tl + 1) * 128], w_out, 128, DM, 8, cb)
            for tl in range(4):
                self.rmsnorm(x1[tl], self.g_cross, h2[tl % 2], 128, DM)
                self.transposes(h2T[:, :, tl * 128:(tl + 1) * 128], h2[tl % 2], 128, 8)
            for fc in range(8):
                bk = self.nb()
                for kc in range(8):
                    p.mm(bk, w_xq[:, kc, fc * 128:(fc + 1) * 128], h2T[:, kc, :], start=(kc == 0), stop=(kc == 7))
                p.copy("act" if fc % 2 == 0 else "dve", qxT[:, fc, :], bk)
            for h in range(4):
                pt = PTx[h % 2]
                for mc in range(2):
                    bk = self.nb()
                    for j in range(2):
                        p.mm(bk, self.mkT[:, 2 * h + j, mc * 128:(mc + 1) * 128], qxT[:, 2 * h + j, :],
                             start=(j == 0), stop=(j == 1))
                    p.act(pt[:, mc, :], bk, AF.Exp, scale=1.0 / 16.0)
                bd = self.nb()
                for mc in range(2):
                    p.mm(bd, self.ones_bf, pt[:, mc, :], start=(mc == 0), stop=(mc == 1))
                rdn = rden[h % 2]
                p.act(rdn, bd, AF.Ln)
                p.act(rdn, rdn, AF.Exp, scale=-1.0)
                for dj in range(2):
                    bk = self.nb()
                    for mc in range(2):
                        p.mm(bk, self.mv[:, mc, h * 256 + dj * 128:h * 256 + (dj + 1) * 128], pt[:, mc, :],
                             start=(mc == 0), stop=(mc == 1))
                    p.tt("dve", oxT[:, 2 * h + dj, :], bk, rdn, ALU.mult)

        def back_tile(sti, tl):
            t = sti * 4 + tl
            rows = slice(t * 128, (t + 1) * 128)

            def cb(bk, c0, n, tl=tl):
                p.tt("dve", x1[tl][:, c0:c0 + n], x1[tl][:, c0:c0 + n], bk, ALU.add)
            self.proj_tok(oxT[:, :, tl * 128:(tl + 1) * 128], w_xo, 128, DM, 8, cb)
            self.x2_tokens.append(p.dma("sp", D["scrX2"][rows, :], x1[tl], f"st_x1_{tl}"))

        nsup = NT // 4
        for tl in range(4):
            front_norm_tile(0, tl)
        front_T(0)
        for sti in range(nsup):
            mid(sti)
            for tl in range(4):
                if sti + 1 < nsup:
                    front_norm_tile(sti + 1, tl)
                back_tile(sti, tl)
            if sti + 1 < nsup:
                front_T(sti + 1)

    def combine(self, P, ol, r, c, hmv, esink):
        p = self.p
        if ol[1] is not None:
            p.tt("pool", ol[0], ol[0], ol[1], ALU.add)
            p.tt("pool", ol[0], ol[0], ol[2], ALU.add)
        a3 = ol[0].re("p (h c) -> p h c", c=65)
        b3 = ol[3].re("p (h c) -> p h c", c=65)
        p.recip(r[:, 0:8], a3[:, :, 64])
        p.tt("dve", c[:, 0:512].re("p (h d) -> p h d", d=64), a3[:, :, 0:64], bc_last(r[:, 0:8], 64), ALU.mult)
        p.tt("dve", r[:, 8:16], b3[:, :, 64], esink[0:P, :], ALU.add)
        p.recip(r[:, 8:16], r[:, 8:16])
        p.tt("dve", c[:, 512:1024].re("p (h d) -> p h d", d=64), b3[:, :, 0:64], bc_last(r[:, 8:16], 64), ALU.mult)
        self.rmsnorm(c[:, 0:512], self.g_out[0:P, 0:512], hmv[:, 0:512], P, 512)
        self.rmsnorm(c[:, 512:1024], self.g_out[0:P, 512:1024], hmv[:, 512:1024], P, 512)

    def phase3(self):
        p, D = self.p, self.D
        b = Bump(self.arena, self.PERS_A, self.ARENA - 4096)
        w_up = b.alloc([128, 8, 2 * DFF], BF16, "w_up")
        w_dn = b.alloc([128, NFC, DM], BF16, "w_dn")
        w_up = self.load_w(w_up, "w_up", 8, "ld_w3a")
        w_dn = self.load_w(w_dn, "w_down", NFC, "ld_w3b")
        self.w3 = (w_up, w_dn)
        cw = b.alloc([128, 3, NFC], F32, "cw")
        cbias = b.alloc([128, NFC], F32, "cbias")
        for i3 in range(3):
            p.dma("sp", cw[:, i3, :], D["conv_w"][i3].re("(fc f) -> f fc", f=128), "ld_cw",
                  allow_slow_non_contiguous=True)
        p.dma("sp", cbias, D["conv_b"].re("(fc f) -> f fc", f=128), "ld_cb", allow_slow_non_contiguous=True)
        gcar = b.alloc([128, NFC, 2], F32, "gcar")
        p.memset("pool", gcar, 0.0)
        ST = 256
        self.p3_tmp_start = b.off
        xl = [b.alloc([128, DM], F32, f"xl{i}") for i in range(4)]
        h3 = [b.alloc([128, DM], BF16, f"h3_{i}") for i in range(2)]
        h3T = b.alloc([128, 8, ST], BF16, "h3T")
        aT = b.alloc([128, NFC, ST], BF16, "aT")
        gsb = [b.alloc([128, ST + 2], F32, f"gsb{i}") for i in range(2)]
        tq = [b.alloc([128, ST], F32, f"tq{i}") for i in range(2)]
        sq = [b.alloc([128, ST], F32, f"sq{i}") for i in range(2)]
        yt = [b.alloc([128, DM], F32, f"yt{i}") for i in range(1)]
        h3Tb = [h3T, b.alloc([128, 8, ST], BF16, "h3T1")]
        p.wait_for("sp", list(self.x2_tokens))

        def xload(s_):
            for tl in range(2):
                t = 2 * s_ + tl
                xi = (s_ % 2) * 2 + tl
                p.dma("sp", xl[xi], D["scrX2"][t * 128:(t + 1) * 128, :], f"ld_xl{xi}")

        def front_norm(s_):
            for tl in range(2):
                xi = (s_ % 2) * 2 + tl
                self.rmsnorm(xl[xi], self.g_ffn, h3[tl], 128, DM)

        def front_T(s_):
            for tl in range(2):
                self.transposes(h3Tb[s_ % 2][:, :, tl * 128:(tl + 1) * 128], h3[tl], 128, 8)

        def mid(s_):
            hT_ = h3Tb[s_ % 2]
            for fc in range(NFC):
                bg = self.nb()
                bv = self.nb()
                for kc in range(8):
                    p.mm(bg[:, 0:ST], w_up[:, kc, fc * 128:(fc + 1) * 128], hT_[:, kc, :], start=(kc == 0), stop=(kc == 7))
                for kc in range(8):
                    p.mm(bv[:, 0:ST], w_up[:, kc, DFF + fc * 128:DFF + (fc + 1) * 128], hT_[:, kc, :],
                         start=(kc == 0), stop=(kc == 7))
                g = gsb[fc % 2]
                tt_ = tq[fc % 2]
                ss_ = sq[fc % 2]
                p.copy("pool", g[:, 0:2], gcar[:, fc, :])
                p.copy("act", g[:, 2:ST + 2], bg[:, 0:ST])
                p.copy("pool", gcar[:, fc, :], g[:, ST:ST + 2])
                p.ts("dve", tt_, g[:, 0:ST], cw[:, 0, fc:fc + 1], cbias[:, fc:fc + 1], ALU.mult, ALU.add)
                p.stt("dve", tt_, g[:, 1:ST + 1], cw[:, 1, fc:fc + 1], tt_, ALU.mult, ALU.add)
                p.stt("dve", tt_, g[:, 2:ST + 2], cw[:, 2, fc:fc + 1], tt_, ALU.mult, ALU.add)
                p.act(ss_, tt_, AF.Silu)
                p.tt("dve", aT[:, fc, :], ss_, bv[:, 0:ST], ALU.mult)

        def back(s_):
            for tl in range(2):
                t = 2 * s_ + tl
                rows = slice(t * 128, (t + 1) * 128)
                x = xl[(s_ % 2) * 2 + tl]

                def cb(bk, c0, n, x=x):
                    p.tt("dve", x[:, c0:c0 + n], x[:, c0:c0 + n], bk, ALU.add)
                self.proj_tok(aT[:, :, tl * 128:(tl + 1) * 128], w_dn, 128, DM, NFC, cb)
                y = yt[0]
                self.rmsnorm(x, self.g_final, y, 128, DM)
                p.dma("sp", D["yp"][rows, :], y, "st_y0", is_output=True)

        nsup = SEQ // ST
        xload(0)
        front_norm(0)
        front_T(0)
        for s_ in range(nsup):
            if s_ + 1 < nsup:
                xload(s_ + 1)
            mid(s_)
            if s_ + 1 < nsup:
                front_norm(s_ + 1)
            back(s_)
            if s_ + 1 < nsup:
                front_T(s_ + 1)
        for ti in range(2):
            p.dma("sp", D["pconv"][ti].re("(fc f) -> f fc", f=128), gcar[:, :, ti], "st_pconv", is_output=True,
                  allow_slow_non_contiguous=True)

    def samp_attn(self):
        p, D, sb = self.p, self.D, self.samp
        b = Bump(self.arena, self.PERS + 3 * 16384, self.PH_END)
        sel = b.alloc([NS, NS, 128], F32, "sel")
        p.dma("sp", sel.re("p a b -> p (a b)"), D["c_sel"], "ld_sel")
        qb = [b.alloc([128, 512], F32, f"s_qb{i}") for i in range(2)]
        Kt = [b.alloc([128, 512], F32, f"s_Kt{i}") for i in range(3)]
        Vt = [b.alloc([128, 8, 65], F32, f"s_Vt{i}") for i in range(3)]
        prod = [b.alloc([128, 512], F32, f"s_prod{i}") for i in range(2)]
        sc = [b.alloc([128, 8], F32, f"s_sc{i}") for i in range(2)]
        Pz = [b.alloc([128, 8, NS], F32, f"s_Pz{i}") for i in range(4)]
        oA = b.alloc([NS, 8, 65], F32, "s_oA")
        oB = b.alloc([NS, 8, 65], F32, "s_oB")
        prn = b.alloc([NS, 512], F32, "s_prn")
        sn = b.alloc([NS, 8], F32, "s_sn")
        en = b.alloc([NS, 8], F32, "s_en")
        tv = b.alloc([NS, 8, 64], F32, "s_tv")
        r16 = b.alloc([NS, 16], F32, "s_r16")
        for v in Vt:
            p.memset("pool", v, 1.0)
        for v in Pz:
            p.memset("pool", v, 0.0)
        p.memset("pool", oA, 0.0)
        p.memset("pool", oB, 0.0)
        cnt = 0
        for n in range(NS):
            bk = self.nb()
            p.mm(bk, sel[:, n, :], sb["zqk"][:, 0:512])
            q = qb[n % 2]
            p.copy("act", q, bk)
            for g in range(3):
                if g == 0:
                    ksrc = D["cak"][n, LA - 128:LA, :]
                    vsrc = D["cav"][n, LA - 128:LA, :]
                elif g == 1:
                    ksrc = D["cak"][n].re("(j r) c -> r j c", r=4)[0, 384:512, :]
                    vsrc = D["cav"][n].re("(j r) c -> r j c", r=4)[0, 384:512, :]
                else:
                    ksrc = D["cak"][n].re("(j r) c -> r j c", r=16)[0, 0:128, :]
                    vsrc = D["cav"][n].re("(j r) c -> r j c", r=16)[0, 0:128, :]
                kt, vt = Kt[cnt % 3], Vt[cnt % 3]
                p.dma("sp", kt, ksrc, f"ld_sK{cnt % 3}")
                p.dma("sp", vt[:, :, 0:64], vsrc.re("j (h d) -> j h d", d=64), f"ld_sV{cnt % 3}")
                pr = prod[cnt % 2]
                p.tt("dve", pr, kt, q, ALU.mult)
                s8 = sc[cnt % 2]
                p.reduce("dve", s8, pr.re("p (h d) -> p h d", d=64), ALU.add)
                pz = Pz[cnt % 4]
                p.act(pz[:, :, n], s8, AF.Exp, scale=0.125)
                psA = [self.nb(), self.nb()]
                for h in range(8):
                    o = psA[h // 4][0:NS, (h % 4) * 65:(h % 4) * 65 + 65]
                    p.mm(o, pz[:, h, :], vt[:, h, :])
                for hb in range(2):
                    av = oA[:, 4 * hb:4 * hb + 4, :]
                    p.tt("dve", av, av, psA[hb][0:NS, 0:260].re("p (h c) -> p h c", c=65), ALU.add)
                p.memset("pool", pz[:, :, n], 0.0)
                cnt += 1
        zqk, zv = sb["zqk"], sb["zv"]
        p.tt("dve", prn, zqk[:, 0:512], zqk[:, 512:1024], ALU.mult)
        p.reduce("dve", sn, prn.re("p (h d) -> p h d", d=64), ALU.add)
        p.act(en, sn, AF.Exp, scale=0.125)
        p.ts("dve", en, en, 3.0, None, ALU.mult)
        p.tt("dve", tv, zv.re("p (h d) -> p h d", d=64), bc_last(en, 64), ALU.mult)
        p.tt("dve", oA[:, :, 0:64], oA[:, :, 0:64], tv, ALU.add)
        p.tt("dve", oA[:, :, 64], oA[:, :, 64], en, ALU.add)
        KtB = [V(k.ap[:, 0:128], k.res) for k in Kt]
        VtB = [V(v.ap[:, 0:2, :], v.res) for v in Vt]
        zqkB, zvB = sb["zqkB"], sb["zvB"]
        for n in range(NS):
            bk = self.nb()
            p.mm(bk, sel[:, n, :], zqkB[:, 0:512])
            q = qb[n % 2]
            p.copy("act", q, bk)
            kt, vt = KtB[cnt % 3], VtB[cnt % 3]
            p.dma("sp", kt, D["cbk"][n], f"ld_sK{cnt % 3}")
            p.dma("sp", vt[:, :, 0:64], D["cbv"][n].re("j (h d) -> j h d", d=64), f"ld_sV{cnt % 3}")
            pr = prod[cnt % 2]
            k4 = V(kt.ap.rearrange("p (k d) -> p k d", d=64).unsqueeze(2).to_broadcast([128, 2, 4, 64]), kt.res)
            p.tt("dve", pr.re("p (k g d) -> p k g d", k=2, g=4), q.re("p (k g d) -> p k g d", k=2, g=4), k4, ALU.mult)
            s8 = sc[cnt % 2]
            p.reduce("dve", s8, pr.re("p (h d) -> p h d", d=64), ALU.add)
            pz = Pz[cnt % 4]
            p.act(pz[:, :, n], s8, AF.Exp, scale=0.125)
            psA = [self.nb(), self.nb()]
            for h in range(8):
                o = psA[h // 4][0:NS, (h % 4) * 65:(h % 4) * 65 + 65]
                p.mm(o, pz[:, h, :], vt[:, h // 4, :])
            for hb in range(2):
                av = oB[:, 4 * hb:4 * hb + 4, :]
                p.tt("dve", av, av, psA[hb][0:NS, 0:260].re("p (h c) -> p h c", c=65), ALU.add)
            p.memset("pool", pz[:, :, n], 0.0)
            cnt += 1
        kn4 = V(zqkB.ap[:, 512:640].rearrange("p (k d) -> p k d", d=64).unsqueeze(2).to_broadcast([NS, 2, 4, 64]), zqkB.res)
        p.tt("dve", prn.re("p (k g d) -> p k g d", k=2, g=4), zqkB[:, 0:512].re("p (k g d) -> p k g d", k=2, g=4), kn4, ALU.mult)
        p.reduce("dve", sn, prn.re("p (h d) -> p h d", d=64), ALU.add)
        p.act(en, sn, AF.Exp, scale=0.125)
        vn4 = V(zvB.ap.rearrange("p (k d) -> p k d", d=64).unsqueeze(2).to_broadcast([NS, 2, 4, 64]), zvB.res)
        e4 = V(en.ap.rearrange("p (k g) -> p k g", k=2).unsqueeze(3).to_broadcast([NS, 2, 4, 64]), en.res)
        p.tt("dve", tv.re("p (k g) d -> p k g d", k=2), vn4, e4, ALU.mult)
        p.tt("dve", oB[:, :, 0:64], oB[:, :, 0:64], tv, ALU.add)
        p.tt("dve", oB[:, :, 64], oB[:, :, 64], en, ALU.add)
        self.combine(NS, [oA.re("p h c -> p (h c)"), None, None, oB.re("p h c -> p (h c)")], r16, sb["cat"], sb["hm"],
                     self.esink)

    def samp_mix(self):
        p, D, sb = self.p, self.D, self.samp
        w_out, w_xq, w_xo = self.w2
        b = Bump(self.arena, self.PERS + 3 * 16384, self.PH_END)
        sel = b.alloc([NS, NS, 128], F32, "sel2")
        p.dma("sp", sel.re("p a b -> p (a b)"), D["c_sel"], "ld_sel2")
        hT = b.alloc([128, 8, NS], BF16, "s_hT")
        h2s = b.alloc([NS, DM], BF16, "s_h2")
        qx = b.alloc([NS, DM], F32, "s_qx")
        qbx = [b.alloc([128, DM], F32, f"s_qbx{i}") for i in range(2)]
        Kx = [b.alloc([128, DM], F32, f"s_Kx{i}") for i in range(2)]
        Vx = [b.alloc([128, 4, 257], F32, f"s_Vx{i}") for i in range(2)]
        prodx = b.alloc([128, DM], F32, "s_prodx")
        s4 = [b.alloc([128, 4], F32, f"s_s4{i}") for i in range(2)]
        Pzx = [b.alloc([128, 4, NS], F32, f"s_Pzx{i}") for i in range(4)]
        oX = b.alloc([NS, 4, 257], F32, "s_oX")
        r4 = b.alloc([NS, 4], F32, "s_r4")
        oxn = b.alloc([NS, DM], BF16, "s_oxn")
        xs = sb["xs"]
        self.transposes(hT, sb["hm"], NS, 8)

        def cb(bk, c0, n):
            p.tt("dve", xs[:, c0:c0 + n], xs[:, c0:c0 + n], bk, ALU.add)
        self.proj_tok(hT, w_out, NS, DM, 8, cb)
        self.rmsnorm(xs, self.g_cross[0:NS, :], h2s, NS, DM)
        self.transposes(hT, h2s, NS, 8)

        def cb2(bk, c0, n):
            p.copy("act", qx[:, c0:c0 + n], bk)
        self.proj_tok(hT, w_xq, NS, DM, 8, cb2)
        for v in Vx:
            p.memset("pool", v, 1.0)
        for v in Pzx:
            p.memset("pool", v, 0.0)
        p.memset("pool", oX, 0.0)
        cnt = 0
        for n in range(NS):
            q = qbx[n % 2]
            for half in range(2):
                bk = self.nb()
                p.mm(bk, sel[:, n, :], qx[:, half * 512:(half + 1) * 512])
                p.copy("act", q[:, half * 512:(half + 1) * 512], bk)
            for mc in range(2):
                kx, vx = Kx[cnt % 2], Vx[cnt % 2]
                rows = slice(mc * 128, (mc + 1) * 128)
                p.dma("sp", kx, D["cmk"][n, rows, :], f"ld_sKx{cnt % 2}")
                p.dma("sp", vx[:, :, 0:256], D["cmv"][n, rows, :].re("j (h d) -> j h d", d=256), f"ld_sVx{cnt % 2}")
                p.tt("dve", prodx, kx, q, ALU.mult)
                s_ = s4[cnt % 2]
                p.reduce("dve", s_, prodx.re("p (h d) -> p h d", d=256), ALU.add)
                pz = Pzx[cnt % 4]
                p.act(pz[:, :, n], s_, AF.Exp, scale=1.0 / 16.0)
                for h in range(4):
                    bo = self.nb()
                    p.mm(bo[0:NS, 0:257], pz[:, h, :], vx[:, h, :])
                    p.tt("dve", oX[:, h, :], oX[:, h, :], bo[0:NS, 0:257], ALU.add)
                p.memset("pool", pz[:, :, n], 0.0)
                cnt += 1
        p.recip(r4, oX[:, :, 256])
        p.tt("dve", oxn.re("p (h d) -> p h d", d=256), oX[:, :, 0:256], bc_last(r4, 256), ALU.mult)
        self.transposes(hT, oxn, NS, 8)
        self.proj_tok(hT, w_xo, NS, DM, 8, cb)

    def samp_ffn(self):
        p, D, sb = self.p, self.D, self.samp
        w_up, w_dn = self.w3
        b = Bump(self.arena, self.p3_tmp_start, self.ARENA - 4096)
        h3s = b.alloc([NS, DM], BF16, "s_h3")
        hT = b.alloc([128, 8, NS], BF16, "s_h3T")
        gs = b.alloc([NS, DFF], F32, "s_gs")
        vs = b.alloc([NS, DFF], F32, "s_vs")
        abf = b.alloc([NS, DFF], BF16, "s_abf")
        aT = b.alloc([128, NFC, NS], BF16, "s_aT")
        ysb = b.alloc([NS, DM], F32, "s_ysb")
        xs = sb["xs"]
        self.rmsnorm(xs, self.g_ffn[0:NS, :], h3s, NS, DM)
        self.transposes(hT, h3s, NS, 8)

        def cbu(bk, c0, n):
            lo, hi = c0, c0 + n
            if lo < DFF:
                m = min(hi, DFF) - lo
                p.copy("act", gs[:, lo:lo + m], bk[:, 0:m])
            if hi > DFF:
                s0 = max(lo, DFF)
                p.copy("act", vs[:, s0 - DFF:hi - DFF], bk[:, s0 - lo:n])
        self.proj_tok(hT, w_up, NS, 2 * DFF, 8, cbu)
        p.dma("sp", D["sconvo"][:, 1, :], gs, "st_sgs", is_output=True)
        b2 = Bump(self.arena, self.PERS_A, self.PERS_A + 90112)
        s0t = b2.alloc([NS, DFF], F32, "s_s0")
        s1t = b2.alloc([NS, DFF], F32, "s_s1")
        cwb = b2.alloc([NS, 3, DFF], F32, "s_cwb")
        cbb = b2.alloc([NS, DFF], F32, "s_cbb")
        t1 = b2.alloc([NS, DFF], F32, "s_t1")
        t2 = b2.alloc([NS, DFF], F32, "s_t2")
        p.dma("sp", s0t, D["sconv"][:, 0, :], "ld_ss0")
        p.dma("sp", s1t, D["sconv"][:, 1, :], "ld_ss1")
        tcw = D["conv_w"].ap.tensor
        p.dma("sp", cwb.re("p a b -> p (a b)"), V(bass.AP(tcw, 0, [[0, NS], [1, 3 * DFF]]), ("dram", "conv_w")), "ld_scw")
        p.dma("sp", cbb, self.bcast_dram("conv_b", DFF, NS), "ld_scb")
        p.tt("dve", t1, s0t, cwb[:, 0, :], ALU.mult)
        p.tt("pool", t2, s1t, cwb[:, 1, :], ALU.mult)
        p.tt("dve", t1, t1, t2, ALU.add)
        p.tt("pool", t2, gs, cwb[:, 2, :], ALU.mult)
        p.tt("dve", t1, t1, t2, ALU.add)
        p.tt("dve", t1, t1, cbb, ALU.add)
        p.act(t2, t1, AF.Silu)
        p.tt("dve", abf, t2, vs, ALU.mult)
        self.transposes(aT, abf, NS, NFC)

        def cb(bk, c0, n):
            p.tt("dve", xs[:, c0:c0 + n], xs[:, c0:c0 + n], bk, ALU.add)
        self.proj_tok(aT, w_dn, NS, DM, NFC, cb)
        self.rmsnorm(xs, self.g_final[0:NS, :], ysb, NS, DM)
        p.dma("sp", D["ys"], ysb, "st_ys", is_output=True)

    def alloc_sample(self):
        top = Bump(self.arena, self.ARENA - 4096)
        b = Bump(self.arena, self.ARENA - 20480, self.ARENA - 4096)
        self.samp = {
            "xs": top.alloc([NS, DM], F32, "s_xs"),
            "zqk": b.alloc([NS, 1024], F32, "s_zqk"),
            "zv": b.alloc([NS, 512], F32, "s_zv"),
            "zqkB": b.alloc([NS, 640], F32, "s_zqkB"),
            "zvB": b.alloc([NS, 128], F32, "s_zvB"),
            "cat": b.alloc([NS, DM], F32, "s_cat"),
            "hm": b.alloc([NS, DM], BF16, "s_hm"),
        }
        self.samp_bump = b
        self.PH_END = self.ARENA - 20480

    def build(self):
        self.scr_tokens = []
        self.alloc_sample()
        self.bulk_copies()
        self.phase0()
        self.phase1()
        if self.stage >= 2:
            self.prefetch_w2()
            self.samp_attn()
            self.phase2()
        if self.stage >= 3:
            self.phase2b()
            self.samp_mix()
        if self.stage >= 4:
            self.phase3()
            self.samp_ffn()
        return self.p.build()


def make_consts():
    half = 32
    inv = np.power(np.float32(10000.0), -np.arange(half, dtype=np.float32) / np.float32(half)).astype(np.float32)

    def tab(pos):
        ang = pos.astype(np.float32)[:, None] * inv[None, :]
        c = np.cos(ang).astype(np.float32)
        s = np.sin(ang).astype(np.float32)
        return np.concatenate([c, c, -s, s], axis=1).astype(np.float32)
    c_rope = tab(np.arange(SEQ))
    c_ropes = tab(np.full((NS,), PAST))
    k = np.arange(128)[:, None]
    q = np.arange(128)[None, :]
    own = (k <= q).astype(np.float32)
    prev = (k >= q).astype(np.float32)
    c_mask = np.concatenate([own, prev], axis=1).astype(np.float32)
    c_ident = np.eye(128, dtype=np.float32)
    c_sel = np.zeros((NS, NS, 128), np.float32)
    for n in range(NS):
        c_sel[n, n, :] = 1.0
    return {"c_rope": c_rope, "c_ropes": c_ropes, "c_mask": c_mask, "c_ident": c_ident,
            "c_sel": c_sel.reshape(NS, NS * 128)}


_STAGE = 4


def kernel(x_prompt, x_sample, cache_a_k, cache_a_v, cache_b_k, cache_b_v, cache_mem_k, cache_mem_v, state_conv,
           mem_prompt, g_mix, w_in, g_out_a, g_out_b, sinks, w_out, g_cross, g_mem, w_xq, w_mem_kv, w_xo,
           g_ffn, w_up, conv_w, conv_b, w_down, g_final):
    f = lambda a: np.ascontiguousarray(np.asarray(a, dtype=np.float32))
    kb = KB(stage=_STAGE)
    nc = kb.build()
    consts = make_consts()
    shared = {
        "g_mix": f(g_mix[0]), "w_in": f(w_in[0]), "g_out_a": f(g_out_a[0]), "g_out_b": f(g_out_b[0]),
        "sinks": f(sinks[0]), "w_out": f(w_out[0]), "g_cross": f(g_cross[0]), "g_mem": f(g_mem[0]),
        "w_xq": f(w_xq[0]), "w_mem_kv": f(w_mem_kv[0]), "w_xo": f(w_xo[0]), "g_ffn": f(g_ffn[0]),
        "w_up": f(w_up[0]), "conv_w": f(conv_w[0]), "conv_b": f(conv_b[0]), "w_down": f(w_down[0]),
        "g_final": f(g_final),
    }
    shared.update(consts)
    in_maps = []
    for c in range(NCORES):
        s = slice(c * NS, (c + 1) * NS)
        m = dict(shared)
        m["xp"] = f(x_prompt[c])
        m["xs"] = f(x_sample[s, 0])
        m["cak"] = f(cache_a_k[0, s]).reshape(NS, LA, 512)
        m["cav"] = f(cache_a_v[0, s]).reshape(NS, LA, 512)
        m["cbk"] = f(cache_b_k[0, s]).reshape(NS, LB, 128)
        m["cbv"] = f(cache_b_v[0, s]).reshape(NS, LB, 128)
        m["cmk"] = f(cache_mem_k[0, s]).reshape(NS, MEM, DM)
        m["cmv"] = f(cache_mem_v[0, s]).reshape(NS, MEM, DM)
        m["sconv"] = f(state_conv[0, s])
        m["memp"] = f(mem_prompt[c])
        in_maps.append(m)
    res = run_bass_kernel_spmd(nc, in_maps, core_ids=list(range(NCORES)))
    R = res.results
    cat = lambda k: np.stack([np.asarray(R[c][k], dtype=np.float32) for c in range(NCORES)])
    catn = lambda k: np.concatenate([np.asarray(R[c][k], dtype=np.float32) for c in range(NCORES)], axis=0)
    y_prompt = cat("yp")
    y_sample = catn("ys").reshape(NCORES * NS, 1, DM)
    p_a_k = cat("pak").reshape(1, NCORES, LA, 8, 64)
    p_a_v = cat("pav").reshape(1, NCORES, LA, 8, 64)
    p_b_k = cat("pbk").reshape(1, NCORES, LB, 2, 64)
    p_b_v = cat("pbv").reshape(1, NCORES, LB, 2, 64)
    p_mem_k = cat("pmk").reshape(1, NCORES, MEM, 4, 256)
    p_mem_v = cat("pmv").reshape(1, NCORES, MEM, 4, 256)
    p_conv = cat("pconv").reshape(1, NCORES, 2, DFF)
    s_a_k = catn("sak").reshape(1, NCORES * NS, LA, 8, 64)
    s_a_v = catn("sav").reshape(1, NCORES * NS, LA, 8, 64)
    s_b_k = catn("sbk").reshape(1, NCORES * NS, LB, 2, 64)
    s_b_v = catn("sbv").reshape(1, NCORES * NS, LB, 2, 64)
    s_conv = catn("sconvo").reshape(1, NCORES * NS, 2, DFF)
    return (y_prompt, y_sample, p_a_k, p_a_v, p_b_k, p_b_v, p_mem_k, p_mem_v, p_conv,
            s_a_k, s_a_v, s_b_k, s_b_v, s_conv)
```

```python
import numpy as np
from contextlib import ExitStack
import ml_dtypes
import concourse.bass as bass
import concourse.mybir as mybir
from concourse.bass_utils import run_bass_kernel_spmd

F32 = mybir.dt.float32
BF16 = mybir.dt.bfloat16
ALU = mybir.AluOpType
AF = mybir.ActivationFunctionType
AX = mybir.AxisListType

NCORES = 8
SEQ = 4096
DM = 1024
NT = SEQ // 128
NS = 16
LA = 2048
LB = 128
MEM = 256
DFF = 2816
NFC = DFF // 128
IN_DIM = 2304
EPS = 1e-6
PAST = 16384


class V:
    __slots__ = ("ap", "res")

    def __init__(self, ap, res):
        self.ap = ap
        self.res = res

    def __getitem__(self, k):
        return V(self.ap[k], self.res)

    def re(self, s, **kw):
        return V(self.ap.rearrange(s, **kw), self.res)

    def bc(self, shape):
        return V(self.ap.to_broadcast(shape), self.res)

    def cast(self, dt):
        return V(self.ap.bitcast(dt), self.res)


def _keys(v):
    if v is None:
        return []
    r = v.res
    if isinstance(r, list):
        return r
    return [r]


class Prog:
    ENGS = ("pe", "act", "dve", "pool", "sp")

    def __init__(self):
        self.nc = bass.Bass("TRN2", target_bir_lowering=False)
        self.es = ExitStack()
        self.ins = []
        self.res = {}
        self.out_tokens = []

    def dram(self, name, shape, dt, kind):
        t = self.nc.dram_tensor(name, list(shape), dt, kind=kind)
        return V(t.ap(), ("dram", name))

    def sbuf(self, name, shape, dt):
        t = self.es.enter_context(self.nc.sbuf_tensor(name, list(shape), dt))
        return V(t[:], ("sb", name))

    def psum(self, name, shape, dt):
        t = self.es.enter_context(self.nc.psum_tensor(name, list(shape), dt))
        return V(t[:], ("ps", name))

    def _rec(self, eng, emit, reads, writes, dma_sem=None, track_dram=False):
        iid = len(self.ins)
        is_dma = dma_sem is not None
        deps = {}

        def add_dep(tok, kind):
            if tok is None:
                return
            pr = self.ins[tok]
            if (not is_dma) and (pr["dma_sem"] is None) and pr["eng"] == eng:
                if eng == "pe":
                    return
            deps[tok] = True

        rk = [k for v in reads for k in _keys(v)]
        wk = [k for v in writes for k in _keys(v)]
        if not track_dram:
            rk = [k for k in rk if k[0] != "dram"]
            wk = [k for k in wk if k[0] != "dram"]
        for r in rk:
            st = self.res.setdefault(r, {"w": None, "r": {}})
            add_dep(st["w"], 0)
        for w in wk:
            st = self.res.setdefault(w, {"w": None, "r": {}})
            add_dep(st["w"], 1)
            for t in st["r"].values():
                add_dep(t, 2)
        semkey = ("dma", dma_sem) if is_dma else ("eng", eng)
        for r in rk:
            self.res[r]["r"][semkey] = iid
        for w in wk:
            self.res[w]["w"] = iid
            self.res[w]["r"] = {}
        self.ins.append({"eng": eng, "emit": emit, "deps": list(deps), "dma_sem": dma_sem,
                         "flag": is_dma, "semkey": semkey, "val": None})
        for d in deps:
            self.ins[d]["flag"] = True
        return iid

    def mm(self, out, lhsT, rhs, start=True, stop=True):
        return self._rec("pe", lambda e: e.matmul(out.ap, lhsT.ap, rhs.ap, start=start, stop=stop),
                         [lhsT, rhs], [out])

    def tr(self, out, in_, ident):
        return self._rec("pe", lambda e: e.transpose(out.ap, in_.ap, ident.ap), [in_, ident], [out])

    def act(self, out, in_, func, bias=None, scale=1.0, accum=None):
        kw = {}
        rd = [in_]
        if bias is not None:
            if isinstance(bias, V):
                kw["bias"] = bias.ap
                rd.append(bias)
            else:
                kw["bias"] = bias
        if isinstance(scale, V):
            kw["scale"] = scale.ap
            rd.append(scale)
        else:
            kw["scale"] = scale
        wr = [out]
        if accum is not None:
            kw["accum_out"] = accum.ap
            wr.append(accum)
        return self._rec("act", lambda e: e.activation(out.ap, in_.ap, func, **kw), rd, wr)

    def tt(self, eng, out, a, b, op):
        return self._rec(eng, lambda e: e.tensor_tensor(out.ap, a.ap, b.ap, op), [a, b], [out])

    def ts(self, eng, out, a, s1, s2, op0, op1=None, accum=None):
        rd = [a]
        s1a = s1.ap if isinstance(s1, V) else s1
        s2a = s2.ap if isinstance(s2, V) else s2
        if isinstance(s1, V):
            rd.append(s1)
        if isinstance(s2, V):
            rd.append(s2)
        wr = [out]
        kw = {}
        if op1 is not None:
            kw["op1"] = op1
        if accum is not None:
            kw["accum_out"] = accum.ap
            wr.append(accum)
        return self._rec(eng, lambda e: e.tensor_scalar(out.ap, a.ap, s1a, s2a, op0, **kw), rd, wr)

    def stt(self, eng, out, a, s, b, op0, op1):
        rd = [a, b]
        sa = s.ap if isinstance(s, V) else s
        if isinstance(s, V):
            rd.append(s)
        return self._rec(eng, lambda e: e.scalar_tensor_tensor(out.ap, a.ap, sa, b.ap, op0, op1), rd, [out])

    def copy(self, eng, out, in_):
        if eng == "act":
            return self._rec("act", lambda e: e.copy(out.ap, in_.ap), [in_], [out])
        return self._rec(eng, lambda e: e.tensor_copy(out.ap, in_.ap), [in_], [out])

    def memset(self, eng, out, val):
        return self._rec(eng, lambda e: e.memset(out.ap, val), [], [out])

    def reduce(self, eng, out, in_, op, axis=AX.X):
        return self._rec(eng, lambda e: e.tensor_reduce(out.ap, in_.ap, axis, op), [in_], [out])

    def recip(self, out, in_):
        return self._rec("dve", lambda e: e.reciprocal(out.ap, in_.ap), [in_], [out])

    def dma(self, q, out, in_, sem, is_output=False, **kw):
        iid = self._rec(q, lambda e: e.dma_start(out=out.ap, in_=in_.ap, **kw), [in_], [out], dma_sem=sem)
        if is_output:
            self.out_tokens.append(iid)
        return iid

    def wait_for(self, eng, tokens):
        iid = self._rec(eng, None, [], [])
        self.ins[iid]["deps"] = list(tokens)
        for d in tokens:
            self.ins[d]["flag"] = True
        return iid

    def build(self):
        nc = self.nc
        self.wait_for("sp", list(self.out_tokens))
        counts = {}
        for r in self.ins:
            if r["flag"]:
                k = r["semkey"]
                inc = 16 if r["dma_sem"] is not None else 1
                counts[k] = counts.get(k, 0) + inc
                r["val"] = counts[k]
        semh = {}
        for k in counts:
            nm = "s_" + "_".join(str(x) for x in k)
            semh[k] = self.es.enter_context(nc.semaphore(nm))
        per_eng = {e: [] for e in self.ENGS}
        for r in self.ins:
            per_eng[r["eng"]].append(r)
        ins = self.ins
        stats = {e: [0, 0] for e in self.ENGS}

        def run(engname, eobj):
            waited = {}
            for r in per_eng[engname]:
                need = {}
                for d in r["deps"]:
                    pr = ins[d]
                    k = pr["semkey"]
                    v = pr["val"]
                    if waited.get(k, 0) >= v:
                        continue
                    if need.get(k, 0) < v:
                        need[k] = v
                for k, v in need.items():
                    eobj.wait_ge(semh[k], v)
                    waited[k] = v
                    stats[engname][1] += 1
                if r["emit"] is not None:
                    bi = r["emit"](eobj)
                    stats[engname][0] += 1
                    if r["flag"]:
                        bi.then_inc(semh[r["semkey"]], 16 if r["dma_sem"] is not None else 1)

        with nc.Block() as block:
            @block.tensor
            def _(e):
                run("pe", e)

            @block.scalar
            def _(e):
                run("act", e)

            @block.vector
            def _(e):
                run("dve", e)

            @block.gpsimd
            def _(e):
                run("pool", e)

            @block.sync
            def _(e):
                run("sp", e)
        self.stats = stats
        self.sem_counts = counts
        self.es.close()
        return nc


class Arena:
    def __init__(self, p, nbytes):
        self.p = p
        self.nbytes = nbytes
        self.base = p.sbuf("arena", [128, nbytes // 2], BF16)
        self.regs = []
        self.n = 0

    def carve(self, off, shape, dt, name):
        esz = 4 if dt == F32 else 2
        n = int(np.prod(shape[1:]))
        nb = n * esz
        assert off % 4 == 0 and off + nb <= self.nbytes, (name, off, nb, self.nbytes)
        a = self.base.ap[0:shape[0], off // 2:(off + nb) // 2]
        if dt == F32:
            a = a.bitcast(F32)
        if len(shape) == 3:
            a = a.rearrange("p (a b) -> p a b", a=shape[1])
        elif len(shape) == 4:
            a = a.rearrange("p (a b c) -> p a b c", a=shape[1], b=shape[2])
        key = ("ar", name, self.n)
        self.n += 1
        inherit = {}
        for (s, e, k) in self.regs:
            if s < off + nb and off < e:
                st = self.p.res.get(k)
                if st:
                    toks = list(st["r"].values())
                    if st["w"] is not None:
                        toks.append(st["w"])
                    for t in toks:
                        sk = self.p.ins[t]["semkey"]
                        if sk not in inherit or inherit[sk] < t:
                            inherit[sk] = t
        self.regs.append((off, off + nb, key))
        self.p.res[key] = {"w": None, "r": inherit}
        return V(a, key)


def subview(arena, v, tag, ap=None):
    key = (v.res, tag)
    p = arena.p
    par = p.res.get(v.res, {"w": None, "r": {}})
    r = dict(par["r"])
    if par["w"] is not None:
        sk = p.ins[par["w"]]["semkey"]
        if sk not in r or r[sk] < par["w"]:
            r[sk] = par["w"]
    p.res[key] = {"w": None, "r": r}
    for (s_, e_, k_) in list(arena.regs):
        if k_ == v.res:
            arena.regs.append((s_, e_, key))
            break
    return V(v.ap if ap is None else ap, key)


class Bump:
    def __init__(self, arena, start, end=None):
        self.a = arena
        self.off = start
        self.end = end if end is not None else arena.nbytes

    def alloc(self, shape, dt, name):
        esz = 4 if dt == F32 else 2
        nb = int(np.prod(shape[1:])) * esz
        nb4 = (nb + 3) // 4 * 4
        assert self.off + nb4 <= self.end, ("bump overflow", name, self.off, nb4, self.end)
        v = self.a.carve(self.off, shape, dt, name)
        self.off += nb4
        return v


def bc_mid(v, h):
    shp = list(v.ap.shape)
    return V(v.ap.unsqueeze(1).to_broadcast([shp[0], h, shp[1]]), v.res)


def bc_last(v, n):
    shp = list(v.ap.shape)
    return V(v.ap.unsqueeze(2).to_broadcast([shp[0], shp[1], n]), v.res)


class KB:
    ARENA = 204800
    PERS = 34816

    def __init__(self, stage=99):
        self.stage = stage
        p = self.p = Prog()
        self.D = D = {}

        def din(name, shape, dt=F32):
            D[name] = p.dram(name, shape, dt, "ExternalInput")

        def dout(name, shape):
            D[name] = p.dram(name, shape, F32, "ExternalOutput")

        def dscr(name, shape, dt):
            D[name] = p.dram(name, shape, dt, "Internal")

        din("xp", [SEQ, DM]); din("xs", [NS, DM])
        din("cak", [NS, LA, 512]); din("cav", [NS, LA, 512])
        din("cbk", [NS, LB, 128]); din("cbv", [NS, LB, 128])
        din("cmk", [NS, MEM, DM]); din("cmv", [NS, MEM, DM])
        din("sconv", [NS, 2, DFF]); din("memp", [MEM, DM])
        din("g_mix", [DM]); din("w_in", [DM, IN_DIM]); din("g_out_a", [512]); din("g_out_b", [512])
        din("sinks", [8]); din("w_out", [DM, DM]); din("g_cross", [DM]); din("g_mem", [DM])
        din("w_xq", [DM, DM]); din("w_mem_kv", [DM, 2 * DM]); din("w_xo", [DM, DM]); din("g_ffn", [DM])
        din("w_up", [DM, 2 * DFF]); din("conv_w", [3, DFF]); din("conv_b", [DFF]); din("w_down", [DFF, DM])
        din("g_final", [DM])
        din("c_rope", [SEQ, 128]); din("c_ropes", [NS, 128]); din("c_mask", [128, 256])
        din("c_ident", [128, 128]); din("c_sel", [NS, NS * 128])
        dout("yp", [SEQ, DM]); dout("ys", [NS, DM])
        dout("pak", [LA, 512]); dout("pav", [LA, 512]); dout("pbk", [LB, 128]); dout("pbv", [LB, 128])
        dout("pmk", [MEM, DM]); dout("pmv", [MEM, DM]); dout("pconv", [2, DFF])
        dout("sak", [NS, LA, 512]); dout("sav", [NS, LA, 512]); dout("sbk", [NS, LB, 128]); dout("sbv", [NS, LB, 128])
        dout("sconvo", [NS, 2, DFF])
        dscr("scrA", [SEQ, 1536], BF16); dscr("scrB", [SEQ, 768], BF16)
        dscr("scrO", [4, SEQ, 520], F32); dscr("scrX2", [SEQ, DM], F32)

        self.arena = Arena(p, self.ARENA)
        self.ps = p.psum("psall", [128, 4096], F32)
        self.bank_ctr = 0
        self.nb_mod = 6
        self.tb_ctr = 0
        self.pers = Bump(self.arena, 0, self.PERS)
        self.setup_consts()

    def bank(self, i):
        return V(self.ps.ap[:, 512 * i:512 * (i + 1)], ("ps", i))

    def nb(self):
        i = self.bank_ctr % self.nb_mod
        self.bank_ctr += 1
        return self.bank(i)

    def tbank(self):
        j = 6 + self.tb_ctr % 2
        self.tb_ctr += 1
        a = self.ps.ap[:, 512 * j:512 * (j + 1)].bitcast(BF16)
        return V(a.rearrange("p (a b) -> p a b", a=8), ("ps", j))

    def bcast_dram(self, name, n, parts=128):
        t = self.D[name].ap.tensor
        return V(bass.AP(t, 0, [[0, parts], [1, n]]), ("dram", name))

    def setup_consts(self):
        p, D, b = self.p, self.D, self.pers
        self.identf = b.alloc([128, 128], F32, "identf")
        self.ident = b.alloc([128, 128], BF16, "ident")
        self.ones_bf = b.alloc([128, 128], BF16, "ones_bf")
        self.epsb = b.alloc([128, 1], F32, "epsb")
        self.ss = [b.alloc([128, 1], F32, f"ss{i}") for i in range(4)]
        self.ss_ctr = 0
        self.junk = b.alloc([128, DM], BF16, "junk")
        self.g_ffn = b.alloc([128, DM], F32, "g_ffn")
        self.g_final = b.alloc([128, DM], F32, "g_final")
        self.PERS_A = b.off
        self.mask2 = b.alloc([128, 2, 128], BF16, "mask2")
        self.g_mix = b.alloc([128, DM], F32, "g_mix")
        self.g_cross = b.alloc([128, DM], F32, "g_cross")
        self.g_out = b.alloc([128, DM], F32, "g_out")
        self.esink = b.alloc([128, 8], F32, "esink")
        self.ropes = b.alloc([NS, 128], F32, "ropes")
        self.mkT = b.alloc([128, 8, MEM], BF16, "mkT")
        self.mv = b.alloc([128, 2, DM], BF16, "mv")
        p.memset("pool", self.epsb, EPS)
        p.dma("sp", self.identf, D["c_ident"], "ld_c_identf")
        p.dma("pool", self.ident, D["c_ident"], "ld_c_ident")
        p.dma("pool", self.mask2.re("p a b -> p (a b)"), D["c_mask"], "ld_c_mask")
        p.memset("pool", self.ones_bf, 1.0)
        p.dma("sp", self.g_mix, self.bcast_dram("g_mix", DM), "ld_c_gmix")
        p.dma("sp", self.g_cross, self.bcast_dram("g_cross", DM), "ld_c_gcross")
        p.dma("sp", self.g_ffn, self.bcast_dram("g_ffn", DM), "ld_c_gffn")
        p.dma("sp", self.g_final, self.bcast_dram("g_final", DM), "ld_c_gfinal")
        p.dma("sp", self.g_out[:, 0:512], self.bcast_dram("g_out_a", 512), "ld_c_goa")
        p.dma("sp", self.g_out[:, 512:1024], self.bcast_dram("g_out_b", 512), "ld_c_gob")
        p.dma("sp", self.esink, self.bcast_dram("sinks", 8), "ld_c_sinks")
        p.dma("sp", self.ropes, D["c_ropes"], "ld_c_ropes")
        p.act(self.esink, self.esink, AF.Exp)

    def load_w(self, dst, name, kchunks, sem):
        src = self.D[name]
        keys = []
        for kc in range(kchunks):
            sub = subview(self.arena, dst, ("kc", kc), dst.ap[:, kc, :])
            self.p.dma("pool", sub, src[kc * 128:(kc + 1) * 128, :], sem)
            keys.append(sub.res)
        return V(dst.ap, keys)

    def next_ss(self):
        s = self.ss[self.ss_ctr % 4]
        self.ss_ctr += 1
        return s

    def rmsnorm(self, x, g, out, P, Dn):
        p = self.p
        ss = self.next_ss()[0:P, :]
        p.memset("dve", ss, 0.0)
        p.act(self.junk[0:P, 0:Dn], x, AF.Square, accum=ss)
        p.act(ss, ss, AF.Ln, scale=1.0 / Dn, bias=self.epsb[0:P, :])
        p.act(ss, ss, AF.Exp, scale=-0.5)
        p.stt("dve", out, x, ss, g, ALU.mult, ALU.mult)

    def transposes(self, dst, src, P, n, evac=("act", "dve"), ident=None):
        p = self.p
        ident = self.ident if ident is None else ident
        gi = 0
        for g in range(0, n, 8):
            m = min(8, n - g)
            pst = self.tbank()
            for j in range(m):
                p.tr(pst[:, j, 0:P], src[0:P, (g + j) * 128:(g + j + 1) * 128], ident[0:P, 0:P])
            p.copy(evac[gi % len(evac)], dst[:, g:g + m, 0:P], pst[:, 0:m, 0:P])
            gi += 1

    def proj_tok(self, hT, w, P, n_out, kchunks, cb):
        p = self.p
        c0 = 0
        while c0 < n_out:
            n = min(512, n_out - c0)
            bk = self.nb()
            for kc in range(kchunks):
                p.mm(bk[0:P, 0:n], hT[:, kc, 0:P], w[:, kc, c0:c0 + n], start=(kc == 0), stop=(kc == kchunks - 1))
            cb(bk[0:P, 0:n], c0, n)
            c0 += n

    def bulk_copies(self):
        p, D = self.p, self.D
        for n in range(NS):
            p.dma("act", D["sak"][n, 0:LA - 1, :], D["cak"][n, 1:LA, :], "bulk", is_output=True)
            p.dma("act", D["sav"][n, 0:LA - 1, :], D["cav"][n, 1:LA, :], "bulk", is_output=True)
        p.dma("act", D["sbk"][:, 0:LB - 1, :], D["cbk"][:, 1:LB, :], "bulk", is_output=True)
        p.dma("act", D["sbv"][:, 0:LB - 1, :], D["cbv"][:, 1:LB, :], "bulk", is_output=True)
        p.dma("act", D["sconvo"][:, 0, :], D["sconv"][:, 1, :], "bulk", is_output=True)

    def phase0(self):
        p, D = self.p, self.D
        b = Bump(self.arena, self.PERS)
        wkv = b.alloc([128, 8, 2 * DM], BF16, "wkv")
        gm = b.alloc([128, DM], F32, "g_mem")
        wkv = self.load_w(wkv, "w_mem_kv", 8, "ld_w0")
        p.dma("sp", gm, self.bcast_dram("g_mem", DM), "ld_c0_12")
        xm = [b.alloc([128, DM], F32, f"xm{i}") for i in range(2)]
        hb = [b.alloc([128, DM], BF16, f"hbm{i}") for i in range(2)]
        hT = [b.alloc([128, 8, 128], BF16, f"hTm{i}") for i in range(2)]
        kvf = [b.alloc([128, 2 * DM], F32, f"kvf{i}") for i in range(2)]
        kb = [b.alloc([128, DM], BF16, f"kbm{i}") for i in range(2)]
        for mt in range(2):
            rows = slice(mt * 128, (mt + 1) * 128)
            p.dma("sp", xm[mt], D["memp"][rows, :], f"ld_xm{mt}")
            self.rmsnorm(xm[mt], gm, hb[mt], 128, DM)
            self.transposes(hT[mt], hb[mt], 128, 8)

            def cb(bk, c0, n, mt=mt):
                p.copy("act", kvf[mt][:, c0:c0 + n], bk)
            self.proj_tok(hT[mt], wkv, 128, 2 * DM, 8, cb)
            p.dma("sp", D["pmk"][rows, :], kvf[mt][:, 0:DM], f"st_kvfk{mt}", is_output=True)
            p.dma("sp", D["pmv"][rows, :], kvf[mt][:, DM:2 * DM], f"st_kvfv{mt}", is_output=True)
            p.copy("pool", kb[mt], kvf[mt][:, 0:DM])
            self.transposes(self.mkT[:, :, rows], kb[mt], 128, 8)
            p.copy("pool", self.mv[:, mt, :], kvf[mt][:, DM:2 * DM])

    def inproj_front(self, P, x, hb, hT):
        self.rmsnorm(x, self.g_mix[0:P, :], hb, P, DM)
        self.transposes(hT, hb, P, 8)

    def inproj_tile(self, P, x, hb, hT, w_in, cos2, sinS, zqk, zv, zqkB, zvB, tmp, zbA, zbB, want_vf32, want_vBf32,
                    skip_front=False):
        p = self.p
        if not skip_front:
            self.inproj_front(P, x, hb, hT)
        cosb = lambda h: bc_mid(cos2, h)
        sin_lo = lambda h: bc_mid(sinS[:, 0:32], h)
        sin_hi = lambda h: bc_mid(sinS[:, 32:64], h)

        def rope(bk, dst, ncols):
            h = ncols // 64
            src = bk[:, 0:ncols].re("p (h d) -> p h d", d=64)
            d3 = dst.re("p (h d) -> p h d", d=64)
            t3 = tmp[:, 0:ncols].re("p (h d) -> p h d", d=64)
            p.tt("dve", d3, src, cosb(h), ALU.mult)
            p.tt("dve", t3[:, :, 0:32], src[:, :, 32:64], sin_lo(h), ALU.mult)
            p.tt("dve", t3[:, :, 32:64], src[:, :, 0:32], sin_hi(h), ALU.mult)
            p.tt("dve", d3, d3, t3, ALU.add)

        def cb(bk, c0, n):
            if c0 == 0:
                rope(bk, zqk[:, 0:512], 512)
            elif c0 == 512:
                rope(bk, zqk[:, 512:1024], 512)
                if zbA is not None:
                    p.copy("act", zbA[0][:, 0:1024], zqk)
            elif c0 == 1024:
                if zbA is not None:
                    p.copy("act", zbA[1][:, 1024:1536], bk)
                if want_vf32:
                    p.copy("act", zv, bk)
            elif c0 == 1536:
                rope(bk, zqkB[:, 0:512], 512)
            else:
                rope(bk, zqkB[:, 512:640], 128)
                if zbB is not None:
                    p.copy("act", zbB[0][:, 0:512].re("p (f t d) -> p f t d", f=4, t=2),
                           zqkB[:, 0:512].re("p (t f d) -> p f t d", t=2, f=4))
                    p.copy("act", zbB[0][:, 512:640], zqkB[:, 512:640])
                    p.copy("act", zbB[1][:, 640:768], bk[:, 128:256])
                if want_vBf32:
                    p.copy("act", zvB, bk[:, 128:256])
        self.proj_tok(hT, w_in, P, IN_DIM, 8, cb)

    def phase1(self):
        p, D = self.p, self.D
        b = Bump(self.arena, self.PERS)
        w_in = b.alloc([128, 8, IN_DIM], BF16, "w_in")
        w_in = self.load_w(w_in, "w_in", 8, "ld_w1")
        rt = b.alloc([128, NT, 128], F32, "rope_tab")
        p.dma("sp", rt, D["c_rope"].re("(t p) c -> p t c", p=128), "ld_c0_13")
        xt = [b.alloc([128, DM], F32, f"xt{i}") for i in range(3)]
        hb = [b.alloc([128, DM], BF16, f"hb{i}") for i in range(2)]
        hT = [b.alloc([128, 8, 128], BF16, f"hT{i}") for i in range(2)]
        zqk = [b.alloc([128, 1024], F32, f"zqk{i}") for i in range(2)]
        zv = [b.alloc([128, 512], F32, f"zv{i}") for i in range(2)]
        zqkB = [b.alloc([128, 640], F32, f"zqkB{i}") for i in range(2)]
        zvB = b.alloc([128, 128], F32, "zvB")
        tmp = [b.alloc([128, 512], F32, f"tmp{i}") for i in range(2)]
        zbA, zbB = [], []
        for i in range(2):
            a = b.alloc([128, 1536], BF16, f"zbA{i}")
            v1, v2 = subview(self.arena, a, "qk"), subview(self.arena, a, "v")
            zbA.append((v1, v2, V(a.ap, [v1.res, v2.res])))
            a = b.alloc([128, 768], BF16, f"zbB{i}")
            v1, v2 = subview(self.arena, a, "qk"), subview(self.arena, a, "v")
            zbB.append((v1, v2, V(a.ap, [v1.res, v2.res])))
        self.p1_end = b.off
        def xload(t):
            p.dma("sp", xt[t % 3], D["xp"][t * 128:(t + 1) * 128, :], f"ld_xt{t % 3}")

        def front(t):
            self.inproj_front(128, xt[t % 3], hb[t % 2], hT[t % 2])
        xload(0)
        xload(1)
        front(0)
        for t in range(NT):
            rows = slice(t * 128, (t + 1) * 128)
            x = xt[t % 3]
            i = t % 2
            if t + 2 < NT:
                xload(t + 2)
            if t + 1 < NT:
                front(t + 1)
            self.inproj_tile(128, x, hb[i], hT[i], w_in, rt[:, t, 0:64], rt[:, t, 64:128],
                             zqk[i], zv[i], zqkB[i], zvB, tmp[i], zbA[i], zbB[i], t >= NT // 2, t == NT - 1,
                             skip_front=True)
            if t >= NT // 2:
                orow = slice((t - NT // 2) * 128, (t - NT // 2 + 1) * 128)
                p.dma("sp", D["pak"][orow, :], zqk[i][:, 512:1024], f"st_zqk{i}", is_output=True)
                p.dma("sp", D["pav"][orow, :], zv[i], f"st_zv{i}", is_output=True)
            if t == NT - 1:
                p.dma("sp", D["pbk"], zqkB[i][:, 512:640], f"st_zqkB{i}", is_output=True)
                p.dma("sp", D["pbv"], zvB, "st_zvB", is_output=True)
            self.scr_tokens.append(p.dma("sp", D["scrA"][rows, :], zbA[i][2], f"st_zbA{i}"))
            self.scr_tokens.append(p.dma("sp", D["scrB"][rows, :], zbB[i][2], f"st_zbB{i}"))
        sb = self.samp
        xs = sb["xs"]
        p.dma("sp", xs, D["xs"], "ld_xs")
        self.inproj_tile(NS, xs, hb[0][0:NS, :], hT[0], w_in, self.ropes[:, 0:64], self.ropes[:, 64:128],
                         sb["zqk"], sb["zv"], sb["zqkB"], sb["zvB"], tmp[0][0:NS, :], None, None, True, True)
        p.dma("sp", D["sak"][:, LA - 1, :], sb["zqk"][:, 512:1024], "st_s1a", is_output=True)
        p.dma("sp", D["sav"][:, LA - 1, :], sb["zv"], "st_s1b", is_output=True)
        p.dma("sp", D["sbk"][:, LB - 1, :], sb["zqkB"][:, 512:640], "st_s1c", is_output=True)
        p.dma("sp", D["sbv"][:, LB - 1, :], sb["zvB"], "st_s1d", is_output=True)

    def phase2(self):
        p, D = self.p, self.D
        b = Bump(self.arena, self.PERS + 3 * 16384, self.PH_END)
        blk = [b.alloc([128, 1536], BF16, f"blk{i}") for i in range(3)]
        QT = [b.alloc([128, 4, 2, 128], BF16, f"QZ{i}") for i in range(2)]
        for v in QT:
            p.memset("pool", v, 0.0)
        KT = [b.alloc([128, 4, 128], BF16, f"KT{i}") for i in range(3)]
        VX = [b.alloc([128, 8, 65], BF16, f"VX{i}") for i in range(3)]
        PT = [b.alloc([128, 2, 2, 128], BF16, f"PT{i}") for i in range(8)]
        OS = [b.alloc([128, 520], F32, f"OS{i}") for i in range(3)]
        for v in VX:
            p.memset("pool", v, 1.0)
        p.wait_for("sp", list(self.scr_tokens))
        mask4 = V(self.mask2.ap.unsqueeze(2).to_broadcast([128, 2, 2, 128]), self.mask2.res)
        mask_own = bc_mid(self.mask2[:, 0, :], 2)
        st = {"s": 0, "pc": 0}
        self.o_tokens = []

        def st_dma(c):
            kind, br, d, r, bb, first, s = c
            cur = blk[s % 3]
            if kind == "A":
                src = D["scrA"].re("(j r) c -> r j c", r=d)[r, 128 * bb:128 * (bb + 1), :]
                p.dma("sp", cur, src, f"ld_blk{s % 3}")
            else:
                src = D["scrB"][128 * bb:128 * (bb + 1), :]
                p.dma("sp", cur[:, 0:768], src, f"ld_blk{s % 3}")

        def st_load(c):
            kind, br, d, r, bb, first, s = c
            cur = blk[s % 3]
            qt, kt, vx = QT[s % 2], KT[s % 3], VX[s % 3]
            pq = self.tbank()
            for j in range(4):
                p.tr(pq[:, j, :], cur[:, j * 128:(j + 1) * 128], self.ident)
            p.copy("dve", qt[0:64, :, 0, :], pq[0:64, 0:4, :])
            p.copy("dve", qt[64:128, :, 1, :], pq[64:128, 0:4, :])
            if kind == "A":
                self.transposes(kt, cur[:, 512:1024], 128, 4, evac=("act",))
                p.copy("pool", vx[:, :, 0:64], cur[:, 1024:1536].re("p (h d) -> p h d", d=64))
            else:
                self.transposes(kt[:, 0:1, :], cur[:, 512:640], 128, 1, evac=("act",))
                p.copy("pool", vx[:, 0:2, 0:64], cur[:, 640:768].re("p (h d) -> p h d", d=64))

        def st_scores(c):
            kind, br, d, r, bb, first, s = c
            qt, kt, ktp = QT[s % 2], KT[s % 3], KT[(s - 1) % 3]
            for j in range(4):
                psS = self.bank(j)
                kj = j if kind == "A" else 0
                q2 = qt[:, j].re("p a q -> p (a q)")
                p.mm(psS[:, 0:256], kt[:, kj, :], q2)
                if not first:
                    p.mm(psS[:, 256:512], ktp[:, kj, :], q2)
                pt = PT[(4 * s + j) % 8]
                if not first:
                    p.act(pt.re("p b a q -> p (b a q)"), psS, AF.Exp, scale=0.125)
                    p.tt("dve", pt, pt, mask4, ALU.mult)
                else:
                    p.act(pt[:, 0].re("p a q -> p (a q)"), psS[:, 0:256], AF.Exp, scale=0.125)
                    p.tt("dve", pt[:, 0], pt[:, 0], mask_own, ALU.mult)

        def st_pv(c):
            kind, br, d, r, bb, first, s = c
            vx, vxp = VX[s % 3], VX[(s - 1) % 3]
            psO = [self.bank(4), self.bank(5)]
            for j in range(4):
                pt = PT[(4 * s + j) % 8]
                for hh in range(2):
                    if kind == "A":
                        h = 2 * j + hh
                        vi = h
                    else:
                        h = j + 4 * hh
                        vi = hh
                    o = psO[h // 4][:, (h % 4) * 65:(h % 4) * 65 + 65]
                    p.mm(o, pt[:, 0, hh, :], vx[:, vi, :], start=True, stop=first)
                    if not first:
                        p.mm(o, pt[:, 1, hh, :], vxp[:, vi, :], start=False, stop=True)
            osb = OS[s % 3]
            p.copy("act", osb[:, 0:260], psO[0][:, 0:260])
            p.copy("act", osb[:, 260:520], psO[1][:, 0:260])
            if kind == "A":
                dst = D["scrO"][br].re("(j r) c -> r j c", r=d)[r, 128 * bb:128 * (bb + 1), :]
            else:
                dst = D["scrO"][3][128 * bb:128 * (bb + 1), :]
            self.o_tokens.append(p.dma("sp", dst, osb, f"st_os{s % 3}"))

        cfgs = []
        for br, d in ((2, 16), (1, 4), (0, 1)):
            nblk = SEQ // d // 128
            for r in range(d):
                for bb in range(nblk):
                    cfgs.append(("A", br, d, r, bb, bb == 0, len(cfgs)))
        for bb in range(NT):
            cfgs.append(("B", 3, 1, 0, bb, bb == 0, len(cfgs)))
        st_dma(cfgs[0])
        st_dma(cfgs[1])
        st_load(cfgs[0])
        for i, c in enumerate(cfgs):
            if i + 2 < len(cfgs):
                st_dma(cfgs[i + 2])
            st_scores(c)
            if i + 1 < len(cfgs):
                st_load(cfgs[i + 1])
            st_pv(c)

    def prefetch_w2(self):
        b = Bump(self.arena, self.PERS, self.PERS + 3 * 16384)
        w_out = b.alloc([128, 8, DM], BF16, "w_out")
        w_xq = b.alloc([128, 8, DM], BF16, "w_xq")
        w_xo = b.alloc([128, 8, DM], BF16, "w_xo")
        w_out = self.load_w(w_out, "w_out", 8, "ld_w2a")
        w_xq = self.load_w(w_xq, "w_xq", 8, "ld_w2b")
        w_xo = self.load_w(w_xo, "w_xo", 8, "ld_w2c")
        self.w2 = (w_out, w_xq, w_xo)

    def phase2b(self):
        p, D = self.p, self.D
        b = Bump(self.arena, self.PERS + 3 * 16384, self.PH_END)
        w_out, w_xq, w_xo = self.w2
        OL = [[b.alloc([128, 520], F32, f"OL{i}_{k}") for k in range(4)] for i in range(2)]
        cat = [b.alloc([128, DM], F32, "cat0")] * 2
        hm = [b.alloc([128, DM], BF16, f"hm{i}") for i in range(2)]
        rd = [b.alloc([128, 16], F32, f"rd{i}") for i in range(2)]
        x1 = [b.alloc([128, DM], F32, f"x1_{i}") for i in range(4)]
        hmT = b.alloc([128, 8, 512], BF16, "hmT")
        h2T = b.alloc([128, 8, 512], BF16, "h2T")
        qxT = b.alloc([128, 8, 512], BF16, "qxT")
        oxT = b.alloc([128, 8, 512], BF16, "oxT")
        h2 = [b.alloc([128, DM], BF16, f"h2_{i}") for i in range(2)]
        PTx = [b.alloc([128, 2, 512], BF16, f"PTx{i}") for i in range(2)]
        rden = [b.alloc([128, 512], F32, f"rden{i}") for i in range(2)]
        self.p2b_end = b.off
        hmTb = [hmT, b.alloc([128, 8, 512], BF16, "hmT1")]
        p.wait_for("sp", list(self.o_tokens))
        self.x2_tokens = []

        hm4 = hm + [b.alloc([128, DM], BF16, f"hm{i}") for i in range(2, 4)]

        def ol_load(t):
            rows = slice(t * 128, (t + 1) * 128)
            ol = OL[t % 2]
            for k in range(4):
                p.dma("sp", ol[k], D["scrO"][k][rows, :], f"ld_ol{t % 2}_{k}")

        def front_norm_tile(sti, tl):
            t = sti * 4 + tl
            if t + 1 < NT:
                ol_load(t + 1)
            self.combine(128, OL[t % 2], rd[t % 2], cat[t % 2], hm4[tl], self.esink)

        def front_T(sti):
            for tl in range(4):
                self.transposes(hmTb[sti % 2][:, :, tl * 128:(tl + 1) * 128], hm4[tl], 128, 8)

        def mid(sti):
            hmT_ = hmTb[sti % 2]
            for tl in range(4):
                t = sti * 4 + tl
                rows = slice(t * 128, (t + 1) * 128)
                p.dma("sp", x1[tl], D["xp"][rows, :], f"ld_x1_{tl}")

                def cb(bk, c0, n, tl=tl):
                    p.tt("dve", x1[tl][:, c0:c0 + n], x1[tl][:, c0:c0 + n], bk, ALU.add)
                self.proj_tok(hmT_[:, :, tl * 128:(tl + 1) * 128], w_out, 128, DM, 8, cb)
            for tl in range(4):
                self.rmsnorm(x1[tl], self.g_cross, h2[tl % 2], 128, DM)
                self.transposes(h2T[:, :, tl * 128:(tl + 1) * 128], h2[tl % 2], 128, 8)
            for fc in range(8):
                bk = self.nb()
                for kc in range(8):
                    p.mm(bk, w_xq[:, kc, fc * 128:(fc + 1) * 128], h2T[:, kc, :], start=(kc == 0), stop=(kc == 7))
                p.copy("act" if fc % 2 == 0 else "dve", qxT[:, fc, :], bk)
            for h in range(4):
                pt = PTx[h % 2]
                for mc in range(2):
                    bk = self.nb()
                    for j in range(2):
                        p.mm(bk, self.mkT[:, 2 * h + j, mc * 128:(mc + 1) * 128], qxT[:, 2 * h + j, :],
                             start=(j == 0), stop=(j == 1))
                    p.act(pt[:, mc, :], bk, AF.Exp, scale=1.0 / 16.0)
                bd = self.nb()
                for mc in range(2):
                    p.mm(bd, self.ones_bf, pt[:, mc, :], start=(mc == 0), stop=(mc == 1))
                rdn = rden[h % 2]
                p.act(rdn, bd, AF.Ln)
                p.act(rdn, rdn, AF.Exp, scale=-1.0)
                for dj in range(2):
                    bk = self.nb()
                    for mc in range(2):
                        p.mm(bk, self.mv[:, mc, h * 256 + dj * 128:h * 256 + (dj + 1) * 128], pt[:, mc, :],
                             start=(mc == 0), stop=(mc == 1))
                    p.tt("dve", oxT[:, 2 * h + dj, :], bk, rdn, ALU.mult)

        def back_tile(sti, tl):
            t = sti * 4 + tl
            rows = slice(t * 128, (t + 1) * 128)

            def cb(bk, c0, n, tl=tl):
                p.tt("dve", x1[tl][:, c0:c0 + n], x1[tl][:, c0:c0 + n], bk, ALU.add)
            self.proj_tok(oxT[:, :, tl * 128:(tl + 1) * 128], w_xo, 128, DM, 8, cb)
            self.x2_tokens.append(p.dma("sp", D["scrX2"][rows, :], x1[tl], f"st_x1_{tl}"))

        nsup = NT // 4
        ol_load(0)
        for tl in range(4):
            front_norm_tile(0, tl)
        front_T(0)
        for sti in range(nsup):
            mid(sti)
            for tl in range(4):
                if sti + 1 < nsup:
                    front_norm_tile(sti + 1, tl)
                back_tile(sti, tl)
            if sti + 1 < nsup:
                front_T(sti + 1)

    def combine(self, P, ol, r, c, hmv, esink):
        p = self.p
        if ol[1] is not None:
            p.tt("pool", ol[0], ol[0], ol[1], ALU.add)
            p.tt("pool", ol[0], ol[0], ol[2], ALU.add)
        a3 = ol[0].re("p (h c) -> p h c", c=65)
        b3 = ol[3].re("p (h c) -> p h c", c=65)
        p.recip(r[:, 0:8], a3[:, :, 64])
        p.tt("dve", c[:, 0:512].re("p (h d) -> p h d", d=64), a3[:, :, 0:64], bc_last(r[:, 0:8], 64), ALU.mult)
        p.tt("dve", r[:, 8:16], b3[:, :, 64], esink[0:P, :], ALU.add)
        p.recip(r[:, 8:16], r[:, 8:16])
        p.tt("dve", c[:, 512:1024].re("p (h d) -> p h d", d=64), b3[:, :, 0:64], bc_last(r[:, 8:16], 64), ALU.mult)
        self.rmsnorm(c[:, 0:512], self.g_out[0:P, 0:512], hmv[:, 0:512], P, 512)
        self.rmsnorm(c[:, 512:1024], self.g_out[0:P, 512:1024], hmv[:, 512:1024], P, 512)

    def phase3(self):
        p, D = self.p, self.D
        b = Bump(self.arena, self.PERS_A, self.ARENA - 4096)
        w_up = b.alloc([128, 8, 2 * DFF], BF16, "w_up")
        w_dn = b.alloc([128, NFC, DM], BF16, "w_dn")
        w_up = self.load_w(w_up, "w_up", 8, "ld_w3a")
        w_dn = self.load_w(w_dn, "w_down", NFC, "ld_w3b")
        self.w3 = (w_up, w_dn)
        cw = b.alloc([128, 3, NFC], F32, "cw")
        cbias = b.alloc([128, NFC], F32, "cbias")
        for i3 in range(3):
            p.dma("sp", cw[:, i3, :], D["conv_w"][i3].re("(fc f) -> f fc", f=128), "ld_cw",
                  allow_slow_non_contiguous=True)
        p.dma("sp", cbias, D["conv_b"].re("(fc f) -> f fc", f=128), "ld_cb", allow_slow_non_contiguous=True)
        gcar = b.alloc([128, NFC, 2], F32, "gcar")
        p.memset("pool", gcar, 0.0)
        ST = 256
        self.p3_tmp_start = b.off
        xl = [b.alloc([128, DM], F32, f"xl{i}") for i in range(4)]
        h3 = [b.alloc([128, DM], BF16, f"h3_{i}") for i in range(2)]
        h3T = b.alloc([128, 8, ST], BF16, "h3T")
        aT = b.alloc([128, NFC, ST], BF16, "aT")
        gsb = [b.alloc([128, ST + 2], F32, f"gsb{i}") for i in range(2)]
        tq = [b.alloc([128, ST], F32, f"tq{i}") for i in range(2)]
        sq = [b.alloc([128, ST], F32, f"sq{i}") for i in range(2)]
        yt = [b.alloc([128, DM], F32, f"yt{i}") for i in range(1)]
        h3Tb = [h3T, b.alloc([128, 8, ST], BF16, "h3T1")]
        p.wait_for("sp", list(self.x2_tokens))

        def xload(s_):
            for tl in range(2):
                t = 2 * s_ + tl
                xi = (s_ % 2) * 2 + tl
                p.dma("sp", xl[xi], D["scrX2"][t * 128:(t + 1) * 128, :], f"ld_xl{xi}")

        def front_norm(s_):
            for tl in range(2):
                xi = (s_ % 2) * 2 + tl
                self.rmsnorm(xl[xi], self.g_ffn, h3[tl], 128, DM)

        def front_T(s_):
            for tl in range(2):
                self.transposes(h3Tb[s_ % 2][:, :, tl * 128:(tl + 1) * 128], h3[tl], 128, 8)

        def mid(s_):
            hT_ = h3Tb[s_ % 2]
            for fc in range(NFC):
                bg = self.nb()
                bv = self.nb()
                for kc in range(8):
                    p.mm(bg[:, 0:ST], w_up[:, kc, fc * 128:(fc + 1) * 128], hT_[:, kc, :], start=(kc == 0), stop=(kc == 7))
                for kc in range(8):
                    p.mm(bv[:, 0:ST], w_up[:, kc, DFF + fc * 128:DFF + (fc + 1) * 128], hT_[:, kc, :],
                         start=(kc == 0), stop=(kc == 7))
                g = gsb[fc % 2]
                tt_ = tq[fc % 2]
                ss_ = sq[fc % 2]
                p.copy("pool", g[:, 0:2], gcar[:, fc, :])
                p.copy("act", g[:, 2:ST + 2], bg[:, 0:ST])
                p.copy("pool", gcar[:, fc, :], g[:, ST:ST + 2])
                p.ts("dve", tt_, g[:, 0:ST], cw[:, 0, fc:fc + 1], cbias[:, fc:fc + 1], ALU.mult, ALU.add)
                p.stt("dve", tt_, g[:, 1:ST + 1], cw[:, 1, fc:fc + 1], tt_, ALU.mult, ALU.add)
                p.stt("dve", tt_, g[:, 2:ST + 2], cw[:, 2, fc:fc + 1], tt_, ALU.mult, ALU.add)
                p.act(ss_, tt_, AF.Silu)
                p.tt("dve", aT[:, fc, :], ss_, bv[:, 0:ST], ALU.mult)

        def back(s_):
            for tl in range(2):
                t = 2 * s_ + tl
                rows = slice(t * 128, (t + 1) * 128)
                x = xl[(s_ % 2) * 2 + tl]

                def cb(bk, c0, n, x=x):
                    p.tt("dve", x[:, c0:c0 + n], x[:, c0:c0 + n], bk, ALU.add)
                self.proj_tok(aT[:, :, tl * 128:(tl + 1) * 128], w_dn, 128, DM, NFC, cb)
                y = yt[0]
                self.rmsnorm(x, self.g_final, y, 128, DM)
                p.dma("sp", D["yp"][rows, :], y, "st_y0", is_output=True)

        nsup = SEQ // ST
        xload(0)
        front_norm(0)
        front_T(0)
        for s_ in range(nsup):
            if s_ + 1 < nsup:
                xload(s_ + 1)
            mid(s_)
            if s_ + 1 < nsup:
                front_norm(s_ + 1)
            back(s_)
            if s_ + 1 < nsup:
                front_T(s_ + 1)
        for ti in range(2):
            p.dma("sp", D["pconv"][ti].re("(fc f) -> f fc", f=128), gcar[:, :, ti], "st_pconv", is_output=True,
                  allow_slow_non_contiguous=True)

    def samp_attn(self):
        p, D, sb = self.p, self.D, self.samp
        b = Bump(self.arena, self.PERS + 3 * 16384, self.PH_END)
        sel = b.alloc([NS, NS, 128], F32, "sel")
        p.dma("sp", sel.re("p a b -> p (a b)"), D["c_sel"], "ld_sel")
        qb = [b.alloc([128, 512], F32, f"s_qb{i}") for i in range(2)]
        Kt = [b.alloc([128, 512], F32, f"s_Kt{i}") for i in range(3)]
        Vt = [b.alloc([128, 8, 65], F32, f"s_Vt{i}") for i in range(3)]
        prod = [b.alloc([128, 512], F32, f"s_prod{i}") for i in range(2)]
        sc = [b.alloc([128, 8], F32, f"s_sc{i}") for i in range(2)]
        Pz = [b.alloc([128, 8, NS], F32, f"s_Pz{i}") for i in range(4)]
        oA = b.alloc([NS, 8, 65], F32, "s_oA")
        oB = b.alloc([NS, 8, 65], F32, "s_oB")
        prn = b.alloc([NS, 512], F32, "s_prn")
        sn = b.alloc([NS, 8], F32, "s_sn")
        en = b.alloc([NS, 8], F32, "s_en")
        tv = b.alloc([NS, 8, 64], F32, "s_tv")
        r16 = b.alloc([NS, 16], F32, "s_r16")
        for v in Vt:
            p.memset("pool", v, 1.0)
        for v in Pz:
            p.memset("pool", v, 0.0)
        p.memset("pool", oA, 0.0)
        p.memset("pool", oB, 0.0)
        cnt = 0
        for n in range(NS):
            bk = self.nb()
            p.mm(bk, sel[:, n, :], sb["zqk"][:, 0:512])
            q = qb[n % 2]
            p.copy("act", q, bk)
            for g in range(3):
                if g == 0:
                    ksrc = D["cak"][n, LA - 128:LA, :]
                    vsrc = D["cav"][n, LA - 128:LA, :]
                elif g == 1:
                    ksrc = D["cak"][n].re("(j r) c -> r j c", r=4)[0, 384:512, :]
                    vsrc = D["cav"][n].re("(j r) c -> r j c", r=4)[0, 384:512, :]
                else:
                    ksrc = D["cak"][n].re("(j r) c -> r j c", r=16)[0, 0:128, :]
                    vsrc = D["cav"][n].re("(j r) c -> r j c", r=16)[0, 0:128, :]
                kt, vt = Kt[cnt % 3], Vt[cnt % 3]
                p.dma("sp", kt, ksrc, f"ld_sK{cnt % 3}")
                p.dma("sp", vt[:, :, 0:64], vsrc.re("j (h d) -> j h d", d=64), f"ld_sV{cnt % 3}")
                pr = prod[cnt % 2]
                p.tt("dve", pr, kt, q, ALU.mult)
                s8 = sc[cnt % 2]
                p.reduce("dve", s8, pr.re("p (h d) -> p h d", d=64), ALU.add)
                pz = Pz[cnt % 4]
                p.act(pz[:, :, n], s8, AF.Exp, scale=0.125)
                psA = [self.nb(), self.nb()]
                for h in range(8):
                    o = psA[h // 4][0:NS, (h % 4) * 65:(h % 4) * 65 + 65]
                    p.mm(o, pz[:, h, :], vt[:, h, :])
                for hb in range(2):
                    av = oA[:, 4 * hb:4 * hb + 4, :]
                    p.tt("dve", av, av, psA[hb][0:NS, 0:260].re("p (h c) -> p h c", c=65), ALU.add)
                p.memset("pool", pz[:, :, n], 0.0)
                cnt += 1
        zqk, zv = sb["zqk"], sb["zv"]
        p.tt("dve", prn, zqk[:, 0:512], zqk[:, 512:1024], ALU.mult)
        p.reduce("dve", sn, prn.re("p (h d) -> p h d", d=64), ALU.add)
        p.act(en, sn, AF.Exp, scale=0.125)
        p.ts("dve", en, en, 3.0, None, ALU.mult)
        p.tt("dve", tv, zv.re("p (h d) -> p h d", d=64), bc_last(en, 64), ALU.mult)
        p.tt("dve", oA[:, :, 0:64], oA[:, :, 0:64], tv, ALU.add)
        p.tt("dve", oA[:, :, 64], oA[:, :, 64], en, ALU.add)
        KtB = [V(k.ap[:, 0:128], k.res) for k in Kt]
        VtB = [V(v.ap[:, 0:2, :], v.res) for v in Vt]
        zqkB, zvB = sb["zqkB"], sb["zvB"]
        for n in range(NS):
            bk = self.nb()
            p.mm(bk, sel[:, n, :], zqkB[:, 0:512])
            q = qb[n % 2]
            p.copy("act", q, bk)
            kt, vt = KtB[cnt % 3], VtB[cnt % 3]
            p.dma("sp", kt, D["cbk"][n], f"ld_sK{cnt % 3}")
            p.dma("sp", vt[:, :, 0:64], D["cbv"][n].re("j (h d) -> j h d", d=64), f"ld_sV{cnt % 3}")
            pr = prod[cnt % 2]
            k4 = V(kt.ap.rearrange("p (k d) -> p k d", d=64).unsqueeze(2).to_broadcast([128, 2, 4, 64]), kt.res)
            p.tt("dve", pr.re("p (k g d) -> p k g d", k=2, g=4), q.re("p (k g d) -> p k g d", k=2, g=4), k4, ALU.mult)
            s8 = sc[cnt % 2]
            p.reduce("dve", s8, pr.re("p (h d) -> p h d", d=64), ALU.add)
            pz = Pz[cnt % 4]
            p.act(pz[:, :, n], s8, AF.Exp, scale=0.125)
            psA = [self.nb(), self.nb()]
            for h in range(8):
                o = psA[h // 4][0:NS, (h % 4) * 65:(h % 4) * 65 + 65]
                p.mm(o, pz[:, h, :], vt[:, h // 4, :])
            for hb in range(2):
                av = oB[:, 4 * hb:4 * hb + 4, :]
                p.tt("dve", av, av, psA[hb][0:NS, 0:260].re("p (h c) -> p h c", c=65), ALU.add)
            p.memset("pool", pz[:, :, n], 0.0)
            cnt += 1
        kn4 = V(zqkB.ap[:, 512:640].rearrange("p (k d) -> p k d", d=64).unsqueeze(2).to_broadcast([NS, 2, 4, 64]), zqkB.res)
        p.tt("dve", prn.re("p (k g d) -> p k g d", k=2, g=4), zqkB[:, 0:512].re("p (k g d) -> p k g d", k=2, g=4), kn4, ALU.mult)
        p.reduce("dve", sn, prn.re("p (h d) -> p h d", d=64), ALU.add)
        p.act(en, sn, AF.Exp, scale=0.125)
        vn4 = V(zvB.ap.rearrange("p (k d) -> p k d", d=64).unsqueeze(2).to_broadcast([NS, 2, 4, 64]), zvB.res)
        e4 = V(en.ap.rearrange("p (k g) -> p k g", k=2).unsqueeze(3).to_broadcast([NS, 2, 4, 64]), en.res)
        p.tt("dve", tv.re("p (k g) d -> p k g d", k=2), vn4, e4, ALU.mult)
        p.tt("dve", oB[:, :, 0:64], oB[:, :, 0:64], tv, ALU.add)
        p.tt("dve", oB[:, :, 64], oB[:, :, 64], en, ALU.add)
        self.combine(NS, [oA.re("p h c -> p (h c)"), None, None, oB.re("p h c -> p (h c)")], r16, sb["cat"], sb["hm"],
                     self.esink)

    def samp_mix(self):
        p, D, sb = self.p, self.D, self.samp
        w_out, w_xq, w_xo = self.w2
        b = Bump(self.arena, self.PERS + 3 * 16384, self.PH_END)
        sel = b.alloc([NS, NS, 128], F32, "sel2")
        p.dma("sp", sel.re("p a b -> p (a b)"), D["c_sel"], "ld_sel2")
        hT = b.alloc([128, 8, NS], BF16, "s_hT")
        h2s = b.alloc([NS, DM], BF16, "s_h2")
        qx = b.alloc([NS, DM], F32, "s_qx")
        qbx = [b.alloc([128, DM], F32, f"s_qbx{i}") for i in range(2)]
        Kx = [b.alloc([128, DM], F32, f"s_Kx{i}") for i in range(2)]
        Vx = [b.alloc([128, 4, 257], F32, f"s_Vx{i}") for i in range(2)]
        prodx = b.alloc([128, DM], F32, "s_prodx")
        s4 = [b.alloc([128, 4], F32, f"s_s4{i}") for i in range(2)]
        Pzx = [b.alloc([128, 4, NS], F32, f"s_Pzx{i}") for i in range(4)]
        oX = b.alloc([NS, 4, 257], F32, "s_oX")
        r4 = b.alloc([NS, 4], F32, "s_r4")
        oxn = b.alloc([NS, DM], BF16, "s_oxn")
        xs = sb["xs"]
        self.transposes(hT, sb["hm"], NS, 8)

        def cb(bk, c0, n):
            p.tt("dve", xs[:, c0:c0 + n], xs[:, c0:c0 + n], bk, ALU.add)
        self.proj_tok(hT, w_out, NS, DM, 8, cb)
        self.rmsnorm(xs, self.g_cross[0:NS, :], h2s, NS, DM)
        self.transposes(hT, h2s, NS, 8)

        def cb2(bk, c0, n):
            p.copy("act", qx[:, c0:c0 + n], bk)
        self.proj_tok(hT, w_xq, NS, DM, 8, cb2)
        for v in Vx:
            p.memset("pool", v, 1.0)
        for v in Pzx:
            p.memset("pool", v, 0.0)
        p.memset("pool", oX, 0.0)
        cnt = 0
        for n in range(NS):
            q = qbx[n % 2]
            for half in range(2):
                bk = self.nb()
                p.mm(bk, sel[:, n, :], qx[:, half * 512:(half + 1) * 512])
                p.copy("act", q[:, half * 512:(half + 1) * 512], bk)
            for mc in range(2):
                kx, vx = Kx[cnt % 2], Vx[cnt % 2]
                rows = slice(mc * 128, (mc + 1) * 128)
                p.dma("sp", kx, D["cmk"][n, rows, :], f"ld_sKx{cnt % 2}")
                p.dma("sp", vx[:, :, 0:256], D["cmv"][n, rows, :].re("j (h d) -> j h d", d=256), f"ld_sVx{cnt % 2}")
                p.tt("dve", prodx, kx, q, ALU.mult)
                s_ = s4[cnt % 2]
                p.reduce("dve", s_, prodx.re("p (h d) -> p h d", d=256), ALU.add)
                pz = Pzx[cnt % 4]
                p.act(pz[:, :, n], s_, AF.Exp, scale=1.0 / 16.0)
                for h in range(4):
                    bo = self.nb()
                    p.mm(bo[0:NS, 0:257], pz[:, h, :], vx[:, h, :])
                    p.tt("dve", oX[:, h, :], oX[:, h, :], bo[0:NS, 0:257], ALU.add)
                p.memset("pool", pz[:, :, n], 0.0)
                cnt += 1
        p.recip(r4, oX[:, :, 256])
        p.tt("dve", oxn.re("p (h d) -> p h d", d=256), oX[:, :, 0:256], bc_last(r4, 256), ALU.mult)
        self.transposes(hT, oxn, NS, 8)
        self.proj_tok(hT, w_xo, NS, DM, 8, cb)

    def samp_ffn(self):
        p, D, sb = self.p, self.D, self.samp
        w_up, w_dn = self.w3
        b = Bump(self.arena, self.p3_tmp_start, self.ARENA - 4096)
        h3s = b.alloc([NS, DM], BF16, "s_h3")
        hT = b.alloc([128, 8, NS], BF16, "s_h3T")
        gs = b.alloc([NS, DFF], F32, "s_gs")
        vs = b.alloc([NS, DFF], F32, "s_vs")
        abf = b.alloc([NS, DFF], BF16, "s_abf")
        aT = b.alloc([128, NFC, NS], BF16, "s_aT")
        ysb = b.alloc([NS, DM], F32, "s_ysb")
        xs = sb["xs"]
        self.rmsnorm(xs, self.g_ffn[0:NS, :], h3s, NS, DM)
        self.transposes(hT, h3s, NS, 8)

        def cbu(bk, c0, n):
            lo, hi = c0, c0 + n
            if lo < DFF:
                m = min(hi, DFF) - lo
                p.copy("act", gs[:, lo:lo + m], bk[:, 0:m])
            if hi > DFF:
                s0 = max(lo, DFF)
                p.copy("act", vs[:, s0 - DFF:hi - DFF], bk[:, s0 - lo:n])
        self.proj_tok(hT, w_up, NS, 2 * DFF, 8, cbu)
        p.dma("sp", D["sconvo"][:, 1, :], gs, "st_sgs", is_output=True)
        b2 = Bump(self.arena, self.PERS_A, self.PERS_A + 90112)
        s0t = b2.alloc([NS, DFF], F32, "s_s0")
        s1t = b2.alloc([NS, DFF], F32, "s_s1")
        cwb = b2.alloc([NS, 3, DFF], F32, "s_cwb")
        cbb = b2.alloc([NS, DFF], F32, "s_cbb")
        t1 = b2.alloc([NS, DFF], F32, "s_t1")
        t2 = b2.alloc([NS, DFF], F32, "s_t2")
        p.dma("sp", s0t, D["sconv"][:, 0, :], "ld_ss0")
        p.dma("sp", s1t, D["sconv"][:, 1, :], "ld_ss1")
        tcw = D["conv_w"].ap.tensor
        p.dma("sp", cwb.re("p a b -> p (a b)"), V(bass.AP(tcw, 0, [[0, NS], [1, 3 * DFF]]), ("dram", "conv_w")), "ld_scw")
        p.dma("sp", cbb, self.bcast_dram("conv_b", DFF, NS), "ld_scb")
        p.tt("dve", t1, s0t, cwb[:, 0, :], ALU.mult)
        p.tt("pool", t2, s1t, cwb[:, 1, :], ALU.mult)
        p.tt("dve", t1, t1, t2, ALU.add)
        p.tt("pool", t2, gs, cwb[:, 2, :], ALU.mult)
        p.tt("dve", t1, t1, t2, ALU.add)
        p.tt("dve", t1, t1, cbb, ALU.add)
        p.act(t2, t1, AF.Silu)
        p.tt("dve", abf, t2, vs, ALU.mult)
        self.transposes(aT, abf, NS, NFC)

        def cb(bk, c0, n):
            p.tt("dve", xs[:, c0:c0 + n], xs[:, c0:c0 + n], bk, ALU.add)
        self.proj_tok(aT, w_dn, NS, DM, NFC, cb)
        self.rmsnorm(xs, self.g_final[0:NS, :], ysb, NS, DM)
        p.dma("sp", D["ys"], ysb, "st_ys", is_output=True)

    def alloc_sample(self):
        top = Bump(self.arena, self.ARENA - 4096)
        b = Bump(self.arena, self.ARENA - 20480, self.ARENA - 4096)
        self.samp = {
            "xs": top.alloc([NS, DM], F32, "s_xs"),
            "zqk": b.alloc([NS, 1024], F32, "s_zqk"),
            "zv": b.alloc([NS, 512], F32, "s_zv"),
            "zqkB": b.alloc([NS, 640], F32, "s_zqkB"),
            "zvB": b.alloc([NS, 128], F32, "s_zvB"),
            "cat": b.alloc([NS, DM], F32, "s_cat"),
            "hm": b.alloc([NS, DM], BF16, "s_hm"),
        }
        self.samp_bump = b
        self.PH_END = self.ARENA - 20480

    def build(self):
        self.scr_tokens = []
        self.alloc_sample()
        self.bulk_copies()
        self.phase0()
        self.phase1()
        if self.stage >= 2:
            self.prefetch_w2()
            self.samp_attn()
            self.phase2()
        if self.stage >= 3:
            self.phase2b()
            self.samp_mix()
        if self.stage >= 4:
            self.phase3()
            self.samp_ffn()
        return self.p.build()


def make_consts():
    half = 32
    inv = np.power(np.float32(10000.0), -np.arange(half, dtype=np.float32) / np.float32(half)).astype(np.float32)

    def tab(pos):
        ang = pos.astype(np.float32)[:, None] * inv[None, :]
        c = np.cos(ang).astype(np.float32)
        s = np.sin(ang).astype(np.float32)
        return np.concatenate([c, c, -s, s], axis=1).astype(np.float32)
    c_rope = tab(np.arange(SEQ))
    c_ropes = tab(np.full((NS,), PAST))
    k = np.arange(128)[:, None]
    q = np.arange(128)[None, :]
    own = (k <= q).astype(np.float32)
    prev = (k >= q).astype(np.float32)
    c_mask = np.concatenate([own, prev], axis=1).astype(np.float32)
    c_ident = np.eye(128, dtype=np.float32)
    c_sel = np.zeros((NS, NS, 128), np.float32)
    for n in range(NS):
        c_sel[n, n, :] = 1.0
    return {"c_rope": c_rope, "c_ropes": c_ropes, "c_mask": c_mask, "c_ident": c_ident,
            "c_sel": c_sel.reshape(NS, NS * 128)}


_STAGE = 4


def kernel(x_prompt, x_sample, cache_a_k, cache_a_v, cache_b_k, cache_b_v, cache_mem_k, cache_mem_v, state_conv,
           mem_prompt, g_mix, w_in, g_out_a, g_out_b, sinks, w_out, g_cross, g_mem, w_xq, w_mem_kv, w_xo,
           g_ffn, w_up, conv_w, conv_b, w_down, g_final):
    f = lambda a: np.ascontiguousarray(np.asarray(a, dtype=np.float32))
    kb = KB(stage=_STAGE)
    nc = kb.build()
    consts = make_consts()
    shared = {
        "g_mix": f(g_mix[0]), "w_in": f(w_in[0]), "g_out_a": f(g_out_a[0]), "g_out_b": f(g_out_b[0]),
        "sinks": f(sinks[0]), "w_out": f(w_out[0]), "g_cross": f(g_cross[0]), "g_mem": f(g_mem[0]),
        "w_xq": f(w_xq[0]), "w_mem_kv": f(w_mem_kv[0]), "w_xo": f(w_xo[0]), "g_ffn": f(g_ffn[0]),
        "w_up": f(w_up[0]), "conv_w": f(conv_w[0]), "conv_b": f(conv_b[0]), "w_down": f(w_down[0]),
        "g_final": f(g_final),
    }
    shared.update(consts)
    in_maps = []
    for c in range(NCORES):
        s = slice(c * NS, (c + 1) * NS)
        m = dict(shared)
        m["xp"] = f(x_prompt[c])
        m["xs"] = f(x_sample[s, 0])
        m["cak"] = f(cache_a_k[0, s]).reshape(NS, LA, 512)
        m["cav"] = f(cache_a_v[0, s]).reshape(NS, LA, 512)
        m["cbk"] = f(cache_b_k[0, s]).reshape(NS, LB, 128)
        m["cbv"] = f(cache_b_v[0, s]).reshape(NS, LB, 128)
        m["cmk"] = f(cache_mem_k[0, s]).reshape(NS, MEM, DM)
        m["cmv"] = f(cache_mem_v[0, s]).reshape(NS, MEM, DM)
        m["sconv"] = f(state_conv[0, s])
        m["memp"] = f(mem_prompt[c])
        in_maps.append(m)
    res = run_bass_kernel_spmd(nc, in_maps, core_ids=list(range(NCORES)))
    R = res.results
    cat = lambda k: np.stack([np.asarray(R[c][k], dtype=np.float32) for c in range(NCORES)])
    catn = lambda k: np.concatenate([np.asarray(R[c][k], dtype=np.float32) for c in range(NCORES)], axis=0)
    y_prompt = cat("yp")
    y_sample = catn("ys").reshape(NCORES * NS, 1, DM)
    p_a_k = cat("pak").reshape(1, NCORES, LA, 8, 64)
    p_a_v = cat("pav").reshape(1, NCORES, LA, 8, 64)
    p_b_k = cat("pbk").reshape(1, NCORES, LB, 2, 64)
    p_b_v = cat("pbv").reshape(1, NCORES, LB, 2, 64)
    p_mem_k = cat("pmk").reshape(1, NCORES, MEM, 4, 256)
    p_mem_v = cat("pmv").reshape(1, NCORES, MEM, 4, 256)
    p_conv = cat("pconv").reshape(1, NCORES, 2, DFF)
    s_a_k = catn("sak").reshape(1, NCORES * NS, LA, 8, 64)
    s_a_v = catn("sav").reshape(1, NCORES * NS, LA, 8, 64)
    s_b_k = catn("sbk").reshape(1, NCORES * NS, LB, 2, 64)
    s_b_v = catn("sbv").reshape(1, NCORES * NS, LB, 2, 64)
    s_conv = catn("sconvo").reshape(1, NCORES * NS, 2, DFF)
    return (y_prompt, y_sample, p_a_k, p_a_v, p_b_k, p_b_v, p_mem_k, p_mem_v, p_conv,
            s_a_k, s_a_v, s_b_k, s_b_v, s_conv)
```
